# Optimizing a Trainium2 kernel written in Bass

```python
import math
import jax, jax.numpy as jnp
from jax import lax
import numpy as np


D_MODEL = 1024
BATCH = 8
SEQ = 2048
DEPTH = 2
DEC_BATCH = 8
DEC_SEQ = 32
PAST_LEN = 2048

CHUNK = 64
Q_BLOCK = 128
POOL_WIDTH = D_MODEL // 4
POOL_WINDOWS = (2, 4, 8, 16)
POOL_GROUPS = len(POOL_WINDOWS)
POOL_GROUP_DIM = POOL_WIDTH // POOL_GROUPS
POOL_HIST = max(POOL_WINDOWS) - 1
ATTN_WIDTH = D_MODEL // 2
N_HEADS = 4
HEAD_DIM = ATTN_WIDTH // (2 * N_HEADS)
V_DIM = 2 * HEAD_DIM
ROPE_THETA = 10000.0
CONV_WIDTH = D_MODEL // 4
CONV_K = 31
CONV_HIST = CONV_K - 1
MIX_WIDTH = POOL_WIDTH + ATTN_WIDTH + CONV_WIDTH
IN_WIDTH = POOL_WIDTH + 3 * ATTN_WIDTH + 2 * CONV_WIDTH
D_FF = ((-(-8 * D_MODEL // 3) + 255) // 256) * 256
EPS = 1e-6

kernel_name = "hybrid_pool_diffattn_conformer_stream_step"


def rms_norm(x, g):
    xf = x.astype(jnp.float32)
    y = xf * lax.rsqrt(jnp.mean(xf * xf, axis=-1, keepdims=True) + EPS)
    return (y * g.astype(jnp.float32)).astype(x.dtype)


def layer_norm(x, g, b):
    xf = x.astype(jnp.float32)
    mu = jnp.mean(xf, axis=-1, keepdims=True)
    xc = xf - mu
    y = xc * lax.rsqrt(jnp.mean(xc * xc, axis=-1, keepdims=True) + EPS)
    return (y * g.astype(jnp.float32) + b.astype(jnp.float32)).astype(x.dtype)


def rope(x, pos):
    half = HEAD_DIM // 2
    inv = ROPE_THETA ** (-jnp.arange(half, dtype=jnp.float32) / half)
    ang = pos.astype(jnp.float32)[:, None] * inv[None, :]
    cos = jnp.cos(ang)[:, None, None, :]
    sin = jnp.sin(ang)[:, None, None, :]
    xf = x.astype(jnp.float32)
    x1, x2 = xf[..., :half], xf[..., half:]
    return jnp.concatenate([x1 * cos - x2 * sin, x2 * cos + x1 * sin], axis=-1).astype(x.dtype)


def pool_mixer(u, hist, pos, pool_w, pool_scale):
    L = u.shape[1]
    ext = jnp.concatenate([hist, u], axis=1)
    c = jnp.cumsum(ext.astype(jnp.float32), axis=1)
    cz = jnp.concatenate([jnp.zeros_like(c[:, :1]), c], axis=1)
    uf = u.astype(jnp.float32)
    outs = []
    for g, w in enumerate(POOL_WINDOWS):
        sl = slice(g * POOL_GROUP_DIM, (g + 1) * POOL_GROUP_DIM)
        win_sum = cz[:, POOL_HIST + 1:POOL_HIST + 1 + L, sl] - cz[:, POOL_HIST + 1 - w:POOL_HIST + 1 - w + L, sl]
        count = jnp.minimum(pos + 1, w).astype(jnp.float32)[None, :, None]
        d = win_sum / count - uf[..., sl]
        outs.append(jnp.einsum('blc,cd->bld', d.astype(u.dtype), pool_w[g]))
    y = jnp.concatenate(outs, axis=-1) * pool_scale
    return y, ext[:, -POOL_HIST:]


def diff_attention(q, k, v, lam, mask):
    s = jnp.einsum('bqhcd,bkhcd->bhcqk', q, k, preferred_element_type=jnp.float32) * (HEAD_DIM ** -0.5)
    if mask is not None:
        s = jnp.where(mask, s, -1e30)
    p = jax.nn.softmax(s, axis=-1)
    pd = p[:, :, 0] - lam * p[:, :, 1]
    return jnp.einsum('bhqk,bkhe->bqhe', pd.astype(v.dtype), v)


def prompt_attention(q, k, v, lam):
    B, S = q.shape[0], q.shape[1]
    nb = S // Q_BLOCK
    qb = q.reshape(B, nb, Q_BLOCK, N_HEADS, 2, HEAD_DIM).transpose(1, 0, 2, 3, 4, 5)
    key_chunk = jnp.arange(S) // CHUNK

    def block(args):
        qi, i = args
        q_chunk = (i * Q_BLOCK + jnp.arange(Q_BLOCK)) // CHUNK
        mask = key_chunk[None, :] <= q_chunk[:, None]
        return diff_attention(qi, k, v, lam, mask)

    o = lax.map(block, (qb, jnp.arange(nb)))
    return o.transpose(1, 0, 2, 3, 4).reshape(B, S, N_HEADS, V_DIM)


def conv_mixer(a, b, hist, conv_dw, conv_dw_b, conv_ln_g, conv_ln_b, conv_pw):
    u = a * jax.nn.sigmoid(b)
    ext = jnp.concatenate([hist, u], axis=1)
    y = lax.conv_general_dilated(ext, conv_dw[:, None, :], window_strides=(1,), padding='VALID',
                                 dimension_numbers=('NWC', 'WIO', 'NWC'), feature_group_count=CONV_WIDTH)
    y = layer_norm(y + conv_dw_b, conv_ln_g, conv_ln_b)
    y = jax.nn.silu(y) @ conv_pw
    return y, ext[:, -CONV_HIST:]


def hybrid_layer(x, pos, pool_hist, conv_hist, past_k, past_v, layer,
                 norm_mix_g, w_in, pool_w, pool_scale, lambda_qk, diff_norm_g,
                 conv_dw, conv_dw_b, conv_ln_g, conv_ln_b, conv_pw, w_out,
                 norm_ffn_g, w_gate_up, w_down):
    B, L, _ = x.shape
    h = rms_norm(x, norm_mix_g)
    z = h @ w_in
    o1 = POOL_WIDTH
    o2 = o1 + ATTN_WIDTH
    o3 = o2 + ATTN_WIDTH
    o4 = o3 + ATTN_WIDTH
    o5 = o4 + CONV_WIDTH
    u_pool = z[..., :o1]
    q = rope(z[..., o1:o2].reshape(B, L, N_HEADS, 2, HEAD_DIM), pos)
    k = rope(z[..., o2:o3].reshape(B, L, N_HEADS, 2, HEAD_DIM), pos)
    v = z[..., o3:o4].reshape(B, L, N_HEADS, V_DIM)
    a_conv = z[..., o4:o5]
    b_conv = z[..., o5:]

    y_pool, new_pool = pool_mixer(u_pool, pool_hist, pos, pool_w, pool_scale)

    lam_init = 0.8 - 0.6 * math.exp(-0.3 * layer)
    lq = lambda_qk.astype(jnp.float32)
    lam = jnp.exp(jnp.sum(lq[0] * lq[1])) - jnp.exp(jnp.sum(lq[2] * lq[3])) + lam_init
    if past_k is None:
        o = prompt_attention(q, k, v, lam)
    else:
        o = diff_attention(q, jnp.concatenate([past_k, k], axis=1), jnp.concatenate([past_v, v], axis=1), lam, None)
    o = (rms_norm(o, diff_norm_g) * (1.0 - lam_init)).reshape(B, L, ATTN_WIDTH)

    y_conv, new_conv = conv_mixer(a_conv, b_conv, conv_hist, conv_dw, conv_dw_b, conv_ln_g, conv_ln_b, conv_pw)

    x = x + jnp.concatenate([y_pool, o, y_conv], axis=-1) @ w_out
    h = rms_norm(x, norm_ffn_g)
    g, u = jnp.split(h @ w_gate_up, 2, axis=-1)
    x = x + (jax.nn.silu(g) * u) @ w_down
    return x, k, v, new_pool, new_conv


def setup_inputs(seed: int = 0) -> dict:
    key = jax.random.key(seed)
    ks = jax.random.split(key, 24)
    f = jnp.float32
    nrm = lambda k, shape, s: jax.random.normal(k, shape, f) * s
    return {
        "x_prompt": nrm(ks[0], (BATCH, SEQ, D_MODEL), 1.0),
        "x_sample": nrm(ks[1], (DEC_BATCH, DEC_SEQ, D_MODEL), 1.0),
        "cache_k": nrm(ks[2], (DEPTH, DEC_BATCH, PAST_LEN, N_HEADS, 2, HEAD_DIM), 1.0),
        "cache_v": nrm(ks[3], (DEPTH, DEC_BATCH, PAST_LEN, N_HEADS, V_DIM), 1.0),
        "state_pool": nrm(ks[4], (DEPTH, DEC_BATCH, POOL_HIST, POOL_WIDTH), 1.0),
        "state_conv": nrm(ks[5], (DEPTH, DEC_BATCH, CONV_HIST, CONV_WIDTH), 0.5),
        "norm_mix_g": 1.0 + nrm(ks[6], (DEPTH, D_MODEL), 0.05),
        "w_in": nrm(ks[7], (DEPTH, D_MODEL, IN_WIDTH), D_MODEL ** -0.5),
        "pool_w": nrm(ks[8], (DEPTH, POOL_GROUPS, POOL_GROUP_DIM, POOL_GROUP_DIM), POOL_GROUP_DIM ** -0.5),
        "pool_scale": 1.0 + nrm(ks[9], (DEPTH, POOL_WIDTH), 0.1),
        "lambda_qk": nrm(ks[10], (DEPTH, 4, HEAD_DIM), 0.1),
        "diff_norm_g": 1.0 + nrm(ks[11], (DEPTH, V_DIM), 0.05),
        "conv_dw": nrm(ks[12], (DEPTH, CONV_K, CONV_WIDTH), CONV_K ** -0.5),
        "conv_dw_b": nrm(ks[13], (DEPTH, CONV_WIDTH), 0.01),
        "conv_ln_g": 1.0 + nrm(ks[14], (DEPTH, CONV_WIDTH), 0.05),
        "conv_ln_b": nrm(ks[15], (DEPTH, CONV_WIDTH), 0.01),
        "conv_pw": nrm(ks[16], (DEPTH, CONV_WIDTH, CONV_WIDTH), CONV_WIDTH ** -0.5),
        "w_out": nrm(ks[17], (DEPTH, MIX_WIDTH, D_MODEL), MIX_WIDTH ** -0.5),
        "norm_ffn_g": 1.0 + nrm(ks[18], (DEPTH, D_MODEL), 0.05),
        "w_gate_up": nrm(ks[19], (DEPTH, D_MODEL, 2 * D_FF), D_MODEL ** -0.5),
        "w_down": nrm(ks[20], (DEPTH, D_FF, D_MODEL), D_FF ** -0.5),
        "final_norm_g": 1.0 + nrm(ks[21], (D_MODEL,), 0.05),
    }


def reference(x_prompt, x_sample, cache_k, cache_v, state_pool, state_conv,
              norm_mix_g, w_in, pool_w, pool_scale, lambda_qk, diff_norm_g,
              conv_dw, conv_dw_b, conv_ln_g, conv_ln_b, conv_pw, w_out,
              norm_ffn_g, w_gate_up, w_down, final_norm_g):
    B, S, _ = x_prompt.shape
    Ls = x_sample.shape[1]
    past_len = cache_k.shape[2]
    pos_p = jnp.arange(S, dtype=jnp.int32)
    pos_s = past_len + jnp.arange(Ls, dtype=jnp.int32)
    zero_pool = jnp.zeros((B, POOL_HIST, POOL_WIDTH), x_prompt.dtype)
    zero_conv = jnp.zeros((B, CONV_HIST, CONV_WIDTH), x_prompt.dtype)

    xp, xs = x_prompt, x_sample
    kp, vp, pp, cp = [], [], [], []
    ks_, vs_, ps_, cs_ = [], [], [], []
    for l in range(DEPTH):
        w = (norm_mix_g[l], w_in[l], pool_w[l], pool_scale[l], lambda_qk[l], diff_norm_g[l],
             conv_dw[l], conv_dw_b[l], conv_ln_g[l], conv_ln_b[l], conv_pw[l], w_out[l],
             norm_ffn_g[l], w_gate_up[l], w_down[l])
        xp, k1, v1, p1, c1 = hybrid_layer(xp, pos_p, zero_pool, zero_conv, None, None, l, *w)
        xs, k2, v2, p2, c2 = hybrid_layer(xs, pos_s, state_pool[l], state_conv[l], cache_k[l], cache_v[l], l, *w)
        kp.append(k1); vp.append(v1); pp.append(p1); cp.append(c1)
        ks_.append(k2); vs_.append(v2); ps_.append(p2); cs_.append(c2)

    y_prompt = rms_norm(xp, final_norm_g)
    y_sample = rms_norm(xs, final_norm_g)
    return (y_prompt, y_sample,
            jnp.stack(kp), jnp.stack(vp), jnp.stack(pp), jnp.stack(cp),
            jnp.stack(ks_), jnp.stack(vs_), jnp.stack(ps_), jnp.stack(cs_))
```

```python
import math
from contextlib import ExitStack

import numpy as np
import concourse.bass as bass
import concourse.mybir as mybir
from concourse.bass_utils import run_bass_kernel_spmd

F32 = mybir.dt.float32
BF16 = mybir.dt.bfloat16
ALU = mybir.AluOpType
AF = mybir.ActivationFunctionType

EPS = 1e-6
SCALE = 0.125
NT = 17
NTOK = 2080
DEPTH = 2
NPV = 73
FFN_PASSES = [(0, 8), (8, 15), (15, 22)]
GROUPS = [(0, 512, [0, 1, 2, 3]), (512, 512, [4, 5, 6, 7]), (1024, 512, [8, 9, 10, 11]),
          (1536, 512, [12, 13, 14, 15]), (2048, 32, [16])]
NEG = -30000.0


def TP(t):
    return 128 if t < 16 else 32


import os
STOP_AFTER = 0


class _Stop(Exception):
    pass


class Sched:
    def __init__(self, nc, es):
        self.nc = nc
        self.es = es
        self.eng = {'pe': nc.tensor, 'act': nc.scalar, 'dve': nc.vector, 'pool': nc.gpsimd, 'sp': nc.sync}
        self.csem = {e: es.enter_context(nc.semaphore('c_' + e)) for e in ('pe', 'act', 'dve', 'pool')}
        self.ccount = {e: 0 for e in self.csem}
        self.dsem = {}
        self.lastw = {}
        self.readers = {}
        self.waited = {e: {} for e in self.eng}
        self.out_events = []
        self.stopped = False
        self.nbar = 0

    def _wait(self, eng, ev):
        sem, val, src, kind, sid = ev
        w = self.waited[eng]
        if w.get(sid, 0) >= val:
            return
        w[sid] = val
        self.eng[eng].wait_ge(sem, val)

    def _deps(self, eng, reads, writes):
        deps = []
        for k in reads:
            ev = self.lastw.get(k)
            if ev is not None:
                deps.append((ev, True))
        for k in writes:
            ev = self.lastw.get(k)
            if ev is not None:
                deps.append((ev, False))
            for ev in self.readers.get(k, {}).values():
                deps.append((ev, False))
        for ev, raw in deps:
            if ev[3] == 'c' and ev[2] == eng:
                if eng == 'pe' or not raw:
                    continue
            self._wait(eng, ev)

    def _commit(self, ev, reads, writes):
        for k in writes:
            self.lastw[k] = ev
            self.readers[k] = {}
        for k in reads:
            self.readers.setdefault(k, {})[ev[4]] = ev

    def op(self, eng, fn, reads=(), writes=()):
        if self.stopped:
            return
        self._deps(eng, reads, writes)
        ins = fn(self.eng[eng])
        self.ccount[eng] += 1
        ins.then_inc(self.csem[eng], 1)
        ev = (self.csem[eng], self.ccount[eng], eng, 'c', 'c_' + eng)
        self._commit(ev, reads, writes)

    def dma(self, q, key, out, in_, reads=(), writes=(), is_out=False):
        if self.stopped:
            return
        self._deps(q, reads, writes)
        if key not in self.dsem:
            self.dsem[key] = [self.es.enter_context(self.nc.semaphore('d%d' % len(self.dsem))), 0]
        d = self.dsem[key]
        d[1] += 16
        self.eng[q].dma_start(out=out, in_=in_).then_inc(d[0], 16)
        ev = (d[0], d[1], q, 'd', 'd_' + str(key))
        self._commit(ev, reads, writes)
        if is_out:
            self.out_events.append(ev)

    def barrier(self):
        if self.stopped:
            return
        self.nbar += 1
        self._barrier()
        if self.nbar == STOP_AFTER:
            self.stopped = True

    def _barrier(self):
        evs = [(self.csem[e], self.ccount[e], e, 'c', 'c_' + e) for e in self.csem if self.ccount[e] > 0]
        evs += [(d[0], d[1], 'x', 'd', 'd_' + str(k)) for k, d in self.dsem.items() if d[1] > 0]
        for eng in self.eng:
            for ev in evs:
                if ev[3] == 'c' and ev[2] == eng:
                    continue
                self._wait(eng, ev)
        self.lastw = {}
        self.readers = {}

    def finish(self):
        for ev in self.out_events:
            self._wait('sp', ev)


def build_program():
    nc = bass.Bass("TRN2", target_bir_lowering=False)

    def din(name, shape):
        return nc.dram_tensor(name, list(shape), F32, kind="ExternalInput").ap()

    def dout(name, shape):
        return nc.dram_tensor(name, list(shape), F32, kind="ExternalOutput").ap()

    xp = din("xp", [2048, 1024]); xs = din("xs", [32, 1024])
    ck = din("ck", [2, 2048, 512]); cv = din("cv", [2, 2048, 512])
    stp = din("stp", [2, 128, 2, 15]); stc = din("stc", [2, 128, 2, 30])
    w_in = din("w_in", [2, 1024, 2304]); w_out = din("w_out", [2, 1024, 1024])
    w_gu = din("w_gu", [2, 1024, 5632]); w_dn = din("w_dn", [2, 2816, 1024])
    conv_pw = din("conv_pw", [2, 256, 256]); plw = din("plw", [2, 128, 2, 128])
    gmix = din("gmix", [2, 1024]); gffn = din("gffn", [2, 1024]); gfin = din("gfin", [1, 1024])
    pvec = din("pvec", [2, 128, NPV]); lq = din("lq", [2, 256])
    ident = din("ident", [128, 128]); cos_t = din("cos_t", [128, 17, 32]); sin_t = din("sin_t", [128, 17, 32])
    invc = din("invc", [128, 2, 16])

    yp = dout("yp", [2048, 1024]); ys = dout("ys", [32, 1024])
    nkp = dout("nkp", [2, 2048, 512]); nvp = dout("nvp", [2, 2048, 512])
    npp = dout("npp", [2, 128, 2, 15]); ncp = dout("ncp", [2, 128, 2, 30])
    nks = dout("nks", [2, 32, 512]); nvs = dout("nvs", [2, 32, 512])
    nps = dout("nps", [2, 128, 2, 15]); ncs = dout("ncs", [2, 128, 2, 30])

    def rows_y(t):
        return yp[t * 128:(t + 1) * 128, :] if t < 16 else ys[0:32, :]

    def rows_k(l, t):
        return nkp[l, t * 128:(t + 1) * 128, :] if t < 16 else nks[l, 0:32, :]

    def rows_v(l, t):
        return nvp[l, t * 128:(t + 1) * 128, :] if t < 16 else nvs[l, 0:32, :]

    with ExitStack() as es:
        S = Sched(nc, es)

        uid = [0]

        def sb(stack, name, shape, dt):
            uid[0] += 1
            return stack.enter_context(nc.sbuf_tensor("%s_%d" % (name, uid[0]), list(shape), dt))

        PS = es.enter_context(nc.psum_tensor("PS", [128, 8, 512], F32))

        def psb(b):
            return PS[:, b, :].bitcast(BF16)

        X = sb(es, "X", [128, NT, 1024], F32)
        H = sb(es, "H", [128, 8, NTOK], BF16)
        FA = sb(es, "FA", [128, 25344], BF16)
        QT = FA[:, 0:8320].rearrange("p (h n) -> p h n", h=4)
        KT = FA[:, 8320:16640].rearrange("p (h n) -> p h n", h=4)
        VB = FA[:, 16640:25344].rearrange("p (t n) -> p t n", t=NT)
        ACTH = FA[:, 0:16640].rearrange("p (k n) -> p k n", k=8)
        idb = sb(es, "idb", [128, 128], BF16)
        onesb = sb(es, "onesb", [128, 128], BF16)
        onesf = sb(es, "onesf", [128, 3, 128], F32)
        PV = sb(es, "PV", [128, 2, NPV], F32)
        LAM = sb(es, "LAM", [128, 2, 8], F32)
        LQ = sb(es, "LQ", [128, 2, 256], F32)
        ljunk = sb(es, "ljunk", [128, 64], F32)

        S.dma('pool', 'c_id', out=idb[:], in_=ident[:, :], writes=['idb'])
        S.dma('sp', 'c_pv', out=PV[:], in_=pvec.rearrange("l p n -> p l n"), writes=['PV'])
        S.dma('sp', 'c_lq', out=LQ[:, 0, :], in_=lq[0:1, :].partition_broadcast(128), writes=[('LQ', 0)])
        S.dma('sp', 'c_lq1', out=LQ[:, 1, :], in_=lq[1:2, :].partition_broadcast(128), writes=[('LQ', 1)])
        S.dma('sp', 'xs', out=X[:32, 16, :], in_=xs[:, :], writes=[('X', 16)])
        xpv = xp.rearrange("(t p) d -> p t d", p=128)
        for q4 in range(4):
            S.dma('sp', 'x%d' % q4, out=X[:, q4 * 4:(q4 + 1) * 4, :], in_=xpv[:, q4 * 4:(q4 + 1) * 4, :],
                  writes=[('X', t) for t in range(q4 * 4, q4 * 4 + 4)])
        S.op('dve', lambda e: e.memset(onesb[:], 1.0), writes=['onesb'])
        S.op('dve', lambda e: e.memset(onesf[:, 0, :], 1.0 / 128), writes=['onesf0'])
        S.op('dve', lambda e: e.memset(onesf[:, 1, :], 1.0 / 256), writes=['onesf1'])
        S.op('dve', lambda e: e.memset(onesf[:, 2, :], 1.0), writes=['onesf2'])
        for l in range(DEPTH):
            lam_init = 0.8 - 0.6 * math.exp(-0.3 * l)
            S.op('dve', lambda e: e.scalar_tensor_tensor(out=ljunk[:], in0=LQ[:, l, 0:64], scalar=1.0, in1=LQ[:, l, 64:128],
                                                         op0=ALU.mult, op1=ALU.mult, accum_out=LAM[:, l, 0:1]),
                 reads=[('LQ', l)], writes=['ljunk', ('LAM', l, 0)])
            S.op('dve', lambda e: e.scalar_tensor_tensor(out=ljunk[:], in0=LQ[:, l, 128:192], scalar=1.0, in1=LQ[:, l, 192:256],
                                                         op0=ALU.mult, op1=ALU.mult, accum_out=LAM[:, l, 1:2]),
                 reads=[('LQ', l)], writes=['ljunk', ('LAM', l, 1)])
            S.op('act', lambda e: e.activation(out=LAM[:, l, 2:4], in_=LAM[:, l, 0:2], func=AF.Exp),
                 reads=[('LAM', l, 0), ('LAM', l, 1)], writes=[('LAM', l, 2)])
            S.op('dve', lambda e: e.tensor_tensor(out=LAM[:, l, 6:7], in0=LAM[:, l, 3:4], in1=LAM[:, l, 2:3], op=ALU.subtract),
                 reads=[('LAM', l, 2)], writes=[('LAM', l, 6)])
            S.op('dve', lambda e: e.tensor_scalar(out=LAM[:, l, 4:5], in0=LAM[:, l, 6:7], scalar1=-lam_init, scalar2=None, op0=ALU.add),
                 reads=[('LAM', l, 6)], writes=[('LAM', l, 4)])
            S.op('dve', lambda e: e.tensor_scalar(out=LAM[:, l, 5:6], in0=PV[:, l, 8:9], scalar1=(1.0 - lam_init), scalar2=None, op0=ALU.mult),
                 reads=['PV'], writes=[('LAM', l, 5)])
        S._barrier()

        def rstd_chain(stt, P, t, src_key):
            S.op('dve', lambda e: e.tensor_scalar(out=stt[:P, t, 1:2], in0=stt[:P, t, 0:1], scalar1=1.0 / 1024, scalar2=EPS,
                                                  op0=ALU.mult, op1=ALU.add),
                 reads=[('st', t, 0)], writes=[('st', t, 1)])
            S.op('act', lambda e: e.activation(out=stt[:P, t, 2:3], in_=stt[:P, t, 1:2], func=AF.Ln),
                 reads=[('st', t, 1)], writes=[('st', t, 2)])
            S.op('act', lambda e: e.activation(out=stt[:P, t, 3:4], in_=stt[:P, t, 2:3], func=AF.Exp, scale=-0.5),
                 reads=[('st', t, 2)], writes=[('st', t, 3)])

        def phase_norm(g_row, tgroups=None):
            tgroups = tgroups or [list(range(NT))]
            with ExitStack() as ph:
                G = sb(ph, "G", [128, 1024], F32)
                hb = [sb(ph, "hb%d" % i, [128, 1024], BF16) for i in range(3)]
                junk = sb(ph, "junk", [128, 1024], BF16)
                junk2 = sb(ph, "junk2", [128, 1024], BF16)
                stt = sb(ph, "nst", [128, 4, NT], F32)
                S.dma('sp', 'G', out=G[:], in_=g_row.partition_broadcast(128), writes=['G'])
                S.op('dve', lambda e: e.memset(stt[:, 0, :], 1.0), writes=['st0i'])
                for tg in tgroups:
                    t0g, t1g = tg[0], tg[-1] + 1
                    for t in tg:
                        P = TP(t)
                        if t % 2 == 0:
                            S.op('act', lambda e: e.activation(out=junk[:P], in_=X[:P, t, :], func=AF.Square, accum_out=stt[:P, 0, t:t + 1]),
                                 reads=[('X', t), 'st0i'], writes=['junk', ('st0', t)])
                        else:
                            S.op('dve', lambda e: e.scalar_tensor_tensor(out=junk2[:P], in0=X[:P, t, :], scalar=1.0, in1=X[:P, t, :],
                                                                         op0=ALU.mult, op1=ALU.mult, accum_out=stt[:P, 0, t:t + 1]),
                                 reads=[('X', t), 'st0i'], writes=['junk2', ('st0', t)])
                    S.op('dve', lambda e: e.tensor_scalar(out=stt[:, 1, t0g:t1g], in0=stt[:, 0, t0g:t1g], scalar1=1.0 / 1024, scalar2=EPS,
                                                          op0=ALU.mult, op1=ALU.add),
                         reads=[('st0', t) for t in tg], writes=[('st1', t0g)])
                    S.op('act', lambda e: e.activation(out=stt[:, 2, t0g:t1g], in_=stt[:, 1, t0g:t1g], func=AF.Ln),
                         reads=[('st1', t0g)], writes=[('st2', t0g)])
                    S.op('act', lambda e: e.activation(out=stt[:, 3, t0g:t1g], in_=stt[:, 2, t0g:t1g], func=AF.Exp, scale=-0.5),
                         reads=[('st2', t0g)], writes=[('st3', t0g)])
                    for t in tg:
                        P = TP(t); c = t * 128
                        hs = t % 3
                        S.op('dve', lambda e: e.scalar_tensor_tensor(out=hb[hs][:P], in0=X[:P, t, :], scalar=stt[:P, 3, t:t + 1], in1=G[:P],
                                                                     op0=ALU.mult, op1=ALU.mult),
                             reads=[('X', t), ('st3', t0g), 'G'], writes=[('hb', hs)])
                        bank = t % 4
                        pv = psb(bank).rearrange("p (k n) -> p k n", k=8)

                        def tr(e):
                            for kc in range(8):
                                ins = e.transpose(out=pv[:, kc, :P], in_=hb[hs][:P, kc * 128:(kc + 1) * 128], identity=idb[:P, :P])
                            return ins
                        S.op('pe', tr, reads=[('hb', hs), 'idb'], writes=[('ps', bank)])
                        S.op('act', lambda e: e.activation(out=H[:, :, c:c + P], in_=pv[:, :, :P], func=AF.Copy),
                             reads=[('ps', bank)], writes=[('H', t)])
                S.barrier()

        def load_slab(q, key, dst, src3, kcs, c0, w, wkey):
            S.dma(q, key, out=dst, in_=src3[:, kcs[0]:kcs[-1] + 1, c0:c0 + w], writes=[wkey])

        if True:
          for l in range(DEPTH):
            w_in3 = w_in[l].rearrange("(k p) c -> p k c", p=128)
            w_out3 = w_out[l].rearrange("(k p) c -> p k c", p=128)
            w_gu3 = w_gu[l].rearrange("(k p) c -> p k c", p=128)
            w_dn3 = w_dn[l].rearrange("(k p) c -> p k c", p=128)
            pw3 = conv_pw[l].rearrange("(k p) c -> p k c", p=128)

            WP = FA[:, 16640:22784].rearrange("p (k c) -> p k c", k=8)
            WOP = FA[:, 0:4096].rearrange("p (k c) -> p k c", k=4)
            if l == 0:
                S.dma('pool', 'wp0', out=WP[:, :, 0:256], in_=w_in3[:, :, 0:256], reads=[('X', 11)], writes=['WP0'])
                S.dma('pool', 'wp1', out=WP[:, :, 256:768], in_=w_in3[:, :, 1792:2304], writes=['WP1'])
            S.dma('pool', 'wop0', out=WOP[:, 0:2, :], in_=w_out3[:, 0:2, :], writes=['WOP0'])
            S.dma('pool', 'wop1', out=WOP[:, 2:4, :], in_=w_out3[:, 6:8, :], writes=['WOP1'])
            if l == 0:
                phase_norm(gmix[l:l + 1, :], [[0, 1, 2, 3], [4, 5, 6, 7], [8, 9, 10, 11], [12, 13, 14, 15, 16]])

            phq = ExitStack()
            Wq0 = sb(phq, "Wq0", [128, 8, 512], BF16)
            S.dma('pool', 'wq0', out=Wq0[:], in_=w_in3[:, :, 256:768], writes=[('W', 0)])
            with ExitStack() as ph:
                DG = FA[:, 4096:12032].rearrange("p (k c) -> p k c", k=62)
                CU = [FA[:, 12032 + i * 1084:12032 + (i + 1) * 1084].rearrange("p (j c) -> p j c", j=2) for i in range(2)]
                dsb = FA[:, 22784:23808].rearrange("p (j c) -> p j c", j=2)
                MPC2 = [FA[:, 14200:16248].rearrange("p (j c) -> p j c", j=4), sb(ph, "MPC1", [128, 4, 512], BF16)]
                sl2 = [FA[:, 23808:24832].rearrange("p (j c) -> p j c", j=2), sb(ph, "sl1", [128, 2, 512], BF16)]
                PW = sb(ph, "PW", [128, 2, 256], BF16)
                PLW = sb(ph, "PLW", [128, 2, 128], BF16)
                INVC = sb(ph, "INVC", [128, 2, 16], F32)
                PU = [sb(ph, "PU%d" % i, [128, 2, 527], F32) for i in range(2)]
                U32 = sb(ph, "U32", [128, 2, 512], F32)
                WA = sb(ph, "WA", [128, 526], F32); WB = sb(ph, "WB", [128, 524], F32)
                WC = WA; WD = WB
                t16 = sb(ph, "t16", [128, 16], F32)
                sg2 = [sb(ph, "sg%d" % i, [128, 512], F32) for i in range(2)]
                ysb = sb(ph, "ysb", [128, 2, 512], F32)
                ysq = sb(ph, "ysq", [128, 2, 512], F32)
                m2 = sb(ph, "m2", [128, 512], F32); var = sb(ph, "var", [128, 512], F32)

                S.dma('pool', 'pw', out=PW[:], in_=pw3, writes=['PW'])
                S.dma('pool', 'plw', out=PLW[:], in_=plw[l], writes=['PLW'])
                S.dma('sp', 'invc', out=INVC[:], in_=invc[:, :, :], writes=['INVC'])
                S.op('dve', lambda e: e.tensor_tensor(out=DG, in0=idb[:].unsqueeze(1).broadcast_to([128, 62, 128]),
                                                      in1=PV[:, l, 11:73].unsqueeze(2).broadcast_to([128, 62, 128]), op=ALU.mult),
                     reads=['PV', 'idb'], writes=['DG'])
                S.op('dve', lambda e: e.memset(PU[0][:, :, 0:15], 0.0), writes=[('PUc', 0)])
                S.op('dve', lambda e: e.memset(CU[0][:, :, 0:30], 0.0), writes=[('CUc', 0)])

                def stageA(gi):
                    c0, n, tiles = GROUPS[gi]
                    sample = (gi == 4)
                    slot = 0 if sample else gi % 2
                    if sample:
                        S.dma('sp', 'stp', out=PU[0][:, :, 0:15], in_=stp[l], writes=[('PUc', 0)])
                        S.dma('pool', 'stc', out=CU[0][:, :, 0:30], in_=stc[l], writes=[('CUc', 0)])
                    def proj(bank, col):
                        def f(e):
                            for kc in range(8):
                                ins = e.matmul(PS[:, bank, 0:n], lhsT=WP[:, kc, col:col + 128], rhs=H[:, kc, c0:c0 + n],
                                               start=(kc == 0), stop=(kc == 7))
                            return ins
                        S.op('pe', f, reads=['WP0', 'WP1'] + [('H', t) for t in tiles], writes=[('ps', bank)])
                    for j in range(2):
                        proj(j, j * 128)
                        S.op('act', lambda e: e.activation(out=PU[slot][:, j, 15:15 + n], in_=PS[:, j, 0:n], func=AF.Copy),
                             reads=[('ps', j)], writes=[('PUn', slot, j)])
                    if gi == 3 or sample:
                        S.dma('sp', 'o_np%d' % gi, out=(nps[l] if sample else npp[l]), in_=PU[slot][:, :, n:n + 15],
                              reads=[('PUc', slot), ('PUn', slot, 0), ('PUn', slot, 1)], is_out=True)
                    if gi < 3:
                        S.op('dve', lambda e: e.tensor_copy(out=PU[(gi + 1) % 2][:, :, 0:15], in_=PU[slot][:, :, 512:527]),
                             reads=[('PUn', slot, 0), ('PUn', slot, 1)], writes=[('PUc', (gi + 1) % 2)])
                    for j in range(2):
                        ext = PU[slot][:, j, :]
                        rk = [('PUc', slot), ('PUn', slot, j)]
                        S.op('dve', lambda e: e.tensor_tensor(out=WA[:, 0:14 + n], in0=ext[:, 1:15 + n], in1=ext[:, 0:14 + n], op=ALU.add),
                             reads=rk, writes=['WA'])
                        S.op('dve', lambda e: e.tensor_tensor(out=WB[:, 0:12 + n], in0=WA[:, 2:14 + n], in1=WA[:, 0:12 + n], op=ALU.add),
                             reads=['WA'], writes=['WB'])
                        if j == 0:
                            lo, lo_off, hi, hi_off = WA, 14, WB, 12
                            kk = ['WA', 'WB']
                        else:
                            S.op('dve', lambda e: e.tensor_tensor(out=WC[:, 0:8 + n], in0=WB[:, 4:12 + n], in1=WB[:, 0:8 + n], op=ALU.add),
                                 reads=['WB'], writes=['WA'])
                            S.op('dve', lambda e: e.tensor_tensor(out=WD[:, 0:n], in0=WC[:, 8:8 + n], in1=WC[:, 0:n], op=ALU.add),
                                 reads=['WA'], writes=['WB'])
                            lo, lo_off, hi, hi_off = WC, 8, WD, 0
                            kk = ['WA', 'WB']
                        for (src, off, p0) in ((lo, lo_off, 0), (hi, hi_off, 64)):
                            S.op('dve', lambda e: e.scalar_tensor_tensor(out=dsb[p0:p0 + 64, j, 0:n], in0=src[p0:p0 + 64, off:off + n],
                                                                         scalar=PV[p0:p0 + 64, l, 9 + j:10 + j], in1=ext[p0:p0 + 64, 15:15 + n],
                                                                         op0=ALU.mult, op1=ALU.subtract),
                                 reads=kk + rk + ['PV'], writes=[('dsb', j)])
                            if gi == 0:
                                S.op('dve', lambda e: e.tensor_tensor(out=t16[p0:p0 + 64, :], in0=src[p0:p0 + 64, off:off + 16],
                                                                      in1=INVC[p0:p0 + 64, j, :], op=ALU.mult),
                                     reads=kk + ['INVC'], writes=['t16'])
                                S.op('dve', lambda e: e.tensor_tensor(out=dsb[p0:p0 + 64, j, 0:16], in0=t16[p0:p0 + 64, :],
                                                                      in1=ext[p0:p0 + 64, 15:31], op=ALU.subtract),
                                     reads=['t16'] + rk, writes=[('dsb', j)])
                    for j in range(2):
                        proj(2 + 2 * j, 256 + j * 128)
                        proj(3 + 2 * j, 512 + j * 128)
                        S.op('act', lambda e: e.activation(out=sg2[j][:, 0:n], in_=PS[:, 3 + 2 * j, 0:n], func=AF.Sigmoid),
                             reads=[('ps', 3 + 2 * j)], writes=[('sg', j)])
                        S.op('dve', lambda e: e.tensor_tensor(out=U32[:, j, 0:n], in0=PS[:, 2 + 2 * j, 0:n], in1=sg2[j][:, 0:n], op=ALU.mult),
                             reads=[('ps', 2 + 2 * j), ('sg', j)], writes=[('U32', j)])
                        S.op('pool', lambda e: e.tensor_copy(out=CU[slot][:, j, 30:30 + n], in_=U32[:, j, 0:n]),
                             reads=[('U32', j)], writes=[('CUn', slot, j)])
                    if gi == 3 or sample:
                        S.dma('sp', 'o_nc%d' % gi, out=(ncs[l] if sample else ncp[l]), in_=U32[:, :, n - 30:n],
                              reads=[('U32', 0), ('U32', 1)], is_out=True)
                    if gi < 3:
                        S.op('dve', lambda e: e.tensor_copy(out=CU[(gi + 1) % 2][:, :, 0:30], in_=CU[slot][:, :, 512:542]),
                             reads=[('CUn', slot, 0), ('CUn', slot, 1)], writes=[('CUc', (gi + 1) % 2)])

                def stageC(gi):
                    c0, n, tiles = GROUPS[gi]
                    sample = (gi == 4)
                    slot = 0 if sample else gi % 2
                    MPC = MPC2[gi % 2]; sl = sl2[gi % 2]
                    for j in range(2):
                        S.op('pe', lambda e: e.matmul(PS[:, j, 0:n], lhsT=PLW[:, j, :], rhs=dsb[:, j, 0:n], start=True, stop=True),
                             reads=[('dsb', j), 'PLW'], writes=[('ps', j)])
                        S.op('act', lambda e: e.activation(out=MPC[:, j, 0:n], in_=PS[:, j, 0:n], func=AF.Copy, scale=PV[:, l, j:j + 1]),
                             reads=[('ps', j), 'PV'], writes=[('MPC', gi % 2, j)])
                    for j in range(2):
                        def cv_mm(e):
                            for tap in range(31):
                                ins = e.matmul(PS[:, 4 + j, 0:n], lhsT=DG[:, j * 31 + tap, :], rhs=CU[slot][:, j, tap:tap + n],
                                               start=(tap == 0), stop=(tap == 30))
                            return ins
                        S.op('pe', cv_mm, reads=['DG', ('CUc', slot), ('CUn', slot, j)], writes=[('ps', 4 + j)])
                        S.op('act', lambda e: e.activation(out=ysb[:, j, 0:n], in_=PS[:, 4 + j, 0:n], func=AF.Identity, bias=PV[:, l, 2 + j:3 + j]),
                             reads=[('ps', 4 + j), 'PV'], writes=[('ysb', j)])
                        S.op('act', lambda e: e.activation(out=ysq[:, j, 0:n], in_=PS[:, 4 + j, 0:n], func=AF.Square, bias=PV[:, l, 2 + j:3 + j]),
                             reads=[('ps', 4 + j), 'PV'], writes=[('ysq', j)])

                    def st_mm(e):
                        for j in range(2):
                            e.matmul(PS[:, 6, 0:n], lhsT=onesf[:, 1, :], rhs=ysb[:, j, 0:n], start=(j == 0), stop=(j == 1))
                        for j in range(2):
                            ins = e.matmul(PS[:, 7, 0:n], lhsT=onesf[:, 1, :], rhs=ysq[:, j, 0:n], start=(j == 0), stop=(j == 1))
                        return ins
                    S.op('pe', st_mm, reads=[('ysb', 0), ('ysb', 1), ('ysq', 0), ('ysq', 1), 'onesf1'], writes=[('ps', 6), ('ps', 7)])
                    S.op('act', lambda e: e.activation(out=m2[:, 0:n], in_=PS[:, 6, 0:n], func=AF.Square), reads=[('ps', 6)], writes=['m2'])
                    S.op('dve', lambda e: e.scalar_tensor_tensor(out=var[:, 0:n], in0=PS[:, 7, 0:n], scalar=EPS, in1=m2[:, 0:n],
                                                                 op0=ALU.add, op1=ALU.subtract),
                         reads=[('ps', 7), 'm2'], writes=['var'])
                    S.op('act', lambda e: e.activation(out=m2[:, 0:n], in_=var[:, 0:n], func=AF.Ln), reads=['var'], writes=['m2'])
                    S.op('act', lambda e: e.activation(out=var[:, 0:n], in_=m2[:, 0:n], func=AF.Exp, scale=-0.5), reads=['m2'], writes=['var'])
                    for j in range(2):
                        S.op('dve', lambda e: e.tensor_tensor(out=ysq[:, j, 0:n], in0=ysb[:, j, 0:n], in1=PS[:, 6, 0:n], op=ALU.subtract),
                             reads=[('ysb', j), ('ps', 6)], writes=[('ysq', j)])
                        S.op('dve', lambda e: e.tensor_tensor(out=ysb[:, j, 0:n], in0=ysq[:, j, 0:n], in1=var[:, 0:n], op=ALU.mult),
                             reads=[('ysq', j), 'var'], writes=[('ysb', j)])
                        S.op('act', lambda e: e.activation(out=sl[:, j, 0:n], in_=ysb[:, j, 0:n], func=AF.Silu,
                                                           scale=PV[:, l, 4 + j:5 + j], bias=PV[:, l, 6 + j:7 + j]),
                             reads=[('ysb', j), 'PV'], writes=[('sl', gi % 2, j)])

                def stageB(gi):
                    c0, n, tiles = GROUPS[gi]
                    MPC = MPC2[gi % 2]; sl = sl2[gi % 2]
                    for jo in range(2):
                        def pw_mm(e):
                            for j in range(2):
                                ins = e.matmul(PS[:, jo, 0:n], lhsT=PW[:, j, jo * 128:(jo + 1) * 128], rhs=sl[:, j, 0:n],
                                               start=(j == 0), stop=(j == 1))
                            return ins
                        S.op('pe', pw_mm, reads=[('sl', gi % 2, 0), ('sl', gi % 2, 1), 'PW'], writes=[('ps', jo)])
                        S.op('act', lambda e: e.activation(out=MPC[:, 2 + jo, 0:n], in_=PS[:, jo, 0:n], func=AF.Copy),
                             reads=[('ps', jo)], writes=[('MPC', gi % 2, 2 + jo)])
                    for ti, t in enumerate(tiles):
                        P = TP(t)
                        for hh in range(2):
                            bank = 6 + hh

                            def wo_mm(e):
                                for kc in range(4):
                                    ins = e.matmul(PS[:P, bank, :], lhsT=MPC[:, kc, ti * 128:ti * 128 + P],
                                                   rhs=WOP[:, kc, hh * 512:(hh + 1) * 512], start=(kc == 0), stop=(kc == 3))
                                return ins
                            S.op('pe', wo_mm, reads=[('MPC', gi % 2, k) for k in range(4)] + ['WOP0', 'WOP1'], writes=[('ps', bank)])
                            S.op('dve', lambda e: e.tensor_tensor(out=X[:P, t, hh * 512:(hh + 1) * 512], in0=X[:P, t, hh * 512:(hh + 1) * 512],
                                                                  in1=PS[:P, bank, :], op=ALU.add),
                                 reads=[('ps', bank), ('X', t)], writes=[('X', t)])

                for gi in range(len(GROUPS)):
                    stageA(gi)
                    if gi > 0:
                        stageB(gi - 1)
                    stageC(gi)
                stageB(len(GROUPS) - 1)
                S.barrier()

            with ExitStack() as ph:
                W = [Wq0] + [sb(ph, "Wq%d" % i, [128, 8, 512], BF16) for i in (1, 2)]
                COS = sb(ph, "COS", [128, 17, 32], F32); SIN = sb(ph, "SIN", [128, 17, 32], F32)
                zs = [sb(ph, "zs%d" % i, [128, 512], F32) for i in range(3)]
                kr = [sb(ph, "kr%d" % i, [128, 512], F32) for i in range(2)]
                t1 = [sb(ph, "t1%d" % i, [128, 512], F32) for i in range(2)]
                t2 = [sb(ph, "t2%d" % i, [128, 512], F32) for i in range(2)]
                qb = [sb(ph, "qb%d" % i, [128, 512], BF16) for i in range(3)]
                S.dma('sp', 'cos', out=COS[:], in_=cos_t[:, :, :], writes=['COS'])
                S.dma('sp', 'sin', out=SIN[:], in_=sin_t[:, :, :], writes=['SIN'])
                for pi, c0 in ((1, 768), (2, 1280)):
                    S.dma('pool', 'wq%d' % pi, out=W[pi][:], in_=w_in3[:, :, c0:c0 + 512], writes=[('W', pi)])
                pend = []
                cnt = 0
                rc = 0
                NQ0 = 6
                items = [(t, 0) for t in range(NQ0)] + [(t, pi) for t in range(NQ0) for pi in (1, 2)] + \
                        [(t, pi) for t in range(NQ0, NT) for pi in (0, 1, 2)]
                for (t, pi) in items:
                    P = TP(t); c = t * 128
                    nm = ('q', 'k', 'v')[pi]
                    if True:
                        Wc = W[pi]
                        bank = cnt % 4
                        s2 = cnt % 3
                        cnt += 1

                        def mm(e):
                            for kc in range(8):
                                ins = e.matmul(PS[:P, bank, :], lhsT=H[:, kc, c:c + P], rhs=Wc[:, kc, :], start=(kc == 0), stop=(kc == 7))
                            return ins
                        S.op('pe', mm, reads=[('H', t), ('W', pi)], writes=[('ps', bank)])
                        if len(pend) >= 2:
                            pend.pop(0)()
                        S.op('act', lambda e: e.activation(out=zs[s2][:P], in_=PS[:P, bank, :], func=AF.Copy),
                             reads=[('ps', bank)], writes=[('zs', s2)])
                        if nm == 'v':
                            S.dma('sp', 'o_v%d' % s2, out=rows_v(l, t), in_=zs[s2][:P], reads=[('zs', s2)], is_out=True)
                            S.op('pool', lambda e: e.tensor_copy(out=VB[:P, t, :], in_=zs[s2][:P]), reads=[('zs', s2)], writes=[('VB', t)])
                            continue
                        r2 = rc % 2
                        rc += 1
                        z4 = zs[s2][:P].rearrange("p (g two i) -> p g two i", g=8, two=2)
                        cosb = COS[:P, t:t + 1, :].unsqueeze(1).broadcast_to([P, 8, 2, 32])
                        sinb = SIN[:P, t:t + 1, :].broadcast_to([P, 8, 32])
                        t14 = t1[r2][:P].rearrange("p (g two i) -> p g two i", g=8, two=2)
                        t24 = t2[r2][:P].rearrange("p (g two i) -> p g two i", g=8, two=2)
                        S.op('dve', lambda e: e.tensor_tensor(out=t14, in0=z4, in1=cosb, op=ALU.mult),
                             reads=[('zs', s2), 'COS'], writes=[('t1', r2)])
                        S.op('dve', lambda e: e.tensor_tensor(out=t24[:, :, 0, :], in0=z4[:, :, 1, :], in1=sinb, op=ALU.mult),
                             reads=[('zs', s2), 'SIN'], writes=[('t2a', r2)])
                        S.op('pool', lambda e: e.tensor_tensor(out=t24[:, :, 1, :], in0=z4[:, :, 0, :], in1=sinb, op=ALU.mult),
                             reads=[('zs', s2), 'SIN'], writes=[('t2b', r2)])
                        if nm == 'q':
                            o4 = qb[s2][:P].rearrange("p (g two i) -> p g two i", g=8, two=2)
                        else:
                            o4 = kr[r2][:P].rearrange("p (g two i) -> p g two i", g=8, two=2)
                        okey = ('qb', s2) if nm == 'q' else ('kr', r2)
                        S.op('dve', lambda e: e.tensor_tensor(out=o4[:, :, 0, :], in0=t14[:, :, 0, :], in1=t24[:, :, 0, :], op=ALU.subtract),
                             reads=[('t1', r2), ('t2a', r2)], writes=[(okey, 'a')])
                        S.op('dve', lambda e: e.tensor_tensor(out=o4[:, :, 1, :], in0=t14[:, :, 1, :], in1=t24[:, :, 1, :], op=ALU.add),
                             reads=[('t1', r2), ('t2b', r2)], writes=[(okey, 'b')])
                        if nm == 'k':
                            S.dma('sp', 'o_k%d' % r2, out=rows_k(l, t), in_=kr[r2][:P], reads=[(okey, 'a'), (okey, 'b')], is_out=True)
                            S.op('act', lambda e: e.activation(out=qb[s2][:P], in_=kr[r2][:P], func=AF.Copy),
                                 reads=[(okey, 'a'), (okey, 'b')], writes=[(('qb', s2), 'a'), (('qb', s2), 'b')])

                        def mk(t=t, P=P, c=c, s2=s2, nm=nm, tb=4 + (rc % 2)):
                            def run():
                                pv = psb(tb).rearrange("p (k n) -> p k n", k=8)

                                def tr(e):
                                    for h in range(4):
                                        ins = e.transpose(out=pv[:, h, :P], in_=qb[s2][:P, h * 128:(h + 1) * 128], identity=idb[:P, :P])
                                    return ins
                                S.op('pe', tr, reads=[(('qb', s2), 'a'), (('qb', s2), 'b'), 'idb'], writes=[('ps', tb)])
                                dst = QT if nm == 'q' else KT
                                S.op('act', lambda e: e.activation(out=dst[:, :, c:c + P], in_=pv[:, 0:4, :P], func=AF.Copy),
                                     reads=[('ps', tb)], writes=[(nm + 'T', t)])
                            return run
                        pend.append(mk())
                while pend:
                    pend.pop(0)()
                S.barrier()
            phq.close()

            phw = ExitStack()
            WOA = sb(phw, "WOA", [128, 4, 1024], BF16)
            S.dma('pool', 'woa', out=WOA[:], in_=w_out3[:, 2:6, :], writes=['WOA'])
            ckb = [sb(phw, "ckb%d" % i, [128, 512], BF16) for i in range(2)]
            cvb = [sb(phw, "cvb%d" % i, [128, 512], BF16) for i in range(2)]
            for i in range(2):
                S.dma('pool', 'ck%d' % i, out=ckb[i][:], in_=ck[l, i * 128:(i + 1) * 128, :], writes=[('ckb', i)])
                S.dma('pool', 'cv%d' % i, out=cvb[i][:], in_=cv[l, i * 128:(i + 1) * 128, :], writes=[('cvb', i)])
            with ExitStack() as ph:
                PT = [sb(ph, "PT%d" % i, [128, 2, 512], BF16) for i in range(3)]
                OSB = [sb(ph, "OSB%d" % i, [128, 2, 512], F32) for i in range(2)]
                LNL = [sb(ph, "LNL%d" % i, [128, 2, 512], F32) for i in range(2)]
                dd = sb(ph, "dd", [128, 512], F32); sq = sb(ph, "sq", [128, 512], F32)
                msb = sb(ph, "msb", [128, 512], F32); rs = sb(ph, "rs", [128, 512], F32)
                MB = sb(ph, "MB", [128, 1], F32)
                S.op('dve', lambda e: e.memset(MB[0:64, :], 0.0), writes=['MBa'])
                S.op('dve', lambda e: e.memset(MB[64:128, :], NEG), writes=['MBb'])
                negl = LAM[:, l, 4:5]; gsc = LAM[:, l, 5:6]
                blocks = []
                for n_it, (h, j) in enumerate([(h, j) for h in range(4) for j in range(4)]):
                    for i in range(4 * j + 4):
                        blocks.append((n_it, h, j, i))
                NBK = len(blocks)

                def off_of(j, i):
                    r = i - 4 * j
                    return 128 * r if r > 0 else 0

                def qk(g):
                    n_it, h, j, i = blocks[g]
                    slot = g % 2
                    off = off_of(j, i)
                    q0 = j * 512

                    def f(e):
                        for c in range(2):
                            ins = e.matmul(PS[:, 2 * slot + c, off:512], lhsT=KT[c * 64:(c + 1) * 64, h, i * 128:(i + 1) * 128],
                                           rhs=QT[c * 64:(c + 1) * 64, h, q0 + off:q0 + 512], start=True, stop=True)
                        return ins
                    S.op('pe', f, reads=[('QT', h, j)], writes=[('ps', 2 * slot), ('ps', 2 * slot + 1)])

                def ex(g):
                    n_it, h, j, i = blocks[g]
                    slot = g % 2
                    off = off_of(j, i)
                    ps_ = g % 3
                    diag = (i - 4 * j) >= 0
                    src = PS[:, 2 * slot:2 * slot + 2, :]
                    if diag:
                        S.op('act', lambda e: e.activation(out=PT[ps_][:, :, off:off + 64], in_=src[:, :, off:off + 64], func=AF.Exp,
                                                           scale=SCALE, bias=MB[:, 0:1]),
                             reads=[('ps', 2 * slot), ('ps', 2 * slot + 1), 'MBa', 'MBb'], writes=[('PT', ps_, 'm')])
                        S.op('act', lambda e: e.activation(out=PT[ps_][:, :, off + 64:512], in_=src[:, :, off + 64:512], func=AF.Exp,
                                                           scale=SCALE),
                             reads=[('ps', 2 * slot), ('ps', 2 * slot + 1)], writes=[('PT', ps_)])
                    else:
                        S.op('act', lambda e: e.activation(out=PT[ps_][:, :, :], in_=src, func=AF.Exp, scale=SCALE),
                             reads=[('ps', 2 * slot), ('ps', 2 * slot + 1)], writes=[('PT', ps_), ('PT', ps_, 'm')])

                def pvm(g):
                    n_it, h, j, i = blocks[g]
                    off = off_of(j, i)
                    ps_ = g % 3
                    nb = 4 * j + 4

                    def f(e):
                        for c in range(2):
                            e.matmul(PS[:, 4 + c, off:512], lhsT=VB[:, i, h * 128:(h + 1) * 128], rhs=PT[ps_][:, c, off:512],
                                     start=(i == 0), stop=(i == nb - 1))
                        for c in range(2):
                            ins = e.matmul(PS[:, 6 + c, off:512], lhsT=onesb[:], rhs=PT[ps_][:, c, off:512],
                                           start=(i == 0), stop=(i == nb - 1))
                        return ins
                    S.op('pe', f, reads=[('PT', ps_), ('PT', ps_, 'm'), 'onesb'], writes=[('ps', 4), ('ps', 5), ('ps', 6), ('ps', 7)])

                def mkB(st):
                    def run(sbk):
                        S.op('act', lambda e: e.activation(out=LNL[st][:], in_=LNL[st][:], func=AF.Exp, scale=-1.0),
                             reads=[('LNL', st)], writes=[('LNL', st)])
                        S.op('dve', lambda e: e.tensor_tensor(out=OSB[st][:], in0=OSB[st][:], in1=LNL[st][:], op=ALU.mult),
                             reads=[('OSB', st), ('LNL', st)], writes=[('OSB', st)])
                        S.op('dve', lambda e: e.scalar_tensor_tensor(out=dd[:], in0=OSB[st][:, 1, :], scalar=negl, in1=OSB[st][:, 0, :],
                                                                     op0=ALU.mult, op1=ALU.add),
                             reads=[('OSB', st)], writes=['dd'])
                        S.op('dve', lambda e: e.tensor_tensor(out=sq[:], in0=dd[:], in1=dd[:], op=ALU.mult), reads=['dd'], writes=['sq'])
                    return run

                def mkC1():
                    def run(sbk):
                        S.op('pe', lambda e: e.matmul(PS[:, sbk, :], lhsT=onesf[:, 0, :], rhs=sq[:], start=True, stop=True),
                             reads=['sq', 'onesf0'], writes=[('ps', sbk)])
                        S.op('dve', lambda e: e.tensor_scalar(out=msb[:], in0=PS[:, sbk, :], scalar1=EPS, scalar2=None, op0=ALU.add),
                             reads=[('ps', sbk)], writes=['msb'])
                    return run

                def mkC2(h, j):
                    q0 = j * 512

                    def run():
                        S.op('act', lambda e: e.activation(out=msb[:], in_=msb[:], func=AF.Ln), reads=['msb'], writes=['msb'])
                        S.op('act', lambda e: e.activation(out=rs[:], in_=msb[:], func=AF.Exp, scale=-0.5), reads=['msb'], writes=['rs'])
                        S.op('dve', lambda e: e.scalar_tensor_tensor(out=QT[:, h, q0:q0 + 512], in0=dd[:], scalar=gsc, in1=rs[:],
                                                                     op0=ALU.mult, op1=ALU.mult),
                             reads=['dd', 'rs'], writes=[('QT', h, j)])
                    return run

                pendB = None
                savedC = None
                pendC2 = None
                qk(0)
                qk(1)
                for g in range(NBK):
                    n_it, h, j, i = blocks[g]
                    nb = 4 * j + 4
                    st = n_it % 2
                    ex(g)
                    if pendC2 is not None:
                        pendC2()
                        pendC2 = None
                    pvm(g)
                    if i == nb - 1:
                        S.op('dve', lambda e: e.tensor_copy(out=OSB[st][:], in_=PS[:, 4:6, :]),
                             reads=[('ps', 4), ('ps', 5)], writes=[('OSB', st)])
                        S.op('act', lambda e: e.activation(out=LNL[st][:], in_=PS[:, 6:8, :], func=AF.Ln),
                             reads=[('ps', 6), ('ps', 7)], writes=[('LNL', st)])
                        if savedC is not None:
                            savedC[0](2 * (g % 2))
                            pendC2 = savedC[1]
                        savedC = (mkC1(), mkC2(h, j))
                        pendB = (g + 2, mkB(st))
                    if pendB is not None and pendB[0] <= g:
                        pendB[1](0)
                        pendB = None
                    if g + 2 < NBK:
                        qk(g + 2)
                if pendC2 is not None:
                    pendC2()
                if pendB is not None:
                    pendB[1](0)
                savedC[0](0); savedC[1]()
                S.barrier()

            phf = ExitStack()
            GU = [sb(phf, "GU%d" % i, [128, 2, 8, 256], BF16) for i in range(2)]

            def issue_gu_abs(fs, w, slot):
                S.dma('pool', 'gug%d' % slot, out=GU[slot][:, 0, :, 0:w * 128], in_=w_gu3[:, :, fs * 128:(fs + w) * 128],
                      writes=[('GUg', slot)])
                S.dma('pool', 'guu%d' % slot, out=GU[slot][:, 1, :, 0:w * 128], in_=w_gu3[:, :, 2816 + fs * 128:2816 + (fs + w) * 128],
                      writes=[('GUu', slot)])
            issue_gu_abs(0, 2, 0)
            def wo_tile(t, banks):
                P = TP(t); c = t * 128
                for hh in range(2):
                    bank = banks[hh]

                    def wo_mm(e):
                        for kc in range(4):
                            ins = e.matmul(PS[:P, bank, :], lhsT=QT[:, kc, c:c + P], rhs=WOA[:, kc, hh * 512:(hh + 1) * 512],
                                           start=(kc == 0), stop=(kc == 3))
                        return ins
                    S.op('pe', wo_mm, reads=['WOA', ('QT', 's')] if t == 16 else ['WOA'], writes=[('ps', bank)])
                    S.op('dve', lambda e: e.tensor_tensor(out=X[:P, t, hh * 512:(hh + 1) * 512], in0=X[:P, t, hh * 512:(hh + 1) * 512],
                                                          in1=PS[:P, bank, :], op=ALU.add),
                         reads=[('ps', bank), ('X', t)], writes=[('X', t)])

            with ExitStack() as ph:
                ckT = [sb(ph, "ckT%d" % i, [128, 4, 128], BF16) for i in range(2)]
                PTs = [sb(ph, "PTs%d" % i, [128, 8, 32], BF16) for i in range(2)]
                rr = sb(ph, "rr", [128, 8, 32], F32); a8 = sb(ph, "a8", [128, 2, 4, 32], F32)
                d4 = sb(ph, "d4", [128, 4, 32], F32); sq4 = sb(ph, "sq4", [128, 4, 32], F32)
                ms4 = sb(ph, "ms4", [128, 128], F32); ln4 = sb(ph, "ln4", [128, 128], F32); rs4 = sb(ph, "rs4", [128, 128], F32)
                negl = LAM[:, l, 4:5]; gsc = LAM[:, l, 5:6]
                OS = PS[:, 4, 0:256].rearrange("p (g n) -> p g n", g=8)
                LS = PS[:, 5, 0:256].rearrange("p (g n) -> p g n", g=8)
                ATSM = 9
                for i in range(17 if ATSM > 0 else 0):
                    slot = i % 2
                    KP = 128 if i < 16 else 32
                    if i < 16:
                        if i >= 2:
                            S.dma('pool', 'ck%d' % slot, out=ckb[slot][:], in_=ck[l, i * 128:(i + 1) * 128, :], writes=[('ckb', slot)])
                            S.dma('pool', 'cv%d' % slot, out=cvb[slot][:], in_=cv[l, i * 128:(i + 1) * 128, :], writes=[('cvb', slot)])
                        pv = psb(slot).rearrange("p (k n) -> p k n", k=8)

                        def tr(e):
                            for h in range(4):
                                ins = e.transpose(out=pv[:, h, :], in_=ckb[slot][:, h * 128:(h + 1) * 128], identity=idb[:])
                            return ins
                        S.op('pe', tr, reads=[('ckb', slot), 'idb'], writes=[('ps', slot)])
                        S.op('act', lambda e: e.activation(out=ckT[slot][:], in_=pv[:, 0:4, :], func=AF.Copy),
                             reads=[('ps', slot)], writes=[('ckT', slot)])
                    sb0 = 2 if slot == 0 else 6
                    SS2 = PS[:, sb0:sb0 + 2, 0:128]
                    if ATSM < 2:
                        continue

                    def qk(e):
                        for h in range(4):
                            for c in range(2):
                                if i < 16:
                                    kT = ckT[slot][c * 64:(c + 1) * 64, h, :]
                                else:
                                    kT = KT[c * 64:(c + 1) * 64, h, 2048:2080]
                                ins = e.matmul(PS[:KP, sb0 + c, h * 32:(h + 1) * 32], lhsT=kT, rhs=QT[c * 64:(c + 1) * 64, h, 2048:2080],
                                               start=True, stop=True)
                        return ins
                    S.op('pe', qk, reads=[('ckT', slot), ('QT', 's')], writes=[('ps', sb0), ('ps', sb0 + 1)])
                    S.op('act', lambda e: e.activation(out=PTs[slot][:KP].rearrange("p (c x) n -> p c (x n)", c=2), in_=SS2[:KP], func=AF.Exp, scale=SCALE),
                         reads=[('ps', sb0), ('ps', sb0 + 1)], writes=[('PTs', slot)])
                    if i < 16:
                        wo_tile(i, (6, 7) if slot == 0 else (2, 3))
                    if ATSM < 3:
                        continue

                    def pvm(e):
                        first = True
                        for c in range(2):
                            for h in range(4):
                                if i < 16:
                                    vv = cvb[slot][:, h * 128:(h + 1) * 128]
                                else:
                                    vv = VB[:32, 16, h * 128:(h + 1) * 128]
                                e.matmul(OS[:, c * 4 + h, :], lhsT=vv, rhs=PTs[slot][:KP, c * 4 + h, :],
                                         start=(i == 0 and first), stop=(i == 16), skip_group_check=True)
                                first = False
                        ins = e.matmul(PS[:, 5, 0:256], lhsT=onesb[:KP, :], rhs=PTs[slot][:KP].rearrange("p g n -> p (g n)"),
                                       start=(i == 0), stop=(i == 16))
                        return ins
                    S.op('pe', pvm, reads=[('PTs', slot), ('cvb', slot), 'onesb'], writes=[('ps', 4), ('ps', 5)])
                if ATSM < 4:
                    S.stopped = True
                S.op('dve', lambda e: e.reciprocal(out=rr[:], in_=LS), reads=[('ps', 5)], writes=['rr'])
                S.op('dve', lambda e: e.tensor_tensor(out=a8[:].rearrange("p c h n -> p (c h) n"), in0=OS, in1=rr[:], op=ALU.mult),
                     reads=[('ps', 4), 'rr'], writes=['a8'])
                S.op('dve', lambda e: e.scalar_tensor_tensor(out=d4[:], in0=a8[:, 1, :, :], scalar=negl, in1=a8[:, 0, :, :],
                                                             op0=ALU.mult, op1=ALU.add), reads=['a8'], writes=['d4'])
                S.op('act', lambda e: e.activation(out=sq4[:], in_=d4[:], func=AF.Square), reads=['d4'], writes=['sq4'])
                S.op('pe', lambda e: e.matmul(PS[:, 0, 0:128], lhsT=onesf[:, 0, :], rhs=sq4[:].rearrange("p h n -> p (h n)"), start=True, stop=True),
                     reads=['sq4', 'onesf0'], writes=[('ps', 0)])
                S.op('dve', lambda e: e.tensor_scalar(out=ms4[:], in0=PS[:, 0, 0:128], scalar1=EPS, scalar2=None, op0=ALU.add),
                     reads=[('ps', 0)], writes=['ms4'])
                S.op('act', lambda e: e.activation(out=ln4[:], in_=ms4[:], func=AF.Ln), reads=['ms4'], writes=['ln4'])
                S.op('act', lambda e: e.activation(out=rs4[:], in_=ln4[:], func=AF.Exp, scale=-0.5), reads=['ln4'], writes=['rs4'])
                S.op('dve', lambda e: e.scalar_tensor_tensor(out=QT[:, :, 2048:2080], in0=d4[:], scalar=gsc,
                                                             in1=rs4[:].rearrange("p (h n) -> p h n", h=4), op0=ALU.mult, op1=ALU.mult),
                     reads=['d4', 'rs4'], writes=[('QT', 's')])
                wo_tile(16, (2, 3))
                S.barrier()

            phase_norm(gffn[l:l + 1, :])
            with ExitStack() as ph:
                DN = [sb(ph, "DN%d" % i, [128, 8, 512], BF16) for i in range(2)]
                sgl = [sb(ph, "sgl%d" % i, [128, 512], F32) for i in range(2)]
                gcount = 0
                dcount = 0
                ev = 0
                for (f0, f1) in FFN_PASSES:
                    nf = f1 - f0
                    slabs = []
                    f = f0
                    while f < f1:
                        w = min(2, f1 - f)
                        slabs.append((f, w))
                        f += w

                    def issue_gu(si):
                        fs, w = slabs[si]
                        slot = (gcount + si) % 2
                        issue_gu_abs(fs, w, slot)
                    if f0 > 0:
                        issue_gu(0)
                    for hh in range(2):
                        S.dma('pool', 'dn%d' % hh, out=DN[hh][:, 0:nf, :], in_=w_dn3[:, f0:f1, hh * 512:(hh + 1) * 512], writes=[('DN', hh)])
                    for si, (fs, w) in enumerate(slabs):
                        slot = (gcount + si) % 2
                        if si + 1 < len(slabs):
                            issue_gu(si + 1)
                        for fo in range(w):
                            fi = fs + fo - f0
                            for (c0, n, tiles) in GROUPS:
                                bg = (ev % 2) * 2
                                ev += 1

                                def gu_mm(e):
                                    for kc in range(8):
                                        e.matmul(PS[:, bg, 0:n], lhsT=GU[slot][:, 0, kc, fo * 128:(fo + 1) * 128], rhs=H[:, kc, c0:c0 + n],
                                                 start=(kc == 0), stop=(kc == 7))
                                    for kc in range(8):
                                        ins = e.matmul(PS[:, bg + 1, 0:n], lhsT=GU[slot][:, 1, kc, fo * 128:(fo + 1) * 128], rhs=H[:, kc, c0:c0 + n],
                                                       start=(kc == 0), stop=(kc == 7))
                                    return ins
                                S.op('pe', gu_mm, reads=[('GUg', slot), ('GUu', slot)], writes=[('ps', bg), ('ps', bg + 1)])
                                ss = ev % 2
                                S.op('act', lambda e: e.activation(out=sgl[ss][:, 0:n], in_=PS[:, bg, 0:n], func=AF.Silu),
                                     reads=[('ps', bg)], writes=[('sgl', ss)])
                                S.op('dve', lambda e: e.tensor_tensor(out=ACTH[:, fi, c0:c0 + n], in0=sgl[ss][:, 0:n], in1=PS[:, bg + 1, 0:n], op=ALU.mult),
                                     reads=[('sgl', ss), ('ps', bg + 1)], writes=[('ACTH', fi)])
                    gcount += len(slabs)
                    lastpass = (f1 == 22)
                    last = lastpass and l == DEPTH - 1
                    mid = lastpass and l < DEPTH - 1
                    if lastpass:
                        GUf = [GU[i][:].rearrange("p a b c -> p (a b c)").bitcast(F32) for i in range(2)]
                        GU1b = GU[1][:].rearrange("p a b c -> p (a b c)")
                        Gfin = GUf[0][:, 0:1024]
                        stf = GUf[0][:, 1024:1024 + 4 * NT].rearrange("p (t k) -> p t k", k=4)
                        yof = [GUf[1][:, i * 1024:(i + 1) * 1024] for i in range(2)]
                        hbf = [GU1b[:, i * 1024:(i + 1) * 1024] for i in range(3)]
                        junkf = GU1b[:, 3072:4096]
                        grow = gfin[0:1, :] if last else gmix[l + 1:l + 2, :]
                    if mid:
                        w_in3n = w_in[l + 1].rearrange("(k p) c -> p k c", p=128)
                        WPn = FA[:, 16640:22784].rearrange("p (k c) -> p k c", k=8)
                        S.dma('pool', 'wp0', out=WPn[:, :, 0:256], in_=w_in3n[:, :, 0:256], writes=['WP0'])
                        S.dma('pool', 'wp1', out=WPn[:, :, 256:768], in_=w_in3n[:, :, 1792:2304], writes=['WP1'])
                    order = [(t, hh) for t in range(NT) for hh in range(2)] if lastpass else [(t, hh) for hh in range(2) for t in range(NT)]
                    for (t, hh) in order:
                        dslot = hh
                        P = TP(t); c = t * 128
                        bank = 4 + ((2 * t + hh) % 4 if lastpass else t % 4)

                        def dn_mm(e):
                            for fi in range(nf):
                                ins = e.matmul(PS[:P, bank, :], lhsT=ACTH[:, fi, c:c + P], rhs=DN[dslot][:, fi, :],
                                               start=(fi == 0), stop=(fi == nf - 1))
                            return ins
                        S.op('pe', dn_mm, reads=[('ACTH', fi) for fi in range(nf)] + [('DN', dslot)], writes=[('ps', bank)])
                        S.op('dve', lambda e: e.tensor_tensor(out=X[:P, t, hh * 512:(hh + 1) * 512], in0=X[:P, t, hh * 512:(hh + 1) * 512],
                                                              in1=PS[:P, bank, :], op=ALU.add),
                             reads=[('ps', bank), ('X', t)], writes=[('X', t)])
                        if lastpass and hh == 1:
                            if t == 0:
                                S.dma('sp', 'Gf', out=Gfin, in_=grow.partition_broadcast(128), reads=[('X', 0)], writes=['Gfin'])

                            def fin1(t):
                                P = TP(t)
                                jout = yof[t % 2][:P] if last else junkf[:P]
                                jkey = ('yo', t % 2) if last else 'junkf'
                                S.op('act', lambda e: e.activation(out=jout, in_=X[:P, t, :], func=AF.Square, accum_out=stf[:P, t, 0:1]),
                                     reads=[('X', t)], writes=[jkey, ('st', t, 0)])

                            def fin2(t):
                                rstd_chain(stf, TP(t), t, None)

                            def fin3(t):
                                P = TP(t)
                                if last:
                                    S.op('dve', lambda e: e.scalar_tensor_tensor(out=yof[t % 2][:P], in0=X[:P, t, :], scalar=stf[:P, t, 3:4],
                                                                                 in1=Gfin[:P], op0=ALU.mult, op1=ALU.mult),
                                         reads=[('X', t), ('st', t, 3), 'Gfin'], writes=[('yo', t % 2)])
                                    S.dma('sp', 'o_y%d' % (t % 2), out=rows_y(t), in_=yof[t % 2][:P], reads=[('yo', t % 2)], is_out=True)
                                else:
                                    S.op('dve', lambda e: e.scalar_tensor_tensor(out=hbf[t % 3][:P], in0=X[:P, t, :], scalar=stf[:P, t, 3:4],
                                                                                 in1=Gfin[:P], op0=ALU.mult, op1=ALU.mult),
                                         reads=[('X', t), ('st', t, 3), 'Gfin'], writes=[('hbf', t % 3)])

                            def fin4(t):
                                if last:
                                    return
                                P = TP(t); c4 = t * 128
                                tbk = t % 4
                                pvf = psb(tbk).rearrange("p (k n) -> p k n", k=8)

                                def tr(e):
                                    for kc in range(8):
                                        ins = e.transpose(out=pvf[:, kc, :P], in_=hbf[t % 3][:P, kc * 128:(kc + 1) * 128], identity=idb[:P, :P])
                                    return ins
                                S.op('pe', tr, reads=[('hbf', t % 3), 'idb'], writes=[('ps', tbk)])
                                S.op('act', lambda e: e.activation(out=H[:, :, c4:c4 + P], in_=pvf[:, :, :P], func=AF.Copy),
                                     reads=[('ps', tbk)], writes=[('H', t)])
                            if t >= 3:
                                fin4(t - 3)
                            if t >= 2:
                                fin3(t - 2)
                            if t >= 1:
                                fin2(t - 1)
                            fin1(t)
                            if t == NT - 1:
                                fin4(t - 2)
                                fin3(t - 1)
                                fin2(t)
                                fin4(t - 1)
                                fin3(t)
                                fin4(t)
                S.barrier()
            phf.close()
            phw.close()

        S.stopped = False

        if True:
            S.finish()
    return nc


def _host_consts():
    half = 32
    inv = (np.float32(10000.0) ** (-np.arange(half, dtype=np.float32) / np.float32(half))).astype(np.float32)
    pos = np.zeros((128, 17), np.float32)
    for t in range(16):
        pos[:, t] = t * 128 + np.arange(128)
    pos[:, 16] = 2048 + np.arange(128)
    ang = (pos[:, :, None] * inv[None, None, :]).astype(np.float32)
    cos_t = np.cos(ang).astype(np.float32)
    sin_t = np.sin(ang).astype(np.float32)
    wins = np.array([[2, 4], [8, 16]])
    invc = np.zeros((128, 2, 16), np.float32)
    invw = np.zeros((128, 2), np.float32)
    for j in range(2):
        for p in range(128):
            w = wins[j][p // 64]
            invw[p, j] = 1.0 / w
            invc[p, j, :] = 1.0 / np.minimum(np.arange(16) + 1, w)
    return cos_t, sin_t, invc, invw


_NC_CACHE = {}


def kernel(x_prompt, x_sample, cache_k, cache_v, state_pool, state_conv,
           norm_mix_g, w_in, pool_w, pool_scale, lambda_qk, diff_norm_g,
           conv_dw, conv_dw_b, conv_ln_g, conv_ln_b, conv_pw, w_out,
           norm_ffn_g, w_gate_up, w_down, final_norm_g):
    f = lambda a: np.ascontiguousarray(np.asarray(a, dtype=np.float32))
    x_prompt, x_sample, cache_k, cache_v = f(x_prompt), f(x_sample), f(cache_k), f(cache_v)
    state_pool, state_conv = f(state_pool), f(state_conv)
    B = 8
    cos_t, sin_t, invc, invw = _host_consts()

    def pc(v):
        return np.asarray(v, np.float32).reshape(2, 128).T
    pvec = np.zeros((2, 128, NPV), np.float32)
    plw = np.zeros((2, 128, 2, 128), np.float32)
    pool_w = f(pool_w); conv_dw = f(conv_dw)
    for l in range(2):
        pvec[l, :, 0:2] = pc(pool_scale[l])
        pvec[l, :, 2:4] = pc(conv_dw_b[l])
        pvec[l, :, 4:6] = pc(conv_ln_g[l])
        pvec[l, :, 6:8] = pc(conv_ln_b[l])
        pvec[l, :, 8] = np.asarray(diff_norm_g[l], np.float32)
        pvec[l, :, 9:11] = invw
        dwl = conv_dw[l].reshape(31, 2, 128)
        pvec[l, :, 11:73] = dwl.transpose(2, 1, 0).reshape(128, 62)
        for j in range(2):
            for hf in range(2):
                plw[l, hf * 64:(hf + 1) * 64, j, hf * 64:(hf + 1) * 64] = pool_w[l, 2 * j + hf]
    common = dict(
        w_in=f(w_in), w_out=f(w_out), w_gu=f(w_gate_up), w_dn=f(w_down), conv_pw=f(conv_pw), plw=plw,
        gmix=f(norm_mix_g), gffn=f(norm_ffn_g), gfin=f(final_norm_g).reshape(1, 1024),
        pvec=pvec, lq=f(lambda_qk).reshape(2, 256),
        ident=np.eye(128, dtype=np.float32), cos_t=cos_t, sin_t=sin_t, invc=invc,
    )
    in_maps = []
    for b in range(B):
        m = dict(common)
        m["xp"] = x_prompt[b]
        m["xs"] = x_sample[b]
        m["ck"] = np.ascontiguousarray(cache_k[:, b].reshape(2, 2048, 512))
        m["cv"] = np.ascontiguousarray(cache_v[:, b].reshape(2, 2048, 512))
        m["stp"] = np.ascontiguousarray(state_pool[:, b].reshape(2, 15, 2, 128).transpose(0, 3, 2, 1))
        m["stc"] = np.ascontiguousarray(state_conv[:, b].reshape(2, 30, 2, 128).transpose(0, 3, 2, 1))
        in_maps.append(m)
    if "nc" not in _NC_CACHE:
        _NC_CACHE["nc"] = build_program()
    nc = _NC_CACHE["nc"]
    res = run_bass_kernel_spmd(nc, in_maps, core_ids=list(range(B)))
    R = res.results

    def st(name):
        return np.stack([np.asarray(R[b][name], np.float32) for b in range(B)], axis=0)
    y_prompt = st("yp")
    y_sample = st("ys")
    nk_p = st("nkp").transpose(1, 0, 2, 3).reshape(2, B, 2048, 4, 2, 64)
    nv_p = st("nvp").transpose(1, 0, 2, 3).reshape(2, B, 2048, 4, 128)
    nk_s = st("nks").transpose(1, 0, 2, 3).reshape(2, B, 32, 4, 2, 64)
    nv_s = st("nvs").transpose(1, 0, 2, 3).reshape(2, B, 32, 4, 128)

    def unT(name, T):
        a = st(name)
        return np.ascontiguousarray(a.transpose(1, 0, 4, 3, 2).reshape(2, B, T, 256))
    np_p = unT("npp", 15); nc_p = unT("ncp", 30); np_s = unT("nps", 15); nc_s = unT("ncs", 30)
    return (y_prompt, y_sample, np.ascontiguousarray(nk_p), np.ascontiguousarray(nv_p), np_p, nc_p,
            np.ascontiguousarray(nk_s), np.ascontiguousarray(nv_s), np_s, nc_s)
```

```python
import math
from contextlib import ExitStack

import numpy as np
import concourse.bass as bass
import concourse.mybir as mybir
from concourse.bass_utils import run_bass_kernel_spmd

F32 = mybir.dt.float32
BF16 = mybir.dt.bfloat16
ALU = mybir.AluOpType
AF = mybir.ActivationFunctionType

EPS = 1e-6
SCALE = 0.125
NT = 17
NTOK = 2080
DEPTH = 2
NPV = 73
FFN_PASSES = [(0, 8), (8, 15), (15, 22)]
GROUPS = [(0, 512, [0, 1, 2, 3]), (512, 512, [4, 5, 6, 7]), (1024, 512, [8, 9, 10, 11]),
          (1536, 512, [12, 13, 14, 15]), (2048, 32, [16])]
NEG = -30000.0


def TP(t):
    return 128 if t < 16 else 32


import os
STOP_AFTER = 0


class _Stop(Exception):
    pass


class Sched:
    def __init__(self, nc, es):
        self.nc = nc
        self.es = es
        self.eng = {'pe': nc.tensor, 'act': nc.scalar, 'dve': nc.vector, 'pool': nc.gpsimd, 'sp': nc.sync}
        self.csem = {e: es.enter_context(nc.semaphore('c_' + e)) for e in ('pe', 'act', 'dve', 'pool')}
        self.ccount = {e: 0 for e in self.csem}
        self.dsem = {}
        self.lastw = {}
        self.readers = {}
        self.waited = {e: {} for e in self.eng}
        self.out_events = []
        self.stopped = False
        self.nbar = 0

    def _wait(self, eng, ev):
        sem, val, src, kind, sid = ev
        w = self.waited[eng]
        if w.get(sid, 0) >= val:
            return
        w[sid] = val
        self.eng[eng].wait_ge(sem, val)

    def _deps(self, eng, reads, writes):
        deps = []
        for k in reads:
            ev = self.lastw.get(k)
            if ev is not None:
                deps.append((ev, True))
        for k in writes:
            ev = self.lastw.get(k)
            if ev is not None:
                deps.append((ev, False))
            for ev in self.readers.get(k, {}).values():
                deps.append((ev, False))
        for ev, raw in deps:
            if ev[3] == 'c' and ev[2] == eng:
                if eng == 'pe' or not raw:
                    continue
            self._wait(eng, ev)

    def _commit(self, ev, reads, writes):
        for k in writes:
            self.lastw[k] = ev
            self.readers[k] = {}
        for k in reads:
            self.readers.setdefault(k, {})[ev[4]] = ev

    def op(self, eng, fn, reads=(), writes=()):
        if self.stopped:
            return
        self._deps(eng, reads, writes)
        ins = fn(self.eng[eng])
        self.ccount[eng] += 1
        ins.then_inc(self.csem[eng], 1)
        ev = (self.csem[eng], self.ccount[eng], eng, 'c', 'c_' + eng)
        self._commit(ev, reads, writes)

    def dma(self, q, key, out, in_, reads=(), writes=(), is_out=False):
        if self.stopped:
            return
        self._deps(q, reads, writes)
        if key not in self.dsem:
            self.dsem[key] = [self.es.enter_context(self.nc.semaphore('d%d' % len(self.dsem))), 0]
        d = self.dsem[key]
        d[1] += 16
        self.eng[q].dma_start(out=out, in_=in_).then_inc(d[0], 16)
        ev = (d[0], d[1], q, 'd', 'd_' + str(key))
        self._commit(ev, reads, writes)
        if is_out:
            self.out_events.append(ev)

    def barrier(self):
        if self.stopped:
            return
        self.nbar += 1
        self._barrier()
        if self.nbar == STOP_AFTER:
            self.stopped = True

    def _barrier(self):
        evs = [(self.csem[e], self.ccount[e], e, 'c', 'c_' + e) for e in self.csem if self.ccount[e] > 0]
        evs += [(d[0], d[1], 'x', 'd', 'd_' + str(k)) for k, d in self.dsem.items() if d[1] > 0]
        for eng in self.eng:
            for ev in evs:
                if ev[3] == 'c' and ev[2] == eng:
                    continue
                self._wait(eng, ev)
        self.lastw = {}
        self.readers = {}

    def finish(self):
        for ev in self.out_events:
            self._wait('sp', ev)


def build_program():
    nc = bass.Bass("TRN2", target_bir_lowering=False)

    def din(name, shape):
        return nc.dram_tensor(name, list(shape), F32, kind="ExternalInput").ap()

    def dout(name, shape):
        return nc.dram_tensor(name, list(shape), F32, kind="ExternalOutput").ap()

    xp = din("xp", [2048, 1024]); xs = din("xs", [32, 1024])
    ck = din("ck", [2, 2048, 512]); cv = din("cv", [2, 2048, 512])
    stp = din("stp", [2, 128, 2, 15]); stc = din("stc", [2, 128, 2, 30])
    w_in = din("w_in", [2, 1024, 2304]); w_out = din("w_out", [2, 1024, 1024])
    w_gu = din("w_gu", [2, 1024, 5632]); w_dn = din("w_dn", [2, 2816, 1024])
    conv_pw = din("conv_pw", [2, 256, 256]); plw = din("plw", [2, 128, 2, 128])
    gmix = din("gmix", [2, 1024]); gffn = din("gffn", [2, 1024]); gfin = din("gfin", [1, 1024])
    pvec = din("pvec", [2, 128, NPV]); lq = din("lq", [2, 256])
    ident = din("ident", [128, 128]); cos_t = din("cos_t", [128, 17, 32]); sin_t = din("sin_t", [128, 17, 32])
    invc = din("invc", [128, 2, 16])

    yp = dout("yp", [2048, 1024]); ys = dout("ys", [32, 1024])
    nkp = dout("nkp", [2, 2048, 512]); nvp = dout("nvp", [2, 2048, 512])
    npp = dout("npp", [2, 128, 2, 15]); ncp = dout("ncp", [2, 128, 2, 30])
    nks = dout("nks", [2, 32, 512]); nvs = dout("nvs", [2, 32, 512])
    nps = dout("nps", [2, 128, 2, 15]); ncs = dout("ncs", [2, 128, 2, 30])

    def rows_y(t):
        return yp[t * 128:(t + 1) * 128, :] if t < 16 else ys[0:32, :]

    def rows_k(l, t):
        return nkp[l, t * 128:(t + 1) * 128, :] if t < 16 else nks[l, 0:32, :]

    def rows_v(l, t):
        return nvp[l, t * 128:(t + 1) * 128, :] if t < 16 else nvs[l, 0:32, :]

    with ExitStack() as es:
        S = Sched(nc, es)

        uid = [0]

        def sb(stack, name, shape, dt):
            uid[0] += 1
            return stack.enter_context(nc.sbuf_tensor("%s_%d" % (name, uid[0]), list(shape), dt))

        PS = es.enter_context(nc.psum_tensor("PS", [128, 8, 512], F32))

        def psb(b):
            return PS[:, b, :].bitcast(BF16)

        X = sb(es, "X", [128, NT, 1024], F32)
        H = sb(es, "H", [128, 8, NTOK], BF16)
        FA = sb(es, "FA", [128, 25344], BF16)
        QT = FA[:, 0:8320].rearrange("p (h n) -> p h n", h=4)
        KT = FA[:, 8320:16640].rearrange("p (h n) -> p h n", h=4)
        VB = FA[:, 16640:25344].rearrange("p (t n) -> p t n", t=NT)
        ACTH = FA[:, 0:16640].rearrange("p (k n) -> p k n", k=8)
        idb = sb(es, "idb", [128, 128], BF16)
        onesb = sb(es, "onesb", [128, 128], BF16)
        onesf = sb(es, "onesf", [128, 3, 128], F32)
        PV = sb(es, "PV", [128, 2, NPV], F32)
        LAM = sb(es, "LAM", [128, 2, 8], F32)
        LQ = sb(es, "LQ", [128, 2, 256], F32)
        ljunk = sb(es, "ljunk", [128, 64], F32)

        S.dma('pool', 'c_id', out=idb[:], in_=ident[:, :], writes=['idb'])
        S.dma('sp', 'c_pv', out=PV[:], in_=pvec.rearrange("l p n -> p l n"), writes=['PV'])
        S.dma('sp', 'c_lq', out=LQ[:, 0, :], in_=lq[0:1, :].partition_broadcast(128), writes=[('LQ', 0)])
        S.dma('sp', 'c_lq1', out=LQ[:, 1, :], in_=lq[1:2, :].partition_broadcast(128), writes=[('LQ', 1)])
        S.dma('sp', 'xs', out=X[:32, 16, :], in_=xs[:, :], writes=[('X', 16)])
        xpv = xp.rearrange("(t p) d -> p t d", p=128)
        for q4 in range(4):
            S.dma('sp', 'x%d' % q4, out=X[:, q4 * 4:(q4 + 1) * 4, :], in_=xpv[:, q4 * 4:(q4 + 1) * 4, :],
                  writes=[('X', t) for t in range(q4 * 4, q4 * 4 + 4)])
        S.op('dve', lambda e: e.memset(onesb[:], 1.0), writes=['onesb'])
        S.op('dve', lambda e: e.memset(onesf[:, 0, :], 1.0 / 128), writes=['onesf0'])
        S.op('dve', lambda e: e.memset(onesf[:, 1, :], 1.0 / 256), writes=['onesf1'])
        S.op('dve', lambda e: e.memset(onesf[:, 2, :], 1.0), writes=['onesf2'])
        for l in range(DEPTH):
            lam_init = 0.8 - 0.6 * math.exp(-0.3 * l)
            S.op('dve', lambda e: e.scalar_tensor_tensor(out=ljunk[:], in0=LQ[:, l, 0:64], scalar=1.0, in1=LQ[:, l, 64:128],
                                                         op0=ALU.mult, op1=ALU.mult, accum_out=LAM[:, l, 0:1]),
                 reads=[('LQ', l)], writes=['ljunk', ('LAM', l, 0)])
            S.op('dve', lambda e: e.scalar_tensor_tensor(out=ljunk[:], in0=LQ[:, l, 128:192], scalar=1.0, in1=LQ[:, l, 192:256],
                                                         op0=ALU.mult, op1=ALU.mult, accum_out=LAM[:, l, 1:2]),
                 reads=[('LQ', l)], writes=['ljunk', ('LAM', l, 1)])
            S.op('act', lambda e: e.activation(out=LAM[:, l, 2:4], in_=LAM[:, l, 0:2], func=AF.Exp),
                 reads=[('LAM', l, 0), ('LAM', l, 1)], writes=[('LAM', l, 2)])
            S.op('dve', lambda e: e.tensor_tensor(out=LAM[:, l, 6:7], in0=LAM[:, l, 3:4], in1=LAM[:, l, 2:3], op=ALU.subtract),
                 reads=[('LAM', l, 2)], writes=[('LAM', l, 6)])
            S.op('dve', lambda e: e.tensor_scalar(out=LAM[:, l, 4:5], in0=LAM[:, l, 6:7], scalar1=-lam_init, scalar2=None, op0=ALU.add),
                 reads=[('LAM', l, 6)], writes=[('LAM', l, 4)])
            S.op('dve', lambda e: e.tensor_scalar(out=LAM[:, l, 5:6], in0=PV[:, l, 8:9], scalar1=(1.0 - lam_init), scalar2=None, op0=ALU.mult),
                 reads=['PV'], writes=[('LAM', l, 5)])
        S._barrier()

        def rstd_chain(stt, P, t, src_key):
            S.op('dve', lambda e: e.tensor_scalar(out=stt[:P, t, 1:2], in0=stt[:P, t, 0:1], scalar1=1.0 / 1024, scalar2=EPS,
                                                  op0=ALU.mult, op1=ALU.add),
                 reads=[('st', t, 0)], writes=[('st', t, 1)])
            S.op('act', lambda e: e.activation(out=stt[:P, t, 2:3], in_=stt[:P, t, 1:2], func=AF.Ln),
                 reads=[('st', t, 1)], writes=[('st', t, 2)])
            S.op('act', lambda e: e.activation(out=stt[:P, t, 3:4], in_=stt[:P, t, 2:3], func=AF.Exp, scale=-0.5),
                 reads=[('st', t, 2)], writes=[('st', t, 3)])

        def phase_norm(g_row, tgroups=None):
            tgroups = tgroups or [list(range(NT))]
            with ExitStack() as ph:
                G = sb(ph, "G", [128, 1024], F32)
                hb = [sb(ph, "hb%d" % i, [128, 1024], BF16) for i in range(3)]
                junk = sb(ph, "junk", [128, 1024], BF16)
                junk2 = sb(ph, "junk2", [128, 1024], BF16)
                stt = sb(ph, "nst", [128, 4, NT], F32)
                S.dma('sp', 'G', out=G[:], in_=g_row.partition_broadcast(128), writes=['G'])
                S.op('dve', lambda e: e.memset(stt[:, 0, :], 1.0), writes=['st0i'])
                for tg in tgroups:
                    t0g, t1g = tg[0], tg[-1] + 1
                    for t in tg:
                        P = TP(t)
                        if t % 2 == 0:
                            S.op('act', lambda e: e.activation(out=junk[:P], in_=X[:P, t, :], func=AF.Square, accum_out=stt[:P, 0, t:t + 1]),
                                 reads=[('X', t), 'st0i'], writes=['junk', ('st0', t)])
                        else:
                            S.op('dve', lambda e: e.scalar_tensor_tensor(out=junk2[:P], in0=X[:P, t, :], scalar=1.0, in1=X[:P, t, :],
                                                                         op0=ALU.mult, op1=ALU.mult, accum_out=stt[:P, 0, t:t + 1]),
                                 reads=[('X', t), 'st0i'], writes=['junk2', ('st0', t)])
                    S.op('dve', lambda e: e.tensor_scalar(out=stt[:, 1, t0g:t1g], in0=stt[:, 0, t0g:t1g], scalar1=1.0 / 1024, scalar2=EPS,
                                                          op0=ALU.mult, op1=ALU.add),
                         reads=[('st0', t) for t in tg], writes=[('st1', t0g)])
                    S.op('act', lambda e: e.activation(out=stt[:, 2, t0g:t1g], in_=stt[:, 1, t0g:t1g], func=AF.Ln),
                         reads=[('st1', t0g)], writes=[('st2', t0g)])
                    S.op('act', lambda e: e.activation(out=stt[:, 3, t0g:t1g], in_=stt[:, 2, t0g:t1g], func=AF.Exp, scale=-0.5),
                         reads=[('st2', t0g)], writes=[('st3', t0g)])
                    for t in tg:
                        P = TP(t); c = t * 128
                        hs = t % 3
                        S.op('dve', lambda e: e.scalar_tensor_tensor(out=hb[hs][:P], in0=X[:P, t, :], scalar=stt[:P, 3, t:t + 1], in1=G[:P],
                                                                     op0=ALU.mult, op1=ALU.mult),
                             reads=[('X', t), ('st3', t0g), 'G'], writes=[('hb', hs)])
                        bank = t % 4
                        pv = psb(bank).rearrange("p (k n) -> p k n", k=8)

                        def tr(e):
                            for kc in range(8):
                                ins = e.transpose(out=pv[:, kc, :P], in_=hb[hs][:P, kc * 128:(kc + 1) * 128], identity=idb[:P, :P])
                            return ins
                        S.op('pe', tr, reads=[('hb', hs), 'idb'], writes=[('ps', bank)])
                        S.op('act', lambda e: e.activation(out=H[:, :, c:c + P], in_=pv[:, :, :P], func=AF.Copy),
                             reads=[('ps', bank)], writes=[('H', t)])
                S.barrier()

        def load_slab(q, key, dst, src3, kcs, c0, w, wkey):
            S.dma(q, key, out=dst, in_=src3[:, kcs[0]:kcs[-1] + 1, c0:c0 + w], writes=[wkey])

        if True:
          for l in range(DEPTH):
            w_in3 = w_in[l].rearrange("(k p) c -> p k c", p=128)
            w_out3 = w_out[l].rearrange("(k p) c -> p k c", p=128)
            w_gu3 = w_gu[l].rearrange("(k p) c -> p k c", p=128)
            w_dn3 = w_dn[l].rearrange("(k p) c -> p k c", p=128)
            pw3 = conv_pw[l].rearrange("(k p) c -> p k c", p=128)

            WP = FA[:, 16640:22784].rearrange("p (k c) -> p k c", k=8)
            WOP = FA[:, 0:4096].rearrange("p (k c) -> p k c", k=4)
            if l == 0:
                S.dma('pool', 'wp0', out=WP[:, :, 0:256], in_=w_in3[:, :, 0:256], reads=[('X', 11)], writes=['WP0'])
                S.dma('pool', 'wp1', out=WP[:, :, 256:768], in_=w_in3[:, :, 1792:2304], writes=['WP1'])
            S.dma('pool', 'wop0', out=WOP[:, 0:2, :], in_=w_out3[:, 0:2, :], writes=['WOP0'])
            S.dma('pool', 'wop1', out=WOP[:, 2:4, :], in_=w_out3[:, 6:8, :], writes=['WOP1'])
            if l == 0:
                phase_norm(gmix[l:l + 1, :], [[0, 1, 2, 3], [4, 5, 6, 7], [8, 9, 10, 11], [12, 13, 14, 15, 16]])

            phq = ExitStack()
            Wq0 = sb(phq, "Wq0", [128, 8, 512], BF16)
            S.dma('pool', 'wq0', out=Wq0[:], in_=w_in3[:, :, 256:768], writes=[('W', 0)])
            with ExitStack() as ph:
                DG = FA[:, 4096:12032].rearrange("p (k c) -> p k c", k=62)
                CU = [FA[:, 12032 + i * 1084:12032 + (i + 1) * 1084].rearrange("p (j c) -> p j c", j=2) for i in range(2)]
                dsb = FA[:, 22784:23808].rearrange("p (j c) -> p j c", j=2)
                MPC2 = [FA[:, 14200:16248].rearrange("p (j c) -> p j c", j=4), sb(ph, "MPC1", [128, 4, 512], BF16)]
                sl2 = [FA[:, 23808:24832].rearrange("p (j c) -> p j c", j=2), sb(ph, "sl1", [128, 2, 512], BF16)]
                PW = sb(ph, "PW", [128, 2, 256], BF16)
                PLW = sb(ph, "PLW", [128, 2, 128], BF16)
                INVC = sb(ph, "INVC", [128, 2, 16], F32)
                PU = [sb(ph, "PU%d" % i, [128, 2, 527], F32) for i in range(2)]
                U32 = sb(ph, "U32", [128, 2, 512], F32)
                WA = sb(ph, "WA", [128, 526], F32); WB = sb(ph, "WB", [128, 524], F32)
                WC = WA; WD = WB
                t16 = sb(ph, "t16", [128, 16], F32)
                sg2 = [sb(ph, "sg%d" % i, [128, 512], F32) for i in range(2)]
                ysb = sb(ph, "ysb", [128, 2, 512], F32)
                ysq = sb(ph, "ysq", [128, 2, 512], F32)
                m2 = sb(ph, "m2", [128, 512], F32); var = sb(ph, "var", [128, 512], F32)

                S.dma('pool', 'pw', out=PW[:], in_=pw3, writes=['PW'])
                S.dma('pool', 'plw', out=PLW[:], in_=plw[l], writes=['PLW'])
                S.dma('sp', 'invc', out=INVC[:], in_=invc[:, :, :], writes=['INVC'])
                S.op('dve', lambda e: e.tensor_tensor(out=DG, in0=idb[:].unsqueeze(1).broadcast_to([128, 62, 128]),
                                                      in1=PV[:, l, 11:73].unsqueeze(2).broadcast_to([128, 62, 128]), op=ALU.mult),
                     reads=['PV', 'idb'], writes=['DG'])
                S.op('dve', lambda e: e.memset(PU[0][:, :, 0:15], 0.0), writes=[('PUc', 0)])
                S.op('dve', lambda e: e.memset(CU[0][:, :, 0:30], 0.0), writes=[('CUc', 0)])

                def stageA(gi):
                    c0, n, tiles = GROUPS[gi]
                    sample = (gi == 4)
                    slot = 0 if sample else gi % 2
                    if sample:
                        S.dma('sp', 'stp', out=PU[0][:, :, 0:15], in_=stp[l], writes=[('PUc', 0)])
                        S.dma('pool', 'stc', out=CU[0][:, :, 0:30], in_=stc[l], writes=[('CUc', 0)])
                    def proj(bank, col):
                        def f(e):
                            for kc in range(8):
                                ins = e.matmul(PS[:, bank, 0:n], lhsT=WP[:, kc, col:col + 128], rhs=H[:, kc, c0:c0 + n],
                                               start=(kc == 0), stop=(kc == 7))
                            return ins
                        S.op('pe', f, reads=['WP0', 'WP1'] + [('H', t) for t in tiles], writes=[('ps', bank)])
                    for j in range(2):
                        proj(j, j * 128)
                        S.op('act', lambda e: e.activation(out=PU[slot][:, j, 15:15 + n], in_=PS[:, j, 0:n], func=AF.Copy),
                             reads=[('ps', j)], writes=[('PUn', slot, j)])
                    if gi == 3 or sample:
                        S.dma('sp', 'o_np%d' % gi, out=(nps[l] if sample else npp[l]), in_=PU[slot][:, :, n:n + 15],
                              reads=[('PUc', slot), ('PUn', slot, 0), ('PUn', slot, 1)], is_out=True)
                    if gi < 3:
                        S.op('dve', lambda e: e.tensor_copy(out=PU[(gi + 1) % 2][:, :, 0:15], in_=PU[slot][:, :, 512:527]),
                             reads=[('PUn', slot, 0), ('PUn', slot, 1)], writes=[('PUc', (gi + 1) % 2)])
                    for j in range(2):
                        ext = PU[slot][:, j, :]
                        rk = [('PUc', slot), ('PUn', slot, j)]
                        S.op('dve', lambda e: e.tensor_tensor(out=WA[:, 0:14 + n], in0=ext[:, 1:15 + n], in1=ext[:, 0:14 + n], op=ALU.add),
                             reads=rk, writes=['WA'])
                        S.op('dve', lambda e: e.tensor_tensor(out=WB[:, 0:12 + n], in0=WA[:, 2:14 + n], in1=WA[:, 0:12 + n], op=ALU.add),
                             reads=['WA'], writes=['WB'])
                        if j == 0:
                            lo, lo_off, hi, hi_off = WA, 14, WB, 12
                            kk = ['WA', 'WB']
                        else:
                            S.op('dve', lambda e: e.tensor_tensor(out=WC[:, 0:8 + n], in0=WB[:, 4:12 + n], in1=WB[:, 0:8 + n], op=ALU.add),
                                 reads=['WB'], writes=['WA'])
                            S.op('dve', lambda e: e.tensor_tensor(out=WD[:, 0:n], in0=WC[:, 8:8 + n], in1=WC[:, 0:n], op=ALU.add),
                                 reads=['WA'], writes=['WB'])
                            lo, lo_off, hi, hi_off = WC, 8, WD, 0
                            kk = ['WA', 'WB']
                        for (src, off, p0) in ((lo, lo_off, 0), (hi, hi_off, 64)):
                            S.op('dve', lambda e: e.scalar_tensor_tensor(out=dsb[p0:p0 + 64, j, 0:n], in0=src[p0:p0 + 64, off:off + n],
                                                                         scalar=PV[p0:p0 + 64, l, 9 + j:10 + j], in1=ext[p0:p0 + 64, 15:15 + n],
                                                                         op0=ALU.mult, op1=ALU.subtract),
                                 reads=kk + rk + ['PV'], writes=[('dsb', j)])
                            if gi == 0:
                                S.op('dve', lambda e: e.tensor_tensor(out=t16[p0:p0 + 64, :], in0=src[p0:p0 + 64, off:off + 16],
                                                                      in1=INVC[p0:p0 + 64, j, :], op=ALU.mult),
                                     reads=kk + ['INVC'], writes=['t16'])
                                S.op('dve', lambda e: e.tensor_tensor(out=dsb[p0:p0 + 64, j, 0:16], in0=t16[p0:p0 + 64, :],
                                                                      in1=ext[p0:p0 + 64, 15:31], op=ALU.subtract),
                                     reads=['t16'] + rk, writes=[('dsb', j)])
                    for j in range(2):
                        proj(2 + 2 * j, 256 + j * 128)
                        proj(3 + 2 * j, 512 + j * 128)
                        S.op('act', lambda e: e.activation(out=sg2[j][:, 0:n], in_=PS[:, 3 + 2 * j, 0:n], func=AF.Sigmoid),
                             reads=[('ps', 3 + 2 * j)], writes=[('sg', j)])
                        S.op('dve', lambda e: e.tensor_tensor(out=U32[:, j, 0:n], in0=PS[:, 2 + 2 * j, 0:n], in1=sg2[j][:, 0:n], op=ALU.mult),
                             reads=[('ps', 2 + 2 * j), ('sg', j)], writes=[('U32', j)])
                        S.op('pool', lambda e: e.tensor_copy(out=CU[slot][:, j, 30:30 + n], in_=U32[:, j, 0:n]),
                             reads=[('U32', j)], writes=[('CUn', slot, j)])
                    if gi == 3 or sample:
                        S.dma('sp', 'o_nc%d' % gi, out=(ncs[l] if sample else ncp[l]), in_=U32[:, :, n - 30:n],
                              reads=[('U32', 0), ('U32', 1)], is_out=True)
                    if gi < 3:
                        S.op('dve', lambda e: e.tensor_copy(out=CU[(gi + 1) % 2][:, :, 0:30], in_=CU[slot][:, :, 512:542]),
                             reads=[('CUn', slot, 0), ('CUn', slot, 1)], writes=[('CUc', (gi + 1) % 2)])

                def stageC(gi):
                    c0, n, tiles = GROUPS[gi]
                    sample = (gi == 4)
                    slot = 0 if sample else gi % 2
                    MPC = MPC2[gi % 2]; sl = sl2[gi % 2]
                    for j in range(2):
                        S.op('pe', lambda e: e.matmul(PS[:, j, 0:n], lhsT=PLW[:, j, :], rhs=dsb[:, j, 0:n], start=True, stop=True),
                             reads=[('dsb', j), 'PLW'], writes=[('ps', j)])
                        S.op('act', lambda e: e.activation(out=MPC[:, j, 0:n], in_=PS[:, j, 0:n], func=AF.Copy, scale=PV[:, l, j:j + 1]),
                             reads=[('ps', j), 'PV'], writes=[('MPC', gi % 2, j)])
                    for j in range(2):
                        def cv_mm(e):
                            for tap in range(31):
                                ins = e.matmul(PS[:, 4 + j, 0:n], lhsT=DG[:, j * 31 + tap, :], rhs=CU[slot][:, j, tap:tap + n],
                                               start=(tap == 0), stop=(tap == 30))
                            return ins
                        S.op('pe', cv_mm, reads=['DG', ('CUc', slot), ('CUn', slot, j)], writes=[('ps', 4 + j)])
                        S.op('act', lambda e: e.activation(out=ysb[:, j, 0:n], in_=PS[:, 4 + j, 0:n], func=AF.Identity, bias=PV[:, l, 2 + j:3 + j]),
                             reads=[('ps', 4 + j), 'PV'], writes=[('ysb', j)])
                        S.op('act', lambda e: e.activation(out=ysq[:, j, 0:n], in_=PS[:, 4 + j, 0:n], func=AF.Square, bias=PV[:, l, 2 + j:3 + j]),
                             reads=[('ps', 4 + j), 'PV'], writes=[('ysq', j)])

                    def st_mm(e):
                        for j in range(2):
                            e.matmul(PS[:, 6, 0:n], lhsT=onesf[:, 1, :], rhs=ysb[:, j, 0:n], start=(j == 0), stop=(j == 1))
                        for j in range(2):
                            ins = e.matmul(PS[:, 7, 0:n], lhsT=onesf[:, 1, :], rhs=ysq[:, j, 0:n], start=(j == 0), stop=(j == 1))
                        return ins
                    S.op('pe', st_mm, reads=[('ysb', 0), ('ysb', 1), ('ysq', 0), ('ysq', 1), 'onesf1'], writes=[('ps', 6), ('ps', 7)])
                    S.op('act', lambda e: e.activation(out=m2[:, 0:n], in_=PS[:, 6, 0:n], func=AF.Square), reads=[('ps', 6)], writes=['m2'])
                    S.op('dve', lambda e: e.scalar_tensor_tensor(out=var[:, 0:n], in0=PS[:, 7, 0:n], scalar=EPS, in1=m2[:, 0:n],
                                                                 op0=ALU.add, op1=ALU.subtract),
                         reads=[('ps', 7), 'm2'], writes=['var'])
                    S.op('act', lambda e: e.activation(out=m2[:, 0:n], in_=var[:, 0:n], func=AF.Ln), reads=['var'], writes=['m2'])
                    S.op('act', lambda e: e.activation(out=var[:, 0:n], in_=m2[:, 0:n], func=AF.Exp, scale=-0.5), reads=['m2'], writes=['var'])
                    for j in range(2):
                        S.op('dve', lambda e: e.tensor_tensor(out=ysq[:, j, 0:n], in0=ysb[:, j, 0:n], in1=PS[:, 6, 0:n], op=ALU.subtract),
                             reads=[('ysb', j), ('ps', 6)], writes=[('ysq', j)])
                        S.op('dve', lambda e: e.tensor_tensor(out=ysb[:, j, 0:n], in0=ysq[:, j, 0:n], in1=var[:, 0:n], op=ALU.mult),
                             reads=[('ysq', j), 'var'], writes=[('ysb', j)])
                        S.op('act', lambda e: e.activation(out=sl[:, j, 0:n], in_=ysb[:, j, 0:n], func=AF.Silu,
                                                           scale=PV[:, l, 4 + j:5 + j], bias=PV[:, l, 6 + j:7 + j]),
                             reads=[('ysb', j), 'PV'], writes=[('sl', gi % 2, j)])

                def stageB(gi):
                    c0, n, tiles = GROUPS[gi]
                    MPC = MPC2[gi % 2]; sl = sl2[gi % 2]
                    for jo in range(2):
                        def pw_mm(e):
                            for j in range(2):
                                ins = e.matmul(PS[:, jo, 0:n], lhsT=PW[:, j, jo * 128:(jo + 1) * 128], rhs=sl[:, j, 0:n],
                                               start=(j == 0), stop=(j == 1))
                            return ins
                        S.op('pe', pw_mm, reads=[('sl', gi % 2, 0), ('sl', gi % 2, 1), 'PW'], writes=[('ps', jo)])
                        S.op('act', lambda e: e.activation(out=MPC[:, 2 + jo, 0:n], in_=PS[:, jo, 0:n], func=AF.Copy),
                             reads=[('ps', jo)], writes=[('MPC', gi % 2, 2 + jo)])
                    for ti, t in enumerate(tiles):
                        P = TP(t)
                        for hh in range(2):
                            bank = 6 + hh

                            def wo_mm(e):
                                for kc in range(4):
                                    ins = e.matmul(PS[:P, bank, :], lhsT=MPC[:, kc, ti * 128:ti * 128 + P],
                                                   rhs=WOP[:, kc, hh * 512:(hh + 1) * 512], start=(kc == 0), stop=(kc == 3))
                                return ins
                            S.op('pe', wo_mm, reads=[('MPC', gi % 2, k) for k in range(4)] + ['WOP0', 'WOP1'], writes=[('ps', bank)])
                            S.op('dve', lambda e: e.tensor_tensor(out=X[:P, t, hh * 512:(hh + 1) * 512], in0=X[:P, t, hh * 512:(hh + 1) * 512],
                                                                  in1=PS[:P, bank, :], op=ALU.add),
                                 reads=[('ps', bank), ('X', t)], writes=[('X', t)])

                for gi in range(len(GROUPS)):
                    stageA(gi)
                    if gi > 0:
                        stageB(gi - 1)
                    stageC(gi)
                stageB(len(GROUPS) - 1)
                S.barrier()

            with ExitStack() as ph:
                W = [Wq0] + [sb(ph, "Wq%d" % i, [128, 8, 512], BF16) for i in (1, 2)]
                COS = sb(ph, "COS", [128, 17, 32], F32); SIN = sb(ph, "SIN", [128, 17, 32], F32)
                zs = [sb(ph, "zs%d" % i, [128, 512], F32) for i in range(3)]
                kr = [sb(ph, "kr%d" % i, [128, 512], F32) for i in range(2)]
                t1 = [sb(ph, "t1%d" % i, [128, 512], F32) for i in range(2)]
                t2 = [sb(ph, "t2%d" % i, [128, 512], F32) for i in range(2)]
                qb = [sb(ph, "qb%d" % i, [128, 512], BF16) for i in range(6)]
                S.dma('sp', 'cos', out=COS[:], in_=cos_t[:, :, :], writes=['COS'])
                S.dma('sp', 'sin', out=SIN[:], in_=sin_t[:, :, :], writes=['SIN'])
                for pi, c0 in ((1, 768), (2, 1280)):
                    S.dma('pool', 'wq%d' % pi, out=W[pi][:], in_=w_in3[:, :, c0:c0 + 512], writes=[('W', pi)])
                pend = []
                cnt = 0
                rc = 0
                NQ0 = 6
                items = [(t, 0) for t in range(NQ0)] + [(t, pi) for t in range(NQ0) for pi in (1, 2)] + \
                        [(t, pi) for t in range(NQ0, NT) for pi in (0, 1, 2)]
                for (t, pi) in items:
                    P = TP(t); c = t * 128
                    nm = ('q', 'k', 'v')[pi]
                    if True:
                        Wc = W[pi]
                        bank = cnt % 4
                        s2 = cnt % 3
                        cnt += 1
                        qs = rc % 6

                        def mm(e):
                            for kc in range(8):
                                ins = e.matmul(PS[:P, bank, :], lhsT=H[:, kc, c:c + P], rhs=Wc[:, kc, :], start=(kc == 0), stop=(kc == 7))
                            return ins
                        S.op('pe', mm, reads=[('H', t), ('W', pi)], writes=[('ps', bank)])
                        while len(pend) > 3:
                            pend.pop(0)()
                        S.op('act', lambda e: e.activation(out=zs[s2][:P], in_=PS[:P, bank, :], func=AF.Copy),
                             reads=[('ps', bank)], writes=[('zs', s2)])
                        if nm == 'v':
                            S.dma('sp', 'o_v%d' % s2, out=rows_v(l, t), in_=zs[s2][:P], reads=[('zs', s2)], is_out=True)
                            S.op('act', lambda e: e.activation(out=VB[:P, t, :], in_=zs[s2][:P], func=AF.Copy), reads=[('zs', s2)], writes=[('VB', t)])
                            continue
                        r2 = rc % 2
                        rc += 1
                        z4 = zs[s2][:P].rearrange("p (g two i) -> p g two i", g=8, two=2)
                        cosb = COS[:P, t:t + 1, :].unsqueeze(1).broadcast_to([P, 8, 2, 32])
                        sinb = SIN[:P, t:t + 1, :].broadcast_to([P, 8, 32])
                        t14 = t1[r2][:P].rearrange("p (g two i) -> p g two i", g=8, two=2)
                        t24 = t2[r2][:P].rearrange("p (g two i) -> p g two i", g=8, two=2)
                        S.op('dve', lambda e: e.tensor_tensor(out=t14, in0=z4, in1=cosb, op=ALU.mult),
                             reads=[('zs', s2), 'COS'], writes=[('t1', r2)])
                        S.op('dve', lambda e: e.tensor_tensor(out=t24[:, :, 0, :], in0=z4[:, :, 1, :], in1=sinb, op=ALU.mult),
                             reads=[('zs', s2), 'SIN'], writes=[('t2a', r2)])
                        S.op('dve', lambda e: e.tensor_tensor(out=t24[:, :, 1, :], in0=z4[:, :, 0, :], in1=sinb, op=ALU.mult),
                             reads=[('zs', s2), 'SIN'], writes=[('t2b', r2)])
                        if nm == 'q':
                            o4 = qb[qs][:P].rearrange("p (g two i) -> p g two i", g=8, two=2)
                        else:
                            o4 = kr[r2][:P].rearrange("p (g two i) -> p g two i", g=8, two=2)
                        okey = ('qb', qs) if nm == 'q' else ('kr', r2)
                        S.op('dve', lambda e: e.tensor_tensor(out=o4[:, :, 0, :], in0=t14[:, :, 0, :], in1=t24[:, :, 0, :], op=ALU.subtract),
                             reads=[('t1', r2), ('t2a', r2)], writes=[(okey, 'a')])
                        S.op('dve', lambda e: e.tensor_tensor(out=o4[:, :, 1, :], in0=t14[:, :, 1, :], in1=t24[:, :, 1, :], op=ALU.add),
                             reads=[('t1', r2), ('t2b', r2)], writes=[(okey, 'b')])
                        if nm == 'k':
                            S.dma('sp', 'o_k%d' % r2, out=rows_k(l, t), in_=kr[r2][:P], reads=[(okey, 'a'), (okey, 'b')], is_out=True)
                            S.op('act', lambda e: e.activation(out=qb[qs][:P], in_=kr[r2][:P], func=AF.Copy),
                                 reads=[(okey, 'a'), (okey, 'b')], writes=[(('qb', qs), 'a'), (('qb', qs), 'b')])

                        def mk(t=t, P=P, c=c, qs=qs, nm=nm, tb=4 + (rc % 2)):
                            def run():
                                pv = psb(tb).rearrange("p (k n) -> p k n", k=8)

                                def tr(e):
                                    for h in range(4):
                                        ins = e.transpose(out=pv[:, h, :P], in_=qb[qs][:P, h * 128:(h + 1) * 128], identity=idb[:P, :P])
                                    return ins
                                S.op('pe', tr, reads=[(('qb', qs), 'a'), (('qb', qs), 'b'), 'idb'], writes=[('ps', tb)])
                                dst = QT if nm == 'q' else KT
                                S.op('act', lambda e: e.activation(out=dst[:, :, c:c + P], in_=pv[:, 0:4, :P], func=AF.Copy),
                                     reads=[('ps', tb)], writes=[(nm + 'T', t)])
                            return run
                        pend.append(mk())
                while pend:
                    pend.pop(0)()
                S.barrier()
            phq.close()

            phw = ExitStack()
            WOA = sb(phw, "WOA", [128, 4, 1024], BF16)
            S.dma('pool', 'woa', out=WOA[:], in_=w_out3[:, 2:6, :], writes=['WOA'])
            ckb = [sb(phw, "ckb%d" % i, [128, 512], BF16) for i in range(2)]
            cvb = [sb(phw, "cvb%d" % i, [128, 512], BF16) for i in range(2)]
            for i in range(2):
                S.dma('pool', 'ck%d' % i, out=ckb[i][:], in_=ck[l, i * 128:(i + 1) * 128, :], writes=[('ckb', i)])
                S.dma('pool', 'cv%d' % i, out=cvb[i][:], in_=cv[l, i * 128:(i + 1) * 128, :], writes=[('cvb', i)])
            with ExitStack() as ph:
                PT = [sb(ph, "PT%d" % i, [128, 2, 512], BF16) for i in range(3)]
                OSB = [sb(ph, "OSB%d" % i, [128, 2, 512], F32) for i in range(2)]
                LNL = [sb(ph, "LNL%d" % i, [128, 2, 512], F32) for i in range(2)]
                dd = sb(ph, "dd", [128, 512], F32); sq = sb(ph, "sq", [128, 512], F32)
                msb = sb(ph, "msb", [128, 512], F32); rs = sb(ph, "rs", [128, 512], F32)
                MB = sb(ph, "MB", [128, 1], F32)
                S.op('dve', lambda e: e.memset(MB[0:64, :], 0.0), writes=['MBa'])
                S.op('dve', lambda e: e.memset(MB[64:128, :], NEG), writes=['MBb'])
                negl = LAM[:, l, 4:5]; gsc = LAM[:, l, 5:6]
                blocks = []
                for n_it, (h, j) in enumerate([(h, j) for h in range(4) for j in range(4)]):
                    for i in range(4 * j + 4):
                        blocks.append((n_it, h, j, i))
                NBK = len(blocks)

                def off_of(j, i):
                    r = i - 4 * j
                    return 128 * r if r > 0 else 0

                def qk(g):
                    n_it, h, j, i = blocks[g]
                    slot = g % 2
                    off = off_of(j, i)
                    q0 = j * 512

                    def f(e):
                        for c in range(2):
                            ins = e.matmul(PS[:, 2 * slot + c, off:512], lhsT=KT[c * 64:(c + 1) * 64, h, i * 128:(i + 1) * 128],
                                           rhs=QT[c * 64:(c + 1) * 64, h, q0 + off:q0 + 512], start=True, stop=True)
                        return ins
                    S.op('pe', f, reads=[('QT', h, j)], writes=[('ps', 2 * slot), ('ps', 2 * slot + 1)])

                def ex(g):
                    n_it, h, j, i = blocks[g]
                    slot = g % 2
                    off = off_of(j, i)
                    ps_ = g % 3
                    diag = (i - 4 * j) >= 0
                    src = PS[:, 2 * slot:2 * slot + 2, :]
                    if diag:
                        S.op('act', lambda e: e.activation(out=PT[ps_][:, :, off:off + 64], in_=src[:, :, off:off + 64], func=AF.Exp,
                                                           scale=SCALE, bias=MB[:, 0:1]),
                             reads=[('ps', 2 * slot), ('ps', 2 * slot + 1), 'MBa', 'MBb'], writes=[('PT', ps_, 'm')])
                        S.op('act', lambda e: e.activation(out=PT[ps_][:, :, off + 64:512], in_=src[:, :, off + 64:512], func=AF.Exp,
                                                           scale=SCALE),
                             reads=[('ps', 2 * slot), ('ps', 2 * slot + 1)], writes=[('PT', ps_)])
                    else:
                        S.op('act', lambda e: e.activation(out=PT[ps_][:, :, :], in_=src, func=AF.Exp, scale=SCALE),
                             reads=[('ps', 2 * slot), ('ps', 2 * slot + 1)], writes=[('PT', ps_), ('PT', ps_, 'm')])

                def pvm(g):
                    n_it, h, j, i = blocks[g]
                    off = off_of(j, i)
                    ps_ = g % 3
                    nb = 4 * j + 4

                    def f(e):
                        for c in range(2):
                            e.matmul(PS[:, 4 + c, off:512], lhsT=VB[:, i, h * 128:(h + 1) * 128], rhs=PT[ps_][:, c, off:512],
                                     start=(i == 0), stop=(i == nb - 1))
                        for c in range(2):
                            ins = e.matmul(PS[:, 6 + c, off:512], lhsT=onesb[:], rhs=PT[ps_][:, c, off:512],
                                           start=(i == 0), stop=(i == nb - 1))
                        return ins
                    S.op('pe', f, reads=[('PT', ps_), ('PT', ps_, 'm'), 'onesb'], writes=[('ps', 4), ('ps', 5), ('ps', 6), ('ps', 7)])

                def mkB(st):
                    def run(sbk):
                        S.op('act', lambda e: e.activation(out=LNL[st][:], in_=LNL[st][:], func=AF.Exp, scale=-1.0),
                             reads=[('LNL', st)], writes=[('LNL', st)])
                        S.op('dve', lambda e: e.tensor_tensor(out=OSB[st][:], in0=OSB[st][:], in1=LNL[st][:], op=ALU.mult),
                             reads=[('OSB', st), ('LNL', st)], writes=[('OSB', st)])
                        S.op('dve', lambda e: e.scalar_tensor_tensor(out=dd[:], in0=OSB[st][:, 1, :], scalar=negl, in1=OSB[st][:, 0, :],
                                                                     op0=ALU.mult, op1=ALU.add),
                             reads=[('OSB', st)], writes=['dd'])
                        S.op('dve', lambda e: e.tensor_tensor(out=sq[:], in0=dd[:], in1=dd[:], op=ALU.mult), reads=['dd'], writes=['sq'])
                    return run

                def mkC1():
                    def run(sbk):
                        S.op('pe', lambda e: e.matmul(PS[:, sbk, :], lhsT=onesf[:, 0, :], rhs=sq[:], start=True, stop=True),
                             reads=['sq', 'onesf0'], writes=[('ps', sbk)])
                        S.op('dve', lambda e: e.tensor_scalar(out=msb[:], in0=PS[:, sbk, :], scalar1=EPS, scalar2=None, op0=ALU.add),
                             reads=[('ps', sbk)], writes=['msb'])
                    return run

                def mkC2(h, j):
                    q0 = j * 512

                    def run():
                        S.op('act', lambda e: e.activation(out=msb[:], in_=msb[:], func=AF.Ln), reads=['msb'], writes=['msb'])
                        S.op('act', lambda e: e.activation(out=rs[:], in_=msb[:], func=AF.Exp, scale=-0.5), reads=['msb'], writes=['rs'])
                        S.op('dve', lambda e: e.scalar_tensor_tensor(out=QT[:, h, q0:q0 + 512], in0=dd[:], scalar=gsc, in1=rs[:],
                                                                     op0=ALU.mult, op1=ALU.mult),
                             reads=['dd', 'rs'], writes=[('QT', h, j)])
                    return run

                pendB = None
                savedC = None
                pendC2 = None
                qk(0)
                qk(1)
                for g in range(NBK):
                    n_it, h, j, i = blocks[g]
                    nb = 4 * j + 4
                    st = n_it % 2
                    ex(g)
                    if pendC2 is not None:
                        pendC2()
                        pendC2 = None
                    pvm(g)
                    if i == nb - 1:
                        S.op('dve', lambda e: e.tensor_copy(out=OSB[st][:], in_=PS[:, 4:6, :]),
                             reads=[('ps', 4), ('ps', 5)], writes=[('OSB', st)])
                        S.op('act', lambda e: e.activation(out=LNL[st][:], in_=PS[:, 6:8, :], func=AF.Ln),
                             reads=[('ps', 6), ('ps', 7)], writes=[('LNL', st)])
                        if savedC is not None:
                            savedC[0](2 * (g % 2))
                            pendC2 = savedC[1]
                        savedC = (mkC1(), mkC2(h, j))
                        pendB = (g + 2, mkB(st))
                    if pendB is not None and pendB[0] <= g:
                        pendB[1](0)
                        pendB = None
                    if g + 2 < NBK:
                        qk(g + 2)
                if pendC2 is not None:
                    pendC2()
                if pendB is not None:
                    pendB[1](0)
                savedC[0](0); savedC[1]()
                S.barrier()

            phf = ExitStack()
            GU = [sb(phf, "GU%d" % i, [128, 2, 8, 256], BF16) for i in range(2)]

            def issue_gu_abs(fs, w, slot):
                S.dma('pool', 'gug%d' % slot, out=GU[slot][:, 0, :, 0:w * 128], in_=w_gu3[:, :, fs * 128:(fs + w) * 128],
                      writes=[('GUg', slot)])
                S.dma('pool', 'guu%d' % slot, out=GU[slot][:, 1, :, 0:w * 128], in_=w_gu3[:, :, 2816 + fs * 128:2816 + (fs + w) * 128],
                      writes=[('GUu', slot)])
            issue_gu_abs(0, 2, 0)
            def wo_tile(t, banks):
                P = TP(t); c = t * 128
                for hh in range(2):
                    bank = banks[hh]

                    def wo_mm(e):
                        for kc in range(4):
                            ins = e.matmul(PS[:P, bank, :], lhsT=QT[:, kc, c:c + P], rhs=WOA[:, kc, hh * 512:(hh + 1) * 512],
                                           start=(kc == 0), stop=(kc == 3))
                        return ins
                    S.op('pe', wo_mm, reads=['WOA', ('QT', 's')] if t == 16 else ['WOA'], writes=[('ps', bank)])
                    S.op('dve', lambda e: e.tensor_tensor(out=X[:P, t, hh * 512:(hh + 1) * 512], in0=X[:P, t, hh * 512:(hh + 1) * 512],
                                                          in1=PS[:P, bank, :], op=ALU.add),
                         reads=[('ps', bank), ('X', t)], writes=[('X', t)])

            with ExitStack() as ph:
                ckT = [sb(ph, "ckT%d" % i, [128, 4, 128], BF16) for i in range(2)]
                PTs = [sb(ph, "PTs%d" % i, [128, 8, 32], BF16) for i in range(2)]
                rr = sb(ph, "rr", [128, 8, 32], F32); a8 = sb(ph, "a8", [128, 2, 4, 32], F32)
                d4 = sb(ph, "d4", [128, 4, 32], F32); sq4 = sb(ph, "sq4", [128, 4, 32], F32)
                ms4 = sb(ph, "ms4", [128, 128], F32); ln4 = sb(ph, "ln4", [128, 128], F32); rs4 = sb(ph, "rs4", [128, 128], F32)
                negl = LAM[:, l, 4:5]; gsc = LAM[:, l, 5:6]
                OS = PS[:, 4, 0:256].rearrange("p (g n) -> p g n", g=8)
                LS = PS[:, 5, 0:256].rearrange("p (g n) -> p g n", g=8)
                ATSM = 9
                for i in range(17 if ATSM > 0 else 0):
                    slot = i % 2
                    KP = 128 if i < 16 else 32
                    if i < 16:
                        if i >= 2:
                            S.dma('pool', 'ck%d' % slot, out=ckb[slot][:], in_=ck[l, i * 128:(i + 1) * 128, :], writes=[('ckb', slot)])
                            S.dma('pool', 'cv%d' % slot, out=cvb[slot][:], in_=cv[l, i * 128:(i + 1) * 128, :], writes=[('cvb', slot)])
                        pv = psb(slot).rearrange("p (k n) -> p k n", k=8)

                        def tr(e):
                            for h in range(4):
                                ins = e.transpose(out=pv[:, h, :], in_=ckb[slot][:, h * 128:(h + 1) * 128], identity=idb[:])
                            return ins
                        S.op('pe', tr, reads=[('ckb', slot), 'idb'], writes=[('ps', slot)])
                        S.op('act', lambda e: e.activation(out=ckT[slot][:], in_=pv[:, 0:4, :], func=AF.Copy),
                             reads=[('ps', slot)], writes=[('ckT', slot)])
                    sb0 = 2 if slot == 0 else 6
                    SS2 = PS[:, sb0:sb0 + 2, 0:128]
                    if ATSM < 2:
                        continue

                    def qk(e):
                        for h in range(4):
                            for c in range(2):
                                if i < 16:
                                    kT = ckT[slot][c * 64:(c + 1) * 64, h, :]
                                else:
                                    kT = KT[c * 64:(c + 1) * 64, h, 2048:2080]
                                ins = e.matmul(PS[:KP, sb0 + c, h * 32:(h + 1) * 32], lhsT=kT, rhs=QT[c * 64:(c + 1) * 64, h, 2048:2080],
                                               start=True, stop=True)
                        return ins
                    S.op('pe', qk, reads=[('ckT', slot), ('QT', 's')], writes=[('ps', sb0), ('ps', sb0 + 1)])
                    S.op('act', lambda e: e.activation(out=PTs[slot][:KP].rearrange("p (c x) n -> p c (x n)", c=2), in_=SS2[:KP], func=AF.Exp, scale=SCALE),
                         reads=[('ps', sb0), ('ps', sb0 + 1)], writes=[('PTs', slot)])
                    if i < 16:
                        wo_tile(i, (6, 7) if slot == 0 else (2, 3))
                    if ATSM < 3:
                        continue

                    def pvm(e):
                        first = True
                        for c in range(2):
                            for h in range(4):
                                if i < 16:
                                    vv = cvb[slot][:, h * 128:(h + 1) * 128]
                                else:
                                    vv = VB[:32, 16, h * 128:(h + 1) * 128]
                                e.matmul(OS[:, c * 4 + h, :], lhsT=vv, rhs=PTs[slot][:KP, c * 4 + h, :],
                                         start=(i == 0 and first), stop=(i == 16), skip_group_check=True)
                                first = False
                        ins = e.matmul(PS[:, 5, 0:256], lhsT=onesb[:KP, :], rhs=PTs[slot][:KP].rearrange("p g n -> p (g n)"),
                                       start=(i == 0), stop=(i == 16))
                        return ins
                    S.op('pe', pvm, reads=[('PTs', slot), ('cvb', slot), 'onesb'], writes=[('ps', 4), ('ps', 5)])
                if ATSM < 4:
                    S.stopped = True
                S.op('dve', lambda e: e.reciprocal(out=rr[:], in_=LS), reads=[('ps', 5)], writes=['rr'])
                S.op('dve', lambda e: e.tensor_tensor(out=a8[:].rearrange("p c h n -> p (c h) n"), in0=OS, in1=rr[:], op=ALU.mult),
                     reads=[('ps', 4), 'rr'], writes=['a8'])
                S.op('dve', lambda e: e.scalar_tensor_tensor(out=d4[:], in0=a8[:, 1, :, :], scalar=negl, in1=a8[:, 0, :, :],
                                                             op0=ALU.mult, op1=ALU.add), reads=['a8'], writes=['d4'])
                S.op('act', lambda e: e.activation(out=sq4[:], in_=d4[:], func=AF.Square), reads=['d4'], writes=['sq4'])
                S.op('pe', lambda e: e.matmul(PS[:, 0, 0:128], lhsT=onesf[:, 0, :], rhs=sq4[:].rearrange("p h n -> p (h n)"), start=True, stop=True),
                     reads=['sq4', 'onesf0'], writes=[('ps', 0)])
                S.op('dve', lambda e: e.tensor_scalar(out=ms4[:], in0=PS[:, 0, 0:128], scalar1=EPS, scalar2=None, op0=ALU.add),
                     reads=[('ps', 0)], writes=['ms4'])
                S.op('act', lambda e: e.activation(out=ln4[:], in_=ms4[:], func=AF.Ln), reads=['ms4'], writes=['ln4'])
                S.op('act', lambda e: e.activation(out=rs4[:], in_=ln4[:], func=AF.Exp, scale=-0.5), reads=['ln4'], writes=['rs4'])
                S.op('dve', lambda e: e.scalar_tensor_tensor(out=QT[:, :, 2048:2080], in0=d4[:], scalar=gsc,
                                                             in1=rs4[:].rearrange("p (h n) -> p h n", h=4), op0=ALU.mult, op1=ALU.mult),
                     reads=['d4', 'rs4'], writes=[('QT', 's')])
                wo_tile(16, (2, 3))
                S.barrier()

            phase_norm(gffn[l:l + 1, :])
            with ExitStack() as ph:
                DN = [sb(ph, "DN%d" % i, [128, 8, 512], BF16) for i in range(2)]
                sgl = [sb(ph, "sgl%d" % i, [128, 512], F32) for i in range(2)]
                gcount = 0
                dcount = 0
                ev = 0
                for (f0, f1) in FFN_PASSES:
                    nf = f1 - f0
                    slabs = []
                    f = f0
                    while f < f1:
                        w = min(2, f1 - f)
                        slabs.append((f, w))
                        f += w

                    def issue_gu(si):
                        fs, w = slabs[si]
                        slot = (gcount + si) % 2
                        issue_gu_abs(fs, w, slot)
                    if f0 > 0:
                        issue_gu(0)
                    for hh in range(2):
                        S.dma('pool', 'dn%d' % hh, out=DN[hh][:, 0:nf, :], in_=w_dn3[:, f0:f1, hh * 512:(hh + 1) * 512], writes=[('DN', hh)])
                    for si, (fs, w) in enumerate(slabs):
                        slot = (gcount + si) % 2
                        if si + 1 < len(slabs):
                            issue_gu(si + 1)
                        for fo in range(w):
                            fi = fs + fo - f0
                            for (c0, n, tiles) in GROUPS:
                                bg = (ev % 3) * 2
                                ev += 1

                                def gu_mm(e):
                                    for kc in range(8):
                                        e.matmul(PS[:, bg, 0:n], lhsT=GU[slot][:, 0, kc, fo * 128:(fo + 1) * 128], rhs=H[:, kc, c0:c0 + n],
                                                 start=(kc == 0), stop=(kc == 7))
                                    for kc in range(8):
                                        ins = e.matmul(PS[:, bg + 1, 0:n], lhsT=GU[slot][:, 1, kc, fo * 128:(fo + 1) * 128], rhs=H[:, kc, c0:c0 + n],
                                                       start=(kc == 0), stop=(kc == 7))
                                    return ins
                                S.op('pe', gu_mm, reads=[('GUg', slot), ('GUu', slot)], writes=[('ps', bg), ('ps', bg + 1)])
                                ss = ev % 2
                                S.op('act', lambda e: e.activation(out=sgl[ss][:, 0:n], in_=PS[:, bg, 0:n], func=AF.Silu),
                                     reads=[('ps', bg)], writes=[('sgl', ss)])
                                S.op('dve', lambda e: e.tensor_tensor(out=ACTH[:, fi, c0:c0 + n], in0=sgl[ss][:, 0:n], in1=PS[:, bg + 1, 0:n], op=ALU.mult),
                                     reads=[('sgl', ss), ('ps', bg + 1)], writes=[('ACTH', fi)])
                    gcount += len(slabs)
                    lastpass = (f1 == 22)
                    last = lastpass and l == DEPTH - 1
                    mid = lastpass and l < DEPTH - 1
                    if lastpass:
                        GUf = [GU[i][:].rearrange("p a b c -> p (a b c)").bitcast(F32) for i in range(2)]
                        GU1b = GU[1][:].rearrange("p a b c -> p (a b c)")
                        Gfin = GUf[0][:, 0:1024]
                        stf = GUf[0][:, 1024:1024 + 4 * NT].rearrange("p (t k) -> p t k", k=4)
                        yof = [GUf[1][:, i * 1024:(i + 1) * 1024] for i in range(2)]
                        hbf = [GU1b[:, i * 1024:(i + 1) * 1024] for i in range(3)]
                        junkf = GU1b[:, 3072:4096]
                        grow = gfin[0:1, :] if last else gmix[l + 1:l + 2, :]
                    if mid:
                        w_in3n = w_in[l + 1].rearrange("(k p) c -> p k c", p=128)
                        WPn = FA[:, 16640:22784].rearrange("p (k c) -> p k c", k=8)
                        S.dma('pool', 'wp0', out=WPn[:, :, 0:256], in_=w_in3n[:, :, 0:256], writes=['WP0'])
                        S.dma('pool', 'wp1', out=WPn[:, :, 256:768], in_=w_in3n[:, :, 1792:2304], writes=['WP1'])
                    order = [(t, hh) for t in range(NT) for hh in range(2)] if lastpass else [(t, hh) for hh in range(2) for t in range(NT)]
                    for (t, hh) in order:
                        dslot = hh
                        P = TP(t); c = t * 128
                        bank = 4 + ((2 * t + hh) % 4 if lastpass else t % 4)

                        def dn_mm(e):
                            for fi in range(nf):
                                ins = e.matmul(PS[:P, bank, :], lhsT=ACTH[:, fi, c:c + P], rhs=DN[dslot][:, fi, :],
                                               start=(fi == 0), stop=(fi == nf - 1))
                            return ins
                        S.op('pe', dn_mm, reads=[('ACTH', fi) for fi in range(nf)] + [('DN', dslot)], writes=[('ps', bank)])
                        S.op('dve', lambda e: e.tensor_tensor(out=X[:P, t, hh * 512:(hh + 1) * 512], in0=X[:P, t, hh * 512:(hh + 1) * 512],
                                                              in1=PS[:P, bank, :], op=ALU.add),
                             reads=[('ps', bank), ('X', t)], writes=[('X', t)])
                        if lastpass and hh == 1:
                            if t == 0:
                                S.dma('sp', 'Gf', out=Gfin, in_=grow.partition_broadcast(128), reads=[('X', 0)], writes=['Gfin'])

                            def fin1(t):
                                P = TP(t)
                                jout = yof[t % 2][:P] if last else junkf[:P]
                                jkey = ('yo', t % 2) if last else 'junkf'
                                S.op('act', lambda e: e.activation(out=jout, in_=X[:P, t, :], func=AF.Square, accum_out=stf[:P, t, 0:1]),
                                     reads=[('X', t)], writes=[jkey, ('st', t, 0)])

                            def fin2(t):
                                rstd_chain(stf, TP(t), t, None)

                            def fin3(t):
                                P = TP(t)
                                if last:
                                    S.op('dve', lambda e: e.scalar_tensor_tensor(out=yof[t % 2][:P], in0=X[:P, t, :], scalar=stf[:P, t, 3:4],
                                                                                 in1=Gfin[:P], op0=ALU.mult, op1=ALU.mult),
                                         reads=[('X', t), ('st', t, 3), 'Gfin'], writes=[('yo', t % 2)])
                                    S.dma('sp', 'o_y%d' % (t % 2), out=rows_y(t), in_=yof[t % 2][:P], reads=[('yo', t % 2)], is_out=True)
                                else:
                                    S.op('dve', lambda e: e.scalar_tensor_tensor(out=hbf[t % 3][:P], in0=X[:P, t, :], scalar=stf[:P, t, 3:4],
                                                                                 in1=Gfin[:P], op0=ALU.mult, op1=ALU.mult),
                                         reads=[('X', t), ('st', t, 3), 'Gfin'], writes=[('hbf', t % 3)])

                            def fin4(t):
                                if last:
                                    return
                                P = TP(t); c4 = t * 128
                                tbk = t % 4
                                pvf = psb(tbk).rearrange("p (k n) -> p k n", k=8)

                                def tr(e):
                                    for kc in range(8):
                                        ins = e.transpose(out=pvf[:, kc, :P], in_=hbf[t % 3][:P, kc * 128:(kc + 1) * 128], identity=idb[:P, :P])
                                    return ins
                                S.op('pe', tr, reads=[('hbf', t % 3), 'idb'], writes=[('ps', tbk)])
                                S.op('act', lambda e: e.activation(out=H[:, :, c4:c4 + P], in_=pvf[:, :, :P], func=AF.Copy),
                                     reads=[('ps', tbk)], writes=[('H', t)])
                            if t >= 3:
                                fin4(t - 3)
                            if t >= 2:
                                fin3(t - 2)
                            if t >= 1:
                                fin2(t - 1)
                            fin1(t)
                            if t == NT - 1:
                                fin4(t - 2)
                                fin3(t - 1)
                                fin2(t)
                                fin4(t - 1)
                                fin3(t)
                                fin4(t)
                S.barrier()
            phf.close()
            phw.close()

        S.stopped = False

        if True:
            S.finish()
    return nc


def _host_consts():
    half = 32
    inv = (np.float32(10000.0) ** (-np.arange(half, dtype=np.float32) / np.float32(half))).astype(np.float32)
    pos = np.zeros((128, 17), np.float32)
    for t in range(16):
        pos[:, t] = t * 128 + np.arange(128)
    pos[:, 16] = 2048 + np.arange(128)
    ang = (pos[:, :, None] * inv[None, None, :]).astype(np.float32)
    cos_t = np.cos(ang).astype(np.float32)
    sin_t = np.sin(ang).astype(np.float32)
    wins = np.array([[2, 4], [8, 16]])
    invc = np.zeros((128, 2, 16), np.float32)
    invw = np.zeros((128, 2), np.float32)
    for j in range(2):
        for p in range(128):
            w = wins[j][p // 64]
            invw[p, j] = 1.0 / w
            invc[p, j, :] = 1.0 / np.minimum(np.arange(16) + 1, w)
    return cos_t, sin_t, invc, invw


_NC_CACHE = {}


def kernel(x_prompt, x_sample, cache_k, cache_v, state_pool, state_conv,
           norm_mix_g, w_in, pool_w, pool_scale, lambda_qk, diff_norm_g,
           conv_dw, conv_dw_b, conv_ln_g, conv_ln_b, conv_pw, w_out,
           norm_ffn_g, w_gate_up, w_down, final_norm_g):
    f = lambda a: np.ascontiguousarray(np.asarray(a, dtype=np.float32))
    x_prompt, x_sample, cache_k, cache_v = f(x_prompt), f(x_sample), f(cache_k), f(cache_v)
    state_pool, state_conv = f(state_pool), f(state_conv)
    B = 8
    cos_t, sin_t, invc, invw = _host_consts()

    def pc(v):
        return np.asarray(v, np.float32).reshape(2, 128).T
    pvec = np.zeros((2, 128, NPV), np.float32)
    plw = np.zeros((2, 128, 2, 128), np.float32)
    pool_w = f(pool_w); conv_dw = f(conv_dw)
    for l in range(2):
        pvec[l, :, 0:2] = pc(pool_scale[l])
        pvec[l, :, 2:4] = pc(conv_dw_b[l])
        pvec[l, :, 4:6] = pc(conv_ln_g[l])
        pvec[l, :, 6:8] = pc(conv_ln_b[l])
        pvec[l, :, 8] = np.asarray(diff_norm_g[l], np.float32)
        pvec[l, :, 9:11] = invw
        dwl = conv_dw[l].reshape(31, 2, 128)
        pvec[l, :, 11:73] = dwl.transpose(2, 1, 0).reshape(128, 62)
        for j in range(2):
            for hf in range(2):
                plw[l, hf * 64:(hf + 1) * 64, j, hf * 64:(hf + 1) * 64] = pool_w[l, 2 * j + hf]
    common = dict(
        w_in=f(w_in), w_out=f(w_out), w_gu=f(w_gate_up), w_dn=f(w_down), conv_pw=f(conv_pw), plw=plw,
        gmix=f(norm_mix_g), gffn=f(norm_ffn_g), gfin=f(final_norm_g).reshape(1, 1024),
        pvec=pvec, lq=f(lambda_qk).reshape(2, 256),
        ident=np.eye(128, dtype=np.float32), cos_t=cos_t, sin_t=sin_t, invc=invc,
    )
    in_maps = []
    for b in range(B):
        m = dict(common)
        m["xp"] = x_prompt[b]
        m["xs"] = x_sample[b]
        m["ck"] = np.ascontiguousarray(cache_k[:, b].reshape(2, 2048, 512))
        m["cv"] = np.ascontiguousarray(cache_v[:, b].reshape(2, 2048, 512))
        m["stp"] = np.ascontiguousarray(state_pool[:, b].reshape(2, 15, 2, 128).transpose(0, 3, 2, 1))
        m["stc"] = np.ascontiguousarray(state_conv[:, b].reshape(2, 30, 2, 128).transpose(0, 3, 2, 1))
        in_maps.append(m)
    if "nc" not in _NC_CACHE:
        _NC_CACHE["nc"] = build_program()
    nc = _NC_CACHE["nc"]
    res = run_bass_kernel_spmd(nc, in_maps, core_ids=list(range(B)))
    R = res.results

    def st(name):
        return np.stack([np.asarray(R[b][name], np.float32) for b in range(B)], axis=0)
    y_prompt = st("yp")
    y_sample = st("ys")
    nk_p = st("nkp").transpose(1, 0, 2, 3).reshape(2, B, 2048, 4, 2, 64)
    nv_p = st("nvp").transpose(1, 0, 2, 3).reshape(2, B, 2048, 4, 128)
    nk_s = st("nks").transpose(1, 0, 2, 3).reshape(2, B, 32, 4, 2, 64)
    nv_s = st("nvs").transpose(1, 0, 2, 3).reshape(2, B, 32, 4, 128)

    def unT(name, T):
        a = st(name)
        return np.ascontiguousarray(a.transpose(1, 0, 4, 3, 2).reshape(2, B, T, 256))
    np_p = unT("npp", 15); nc_p = unT("ncp", 30); np_s = unT("nps", 15); nc_s = unT("ncs", 30)
    return (y_prompt, y_sample, np.ascontiguousarray(nk_p), np.ascontiguousarray(nv_p), np_p, nc_p,
            np.ascontiguousarray(nk_s), np.ascontiguousarray(nv_s), np_s, nc_s)
```

```python
import math
from contextlib import ExitStack

import numpy as np
import concourse.bass as bass
import concourse.mybir as mybir
from concourse.bass_utils import run_bass_kernel_spmd

F32 = mybir.dt.float32
BF16 = mybir.dt.bfloat16
ALU = mybir.AluOpType
AF = mybir.ActivationFunctionType

EPS = 1e-6
SCALE = 0.125
NT = 17
NTOK = 2080
DEPTH = 2
NPV = 73
FFN_PASSES = [(0, 8), (8, 15), (15, 22)]
GROUPS = [(0, 512, [0, 1, 2, 3]), (512, 512, [4, 5, 6, 7]), (1024, 512, [8, 9, 10, 11]),
          (1536, 512, [12, 13, 14, 15]), (2048, 32, [16])]
NEG = -30000.0


def TP(t):
    return 128 if t < 16 else 32


import os
STOP_AFTER = 0


class _Stop(Exception):
    pass


class Sched:
    def __init__(self, nc, es):
        self.nc = nc
        self.es = es
        self.eng = {'pe': nc.tensor, 'act': nc.scalar, 'dve': nc.vector, 'pool': nc.gpsimd, 'sp': nc.sync}
        self.csem = {e: es.enter_context(nc.semaphore('c_' + e)) for e in ('pe', 'act', 'dve', 'pool')}
        self.ccount = {e: 0 for e in self.csem}
        self.dsem = {}
        self.lastw = {}
        self.readers = {}
        self.waited = {e: {} for e in self.eng}
        self.out_events = []
        self.stopped = False
        self.nbar = 0

    def _wait(self, eng, ev):
        sem, val, src, kind, sid = ev
        w = self.waited[eng]
        if w.get(sid, 0) >= val:
            return
        w[sid] = val
        self.eng[eng].wait_ge(sem, val)

    def _deps(self, eng, reads, writes):
        deps = []
        for k in reads:
            ev = self.lastw.get(k)
            if ev is not None:
                deps.append((ev, True))
        for k in writes:
            ev = self.lastw.get(k)
            if ev is not None:
                deps.append((ev, False))
            for ev in self.readers.get(k, {}).values():
                deps.append((ev, False))
        for ev, raw in deps:
            if ev[3] == 'c' and ev[2] == eng:
                if eng == 'pe' or not raw:
                    continue
            self._wait(eng, ev)

    def _commit(self, ev, reads, writes):
        for k in writes:
            self.lastw[k] = ev
            self.readers[k] = {}
        for k in reads:
            self.readers.setdefault(k, {})[ev[4]] = ev

    def op(self, eng, fn, reads=(), writes=()):
        if self.stopped:
            return
        self._deps(eng, reads, writes)
        ins = fn(self.eng[eng])
        self.ccount[eng] += 1
        ins.then_inc(self.csem[eng], 1)
        ev = (self.csem[eng], self.ccount[eng], eng, 'c', 'c_' + eng)
        self._commit(ev, reads, writes)

    def dma(self, q, key, out, in_, reads=(), writes=(), is_out=False):
        if self.stopped:
            return
        self._deps(q, reads, writes)
        if key not in self.dsem:
            self.dsem[key] = [self.es.enter_context(self.nc.semaphore('d%d' % len(self.dsem))), 0]
        d = self.dsem[key]
        d[1] += 16
        self.eng[q].dma_start(out=out, in_=in_).then_inc(d[0], 16)
        ev = (d[0], d[1], q, 'd', 'd_' + str(key))
        self._commit(ev, reads, writes)
        if is_out:
            self.out_events.append(ev)

    def barrier(self):
        if self.stopped:
            return
        self.nbar += 1
        self._barrier()
        if self.nbar == STOP_AFTER:
            self.stopped = True

    def _barrier(self):
        evs = [(self.csem[e], self.ccount[e], e, 'c', 'c_' + e) for e in self.csem if self.ccount[e] > 0]
        evs += [(d[0], d[1], 'x', 'd', 'd_' + str(k)) for k, d in self.dsem.items() if d[1] > 0]
        for eng in self.eng:
            for ev in evs:
                if ev[3] == 'c' and ev[2] == eng:
                    continue
                self._wait(eng, ev)
        self.lastw = {}
        self.readers = {}

    def finish(self):
        for ev in self.out_events:
            self._wait('sp', ev)


def build_program():
    nc = bass.Bass("TRN2", target_bir_lowering=False)

    def din(name, shape):
        return nc.dram_tensor(name, list(shape), F32, kind="ExternalInput").ap()

    def dout(name, shape):
        return nc.dram_tensor(name, list(shape), F32, kind="ExternalOutput").ap()

    xp = din("xp", [2048, 1024]); xs = din("xs", [32, 1024])
    ck = din("ck", [2, 2048, 512]); cv = din("cv", [2, 2048, 512])
    stp = din("stp", [2, 128, 2, 15]); stc = din("stc", [2, 128, 2, 30])
    w_in = din("w_in", [2, 1024, 2304]); w_out = din("w_out", [2, 1024, 1024])
    w_gu = din("w_gu", [2, 1024, 5632]); w_dn = din("w_dn", [2, 2816, 1024])
    conv_pw = din("conv_pw", [2, 256, 256]); plw = din("plw", [2, 128, 2, 128])
    gmix = din("gmix", [2, 1024]); gffn = din("gffn", [2, 1024]); gfin = din("gfin", [1, 1024])
    pvec = din("pvec", [2, 128, NPV]); lq = din("lq", [2, 256])
    ident = din("ident", [128, 128]); cos_t = din("cos_t", [128, 17, 32]); sin_t = din("sin_t", [128, 17, 32])
    invc = din("invc", [128, 2, 16])

    yp = dout("yp", [2048, 1024]); ys = dout("ys", [32, 1024])
    nkp = dout("nkp", [2, 2048, 512]); nvp = dout("nvp", [2, 2048, 512])
    npp = dout("npp", [2, 128, 2, 15]); ncp = dout("ncp", [2, 128, 2, 30])
    nks = dout("nks", [2, 32, 512]); nvs = dout("nvs", [2, 32, 512])
    nps = dout("nps", [2, 128, 2, 15]); ncs = dout("ncs", [2, 128, 2, 30])

    def rows_y(t):
        return yp[t * 128:(t + 1) * 128, :] if t < 16 else ys[0:32, :]

    def rows_k(l, t):
        return nkp[l, t * 128:(t + 1) * 128, :] if t < 16 else nks[l, 0:32, :]

    def rows_v(l, t):
        return nvp[l, t * 128:(t + 1) * 128, :] if t < 16 else nvs[l, 0:32, :]

    with ExitStack() as es:
        S = Sched(nc, es)

        uid = [0]

        def sb(stack, name, shape, dt):
            uid[0] += 1
            return stack.enter_context(nc.sbuf_tensor("%s_%d" % (name, uid[0]), list(shape), dt))

        PS = es.enter_context(nc.psum_tensor("PS", [128, 8, 512], F32))

        def psb(b):
            return PS[:, b, :].bitcast(BF16)

        X = sb(es, "X", [128, NT, 1024], F32)
        H = sb(es, "H", [128, 8, NTOK], BF16)
        FA = sb(es, "FA", [128, 25344], BF16)
        QT = FA[:, 0:8320].rearrange("p (h n) -> p h n", h=4)
        KT = FA[:, 8320:16640].rearrange("p (h n) -> p h n", h=4)
        VB = FA[:, 16640:25344].rearrange("p (t n) -> p t n", t=NT)
        ACTH = FA[:, 0:16640].rearrange("p (k n) -> p k n", k=8)
        idb = sb(es, "idb", [128, 128], BF16)
        onesb = sb(es, "onesb", [128, 128], BF16)
        onesf = sb(es, "onesf", [128, 3, 128], F32)
        PV = sb(es, "PV", [128, 2, NPV], F32)
        LAM = sb(es, "LAM", [128, 2, 8], F32)
        LQ = sb(es, "LQ", [128, 2, 256], F32)
        ljunk = sb(es, "ljunk", [128, 64], F32)

        S.dma('pool', 'c_id', out=idb[:], in_=ident[:, :], writes=['idb'])
        S.dma('sp', 'c_pv', out=PV[:], in_=pvec.rearrange("l p n -> p l n"), writes=['PV'])
        S.dma('sp', 'c_lq', out=LQ[:, 0, :], in_=lq[0:1, :].partition_broadcast(128), writes=[('LQ', 0)])
        S.dma('sp', 'c_lq1', out=LQ[:, 1, :], in_=lq[1:2, :].partition_broadcast(128), writes=[('LQ', 1)])
        S.dma('sp', 'xs', out=X[:32, 16, :], in_=xs[:, :], writes=[('X', 16)])
        xpv = xp.rearrange("(t p) d -> p t d", p=128)
        for q4 in range(4):
            S.dma('sp', 'x%d' % q4, out=X[:, q4 * 4:(q4 + 1) * 4, :], in_=xpv[:, q4 * 4:(q4 + 1) * 4, :],
                  writes=[('X', t) for t in range(q4 * 4, q4 * 4 + 4)])
        S.op('dve', lambda e: e.memset(onesb[:], 1.0), writes=['onesb'])
        S.op('dve', lambda e: e.memset(onesf[:, 0, :], 1.0 / 128), writes=['onesf0'])
        S.op('dve', lambda e: e.memset(onesf[:, 1, :], 1.0 / 256), writes=['onesf1'])
        S.op('dve', lambda e: e.memset(onesf[:, 2, :], 1.0), writes=['onesf2'])
        for l in range(DEPTH):
            lam_init = 0.8 - 0.6 * math.exp(-0.3 * l)
            S.op('dve', lambda e: e.scalar_tensor_tensor(out=ljunk[:], in0=LQ[:, l, 0:64], scalar=1.0, in1=LQ[:, l, 64:128],
                                                         op0=ALU.mult, op1=ALU.mult, accum_out=LAM[:, l, 0:1]),
                 reads=[('LQ', l)], writes=['ljunk', ('LAM', l, 0)])
            S.op('dve', lambda e: e.scalar_tensor_tensor(out=ljunk[:], in0=LQ[:, l, 128:192], scalar=1.0, in1=LQ[:, l, 192:256],
                                                         op0=ALU.mult, op1=ALU.mult, accum_out=LAM[:, l, 1:2]),
                 reads=[('LQ', l)], writes=['ljunk', ('LAM', l, 1)])
            S.op('act', lambda e: e.activation(out=LAM[:, l, 2:4], in_=LAM[:, l, 0:2], func=AF.Exp),
                 reads=[('LAM', l, 0), ('LAM', l, 1)], writes=[('LAM', l, 2)])
            S.op('dve', lambda e: e.tensor_tensor(out=LAM[:, l, 6:7], in0=LAM[:, l, 3:4], in1=LAM[:, l, 2:3], op=ALU.subtract),
                 reads=[('LAM', l, 2)], writes=[('LAM', l, 6)])
            S.op('dve', lambda e: e.tensor_scalar(out=LAM[:, l, 4:5], in0=LAM[:, l, 6:7], scalar1=-lam_init, scalar2=None, op0=ALU.add),
                 reads=[('LAM', l, 6)], writes=[('LAM', l, 4)])
            S.op('dve', lambda e: e.tensor_scalar(out=LAM[:, l, 5:6], in0=PV[:, l, 8:9], scalar1=(1.0 - lam_init), scalar2=None, op0=ALU.mult),
                 reads=['PV'], writes=[('LAM', l, 5)])
        S._barrier()

        def rstd_chain(stt, P, t, src_key):
            S.op('dve', lambda e: e.tensor_scalar(out=stt[:P, t, 1:2], in0=stt[:P, t, 0:1], scalar1=1.0 / 1024, scalar2=EPS,
                                                  op0=ALU.mult, op1=ALU.add),
                 reads=[('st', t, 0)], writes=[('st', t, 1)])
            S.op('act', lambda e: e.activation(out=stt[:P, t, 2:3], in_=stt[:P, t, 1:2], func=AF.Ln),
                 reads=[('st', t, 1)], writes=[('st', t, 2)])
            S.op('act', lambda e: e.activation(out=stt[:P, t, 3:4], in_=stt[:P, t, 2:3], func=AF.Exp, scale=-0.5),
                 reads=[('st', t, 2)], writes=[('st', t, 3)])

        def phase_norm(g_row, tgroups=None):
            tgroups = tgroups or [list(range(NT))]
            with ExitStack() as ph:
                G = sb(ph, "G", [128, 1024], F32)
                hb = [sb(ph, "hb%d" % i, [128, 1024], BF16) for i in range(3)]
                junk = sb(ph, "junk", [128, 1024], BF16)
                junk2 = sb(ph, "junk2", [128, 1024], BF16)
                stt = sb(ph, "nst", [128, 4, NT], F32)
                S.dma('sp', 'G', out=G[:], in_=g_row.partition_broadcast(128), writes=['G'])
                S.op('dve', lambda e: e.memset(stt[:, 0, :], 1.0), writes=['st0i'])
                for tg in tgroups:
                    t0g, t1g = tg[0], tg[-1] + 1
                    for t in tg:
                        P = TP(t)
                        if t % 2 == 0:
                            S.op('act', lambda e: e.activation(out=junk[:P], in_=X[:P, t, :], func=AF.Square, accum_out=stt[:P, 0, t:t + 1]),
                                 reads=[('X', t), 'st0i'], writes=['junk', ('st0', t)])
                        else:
                            S.op('dve', lambda e: e.scalar_tensor_tensor(out=junk2[:P], in0=X[:P, t, :], scalar=1.0, in1=X[:P, t, :],
                                                                         op0=ALU.mult, op1=ALU.mult, accum_out=stt[:P, 0, t:t + 1]),
                                 reads=[('X', t), 'st0i'], writes=['junk2', ('st0', t)])
                    S.op('dve', lambda e: e.tensor_scalar(out=stt[:, 1, t0g:t1g], in0=stt[:, 0, t0g:t1g], scalar1=1.0 / 1024, scalar2=EPS,
                                                          op0=ALU.mult, op1=ALU.add),
                         reads=[('st0', t) for t in tg], writes=[('st1', t0g)])
                    S.op('act', lambda e: e.activation(out=stt[:, 2, t0g:t1g], in_=stt[:, 1, t0g:t1g], func=AF.Ln),
                         reads=[('st1', t0g)], writes=[('st2', t0g)])
                    S.op('act', lambda e: e.activation(out=stt[:, 3, t0g:t1g], in_=stt[:, 2, t0g:t1g], func=AF.Exp, scale=-0.5),
                         reads=[('st2', t0g)], writes=[('st3', t0g)])
                    for t in tg:
                        P = TP(t); c = t * 128
                        hs = t % 3
                        S.op('dve', lambda e: e.scalar_tensor_tensor(out=hb[hs][:P], in0=X[:P, t, :], scalar=stt[:P, 3, t:t + 1], in1=G[:P],
                                                                     op0=ALU.mult, op1=ALU.mult),
                             reads=[('X', t), ('st3', t0g), 'G'], writes=[('hb', hs)])
                        bank = t % 4
                        pv = psb(bank).rearrange("p (k n) -> p k n", k=8)

                        def tr(e):
                            for kc in range(8):
                                ins = e.transpose(out=pv[:, kc, :P], in_=hb[hs][:P, kc * 128:(kc + 1) * 128], identity=idb[:P, :P])
                            return ins
                        S.op('pe', tr, reads=[('hb', hs), 'idb'], writes=[('ps', bank)])
                        S.op('act', lambda e: e.activation(out=H[:, :, c:c + P], in_=pv[:, :, :P], func=AF.Copy),
                             reads=[('ps', bank)], writes=[('H', t)])
                S.barrier()

        def load_slab(q, key, dst, src3, kcs, c0, w, wkey):
            S.dma(q, key, out=dst, in_=src3[:, kcs[0]:kcs[-1] + 1, c0:c0 + w], writes=[wkey])

        if True:
          for l in range(DEPTH):
            w_in3 = w_in[l].rearrange("(k p) c -> p k c", p=128)
            w_out3 = w_out[l].rearrange("(k p) c -> p k c", p=128)
            w_gu3 = w_gu[l].rearrange("(k p) c -> p k c", p=128)
            w_dn3 = w_dn[l].rearrange("(k p) c -> p k c", p=128)
            pw3 = conv_pw[l].rearrange("(k p) c -> p k c", p=128)

            WP = FA[:, 16640:22784].rearrange("p (k c) -> p k c", k=8)
            WOP = FA[:, 0:4096].rearrange("p (k c) -> p k c", k=4)
            if l == 0:
                S.dma('pool', 'wp0', out=WP[:, :, 0:256], in_=w_in3[:, :, 0:256], reads=[('X', 11)], writes=['WP0'])
                S.dma('pool', 'wp1', out=WP[:, :, 256:768], in_=w_in3[:, :, 1792:2304], writes=['WP1'])
            S.dma('pool', 'wop0', out=WOP[:, 0:2, :], in_=w_out3[:, 0:2, :], writes=['WOP0'])
            S.dma('pool', 'wop1', out=WOP[:, 2:4, :], in_=w_out3[:, 6:8, :], writes=['WOP1'])
            if l == 0:
                phase_norm(gmix[l:l + 1, :], [[0, 1, 2, 3], [4, 5, 6, 7], [8, 9, 10, 11], [12, 13, 14, 15, 16]])

            phq = ExitStack()
            Wq0 = sb(phq, "Wq0", [128, 8, 512], BF16)
            S.dma('pool', 'wq0', out=Wq0[:], in_=w_in3[:, :, 256:768], writes=[('W', 0)])
            with ExitStack() as ph:
                DG = FA[:, 4096:12032].rearrange("p (k c) -> p k c", k=62)
                CU = [FA[:, 12032 + i * 1084:12032 + (i + 1) * 1084].rearrange("p (j c) -> p j c", j=2) for i in range(2)]
                dsb = FA[:, 22784:23808].rearrange("p (j c) -> p j c", j=2)
                MPC2 = [FA[:, 14200:16248].rearrange("p (j c) -> p j c", j=4), sb(ph, "MPC1", [128, 4, 512], BF16)]
                sl2 = [FA[:, 23808:24832].rearrange("p (j c) -> p j c", j=2), sb(ph, "sl1", [128, 2, 512], BF16)]
                PW = sb(ph, "PW", [128, 2, 256], BF16)
                PLW = sb(ph, "PLW", [128, 2, 128], BF16)
                INVC = sb(ph, "INVC", [128, 2, 16], F32)
                PU = [sb(ph, "PU%d" % i, [128, 2, 527], F32) for i in range(2)]
                U32 = sb(ph, "U32", [128, 2, 512], F32)
                WA = sb(ph, "WA", [128, 526], F32); WB = sb(ph, "WB", [128, 524], F32)
                WC = WA; WD = WB
                t16 = sb(ph, "t16", [128, 16], F32)
                sg2 = [sb(ph, "sg%d" % i, [128, 512], F32) for i in range(2)]
                ysb = sb(ph, "ysb", [128, 2, 512], F32)
                ysq = sb(ph, "ysq", [128, 2, 512], F32)
                m2 = sb(ph, "m2", [128, 512], F32); var = sb(ph, "var", [128, 512], F32)

                S.dma('pool', 'pw', out=PW[:], in_=pw3, writes=['PW'])
                S.dma('pool', 'plw', out=PLW[:], in_=plw[l], writes=['PLW'])
                S.dma('sp', 'invc', out=INVC[:], in_=invc[:, :, :], writes=['INVC'])
                S.op('dve', lambda e: e.tensor_tensor(out=DG, in0=idb[:].unsqueeze(1).broadcast_to([128, 62, 128]),
                                                      in1=PV[:, l, 11:73].unsqueeze(2).broadcast_to([128, 62, 128]), op=ALU.mult),
                     reads=['PV', 'idb'], writes=['DG'])
                S.op('dve', lambda e: e.memset(PU[0][:, :, 0:15], 0.0), writes=[('PUc', 0)])
                S.op('dve', lambda e: e.memset(CU[0][:, :, 0:30], 0.0), writes=[('CUc', 0)])

                def stageA(gi):
                    c0, n, tiles = GROUPS[gi]
                    sample = (gi == 4)
                    slot = 0 if sample else gi % 2
                    if sample:
                        S.dma('sp', 'stp', out=PU[0][:, :, 0:15], in_=stp[l], writes=[('PUc', 0)])
                        S.dma('pool', 'stc', out=CU[0][:, :, 0:30], in_=stc[l], writes=[('CUc', 0)])
                    def proj(bank, col):
                        def f(e):
                            for kc in range(8):
                                ins = e.matmul(PS[:, bank, 0:n], lhsT=WP[:, kc, col:col + 128], rhs=H[:, kc, c0:c0 + n],
                                               start=(kc == 0), stop=(kc == 7))
                            return ins
                        S.op('pe', f, reads=['WP0', 'WP1'] + [('H', t) for t in tiles], writes=[('ps', bank)])
                    for j in range(2):
                        proj(j, j * 128)
                        S.op('act', lambda e: e.activation(out=PU[slot][:, j, 15:15 + n], in_=PS[:, j, 0:n], func=AF.Copy),
                             reads=[('ps', j)], writes=[('PUn', slot, j)])
                    if gi == 3 or sample:
                        S.dma('sp', 'o_np%d' % gi, out=(nps[l] if sample else npp[l]), in_=PU[slot][:, :, n:n + 15],
                              reads=[('PUc', slot), ('PUn', slot, 0), ('PUn', slot, 1)], is_out=True)
                    if gi < 3:
                        S.op('dve', lambda e: e.tensor_copy(out=PU[(gi + 1) % 2][:, :, 0:15], in_=PU[slot][:, :, 512:527]),
                             reads=[('PUn', slot, 0), ('PUn', slot, 1)], writes=[('PUc', (gi + 1) % 2)])
                    for j in range(2):
                        ext = PU[slot][:, j, :]
                        rk = [('PUc', slot), ('PUn', slot, j)]
                        S.op('dve', lambda e: e.tensor_tensor(out=WA[:, 0:14 + n], in0=ext[:, 1:15 + n], in1=ext[:, 0:14 + n], op=ALU.add),
                             reads=rk, writes=['WA'])
                        S.op('dve', lambda e: e.tensor_tensor(out=WB[:, 0:12 + n], in0=WA[:, 2:14 + n], in1=WA[:, 0:12 + n], op=ALU.add),
                             reads=['WA'], writes=['WB'])
                        if j == 0:
                            lo, lo_off, hi, hi_off = WA, 14, WB, 12
                            kk = ['WA', 'WB']
                        else:
                            S.op('dve', lambda e: e.tensor_tensor(out=WC[:, 0:8 + n], in0=WB[:, 4:12 + n], in1=WB[:, 0:8 + n], op=ALU.add),
                                 reads=['WB'], writes=['WA'])
                            S.op('dve', lambda e: e.tensor_tensor(out=WD[:, 0:n], in0=WC[:, 8:8 + n], in1=WC[:, 0:n], op=ALU.add),
                                 reads=['WA'], writes=['WB'])
                            lo, lo_off, hi, hi_off = WC, 8, WD, 0
                            kk = ['WA', 'WB']
                        for (src, off, p0) in ((lo, lo_off, 0), (hi, hi_off, 64)):
                            S.op('dve', lambda e: e.scalar_tensor_tensor(out=dsb[p0:p0 + 64, j, 0:n], in0=src[p0:p0 + 64, off:off + n],
                                                                         scalar=PV[p0:p0 + 64, l, 9 + j:10 + j], in1=ext[p0:p0 + 64, 15:15 + n],
                                                                         op0=ALU.mult, op1=ALU.subtract),
                                 reads=kk + rk + ['PV'], writes=[('dsb', j)])
                            if gi == 0:
                                S.op('dve', lambda e: e.tensor_tensor(out=t16[p0:p0 + 64, :], in0=src[p0:p0 + 64, off:off + 16],
                                                                      in1=INVC[p0:p0 + 64, j, :], op=ALU.mult),
                                     reads=kk + ['INVC'], writes=['t16'])
                                S.op('dve', lambda e: e.tensor_tensor(out=dsb[p0:p0 + 64, j, 0:16], in0=t16[p0:p0 + 64, :],
                                                                      in1=ext[p0:p0 + 64, 15:31], op=ALU.subtract),
                                     reads=['t16'] + rk, writes=[('dsb', j)])
                    for j in range(2):
                        proj(2 + 2 * j, 256 + j * 128)
                        proj(3 + 2 * j, 512 + j * 128)
                        S.op('act', lambda e: e.activation(out=sg2[j][:, 0:n], in_=PS[:, 3 + 2 * j, 0:n], func=AF.Sigmoid),
                             reads=[('ps', 3 + 2 * j)], writes=[('sg', j)])
                        S.op('dve', lambda e: e.tensor_tensor(out=CU[slot][:, j, 30:30 + n], in0=PS[:, 2 + 2 * j, 0:n], in1=sg2[j][:, 0:n], op=ALU.mult),
                             reads=[('ps', 2 + 2 * j), ('sg', j)], writes=[('CUn', slot, j)])
                        if gi == 3 or sample:
                            S.op('dve', lambda e: e.tensor_tensor(out=U32[:, j, n - 30:n], in0=PS[:, 2 + 2 * j, n - 30:n], in1=sg2[j][:, n - 30:n], op=ALU.mult),
                                 reads=[('ps', 2 + 2 * j), ('sg', j)], writes=[('U32', j)])
                    if gi == 3 or sample:
                        S.dma('sp', 'o_nc%d' % gi, out=(ncs[l] if sample else ncp[l]), in_=U32[:, :, n - 30:n],
                              reads=[('U32', 0), ('U32', 1)], is_out=True)
                    if gi < 3:
                        S.op('dve', lambda e: e.tensor_copy(out=CU[(gi + 1) % 2][:, :, 0:30], in_=CU[slot][:, :, 512:542]),
                             reads=[('CUn', slot, 0), ('CUn', slot, 1)], writes=[('CUc', (gi + 1) % 2)])

                def stageC(gi):
                    c0, n, tiles = GROUPS[gi]
                    sample = (gi == 4)
                    slot = 0 if sample else gi % 2
                    MPC = MPC2[gi % 2]; sl = sl2[gi % 2]
                    for j in range(2):
                        S.op('pe', lambda e: e.matmul(PS[:, j, 0:n], lhsT=PLW[:, j, :], rhs=dsb[:, j, 0:n], start=True, stop=True),
                             reads=[('dsb', j), 'PLW'], writes=[('ps', j)])
                        S.op('act', lambda e: e.activation(out=MPC[:, j, 0:n], in_=PS[:, j, 0:n], func=AF.Copy, scale=PV[:, l, j:j + 1]),
                             reads=[('ps', j), 'PV'], writes=[('MPC', gi % 2, j)])
                    for j in range(2):
                        def cv_mm(e):
                            for tap in range(31):
                                ins = e.matmul(PS[:, 4 + j, 0:n], lhsT=DG[:, j * 31 + tap, :], rhs=CU[slot][:, j, tap:tap + n],
                                               start=(tap == 0), stop=(tap == 30))
                            return ins
                        S.op('pe', cv_mm, reads=['DG', ('CUc', slot), ('CUn', slot, j)], writes=[('ps', 4 + j)])
                        S.op('act', lambda e: e.activation(out=ysb[:, j, 0:n], in_=PS[:, 4 + j, 0:n], func=AF.Identity, bias=PV[:, l, 2 + j:3 + j]),
                             reads=[('ps', 4 + j), 'PV'], writes=[('ysb', j)])
                        S.op('act', lambda e: e.activation(out=ysq[:, j, 0:n], in_=PS[:, 4 + j, 0:n], func=AF.Square, bias=PV[:, l, 2 + j:3 + j]),
                             reads=[('ps', 4 + j), 'PV'], writes=[('ysq', j)])

                    def st_mm(e):
                        for j in range(2):
                            e.matmul(PS[:, 6, 0:n], lhsT=onesf[:, 1, :], rhs=ysb[:, j, 0:n], start=(j == 0), stop=(j == 1))
                        for j in range(2):
                            ins = e.matmul(PS[:, 7, 0:n], lhsT=onesf[:, 1, :], rhs=ysq[:, j, 0:n], start=(j == 0), stop=(j == 1))
                        return ins
                    S.op('pe', st_mm, reads=[('ysb', 0), ('ysb', 1), ('ysq', 0), ('ysq', 1), 'onesf1'], writes=[('ps', 6), ('ps', 7)])
                    S.op('act', lambda e: e.activation(out=m2[:, 0:n], in_=PS[:, 6, 0:n], func=AF.Square), reads=[('ps', 6)], writes=['m2'])
                    S.op('dve', lambda e: e.scalar_tensor_tensor(out=var[:, 0:n], in0=PS[:, 7, 0:n], scalar=EPS, in1=m2[:, 0:n],
                                                                 op0=ALU.add, op1=ALU.subtract),
                         reads=[('ps', 7), 'm2'], writes=['var'])
                    S.op('act', lambda e: e.activation(out=m2[:, 0:n], in_=var[:, 0:n], func=AF.Ln), reads=['var'], writes=['m2'])
                    S.op('act', lambda e: e.activation(out=var[:, 0:n], in_=m2[:, 0:n], func=AF.Exp, scale=-0.5), reads=['m2'], writes=['var'])
                    for j in range(2):
                        S.op('dve', lambda e: e.tensor_tensor(out=ysq[:, j, 0:n], in0=ysb[:, j, 0:n], in1=PS[:, 6, 0:n], op=ALU.subtract),
                             reads=[('ysb', j), ('ps', 6)], writes=[('ysq', j)])
                        S.op('dve', lambda e: e.tensor_tensor(out=ysb[:, j, 0:n], in0=ysq[:, j, 0:n], in1=var[:, 0:n], op=ALU.mult),
                             reads=[('ysq', j), 'var'], writes=[('ysb', j)])
                        S.op('act', lambda e: e.activation(out=sl[:, j, 0:n], in_=ysb[:, j, 0:n], func=AF.Silu,
                                                           scale=PV[:, l, 4 + j:5 + j], bias=PV[:, l, 6 + j:7 + j]),
                             reads=[('ysb', j), 'PV'], writes=[('sl', gi % 2, j)])

                def stageB(gi):
                    c0, n, tiles = GROUPS[gi]
                    MPC = MPC2[gi % 2]; sl = sl2[gi % 2]
                    for jo in range(2):
                        def pw_mm(e):
                            for j in range(2):
                                ins = e.matmul(PS[:, jo, 0:n], lhsT=PW[:, j, jo * 128:(jo + 1) * 128], rhs=sl[:, j, 0:n],
                                               start=(j == 0), stop=(j == 1))
                            return ins
                        S.op('pe', pw_mm, reads=[('sl', gi % 2, 0), ('sl', gi % 2, 1), 'PW'], writes=[('ps', jo)])
                        S.op('act', lambda e: e.activation(out=MPC[:, 2 + jo, 0:n], in_=PS[:, jo, 0:n], func=AF.Copy),
                             reads=[('ps', jo)], writes=[('MPC', gi % 2, 2 + jo)])
                    for ti, t in enumerate(tiles):
                        P = TP(t)
                        for hh in range(2):
                            bank = (6, 7, 0, 1)[(2 * ti + hh) % 4]

                            def wo_mm(e):
                                for kc in range(4):
                                    ins = e.matmul(PS[:P, bank, :], lhsT=MPC[:, kc, ti * 128:ti * 128 + P],
                                                   rhs=WOP[:, kc, hh * 512:(hh + 1) * 512], start=(kc == 0), stop=(kc == 3))
                                return ins
                            S.op('pe', wo_mm, reads=[('MPC', gi % 2, k) for k in range(4)] + ['WOP0', 'WOP1'], writes=[('ps', bank)])
                            S.op('dve', lambda e: e.tensor_tensor(out=X[:P, t, hh * 512:(hh + 1) * 512], in0=X[:P, t, hh * 512:(hh + 1) * 512],
                                                                  in1=PS[:P, bank, :], op=ALU.add),
                                 reads=[('ps', bank), ('X', t)], writes=[('X', t)])

                for gi in range(len(GROUPS)):
                    stageA(gi)
                    if gi > 0:
                        stageB(gi - 1)
                    stageC(gi)
                stageB(len(GROUPS) - 1)
                S.barrier()

            with ExitStack() as ph:
                W = [Wq0] + [sb(ph, "Wq%d" % i, [128, 8, 512], BF16) for i in (1, 2)]
                COS = sb(ph, "COS", [128, 17, 32], F32); SIN = sb(ph, "SIN", [128, 17, 32], F32)
                zs = [sb(ph, "zs%d" % i, [128, 512], F32) for i in range(3)]
                kr = [sb(ph, "kr%d" % i, [128, 512], F32) for i in range(2)]
                t1 = [sb(ph, "t1%d" % i, [128, 512], F32) for i in range(2)]
                t2 = [sb(ph, "t2%d" % i, [128, 512], F32) for i in range(2)]
                qb = [sb(ph, "qb%d" % i, [128, 512], BF16) for i in range(6)]
                S.dma('sp', 'cos', out=COS[:], in_=cos_t[:, :, :], writes=['COS'])
                S.dma('sp', 'sin', out=SIN[:], in_=sin_t[:, :, :], writes=['SIN'])
                for pi, c0 in ((1, 768), (2, 1280)):
                    S.dma('pool', 'wq%d' % pi, out=W[pi][:], in_=w_in3[:, :, c0:c0 + 512], writes=[('W', pi)])
                pend = []
                cnt = 0
                rc = 0
                NQ0 = 6
                items = [(t, 0) for t in range(NQ0)] + [(t, pi) for t in range(NQ0) for pi in (1, 2)] + \
                        [(t, pi) for t in range(NQ0, NT) for pi in (0, 1, 2)]
                for (t, pi) in items:
                    P = TP(t); c = t * 128
                    nm = ('q', 'k', 'v')[pi]
                    if True:
                        Wc = W[pi]
                        bank = cnt % 4
                        s2 = cnt % 3
                        cnt += 1
                        qs = rc % 6

                        def mm(e):
                            for kc in range(8):
                                ins = e.matmul(PS[:P, bank, :], lhsT=H[:, kc, c:c + P], rhs=Wc[:, kc, :], start=(kc == 0), stop=(kc == 7))
                            return ins
                        S.op('pe', mm, reads=[('H', t), ('W', pi)], writes=[('ps', bank)])
                        while len(pend) > 3:
                            pend.pop(0)()
                        S.op('act', lambda e: e.activation(out=zs[s2][:P], in_=PS[:P, bank, :], func=AF.Copy),
                             reads=[('ps', bank)], writes=[('zs', s2)])
                        if nm == 'v':
                            S.dma('sp', 'o_v%d' % s2, out=rows_v(l, t), in_=zs[s2][:P], reads=[('zs', s2)], is_out=True)
                            S.op('act', lambda e: e.activation(out=VB[:P, t, :], in_=zs[s2][:P], func=AF.Copy), reads=[('zs', s2)], writes=[('VB', t)])
                            continue
                        r2 = rc % 2
                        rc += 1
                        z4 = zs[s2][:P].rearrange("p (g two i) -> p g two i", g=8, two=2)
                        cosb = COS[:P, t:t + 1, :].unsqueeze(1).broadcast_to([P, 8, 2, 32])
                        sinb = SIN[:P, t:t + 1, :].broadcast_to([P, 8, 32])
                        t14 = t1[r2][:P].rearrange("p (g two i) -> p g two i", g=8, two=2)
                        t24 = t2[r2][:P].rearrange("p (g two i) -> p g two i", g=8, two=2)
                        S.op('dve', lambda e: e.tensor_tensor(out=t14, in0=z4, in1=cosb, op=ALU.mult),
                             reads=[('zs', s2), 'COS'], writes=[('t1', r2)])
                        S.op('dve', lambda e: e.tensor_tensor(out=t24[:, :, 0, :], in0=z4[:, :, 1, :], in1=sinb, op=ALU.mult),
                             reads=[('zs', s2), 'SIN'], writes=[('t2a', r2)])
                        S.op('dve', lambda e: e.tensor_tensor(out=t24[:, :, 1, :], in0=z4[:, :, 0, :], in1=sinb, op=ALU.mult),
                             reads=[('zs', s2), 'SIN'], writes=[('t2b', r2)])
                        if nm == 'q':
                            o4 = qb[qs][:P].rearrange("p (g two i) -> p g two i", g=8, two=2)
                        else:
                            o4 = kr[r2][:P].rearrange("p (g two i) -> p g two i", g=8, two=2)
                        okey = ('qb', qs) if nm == 'q' else ('kr', r2)
                        S.op('dve', lambda e: e.tensor_tensor(out=o4[:, :, 0, :], in0=t14[:, :, 0, :], in1=t24[:, :, 0, :], op=ALU.subtract),
                             reads=[('t1', r2), ('t2a', r2)], writes=[(okey, 'a')])
                        S.op('dve', lambda e: e.tensor_tensor(out=o4[:, :, 1, :], in0=t14[:, :, 1, :], in1=t24[:, :, 1, :], op=ALU.add),
                             reads=[('t1', r2), ('t2b', r2)], writes=[(okey, 'b')])
                        if nm == 'k':
                            S.dma('sp', 'o_k%d' % r2, out=rows_k(l, t), in_=kr[r2][:P], reads=[(okey, 'a'), (okey, 'b')], is_out=True)
                            S.op('act', lambda e: e.activation(out=qb[qs][:P], in_=kr[r2][:P], func=AF.Copy),
                                 reads=[(okey, 'a'), (okey, 'b')], writes=[(('qb', qs), 'a'), (('qb', qs), 'b')])

                        def mk(t=t, P=P, c=c, qs=qs, nm=nm, tb=4 + (rc % 4)):
                            def run():
                                pv = psb(tb).rearrange("p (k n) -> p k n", k=8)

                                def tr(e):
                                    for h in range(4):
                                        ins = e.transpose(out=pv[:, h, :P], in_=qb[qs][:P, h * 128:(h + 1) * 128], identity=idb[:P, :P])
                                    return ins
                                S.op('pe', tr, reads=[(('qb', qs), 'a'), (('qb', qs), 'b'), 'idb'], writes=[('ps', tb)])
                                dst = QT if nm == 'q' else KT
                                S.op('act', lambda e: e.activation(out=dst[:, :, c:c + P], in_=pv[:, 0:4, :P], func=AF.Copy),
                                     reads=[('ps', tb)], writes=[(nm + 'T', t)])
                            return run
                        pend.append(mk())
                while pend:
                    pend.pop(0)()
                S.barrier()
            phq.close()

            phw = ExitStack()
            WOA = sb(phw, "WOA", [128, 4, 1024], BF16)
            S.dma('pool', 'woa', out=WOA[:], in_=w_out3[:, 2:6, :], writes=['WOA'])
            ckb = [sb(phw, "ckb%d" % i, [128, 512], BF16) for i in range(2)]
            cvb = [sb(phw, "cvb%d" % i, [128, 512], BF16) for i in range(2)]
            for i in range(2):
                S.dma('pool', 'ck%d' % i, out=ckb[i][:], in_=ck[l, i * 128:(i + 1) * 128, :], writes=[('ckb', i)])
                S.dma('pool', 'cv%d' % i, out=cvb[i][:], in_=cv[l, i * 128:(i + 1) * 128, :], writes=[('cvb', i)])
            with ExitStack() as ph:
                PT = [sb(ph, "PT%d" % i, [128, 2, 512], BF16) for i in range(3)]
                OSB = [sb(ph, "OSB%d" % i, [128, 2, 512], F32) for i in range(2)]
                LNL = [sb(ph, "LNL%d" % i, [128, 2, 512], F32) for i in range(2)]
                dd = sb(ph, "dd", [128, 512], F32); sq = sb(ph, "sq", [128, 512], F32)
                msb = sb(ph, "msb", [128, 512], F32); rs = sb(ph, "rs", [128, 512], F32)
                MB = sb(ph, "MB", [128, 1], F32)
                S.op('dve', lambda e: e.memset(MB[0:64, :], 0.0), writes=['MBa'])
                S.op('dve', lambda e: e.memset(MB[64:128, :], NEG), writes=['MBb'])
                negl = LAM[:, l, 4:5]; gsc = LAM[:, l, 5:6]
                blocks = []
                for n_it, (h, j) in enumerate([(h, j) for h in range(4) for j in range(4)]):
                    for i in range(4 * j + 4):
                        blocks.append((n_it, h, j, i))
                NBK = len(blocks)

                def off_of(j, i):
                    r = i - 4 * j
                    return 128 * r if r > 0 else 0

                def qk(g):
                    n_it, h, j, i = blocks[g]
                    slot = g % 2
                    off = off_of(j, i)
                    q0 = j * 512

                    def f(e):
                        for c in range(2):
                            ins = e.matmul(PS[:, 2 * slot + c, off:512], lhsT=KT[c * 64:(c + 1) * 64, h, i * 128:(i + 1) * 128],
                                           rhs=QT[c * 64:(c + 1) * 64, h, q0 + off:q0 + 512], start=True, stop=True)
                        return ins
                    S.op('pe', f, reads=[('QT', h, j)], writes=[('ps', 2 * slot), ('ps', 2 * slot + 1)])

                def ex(g):
                    n_it, h, j, i = blocks[g]
                    slot = g % 2
                    off = off_of(j, i)
                    ps_ = g % 3
                    diag = (i - 4 * j) >= 0
                    src = PS[:, 2 * slot:2 * slot + 2, :]
                    if diag:
                        S.op('act', lambda e: e.activation(out=PT[ps_][:, :, off:off + 64], in_=src[:, :, off:off + 64], func=AF.Exp,
                                                           scale=SCALE, bias=MB[:, 0:1]),
                             reads=[('ps', 2 * slot), ('ps', 2 * slot + 1), 'MBa', 'MBb'], writes=[('PT', ps_, 'm')])
                        S.op('act', lambda e: e.activation(out=PT[ps_][:, :, off + 64:512], in_=src[:, :, off + 64:512], func=AF.Exp,
                                                           scale=SCALE),
                             reads=[('ps', 2 * slot), ('ps', 2 * slot + 1)], writes=[('PT', ps_)])
                    else:
                        S.op('act', lambda e: e.activation(out=PT[ps_][:, :, :], in_=src, func=AF.Exp, scale=SCALE),
                             reads=[('ps', 2 * slot), ('ps', 2 * slot + 1)], writes=[('PT', ps_), ('PT', ps_, 'm')])

                def pvm(g):
                    n_it, h, j, i = blocks[g]
                    off = off_of(j, i)
                    ps_ = g % 3
                    nb = 4 * j + 4

                    def f(e):
                        for c in range(2):
                            e.matmul(PS[:, 4 + c, off:512], lhsT=VB[:, i, h * 128:(h + 1) * 128], rhs=PT[ps_][:, c, off:512],
                                     start=(i == 0), stop=(i == nb - 1))
                        for c in range(2):
                            ins = e.matmul(PS[:, 6 + c, off:512], lhsT=onesb[:], rhs=PT[ps_][:, c, off:512],
                                           start=(i == 0), stop=(i == nb - 1))
                        return ins
                    S.op('pe', f, reads=[('PT', ps_), ('PT', ps_, 'm'), 'onesb'], writes=[('ps', 4), ('ps', 5), ('ps', 6), ('ps', 7)])

                def mkB(st):
                    def run(sbk):
                        S.op('act', lambda e: e.activation(out=LNL[st][:], in_=LNL[st][:], func=AF.Exp, scale=-1.0),
                             reads=[('LNL', st)], writes=[('LNL', st)])
                        S.op('dve', lambda e: e.tensor_tensor(out=OSB[st][:], in0=OSB[st][:], in1=LNL[st][:], op=ALU.mult),
                             reads=[('OSB', st), ('LNL', st)], writes=[('OSB', st)])
                        S.op('dve', lambda e: e.scalar_tensor_tensor(out=dd[:], in0=OSB[st][:, 1, :], scalar=negl, in1=OSB[st][:, 0, :],
                                                                     op0=ALU.mult, op1=ALU.add),
                             reads=[('OSB', st)], writes=['dd'])
                        S.op('dve', lambda e: e.tensor_tensor(out=sq[:], in0=dd[:], in1=dd[:], op=ALU.mult), reads=['dd'], writes=['sq'])
                    return run

                def mkC1():
                    def run(sbk):
                        S.op('pe', lambda e: e.matmul(PS[:, sbk, :], lhsT=onesf[:, 0, :], rhs=sq[:], start=True, stop=True),
                             reads=['sq', 'onesf0'], writes=[('ps', sbk)])
                        S.op('dve', lambda e: e.tensor_scalar(out=msb[:], in0=PS[:, sbk, :], scalar1=EPS, scalar2=None, op0=ALU.add),
                             reads=[('ps', sbk)], writes=['msb'])
                    return run

                def mkC2(h, j):
                    q0 = j * 512

                    def run():
                        S.op('act', lambda e: e.activation(out=msb[:], in_=msb[:], func=AF.Ln), reads=['msb'], writes=['msb'])
                        S.op('act', lambda e: e.activation(out=rs[:], in_=msb[:], func=AF.Exp, scale=-0.5), reads=['msb'], writes=['rs'])
                        S.op('dve', lambda e: e.scalar_tensor_tensor(out=QT[:, h, q0:q0 + 512], in0=dd[:], scalar=gsc, in1=rs[:],
                                                                     op0=ALU.mult, op1=ALU.mult),
                             reads=['dd', 'rs'], writes=[('QT', h, j)])
                    return run

                pendB = None
                savedC = None
                pendC2 = None
                qk(0)
                qk(1)
                for g in range(NBK):
                    n_it, h, j, i = blocks[g]
                    nb = 4 * j + 4
                    st = n_it % 2
                    ex(g)
                    if pendC2 is not None:
                        pendC2()
                        pendC2 = None
                    pvm(g)
                    if i == nb - 1:
                        S.op('dve', lambda e: e.tensor_copy(out=OSB[st][:], in_=PS[:, 4:6, :]),
                             reads=[('ps', 4), ('ps', 5)], writes=[('OSB', st)])
                        S.op('act', lambda e: e.activation(out=LNL[st][:], in_=PS[:, 6:8, :], func=AF.Ln),
                             reads=[('ps', 6), ('ps', 7)], writes=[('LNL', st)])
                        if savedC is not None:
                            savedC[0](2 * (g % 2))
                            pendC2 = savedC[1]
                        savedC = (mkC1(), mkC2(h, j))
                        pendB = (g + 2, mkB(st))
                    if pendB is not None and pendB[0] <= g:
                        pendB[1](0)
                        pendB = None
                    if g + 2 < NBK:
                        qk(g + 2)
                if pendC2 is not None:
                    pendC2()
                if pendB is not None:
                    pendB[1](0)
                savedC[0](0); savedC[1]()
                S.barrier()

            phf = ExitStack()
            GU = [sb(phf, "GU%d" % i, [128, 2, 8, 256], BF16) for i in range(2)]

            def issue_gu_abs(fs, w, slot):
                S.dma('pool', 'gug%d' % slot, out=GU[slot][:, 0, :, 0:w * 128], in_=w_gu3[:, :, fs * 128:(fs + w) * 128],
                      writes=[('GUg', slot)])
                S.dma('pool', 'guu%d' % slot, out=GU[slot][:, 1, :, 0:w * 128], in_=w_gu3[:, :, 2816 + fs * 128:2816 + (fs + w) * 128],
                      writes=[('GUu', slot)])
            issue_gu_abs(0, 2, 0)
            def wo_tile(t, banks):
                P = TP(t); c = t * 128
                for hh in range(2):
                    bank = banks[hh]

                    def wo_mm(e):
                        for kc in range(4):
                            ins = e.matmul(PS[:P, bank, :], lhsT=QT[:, kc, c:c + P], rhs=WOA[:, kc, hh * 512:(hh + 1) * 512],
                                           start=(kc == 0), stop=(kc == 3))
                        return ins
                    S.op('pe', wo_mm, reads=['WOA', ('QT', 's')] if t == 16 else ['WOA'], writes=[('ps', bank)])
                    S.op('dve', lambda e: e.tensor_tensor(out=X[:P, t, hh * 512:(hh + 1) * 512], in0=X[:P, t, hh * 512:(hh + 1) * 512],
                                                          in1=PS[:P, bank, :], op=ALU.add),
                         reads=[('ps', bank), ('X', t)], writes=[('X', t)])

            with ExitStack() as ph:
                ckT = [sb(ph, "ckT%d" % i, [128, 4, 128], BF16) for i in range(2)]
                PTs = [sb(ph, "PTs%d" % i, [128, 8, 32], BF16) for i in range(2)]
                rr = sb(ph, "rr", [128, 8, 32], F32); a8 = sb(ph, "a8", [128, 2, 4, 32], F32)
                d4 = sb(ph, "d4", [128, 4, 32], F32); sq4 = sb(ph, "sq4", [128, 4, 32], F32)
                ms4 = sb(ph, "ms4", [128, 128], F32); ln4 = sb(ph, "ln4", [128, 128], F32); rs4 = sb(ph, "rs4", [128, 128], F32)
                negl = LAM[:, l, 4:5]; gsc = LAM[:, l, 5:6]
                OS = PS[:, 4, 0:256].rearrange("p (g n) -> p g n", g=8)
                LS = PS[:, 5, 0:256].rearrange("p (g n) -> p g n", g=8)
                ATSM = 9
                for i in range(17 if ATSM > 0 else 0):
                    slot = i % 2
                    KP = 128 if i < 16 else 32
                    if i < 16:
                        if i >= 2:
                            S.dma('pool', 'ck%d' % slot, out=ckb[slot][:], in_=ck[l, i * 128:(i + 1) * 128, :], writes=[('ckb', slot)])
                            S.dma('pool', 'cv%d' % slot, out=cvb[slot][:], in_=cv[l, i * 128:(i + 1) * 128, :], writes=[('cvb', slot)])
                        pv = psb(slot).rearrange("p (k n) -> p k n", k=8)

                        def tr(e):
                            for h in range(4):
                                ins = e.transpose(out=pv[:, h, :], in_=ckb[slot][:, h * 128:(h + 1) * 128], identity=idb[:])
                            return ins
                        S.op('pe', tr, reads=[('ckb', slot), 'idb'], writes=[('ps', slot)])
                        S.op('act', lambda e: e.activation(out=ckT[slot][:], in_=pv[:, 0:4, :], func=AF.Copy),
                             reads=[('ps', slot)], writes=[('ckT', slot)])
                    sb0 = 2 if slot == 0 else 6
                    SS2 = PS[:, sb0:sb0 + 2, 0:128]
                    if ATSM < 2:
                        continue

                    def qk(e):
                        for h in range(4):
                            for c in range(2):
                                if i < 16:
                                    kT = ckT[slot][c * 64:(c + 1) * 64, h, :]
                                else:
                                    kT = KT[c * 64:(c + 1) * 64, h, 2048:2080]
                                ins = e.matmul(PS[:KP, sb0 + c, h * 32:(h + 1) * 32], lhsT=kT, rhs=QT[c * 64:(c + 1) * 64, h, 2048:2080],
                                               start=True, stop=True)
                        return ins
                    S.op('pe', qk, reads=[('ckT', slot), ('QT', 's')], writes=[('ps', sb0), ('ps', sb0 + 1)])
                    S.op('act', lambda e: e.activation(out=PTs[slot][:KP].rearrange("p (c x) n -> p c (x n)", c=2), in_=SS2[:KP], func=AF.Exp, scale=SCALE),
                         reads=[('ps', sb0), ('ps', sb0 + 1)], writes=[('PTs', slot)])
                    if i < 16:
                        wo_tile(i, (6, 7) if slot == 0 else (2, 3))
                    if ATSM < 3:
                        continue

                    def pvm(e):
                        first = True
                        for c in range(2):
                            for h in range(4):
                                if i < 16:
                                    vv = cvb[slot][:, h * 128:(h + 1) * 128]
                                else:
                                    vv = VB[:32, 16, h * 128:(h + 1) * 128]
                                e.matmul(OS[:, c * 4 + h, :], lhsT=vv, rhs=PTs[slot][:KP, c * 4 + h, :],
                                         start=(i == 0 and first), stop=(i == 16), skip_group_check=True)
                                first = False
                        ins = e.matmul(PS[:, 5, 0:256], lhsT=onesb[:KP, :], rhs=PTs[slot][:KP].rearrange("p g n -> p (g n)"),
                                       start=(i == 0), stop=(i == 16))
                        return ins
                    S.op('pe', pvm, reads=[('PTs', slot), ('cvb', slot), 'onesb'], writes=[('ps', 4), ('ps', 5)])
                if ATSM < 4:
                    S.stopped = True
                S.op('dve', lambda e: e.reciprocal(out=rr[:], in_=LS), reads=[('ps', 5)], writes=['rr'])
                S.op('dve', lambda e: e.tensor_tensor(out=a8[:].rearrange("p c h n -> p (c h) n"), in0=OS, in1=rr[:], op=ALU.mult),
                     reads=[('ps', 4), 'rr'], writes=['a8'])
                S.op('dve', lambda e: e.scalar_tensor_tensor(out=d4[:], in0=a8[:, 1, :, :], scalar=negl, in1=a8[:, 0, :, :],
                                                             op0=ALU.mult, op1=ALU.add), reads=['a8'], writes=['d4'])
                S.op('act', lambda e: e.activation(out=sq4[:], in_=d4[:], func=AF.Square), reads=['d4'], writes=['sq4'])
                S.op('pe', lambda e: e.matmul(PS[:, 0, 0:128], lhsT=onesf[:, 0, :], rhs=sq4[:].rearrange("p h n -> p (h n)"), start=True, stop=True),
                     reads=['sq4', 'onesf0'], writes=[('ps', 0)])
                S.op('dve', lambda e: e.tensor_scalar(out=ms4[:], in0=PS[:, 0, 0:128], scalar1=EPS, scalar2=None, op0=ALU.add),
                     reads=[('ps', 0)], writes=['ms4'])
                S.op('act', lambda e: e.activation(out=ln4[:], in_=ms4[:], func=AF.Ln), reads=['ms4'], writes=['ln4'])
                S.op('act', lambda e: e.activation(out=rs4[:], in_=ln4[:], func=AF.Exp, scale=-0.5), reads=['ln4'], writes=['rs4'])
                S.op('dve', lambda e: e.scalar_tensor_tensor(out=QT[:, :, 2048:2080], in0=d4[:], scalar=gsc,
                                                             in1=rs4[:].rearrange("p (h n) -> p h n", h=4), op0=ALU.mult, op1=ALU.mult),
                     reads=['d4', 'rs4'], writes=[('QT', 's')])
                wo_tile(16, (2, 3))
                S.barrier()

            phase_norm(gffn[l:l + 1, :])
            with ExitStack() as ph:
                DN = [sb(ph, "DN%d" % i, [128, 8, 512], BF16) for i in range(2)]
                sgl = [sb(ph, "sgl%d" % i, [128, 512], F32) for i in range(2)]
                gcount = 0
                dcount = 0
                ev = 0
                for (f0, f1) in FFN_PASSES:
                    nf = f1 - f0
                    slabs = []
                    f = f0
                    while f < f1:
                        w = min(2, f1 - f)
                        slabs.append((f, w))
                        f += w

                    def issue_gu(si):
                        fs, w = slabs[si]
                        slot = (gcount + si) % 2
                        issue_gu_abs(fs, w, slot)
                    if f0 > 0:
                        issue_gu(0)
                    for hh in range(2):
                        S.dma('pool', 'dn%d' % hh, out=DN[hh][:, 0:nf, :], in_=w_dn3[:, f0:f1, hh * 512:(hh + 1) * 512], writes=[('DN', hh)])
                    for si, (fs, w) in enumerate(slabs):
                        slot = (gcount + si) % 2
                        if si + 1 < len(slabs):
                            issue_gu(si + 1)
                        for fo in range(w):
                            fi = fs + fo - f0
                            for (c0, n, tiles) in GROUPS:
                                bg = (ev % 3) * 2
                                ev += 1

                                def gu_mm(e):
                                    for kc in range(8):
                                        e.matmul(PS[:, bg, 0:n], lhsT=GU[slot][:, 0, kc, fo * 128:(fo + 1) * 128], rhs=H[:, kc, c0:c0 + n],
                                                 start=(kc == 0), stop=(kc == 7))
                                    for kc in range(8):
                                        ins = e.matmul(PS[:, bg + 1, 0:n], lhsT=GU[slot][:, 1, kc, fo * 128:(fo + 1) * 128], rhs=H[:, kc, c0:c0 + n],
                                                       start=(kc == 0), stop=(kc == 7))
                                    return ins
                                S.op('pe', gu_mm, reads=[('GUg', slot), ('GUu', slot)], writes=[('ps', bg), ('ps', bg + 1)])
                                ss = ev % 2
                                S.op('act', lambda e: e.activation(out=sgl[ss][:, 0:n], in_=PS[:, bg, 0:n], func=AF.Silu),
                                     reads=[('ps', bg)], writes=[('sgl', ss)])
                                S.op('dve', lambda e: e.tensor_tensor(out=ACTH[:, fi, c0:c0 + n], in0=sgl[ss][:, 0:n], in1=PS[:, bg + 1, 0:n], op=ALU.mult),
                                     reads=[('sgl', ss), ('ps', bg + 1)], writes=[('ACTH', fi)])
                    gcount += len(slabs)
                    lastpass = (f1 == 22)
                    last = lastpass and l == DEPTH - 1
                    mid = lastpass and l < DEPTH - 1
                    if lastpass:
                        GUf = [GU[i][:].rearrange("p a b c -> p (a b c)").bitcast(F32) for i in range(2)]
                        GU1b = GU[1][:].rearrange("p a b c -> p (a b c)")
                        Gfin = GUf[0][:, 0:1024]
                        stf = GUf[0][:, 1024:1024 + 4 * NT].rearrange("p (t k) -> p t k", k=4)
                        yof = [GUf[1][:, i * 1024:(i + 1) * 1024] for i in range(2)]
                        hbf = [GU1b[:, i * 1024:(i + 1) * 1024] for i in range(3)]
                        junkf = GU1b[:, 3072:4096]
                        grow = gfin[0:1, :] if last else gmix[l + 1:l + 2, :]
                    if mid:
                        w_in3n = w_in[l + 1].rearrange("(k p) c -> p k c", p=128)
                        WPn = FA[:, 16640:22784].rearrange("p (k c) -> p k c", k=8)
                        S.dma('pool', 'wp0', out=WPn[:, :, 0:256], in_=w_in3n[:, :, 0:256], writes=['WP0'])
                        S.dma('pool', 'wp1', out=WPn[:, :, 256:768], in_=w_in3n[:, :, 1792:2304], writes=['WP1'])
                    order = [(t, hh) for t in range(NT) for hh in range(2)] if lastpass else [(t, hh) for hh in range(2) for t in range(NT)]
                    for (t, hh) in order:
                        dslot = hh
                        P = TP(t); c = t * 128
                        bank = 4 + ((2 * t + hh) % 4 if lastpass else t % 4)

                        def dn_mm(e):
                            for fi in range(nf):
                                ins = e.matmul(PS[:P, bank, :], lhsT=ACTH[:, fi, c:c + P], rhs=DN[dslot][:, fi, :],
                                               start=(fi == 0), stop=(fi == nf - 1))
                            return ins
                        S.op('pe', dn_mm, reads=[('ACTH', fi) for fi in range(nf)] + [('DN', dslot)], writes=[('ps', bank)])
                        S.op('dve', lambda e: e.tensor_tensor(out=X[:P, t, hh * 512:(hh + 1) * 512], in0=X[:P, t, hh * 512:(hh + 1) * 512],
                                                              in1=PS[:P, bank, :], op=ALU.add),
                             reads=[('ps', bank), ('X', t)], writes=[('X', t)])
                        if lastpass and hh == 1:
                            if t == 0:
                                S.dma('sp', 'Gf', out=Gfin, in_=grow.partition_broadcast(128), reads=[('X', 0)], writes=['Gfin'])

                            def fin1(t):
                                P = TP(t)
                                jout = yof[t % 2][:P] if last else junkf[:P]
                                jkey = ('yo', t % 2) if last else 'junkf'
                                S.op('act', lambda e: e.activation(out=jout, in_=X[:P, t, :], func=AF.Square, accum_out=stf[:P, t, 0:1]),
                                     reads=[('X', t)], writes=[jkey, ('st', t, 0)])

                            def fin2(t):
                                rstd_chain(stf, TP(t), t, None)

                            def fin3(t):
                                P = TP(t)
                                if last:
                                    S.op('dve', lambda e: e.scalar_tensor_tensor(out=yof[t % 2][:P], in0=X[:P, t, :], scalar=stf[:P, t, 3:4],
                                                                                 in1=Gfin[:P], op0=ALU.mult, op1=ALU.mult),
                                         reads=[('X', t), ('st', t, 3), 'Gfin'], writes=[('yo', t % 2)])
                                    S.dma('sp', 'o_y%d' % (t % 2), out=rows_y(t), in_=yof[t % 2][:P], reads=[('yo', t % 2)], is_out=True)
                                else:
                                    S.op('dve', lambda e: e.scalar_tensor_tensor(out=hbf[t % 3][:P], in0=X[:P, t, :], scalar=stf[:P, t, 3:4],
                                                                                 in1=Gfin[:P], op0=ALU.mult, op1=ALU.mult),
                                         reads=[('X', t), ('st', t, 3), 'Gfin'], writes=[('hbf', t % 3)])

                            def fin4(t):
                                if last:
                                    return
                                P = TP(t); c4 = t * 128
                                tbk = t % 4
                                pvf = psb(tbk).rearrange("p (k n) -> p k n", k=8)

                                def tr(e):
                                    for kc in range(8):
                                        ins = e.transpose(out=pvf[:, kc, :P], in_=hbf[t % 3][:P, kc * 128:(kc + 1) * 128], identity=idb[:P, :P])
                                    return ins
                                S.op('pe', tr, reads=[('hbf', t % 3), 'idb'], writes=[('ps', tbk)])
                                S.op('act', lambda e: e.activation(out=H[:, :, c4:c4 + P], in_=pvf[:, :, :P], func=AF.Copy),
                                     reads=[('ps', tbk)], writes=[('H', t)])
                            if t >= 3:
                                fin4(t - 3)
                            if t >= 2:
                                fin3(t - 2)
                            if t >= 1:
                                fin2(t - 1)
                            fin1(t)
                            if t == NT - 1:
                                fin4(t - 2)
                                fin3(t - 1)
                                fin2(t)
                                fin4(t - 1)
                                fin3(t)
                                fin4(t)
                S.barrier()
            phf.close()
            phw.close()

        S.stopped = False

        if True:
            S.finish()
    return nc


def _host_consts():
    half = 32
    inv = (np.float32(10000.0) ** (-np.arange(half, dtype=np.float32) / np.float32(half))).astype(np.float32)
    pos = np.zeros((128, 17), np.float32)
    for t in range(16):
        pos[:, t] = t * 128 + np.arange(128)
    pos[:, 16] = 2048 + np.arange(128)
    ang = (pos[:, :, None] * inv[None, None, :]).astype(np.float32)
    cos_t = np.cos(ang).astype(np.float32)
    sin_t = np.sin(ang).astype(np.float32)
    wins = np.array([[2, 4], [8, 16]])
    invc = np.zeros((128, 2, 16), np.float32)
    invw = np.zeros((128, 2), np.float32)
    for j in range(2):
        for p in range(128):
            w = wins[j][p // 64]
            invw[p, j] = 1.0 / w
            invc[p, j, :] = 1.0 / np.minimum(np.arange(16) + 1, w)
    return cos_t, sin_t, invc, invw


_NC_CACHE = {}


def kernel(x_prompt, x_sample, cache_k, cache_v, state_pool, state_conv,
           norm_mix_g, w_in, pool_w, pool_scale, lambda_qk, diff_norm_g,
           conv_dw, conv_dw_b, conv_ln_g, conv_ln_b, conv_pw, w_out,
           norm_ffn_g, w_gate_up, w_down, final_norm_g):
    f = lambda a: np.ascontiguousarray(np.asarray(a, dtype=np.float32))
    x_prompt, x_sample, cache_k, cache_v = f(x_prompt), f(x_sample), f(cache_k), f(cache_v)
    state_pool, state_conv = f(state_pool), f(state_conv)
    B = 8
    cos_t, sin_t, invc, invw = _host_consts()

    def pc(v):
        return np.asarray(v, np.float32).reshape(2, 128).T
    pvec = np.zeros((2, 128, NPV), np.float32)
    plw = np.zeros((2, 128, 2, 128), np.float32)
    pool_w = f(pool_w); conv_dw = f(conv_dw)
    for l in range(2):
        pvec[l, :, 0:2] = pc(pool_scale[l])
        pvec[l, :, 2:4] = pc(conv_dw_b[l])
        pvec[l, :, 4:6] = pc(conv_ln_g[l])
        pvec[l, :, 6:8] = pc(conv_ln_b[l])
        pvec[l, :, 8] = np.asarray(diff_norm_g[l], np.float32)
        pvec[l, :, 9:11] = invw
        dwl = conv_dw[l].reshape(31, 2, 128)
        pvec[l, :, 11:73] = dwl.transpose(2, 1, 0).reshape(128, 62)
        for j in range(2):
            for hf in range(2):
                plw[l, hf * 64:(hf + 1) * 64, j, hf * 64:(hf + 1) * 64] = pool_w[l, 2 * j + hf]
    common = dict(
        w_in=f(w_in), w_out=f(w_out), w_gu=f(w_gate_up), w_dn=f(w_down), conv_pw=f(conv_pw), plw=plw,
        gmix=f(norm_mix_g), gffn=f(norm_ffn_g), gfin=f(final_norm_g).reshape(1, 1024),
        pvec=pvec, lq=f(lambda_qk).reshape(2, 256),
        ident=np.eye(128, dtype=np.float32), cos_t=cos_t, sin_t=sin_t, invc=invc,
    )
    in_maps = []
    for b in range(B):
        m = dict(common)
        m["xp"] = x_prompt[b]
        m["xs"] = x_sample[b]
        m["ck"] = np.ascontiguousarray(cache_k[:, b].reshape(2, 2048, 512))
        m["cv"] = np.ascontiguousarray(cache_v[:, b].reshape(2, 2048, 512))
        m["stp"] = np.ascontiguousarray(state_pool[:, b].reshape(2, 15, 2, 128).transpose(0, 3, 2, 1))
        m["stc"] = np.ascontiguousarray(state_conv[:, b].reshape(2, 30, 2, 128).transpose(0, 3, 2, 1))
        in_maps.append(m)
    if "nc" not in _NC_CACHE:
        _NC_CACHE["nc"] = build_program()
    nc = _NC_CACHE["nc"]
    res = run_bass_kernel_spmd(nc, in_maps, core_ids=list(range(B)))
    R = res.results

    def st(name):
        return np.stack([np.asarray(R[b][name], np.float32) for b in range(B)], axis=0)
    y_prompt = st("yp")
    y_sample = st("ys")
    nk_p = st("nkp").transpose(1, 0, 2, 3).reshape(2, B, 2048, 4, 2, 64)
    nv_p = st("nvp").transpose(1, 0, 2, 3).reshape(2, B, 2048, 4, 128)
    nk_s = st("nks").transpose(1, 0, 2, 3).reshape(2, B, 32, 4, 2, 64)
    nv_s = st("nvs").transpose(1, 0, 2, 3).reshape(2, B, 32, 4, 128)

    def unT(name, T):
        a = st(name)
        return np.ascontiguousarray(a.transpose(1, 0, 4, 3, 2).reshape(2, B, T, 256))
    np_p = unT("npp", 15); nc_p = unT("ncp", 30); np_s = unT("nps", 15); nc_s = unT("ncs", 30)
    return (y_prompt, y_sample, np.ascontiguousarray(nk_p), np.ascontiguousarray(nv_p), np_p, nc_p,
            np.ascontiguousarray(nk_s), np.ascontiguousarray(nv_s), np_s, nc_s)
```

```python
import math
from contextlib import ExitStack

import numpy as np
import concourse.bass as bass
import concourse.mybir as mybir
from concourse.bass_utils import run_bass_kernel_spmd

F32 = mybir.dt.float32
BF16 = mybir.dt.bfloat16
ALU = mybir.AluOpType
AF = mybir.ActivationFunctionType

EPS = 1e-6
SCALE = 0.125
NT = 17
NTOK = 2080
DEPTH = 2
NPV = 73
FFN_PASSES = [(0, 8), (8, 15), (15, 22)]
GROUPS = [(0, 512, [0, 1, 2, 3]), (512, 512, [4, 5, 6, 7]), (1024, 512, [8, 9, 10, 11]),
          (1536, 512, [12, 13, 14, 15]), (2048, 32, [16])]
NEG = -30000.0


def TP(t):
    return 128 if t < 16 else 32


import os
STOP_AFTER = 0


class _Stop(Exception):
    pass


class Sched:
    def __init__(self, nc, es):
        self.nc = nc
        self.es = es
        self.eng = {'pe': nc.tensor, 'act': nc.scalar, 'dve': nc.vector, 'pool': nc.gpsimd, 'sp': nc.sync}
        self.csem = {e: es.enter_context(nc.semaphore('c_' + e)) for e in ('pe', 'act', 'dve', 'pool')}
        self.ccount = {e: 0 for e in self.csem}
        self.dsem = {}
        self.lastw = {}
        self.readers = {}
        self.waited = {e: {} for e in self.eng}
        self.out_events = []
        self.stopped = False
        self.nbar = 0

    def _wait(self, eng, ev):
        sem, val, src, kind, sid = ev
        w = self.waited[eng]
        if w.get(sid, 0) >= val:
            return
        w[sid] = val
        self.eng[eng].wait_ge(sem, val)

    def _deps(self, eng, reads, writes):
        deps = []
        for k in reads:
            ev = self.lastw.get(k)
            if ev is not None:
                deps.append((ev, True))
        for k in writes:
            ev = self.lastw.get(k)
            if ev is not None:
                deps.append((ev, False))
            for ev in self.readers.get(k, {}).values():
                deps.append((ev, False))
        for ev, raw in deps:
            if ev[3] == 'c' and ev[2] == eng:
                if eng == 'pe' or not raw:
                    continue
            self._wait(eng, ev)

    def _commit(self, ev, reads, writes):
        for k in writes:
            self.lastw[k] = ev
            self.readers[k] = {}
        for k in reads:
            self.readers.setdefault(k, {})[ev[4]] = ev

    def op(self, eng, fn, reads=(), writes=()):
        if self.stopped:
            return
        self._deps(eng, reads, writes)
        ins = fn(self.eng[eng])
        self.ccount[eng] += 1
        ins.then_inc(self.csem[eng], 1)
        ev = (self.csem[eng], self.ccount[eng], eng, 'c', 'c_' + eng)
        self._commit(ev, reads, writes)

    def dma(self, q, key, out, in_, reads=(), writes=(), is_out=False):
        if self.stopped:
            return
        self._deps(q, reads, writes)
        if key not in self.dsem:
            self.dsem[key] = [self.es.enter_context(self.nc.semaphore('d%d' % len(self.dsem))), 0]
        d = self.dsem[key]
        d[1] += 16
        self.eng[q].dma_start(out=out, in_=in_).then_inc(d[0], 16)
        ev = (d[0], d[1], q, 'd', 'd_' + str(key))
        self._commit(ev, reads, writes)
        if is_out:
            self.out_events.append(ev)

    def barrier(self):
        if self.stopped:
            return
        self.nbar += 1
        self._barrier()
        if self.nbar == STOP_AFTER:
            self.stopped = True

    def _barrier(self):
        evs = [(self.csem[e], self.ccount[e], e, 'c', 'c_' + e) for e in self.csem if self.ccount[e] > 0]
        evs += [(d[0], d[1], 'x', 'd', 'd_' + str(k)) for k, d in self.dsem.items() if d[1] > 0]
        for eng in self.eng:
            for ev in evs:
                if ev[3] == 'c' and ev[2] == eng:
                    continue
                self._wait(eng, ev)
        self.lastw = {}
        self.readers = {}

    def finish(self):
        for ev in self.out_events:
            self._wait('sp', ev)


def build_program():
    nc = bass.Bass("TRN2", target_bir_lowering=False)

    def din(name, shape):
        return nc.dram_tensor(name, list(shape), F32, kind="ExternalInput").ap()

    def dout(name, shape):
        return nc.dram_tensor(name, list(shape), F32, kind="ExternalOutput").ap()

    xp = din("xp", [2048, 1024]); xs = din("xs", [32, 1024])
    ck = din("ck", [2, 2048, 512]); cv = din("cv", [2, 2048, 512])
    stp = din("stp", [2, 128, 2, 15]); stc = din("stc", [2, 128, 2, 30])
    w_in = din("w_in", [2, 1024, 2304]); w_out = din("w_out", [2, 1024, 1024])
    w_gu = din("w_gu", [2, 1024, 5632]); w_dn = din("w_dn", [2, 2816, 1024])
    conv_pw = din("conv_pw", [2, 256, 256]); plw = din("plw", [2, 128, 2, 128])
    gmix = din("gmix", [2, 1024]); gffn = din("gffn", [2, 1024]); gfin = din("gfin", [1, 1024])
    pvec = din("pvec", [2, 128, NPV]); lq = din("lq", [2, 256])
    ident = din("ident", [128, 128]); cos_t = din("cos_t", [128, 17, 32]); sin_t = din("sin_t", [128, 17, 32])
    invc = din("invc", [128, 2, 16])

    yp = dout("yp", [2048, 1024]); ys = dout("ys", [32, 1024])
    nkp = dout("nkp", [2, 2048, 512]); nvp = dout("nvp", [2, 2048, 512])
    npp = dout("npp", [2, 128, 2, 15]); ncp = dout("ncp", [2, 128, 2, 30])
    nks = dout("nks", [2, 32, 512]); nvs = dout("nvs", [2, 32, 512])
    nps = dout("nps", [2, 128, 2, 15]); ncs = dout("ncs", [2, 128, 2, 30])

    def rows_y(t):
        return yp[t * 128:(t + 1) * 128, :] if t < 16 else ys[0:32, :]

    def rows_k(l, t):
        return nkp[l, t * 128:(t + 1) * 128, :] if t < 16 else nks[l, 0:32, :]

    def rows_v(l, t):
        return nvp[l, t * 128:(t + 1) * 128, :] if t < 16 else nvs[l, 0:32, :]

    with ExitStack() as es:
        S = Sched(nc, es)

        uid = [0]

        def sb(stack, name, shape, dt):
            uid[0] += 1
            return stack.enter_context(nc.sbuf_tensor("%s_%d" % (name, uid[0]), list(shape), dt))

        PS = es.enter_context(nc.psum_tensor("PS", [128, 8, 512], F32))

        def psb(b):
            return PS[:, b, :].bitcast(BF16)

        X = sb(es, "X", [128, NT, 1024], F32)
        H = sb(es, "H", [128, 8, NTOK], BF16)
        FA = sb(es, "FA", [128, 25344], BF16)
        QT = FA[:, 0:8320].rearrange("p (h n) -> p h n", h=4)
        KT = FA[:, 8320:16640].rearrange("p (h n) -> p h n", h=4)
        VB = FA[:, 16640:25344].rearrange("p (t n) -> p t n", t=NT)
        ACTH = FA[:, 0:16640].rearrange("p (k n) -> p k n", k=8)
        idb = sb(es, "idb", [128, 128], BF16)
        onesb = sb(es, "onesb", [128, 128], BF16)
        onesf = sb(es, "onesf", [128, 3, 128], F32)
        PV = sb(es, "PV", [128, 2, NPV], F32)
        LAM = sb(es, "LAM", [128, 2, 8], F32)
        LQ = sb(es, "LQ", [128, 2, 256], F32)
        ljunk = sb(es, "ljunk", [128, 64], F32)

        S.dma('pool', 'c_id', out=idb[:], in_=ident[:, :], writes=['idb'])
        S.dma('sp', 'c_pv', out=PV[:], in_=pvec.rearrange("l p n -> p l n"), writes=['PV'])
        S.dma('sp', 'c_lq', out=LQ[:, 0, :], in_=lq[0:1, :].partition_broadcast(128), writes=[('LQ', 0)])
        S.dma('sp', 'c_lq1', out=LQ[:, 1, :], in_=lq[1:2, :].partition_broadcast(128), writes=[('LQ', 1)])
        S.dma('sp', 'xs', out=X[:32, 16, :], in_=xs[:, :], writes=[('X', 16)])
        xpv = xp.rearrange("(t p) d -> p t d", p=128)
        for q4 in range(4):
            S.dma('sp', 'x%d' % q4, out=X[:, q4 * 4:(q4 + 1) * 4, :], in_=xpv[:, q4 * 4:(q4 + 1) * 4, :],
                  writes=[('X', t) for t in range(q4 * 4, q4 * 4 + 4)])
        S.op('dve', lambda e: e.memset(onesb[:], 1.0), writes=['onesb'])
        S.op('dve', lambda e: e.memset(onesf[:, 0, :], 1.0 / 128), writes=['onesf0'])
        S.op('dve', lambda e: e.memset(onesf[:, 1, :], 1.0 / 256), writes=['onesf1'])
        S.op('dve', lambda e: e.memset(onesf[:, 2, :], 1.0), writes=['onesf2'])
        for l in range(DEPTH):
            lam_init = 0.8 - 0.6 * math.exp(-0.3 * l)
            S.op('dve', lambda e: e.scalar_tensor_tensor(out=ljunk[:], in0=LQ[:, l, 0:64], scalar=1.0, in1=LQ[:, l, 64:128],
                                                         op0=ALU.mult, op1=ALU.mult, accum_out=LAM[:, l, 0:1]),
                 reads=[('LQ', l)], writes=['ljunk', ('LAM', l, 0)])
            S.op('dve', lambda e: e.scalar_tensor_tensor(out=ljunk[:], in0=LQ[:, l, 128:192], scalar=1.0, in1=LQ[:, l, 192:256],
                                                         op0=ALU.mult, op1=ALU.mult, accum_out=LAM[:, l, 1:2]),
                 reads=[('LQ', l)], writes=['ljunk', ('LAM', l, 1)])
            S.op('act', lambda e: e.activation(out=LAM[:, l, 2:4], in_=LAM[:, l, 0:2], func=AF.Exp),
                 reads=[('LAM', l, 0), ('LAM', l, 1)], writes=[('LAM', l, 2)])
            S.op('dve', lambda e: e.tensor_tensor(out=LAM[:, l, 6:7], in0=LAM[:, l, 3:4], in1=LAM[:, l, 2:3], op=ALU.subtract),
                 reads=[('LAM', l, 2)], writes=[('LAM', l, 6)])
            S.op('dve', lambda e: e.tensor_scalar(out=LAM[:, l, 4:5], in0=LAM[:, l, 6:7], scalar1=-lam_init, scalar2=None, op0=ALU.add),
                 reads=[('LAM', l, 6)], writes=[('LAM', l, 4)])
            S.op('dve', lambda e: e.tensor_scalar(out=LAM[:, l, 5:6], in0=PV[:, l, 8:9], scalar1=(1.0 - lam_init), scalar2=None, op0=ALU.mult),
                 reads=['PV'], writes=[('LAM', l, 5)])
        S._barrier()

        def rstd_chain(stt, P, t, src_key):
            S.op('dve', lambda e: e.tensor_scalar(out=stt[:P, t, 1:2], in0=stt[:P, t, 0:1], scalar1=1.0 / 1024, scalar2=EPS,
                                                  op0=ALU.mult, op1=ALU.add),
                 reads=[('st', t, 0)], writes=[('st', t, 1)])
            S.op('act', lambda e: e.activation(out=stt[:P, t, 2:3], in_=stt[:P, t, 1:2], func=AF.Ln),
                 reads=[('st', t, 1)], writes=[('st', t, 2)])
            S.op('act', lambda e: e.activation(out=stt[:P, t, 3:4], in_=stt[:P, t, 2:3], func=AF.Exp, scale=-0.5),
                 reads=[('st', t, 2)], writes=[('st', t, 3)])

        def phase_norm(g_row, tgroups=None):
            tgroups = tgroups or [list(range(NT))]
            with ExitStack() as ph:
                G = sb(ph, "G", [128, 1024], F32)
                hb = [sb(ph, "hb%d" % i, [128, 1024], BF16) for i in range(3)]
                junk = sb(ph, "junk", [128, 1024], BF16)
                junk2 = sb(ph, "junk2", [128, 1024], BF16)
                stt = sb(ph, "nst", [128, 4, NT], F32)
                S.dma('sp', 'G', out=G[:], in_=g_row.partition_broadcast(128), writes=['G'])
                S.op('dve', lambda e: e.memset(stt[:, 0, :], 1.0), writes=['st0i'])
                for tg in tgroups:
                    t0g, t1g = tg[0], tg[-1] + 1
                    for t in tg:
                        P = TP(t)
                        if t % 2 == 0:
                            S.op('act', lambda e: e.activation(out=junk[:P], in_=X[:P, t, :], func=AF.Square, accum_out=stt[:P, 0, t:t + 1]),
                                 reads=[('X', t), 'st0i'], writes=['junk', ('st0', t)])
                        else:
                            S.op('dve', lambda e: e.scalar_tensor_tensor(out=junk2[:P], in0=X[:P, t, :], scalar=1.0, in1=X[:P, t, :],
                                                                         op0=ALU.mult, op1=ALU.mult, accum_out=stt[:P, 0, t:t + 1]),
                                 reads=[('X', t), 'st0i'], writes=['junk2', ('st0', t)])
                    S.op('dve', lambda e: e.tensor_scalar(out=stt[:, 1, t0g:t1g], in0=stt[:, 0, t0g:t1g], scalar1=1.0 / 1024, scalar2=EPS,
                                                          op0=ALU.mult, op1=ALU.add),
                         reads=[('st0', t) for t in tg], writes=[('st1', t0g)])
                    S.op('act', lambda e: e.activation(out=stt[:, 2, t0g:t1g], in_=stt[:, 1, t0g:t1g], func=AF.Ln),
                         reads=[('st1', t0g)], writes=[('st2', t0g)])
                    S.op('act', lambda e: e.activation(out=stt[:, 3, t0g:t1g], in_=stt[:, 2, t0g:t1g], func=AF.Exp, scale=-0.5),
                         reads=[('st2', t0g)], writes=[('st3', t0g)])
                    for t in tg:
                        P = TP(t); c = t * 128
                        hs = t % 3
                        S.op('dve', lambda e: e.scalar_tensor_tensor(out=hb[hs][:P], in0=X[:P, t, :], scalar=stt[:P, 3, t:t + 1], in1=G[:P],
                                                                     op0=ALU.mult, op1=ALU.mult),
                             reads=[('X', t), ('st3', t0g), 'G'], writes=[('hb', hs)])
                        bank = t % 4
                        pv = psb(bank).rearrange("p (k n) -> p k n", k=8)

                        def tr(e):
                            for kc in range(8):
                                ins = e.transpose(out=pv[:, kc, :P], in_=hb[hs][:P, kc * 128:(kc + 1) * 128], identity=idb[:P, :P])
                            return ins
                        S.op('pe', tr, reads=[('hb', hs), 'idb'], writes=[('ps', bank)])
                        S.op('act', lambda e: e.activation(out=H[:, :, c:c + P], in_=pv[:, :, :P], func=AF.Copy),
                             reads=[('ps', bank)], writes=[('H', t)])
                S.barrier()

        def load_slab(q, key, dst, src3, kcs, c0, w, wkey):
            S.dma(q, key, out=dst, in_=src3[:, kcs[0]:kcs[-1] + 1, c0:c0 + w], writes=[wkey])

        if True:
          for l in range(DEPTH):
            w_in3 = w_in[l].rearrange("(k p) c -> p k c", p=128)
            w_out3 = w_out[l].rearrange("(k p) c -> p k c", p=128)
            w_gu3 = w_gu[l].rearrange("(k p) c -> p k c", p=128)
            w_dn3 = w_dn[l].rearrange("(k p) c -> p k c", p=128)
            pw3 = conv_pw[l].rearrange("(k p) c -> p k c", p=128)

            WP = FA[:, 16640:22784].rearrange("p (k c) -> p k c", k=8)
            WOP = FA[:, 0:4096].rearrange("p (k c) -> p k c", k=4)
            if l == 0:
                S.dma('pool', 'wp0', out=WP[:, :, 0:256], in_=w_in3[:, :, 0:256], reads=[('X', 11)], writes=['WP0'])
                S.dma('pool', 'wp1', out=WP[:, :, 256:768], in_=w_in3[:, :, 1792:2304], writes=['WP1'])
            S.dma('pool', 'wop0', out=WOP[:, 0:2, :], in_=w_out3[:, 0:2, :], writes=['WOP0'])
            S.dma('pool', 'wop1', out=WOP[:, 2:4, :], in_=w_out3[:, 6:8, :], writes=['WOP1'])
            if l == 0:
                phase_norm(gmix[l:l + 1, :], [[0, 1, 2, 3], [4, 5, 6, 7], [8, 9, 10, 11], [12, 13, 14, 15, 16]])

            phq = ExitStack()
            Wq0 = sb(phq, "Wq0", [128, 8, 512], BF16)
            S.dma('pool', 'wq0', out=Wq0[:], in_=w_in3[:, :, 256:768], writes=[('W', 0)])
            with ExitStack() as ph:
                DG = FA[:, 4096:12032].rearrange("p (k c) -> p k c", k=62)
                CU = [FA[:, 12032 + i * 1084:12032 + (i + 1) * 1084].rearrange("p (j c) -> p j c", j=2) for i in range(2)]
                dsb = FA[:, 22784:23808].rearrange("p (j c) -> p j c", j=2)
                MPC2 = [FA[:, 14200:16248].rearrange("p (j c) -> p j c", j=4), sb(ph, "MPC1", [128, 4, 512], BF16)]
                sl2 = [FA[:, 23808:24832].rearrange("p (j c) -> p j c", j=2), sb(ph, "sl1", [128, 2, 512], BF16)]
                PW = sb(ph, "PW", [128, 2, 256], BF16)
                PLW = sb(ph, "PLW", [128, 2, 128], BF16)
                INVC = sb(ph, "INVC", [128, 2, 16], F32)
                PU = [sb(ph, "PU%d" % i, [128, 2, 527], F32) for i in range(2)]
                U32 = sb(ph, "U32", [128, 2, 512], F32)
                WA = sb(ph, "WA", [128, 526], F32); WB = sb(ph, "WB", [128, 524], F32)
                WC = WA; WD = WB
                t16 = sb(ph, "t16", [128, 16], F32)
                sg2 = [sb(ph, "sg%d" % i, [128, 512], F32) for i in range(2)]
                ysb = sb(ph, "ysb", [128, 2, 512], F32)
                ysq = sb(ph, "ysq", [128, 2, 512], F32)
                m2 = sb(ph, "m2", [128, 512], F32); var = sb(ph, "var", [128, 512], F32)

                S.dma('pool', 'pw', out=PW[:], in_=pw3, writes=['PW'])
                S.dma('pool', 'plw', out=PLW[:], in_=plw[l], writes=['PLW'])
                S.dma('sp', 'invc', out=INVC[:], in_=invc[:, :, :], writes=['INVC'])
                S.op('dve', lambda e: e.tensor_tensor(out=DG, in0=idb[:].unsqueeze(1).broadcast_to([128, 62, 128]),
                                                      in1=PV[:, l, 11:73].unsqueeze(2).broadcast_to([128, 62, 128]), op=ALU.mult),
                     reads=['PV', 'idb'], writes=['DG'])
                S.op('dve', lambda e: e.memset(PU[0][:, :, 0:15], 0.0), writes=[('PUc', 0)])
                S.op('dve', lambda e: e.memset(CU[0][:, :, 0:30], 0.0), writes=[('CUc', 0)])

                def stageA(gi):
                    c0, n, tiles = GROUPS[gi]
                    sample = (gi == 4)
                    slot = 0 if sample else gi % 2
                    if sample:
                        S.dma('sp', 'stp', out=PU[0][:, :, 0:15], in_=stp[l], writes=[('PUc', 0)])
                        S.dma('pool', 'stc', out=CU[0][:, :, 0:30], in_=stc[l], writes=[('CUc', 0)])
                    def proj(bank, col):
                        def f(e):
                            for kc in range(8):
                                ins = e.matmul(PS[:, bank, 0:n], lhsT=WP[:, kc, col:col + 128], rhs=H[:, kc, c0:c0 + n],
                                               start=(kc == 0), stop=(kc == 7))
                            return ins
                        S.op('pe', f, reads=['WP0', 'WP1'] + [('H', t) for t in tiles], writes=[('ps', bank)])
                    for j in range(2):
                        proj(j, j * 128)
                        S.op('act', lambda e: e.activation(out=PU[slot][:, j, 15:15 + n], in_=PS[:, j, 0:n], func=AF.Copy),
                             reads=[('ps', j)], writes=[('PUn', slot, j)])
                    if gi == 3 or sample:
                        S.dma('sp', 'o_np%d' % gi, out=(nps[l] if sample else npp[l]), in_=PU[slot][:, :, n:n + 15],
                              reads=[('PUc', slot), ('PUn', slot, 0), ('PUn', slot, 1)], is_out=True)
                    if gi < 3:
                        S.op('dve', lambda e: e.tensor_copy(out=PU[(gi + 1) % 2][:, :, 0:15], in_=PU[slot][:, :, 512:527]),
                             reads=[('PUn', slot, 0), ('PUn', slot, 1)], writes=[('PUc', (gi + 1) % 2)])
                    for j in range(2):
                        ext = PU[slot][:, j, :]
                        rk = [('PUc', slot), ('PUn', slot, j)]
                        S.op('dve', lambda e: e.tensor_tensor(out=WA[:, 0:14 + n], in0=ext[:, 1:15 + n], in1=ext[:, 0:14 + n], op=ALU.add),
                             reads=rk, writes=['WA'])
                        S.op('dve', lambda e: e.tensor_tensor(out=WB[:, 0:12 + n], in0=WA[:, 2:14 + n], in1=WA[:, 0:12 + n], op=ALU.add),
                             reads=['WA'], writes=['WB'])
                        if j == 0:
                            lo, lo_off, hi, hi_off = WA, 14, WB, 12
                            kk = ['WA', 'WB']
                        else:
                            S.op('dve', lambda e: e.tensor_tensor(out=WC[:, 0:8 + n], in0=WB[:, 4:12 + n], in1=WB[:, 0:8 + n], op=ALU.add),
                                 reads=['WB'], writes=['WA'])
                            S.op('dve', lambda e: e.tensor_tensor(out=WD[:, 0:n], in0=WC[:, 8:8 + n], in1=WC[:, 0:n], op=ALU.add),
                                 reads=['WA'], writes=['WB'])
                            lo, lo_off, hi, hi_off = WC, 8, WD, 0
                            kk = ['WA', 'WB']
                        for (src, off, p0) in ((lo, lo_off, 0), (hi, hi_off, 64)):
                            S.op('dve', lambda e: e.scalar_tensor_tensor(out=dsb[p0:p0 + 64, j, 0:n], in0=src[p0:p0 + 64, off:off + n],
                                                                         scalar=PV[p0:p0 + 64, l, 9 + j:10 + j], in1=ext[p0:p0 + 64, 15:15 + n],
                                                                         op0=ALU.mult, op1=ALU.subtract),
                                 reads=kk + rk + ['PV'], writes=[('dsb', j)])
                            if gi == 0:
                                S.op('dve', lambda e: e.tensor_tensor(out=t16[p0:p0 + 64, :], in0=src[p0:p0 + 64, off:off + 16],
                                                                      in1=INVC[p0:p0 + 64, j, :], op=ALU.mult),
                                     reads=kk + ['INVC'], writes=['t16'])
                                S.op('dve', lambda e: e.tensor_tensor(out=dsb[p0:p0 + 64, j, 0:16], in0=t16[p0:p0 + 64, :],
                                                                      in1=ext[p0:p0 + 64, 15:31], op=ALU.subtract),
                                     reads=['t16'] + rk, writes=[('dsb', j)])
                    for j in range(2):
                        proj(2 + 2 * j, 256 + j * 128)
                        proj(3 + 2 * j, 512 + j * 128)
                        S.op('act', lambda e: e.activation(out=sg2[j][:, 0:n], in_=PS[:, 3 + 2 * j, 0:n], func=AF.Sigmoid),
                             reads=[('ps', 3 + 2 * j)], writes=[('sg', j)])
                        S.op('dve', lambda e: e.tensor_tensor(out=CU[slot][:, j, 30:30 + n], in0=PS[:, 2 + 2 * j, 0:n], in1=sg2[j][:, 0:n], op=ALU.mult),
                             reads=[('ps', 2 + 2 * j), ('sg', j)], writes=[('CUn', slot, j)])
                        if gi == 3 or sample:
                            S.op('dve', lambda e: e.tensor_tensor(out=U32[:, j, n - 30:n], in0=PS[:, 2 + 2 * j, n - 30:n], in1=sg2[j][:, n - 30:n], op=ALU.mult),
                                 reads=[('ps', 2 + 2 * j), ('sg', j)], writes=[('U32', j)])
                    if gi == 3 or sample:
                        S.dma('sp', 'o_nc%d' % gi, out=(ncs[l] if sample else ncp[l]), in_=U32[:, :, n - 30:n],
                              reads=[('U32', 0), ('U32', 1)], is_out=True)
                    if gi < 3:
                        S.op('dve', lambda e: e.tensor_copy(out=CU[(gi + 1) % 2][:, :, 0:30], in_=CU[slot][:, :, 512:542]),
                             reads=[('CUn', slot, 0), ('CUn', slot, 1)], writes=[('CUc', (gi + 1) % 2)])

                def stageC(gi):
                    c0, n, tiles = GROUPS[gi]
                    sample = (gi == 4)
                    slot = 0 if sample else gi % 2
                    MPC = MPC2[gi % 2]; sl = sl2[gi % 2]
                    for j in range(2):
                        S.op('pe', lambda e: e.matmul(PS[:, j, 0:n], lhsT=PLW[:, j, :], rhs=dsb[:, j, 0:n], start=True, stop=True),
                             reads=[('dsb', j), 'PLW'], writes=[('ps', j)])
                        S.op('act', lambda e: e.activation(out=MPC[:, j, 0:n], in_=PS[:, j, 0:n], func=AF.Copy, scale=PV[:, l, j:j + 1]),
                             reads=[('ps', j), 'PV'], writes=[('MPC', gi % 2, j)])
                    for j in range(2):
                        def cv_mm(e):
                            for tap in range(31):
                                ins = e.matmul(PS[:, 4 + j, 0:n], lhsT=DG[:, j * 31 + tap, :], rhs=CU[slot][:, j, tap:tap + n],
                                               start=(tap == 0), stop=(tap == 30))
                            return ins
                        S.op('pe', cv_mm, reads=['DG', ('CUc', slot), ('CUn', slot, j)], writes=[('ps', 4 + j)])
                        S.op('act', lambda e: e.activation(out=ysb[:, j, 0:n], in_=PS[:, 4 + j, 0:n], func=AF.Identity, bias=PV[:, l, 2 + j:3 + j]),
                             reads=[('ps', 4 + j), 'PV'], writes=[('ysb', j)])
                        S.op('act', lambda e: e.activation(out=ysq[:, j, 0:n], in_=PS[:, 4 + j, 0:n], func=AF.Square, bias=PV[:, l, 2 + j:3 + j]),
                             reads=[('ps', 4 + j), 'PV'], writes=[('ysq', j)])

                    def st_mm(e):
                        for j in range(2):
                            e.matmul(PS[:, 6, 0:n], lhsT=onesf[:, 1, :], rhs=ysb[:, j, 0:n], start=(j == 0), stop=(j == 1))
                        for j in range(2):
                            ins = e.matmul(PS[:, 7, 0:n], lhsT=onesf[:, 1, :], rhs=ysq[:, j, 0:n], start=(j == 0), stop=(j == 1))
                        return ins
                    S.op('pe', st_mm, reads=[('ysb', 0), ('ysb', 1), ('ysq', 0), ('ysq', 1), 'onesf1'], writes=[('ps', 6), ('ps', 7)])
                    S.op('act', lambda e: e.activation(out=m2[:, 0:n], in_=PS[:, 6, 0:n], func=AF.Square), reads=[('ps', 6)], writes=['m2'])
                    S.op('dve', lambda e: e.scalar_tensor_tensor(out=var[:, 0:n], in0=PS[:, 7, 0:n], scalar=EPS, in1=m2[:, 0:n],
                                                                 op0=ALU.add, op1=ALU.subtract),
                         reads=[('ps', 7), 'm2'], writes=['var'])
                    S.op('act', lambda e: e.activation(out=m2[:, 0:n], in_=var[:, 0:n], func=AF.Ln), reads=['var'], writes=['m2'])
                    S.op('act', lambda e: e.activation(out=var[:, 0:n], in_=m2[:, 0:n], func=AF.Exp, scale=-0.5), reads=['m2'], writes=['var'])
                    for j in range(2):
                        S.op('dve', lambda e: e.tensor_tensor(out=ysq[:, j, 0:n], in0=ysb[:, j, 0:n], in1=PS[:, 6, 0:n], op=ALU.subtract),
                             reads=[('ysb', j), ('ps', 6)], writes=[('ysq', j)])
                        S.op('dve', lambda e: e.tensor_tensor(out=ysb[:, j, 0:n], in0=ysq[:, j, 0:n], in1=var[:, 0:n], op=ALU.mult),
                             reads=[('ysq', j), 'var'], writes=[('ysb', j)])
                        S.op('act', lambda e: e.activation(out=sl[:, j, 0:n], in_=ysb[:, j, 0:n], func=AF.Silu,
                                                           scale=PV[:, l, 4 + j:5 + j], bias=PV[:, l, 6 + j:7 + j]),
                             reads=[('ysb', j), 'PV'], writes=[('sl', gi % 2, j)])

                def stageB(gi):
                    c0, n, tiles = GROUPS[gi]
                    MPC = MPC2[gi % 2]; sl = sl2[gi % 2]
                    for jo in range(2):
                        def pw_mm(e):
                            for j in range(2):
                                ins = e.matmul(PS[:, jo, 0:n], lhsT=PW[:, j, jo * 128:(jo + 1) * 128], rhs=sl[:, j, 0:n],
                                               start=(j == 0), stop=(j == 1))
                            return ins
                        S.op('pe', pw_mm, reads=[('sl', gi % 2, 0), ('sl', gi % 2, 1), 'PW'], writes=[('ps', jo)])
                        S.op('act', lambda e: e.activation(out=MPC[:, 2 + jo, 0:n], in_=PS[:, jo, 0:n], func=AF.Copy),
                             reads=[('ps', jo)], writes=[('MPC', gi % 2, 2 + jo)])
                    for ti, t in enumerate(tiles):
                        P = TP(t)
                        for hh in range(2):
                            bank = (6, 7, 0, 1)[(2 * ti + hh) % 4]

                            def wo_mm(e):
                                for kc in range(4):
                                    ins = e.matmul(PS[:P, bank, :], lhsT=MPC[:, kc, ti * 128:ti * 128 + P],
                                                   rhs=WOP[:, kc, hh * 512:(hh + 1) * 512], start=(kc == 0), stop=(kc == 3))
                                return ins
                            S.op('pe', wo_mm, reads=[('MPC', gi % 2, k) for k in range(4)] + ['WOP0', 'WOP1'], writes=[('ps', bank)])
                            S.op('dve', lambda e: e.tensor_tensor(out=X[:P, t, hh * 512:(hh + 1) * 512], in0=X[:P, t, hh * 512:(hh + 1) * 512],
                                                                  in1=PS[:P, bank, :], op=ALU.add),
                                 reads=[('ps', bank), ('X', t)], writes=[('X', t)])

                for gi in range(len(GROUPS)):
                    stageA(gi)
                    if gi > 0:
                        stageB(gi - 1)
                    stageC(gi)
                stageB(len(GROUPS) - 1)
                S.barrier()

            with ExitStack() as ph:
                W = [Wq0] + [sb(ph, "Wq%d" % i, [128, 8, 512], BF16) for i in (1, 2)]
                COS = sb(ph, "COS", [128, 17, 32], F32); SIN = sb(ph, "SIN", [128, 17, 32], F32)
                zs = [sb(ph, "zs%d" % i, [128, 512], F32) for i in range(3)]
                kr = [sb(ph, "kr%d" % i, [128, 512], F32) for i in range(2)]
                t1 = [sb(ph, "t1%d" % i, [128, 512], F32) for i in range(2)]
                t2 = [sb(ph, "t2%d" % i, [128, 512], F32) for i in range(2)]
                qb = [sb(ph, "qb%d" % i, [128, 512], BF16) for i in range(6)]
                S.dma('sp', 'cos', out=COS[:], in_=cos_t[:, :, :], writes=['COS'])
                S.dma('sp', 'sin', out=SIN[:], in_=sin_t[:, :, :], writes=['SIN'])
                for pi, c0 in ((1, 768), (2, 1280)):
                    S.dma('pool', 'wq%d' % pi, out=W[pi][:], in_=w_in3[:, :, c0:c0 + 512], writes=[('W', pi)])
                pend = []
                cnt = 0
                rc = 0
                NQ0 = 6
                items = [(t, 0) for t in range(NQ0)] + [(t, pi) for t in range(NQ0) for pi in (1, 2)] + \
                        [(t, pi) for t in range(NQ0, NT) for pi in (0, 1, 2)]
                for (t, pi) in items:
                    P = TP(t); c = t * 128
                    nm = ('q', 'k', 'v')[pi]
                    if True:
                        Wc = W[pi]
                        bank = cnt % 4
                        s2 = cnt % 3
                        cnt += 1
                        qs = rc % 6

                        def mm(e):
                            for kc in range(8):
                                ins = e.matmul(PS[:P, bank, :], lhsT=H[:, kc, c:c + P], rhs=Wc[:, kc, :], start=(kc == 0), stop=(kc == 7))
                            return ins
                        S.op('pe', mm, reads=[('H', t), ('W', pi)], writes=[('ps', bank)])
                        while len(pend) > 3:
                            pend.pop(0)()
                        S.op('act', lambda e: e.activation(out=zs[s2][:P], in_=PS[:P, bank, :], func=AF.Copy),
                             reads=[('ps', bank)], writes=[('zs', s2)])
                        if nm == 'v':
                            S.dma('sp', 'o_v%d' % s2, out=rows_v(l, t), in_=zs[s2][:P], reads=[('zs', s2)], is_out=True)
                            S.op('act', lambda e: e.activation(out=VB[:P, t, :], in_=zs[s2][:P], func=AF.Copy), reads=[('zs', s2)], writes=[('VB', t)])
                            continue
                        r2 = rc % 2
                        rc += 1
                        z4 = zs[s2][:P].rearrange("p (g two i) -> p g two i", g=8, two=2)
                        cosb = COS[:P, t:t + 1, :].unsqueeze(1).broadcast_to([P, 8, 2, 32])
                        sinb = SIN[:P, t:t + 1, :].broadcast_to([P, 8, 32])
                        t14 = t1[r2][:P].rearrange("p (g two i) -> p g two i", g=8, two=2)
                        t24 = t2[r2][:P].rearrange("p (g two i) -> p g two i", g=8, two=2)
                        S.op('dve', lambda e: e.tensor_tensor(out=t14, in0=z4, in1=cosb, op=ALU.mult),
                             reads=[('zs', s2), 'COS'], writes=[('t1', r2)])
                        S.op('dve', lambda e: e.tensor_tensor(out=t24[:, :, 0, :], in0=z4[:, :, 1, :], in1=sinb, op=ALU.mult),
                             reads=[('zs', s2), 'SIN'], writes=[('t2a', r2)])
                        S.op('dve', lambda e: e.tensor_tensor(out=t24[:, :, 1, :], in0=z4[:, :, 0, :], in1=sinb, op=ALU.mult),
                             reads=[('zs', s2), 'SIN'], writes=[('t2b', r2)])
                        if nm == 'q':
                            o4 = qb[qs][:P].rearrange("p (g two i) -> p g two i", g=8, two=2)
                        else:
                            o4 = kr[r2][:P].rearrange("p (g two i) -> p g two i", g=8, two=2)
                        okey = ('qb', qs) if nm == 'q' else ('kr', r2)
                        S.op('dve', lambda e: e.tensor_tensor(out=o4[:, :, 0, :], in0=t14[:, :, 0, :], in1=t24[:, :, 0, :], op=ALU.subtract),
                             reads=[('t1', r2), ('t2a', r2)], writes=[(okey, 'a')])
                        S.op('dve', lambda e: e.tensor_tensor(out=o4[:, :, 1, :], in0=t14[:, :, 1, :], in1=t24[:, :, 1, :], op=ALU.add),
                             reads=[('t1', r2), ('t2b', r2)], writes=[(okey, 'b')])
                        if nm == 'k':
                            S.dma('sp', 'o_k%d' % r2, out=rows_k(l, t), in_=kr[r2][:P], reads=[(okey, 'a'), (okey, 'b')], is_out=True)
                            S.op('act', lambda e: e.activation(out=qb[qs][:P], in_=kr[r2][:P], func=AF.Copy),
                                 reads=[(okey, 'a'), (okey, 'b')], writes=[(('qb', qs), 'a'), (('qb', qs), 'b')])

                        def mk(t=t, P=P, c=c, qs=qs, nm=nm, tb=4 + (rc % 4)):
                            def run():
                                pv = psb(tb).rearrange("p (k n) -> p k n", k=8)

                                def tr(e):
                                    for h in range(4):
                                        ins = e.transpose(out=pv[:, h, :P], in_=qb[qs][:P, h * 128:(h + 1) * 128], identity=idb[:P, :P])
                                    return ins
                                S.op('pe', tr, reads=[(('qb', qs), 'a'), (('qb', qs), 'b'), 'idb'], writes=[('ps', tb)])
                                dst = QT if nm == 'q' else KT
                                S.op('act', lambda e: e.activation(out=dst[:, :, c:c + P], in_=pv[:, 0:4, :P], func=AF.Copy),
                                     reads=[('ps', tb)], writes=[(nm + 'T', t)])
                            return run
                        pend.append(mk())
                while pend:
                    pend.pop(0)()
                S.barrier()
            phq.close()

            phw = ExitStack()
            WOA = sb(phw, "WOA", [128, 4, 1024], BF16)
            S.dma('pool', 'woa', out=WOA[:], in_=w_out3[:, 2:6, :], writes=['WOA'])
            ckb = [sb(phw, "ckb%d" % i, [128, 512], BF16) for i in range(4)]
            cvb = [sb(phw, "cvb%d" % i, [128, 512], BF16) for i in range(4)]
            for i in range(4):
                S.dma('pool', 'ck%d' % i, out=ckb[i][:], in_=ck[l, i * 128:(i + 1) * 128, :], writes=[('ckb', i)])
                S.dma('pool', 'cv%d' % i, out=cvb[i][:], in_=cv[l, i * 128:(i + 1) * 128, :], writes=[('cvb', i)])
            with ExitStack() as ph:
                PT = [sb(ph, "PT%d" % i, [128, 2, 512], BF16) for i in range(3)]
                OSB = [sb(ph, "OSB%d" % i, [128, 2, 512], F32) for i in range(2)]
                LNL = [sb(ph, "LNL%d" % i, [128, 2, 512], F32) for i in range(2)]
                dd = sb(ph, "dd", [128, 512], F32); sq = sb(ph, "sq", [128, 512], F32)
                msb = sb(ph, "msb", [128, 512], F32); rs = sb(ph, "rs", [128, 512], F32)
                MB = sb(ph, "MB", [128, 1], F32)
                S.op('dve', lambda e: e.memset(MB[0:64, :], 0.0), writes=['MBa'])
                S.op('dve', lambda e: e.memset(MB[64:128, :], NEG), writes=['MBb'])
                negl = LAM[:, l, 4:5]; gsc = LAM[:, l, 5:6]
                blocks = []
                for n_it, (h, j) in enumerate([(h, j) for h in range(4) for j in range(4)]):
                    for i in range(4 * j + 4):
                        blocks.append((n_it, h, j, i))
                NBK = len(blocks)

                def off_of(j, i):
                    r = i - 4 * j
                    return 128 * r if r > 0 else 0

                def qk(g):
                    n_it, h, j, i = blocks[g]
                    slot = g % 2
                    off = off_of(j, i)
                    q0 = j * 512

                    def f(e):
                        for c in range(2):
                            ins = e.matmul(PS[:, 2 * slot + c, off:512], lhsT=KT[c * 64:(c + 1) * 64, h, i * 128:(i + 1) * 128],
                                           rhs=QT[c * 64:(c + 1) * 64, h, q0 + off:q0 + 512], start=True, stop=True)
                        return ins
                    S.op('pe', f, reads=[('QT', h, j)], writes=[('ps', 2 * slot), ('ps', 2 * slot + 1)])

                def ex(g):
                    n_it, h, j, i = blocks[g]
                    slot = g % 2
                    off = off_of(j, i)
                    ps_ = g % 3
                    diag = (i - 4 * j) >= 0
                    src = PS[:, 2 * slot:2 * slot + 2, :]
                    if diag:
                        S.op('act', lambda e: e.activation(out=PT[ps_][:, :, off:off + 64], in_=src[:, :, off:off + 64], func=AF.Exp,
                                                           scale=SCALE, bias=MB[:, 0:1]),
                             reads=[('ps', 2 * slot), ('ps', 2 * slot + 1), 'MBa', 'MBb'], writes=[('PT', ps_, 'm')])
                        S.op('act', lambda e: e.activation(out=PT[ps_][:, :, off + 64:512], in_=src[:, :, off + 64:512], func=AF.Exp,
                                                           scale=SCALE),
                             reads=[('ps', 2 * slot), ('ps', 2 * slot + 1)], writes=[('PT', ps_)])
                    else:
                        S.op('act', lambda e: e.activation(out=PT[ps_][:, :, :], in_=src, func=AF.Exp, scale=SCALE),
                             reads=[('ps', 2 * slot), ('ps', 2 * slot + 1)], writes=[('PT', ps_), ('PT', ps_, 'm')])

                def pvm(g):
                    n_it, h, j, i = blocks[g]
                    off = off_of(j, i)
                    ps_ = g % 3
                    nb = 4 * j + 4

                    def f(e):
                        for c in range(2):
                            e.matmul(PS[:, 4 + c, off:512], lhsT=VB[:, i, h * 128:(h + 1) * 128], rhs=PT[ps_][:, c, off:512],
                                     start=(i == 0), stop=(i == nb - 1))
                        for c in range(2):
                            ins = e.matmul(PS[:, 6 + c, off:512], lhsT=onesb[:], rhs=PT[ps_][:, c, off:512],
                                           start=(i == 0), stop=(i == nb - 1))
                        return ins
                    S.op('pe', f, reads=[('PT', ps_), ('PT', ps_, 'm'), 'onesb'], writes=[('ps', 4), ('ps', 5), ('ps', 6), ('ps', 7)])

                def mkB(st):
                    def run(sbk):
                        S.op('act', lambda e: e.activation(out=LNL[st][:], in_=LNL[st][:], func=AF.Exp, scale=-1.0),
                             reads=[('LNL', st)], writes=[('LNL', st)])
                        S.op('dve', lambda e: e.tensor_tensor(out=OSB[st][:], in0=OSB[st][:], in1=LNL[st][:], op=ALU.mult),
                             reads=[('OSB', st), ('LNL', st)], writes=[('OSB', st)])
                        S.op('dve', lambda e: e.scalar_tensor_tensor(out=dd[:], in0=OSB[st][:, 1, :], scalar=negl, in1=OSB[st][:, 0, :],
                                                                     op0=ALU.mult, op1=ALU.add),
                             reads=[('OSB', st)], writes=['dd'])
                        S.op('dve', lambda e: e.tensor_tensor(out=sq[:], in0=dd[:], in1=dd[:], op=ALU.mult), reads=['dd'], writes=['sq'])
                    return run

                def mkC1():
                    def run(sbk):
                        S.op('pe', lambda e: e.matmul(PS[:, sbk, :], lhsT=onesf[:, 0, :], rhs=sq[:], start=True, stop=True),
                             reads=['sq', 'onesf0'], writes=[('ps', sbk)])
                        S.op('dve', lambda e: e.tensor_scalar(out=msb[:], in0=PS[:, sbk, :], scalar1=EPS, scalar2=None, op0=ALU.add),
                             reads=[('ps', sbk)], writes=['msb'])
                    return run

                def mkC2(h, j):
                    q0 = j * 512

                    def run():
                        S.op('act', lambda e: e.activation(out=msb[:], in_=msb[:], func=AF.Ln), reads=['msb'], writes=['msb'])
                        S.op('act', lambda e: e.activation(out=rs[:], in_=msb[:], func=AF.Exp, scale=-0.5), reads=['msb'], writes=['rs'])
                        S.op('dve', lambda e: e.scalar_tensor_tensor(out=QT[:, h, q0:q0 + 512], in0=dd[:], scalar=gsc, in1=rs[:],
                                                                     op0=ALU.mult, op1=ALU.mult),
                             reads=['dd', 'rs'], writes=[('QT', h, j)])
                    return run

                pendB = None
                savedC = None
                pendC2 = None
                qk(0)
                qk(1)
                for g in range(NBK):
                    n_it, h, j, i = blocks[g]
                    nb = 4 * j + 4
                    st = n_it % 2
                    ex(g)
                    if pendC2 is not None:
                        pendC2()
                        pendC2 = None
                    pvm(g)
                    if i == nb - 1:
                        S.op('dve', lambda e: e.tensor_copy(out=OSB[st][:], in_=PS[:, 4:6, :]),
                             reads=[('ps', 4), ('ps', 5)], writes=[('OSB', st)])
                        S.op('act', lambda e: e.activation(out=LNL[st][:], in_=PS[:, 6:8, :], func=AF.Ln),
                             reads=[('ps', 6), ('ps', 7)], writes=[('LNL', st)])
                        if savedC is not None:
                            savedC[0](2 * (g % 2))
                            pendC2 = savedC[1]
                        savedC = (mkC1(), mkC2(h, j))
                        pendB = (g + 2, mkB(st))
                    if pendB is not None and pendB[0] <= g:
                        pendB[1](0)
                        pendB = None
                    if g + 2 < NBK:
                        qk(g + 2)
                if pendC2 is not None:
                    pendC2()
                if pendB is not None:
                    pendB[1](0)
                savedC[0](0); savedC[1]()
                S.barrier()

            phf = ExitStack()
            GU = [sb(phf, "GU%d" % i, [128, 2, 8, 256], BF16) for i in range(2)]

            def issue_gu_abs(fs, w, slot):
                S.dma('pool', 'gug%d' % slot, out=GU[slot][:, 0, :, 0:w * 128], in_=w_gu3[:, :, fs * 128:(fs + w) * 128],
                      writes=[('GUg', slot)])
                S.dma('pool', 'guu%d' % slot, out=GU[slot][:, 1, :, 0:w * 128], in_=w_gu3[:, :, 2816 + fs * 128:2816 + (fs + w) * 128],
                      writes=[('GUu', slot)])
            issue_gu_abs(0, 2, 0)
            def wo_tile(t, banks):
                P = TP(t); c = t * 128
                for hh in range(2):
                    bank = banks[hh]

                    def wo_mm(e):
                        for kc in range(4):
                            ins = e.matmul(PS[:P, bank, :], lhsT=QT[:, kc, c:c + P], rhs=WOA[:, kc, hh * 512:(hh + 1) * 512],
                                           start=(kc == 0), stop=(kc == 3))
                        return ins
                    S.op('pe', wo_mm, reads=['WOA', ('QT', 's')] if t == 16 else ['WOA'], writes=[('ps', bank)])
                    S.op('dve', lambda e: e.tensor_tensor(out=X[:P, t, hh * 512:(hh + 1) * 512], in0=X[:P, t, hh * 512:(hh + 1) * 512],
                                                          in1=PS[:P, bank, :], op=ALU.add),
                         reads=[('ps', bank), ('X', t)], writes=[('X', t)])

            with ExitStack() as ph:
                ckT = [sb(ph, "ckT%d" % i, [128, 4, 128], BF16) for i in range(2)]
                PTs = [sb(ph, "PTs%d" % i, [128, 8, 32], BF16) for i in range(2)]
                rr = sb(ph, "rr", [128, 8, 32], F32); a8 = sb(ph, "a8", [128, 2, 4, 32], F32)
                d4 = sb(ph, "d4", [128, 4, 32], F32); sq4 = sb(ph, "sq4", [128, 4, 32], F32)
                ms4 = sb(ph, "ms4", [128, 128], F32); ln4 = sb(ph, "ln4", [128, 128], F32); rs4 = sb(ph, "rs4", [128, 128], F32)
                negl = LAM[:, l, 4:5]; gsc = LAM[:, l, 5:6]
                OS = PS[:, 4, 0:256].rearrange("p (g n) -> p g n", g=8)
                LS = PS[:, 5, 0:256].rearrange("p (g n) -> p g n", g=8)
                ATSM = 9
                for i in range(17 if ATSM > 0 else 0):
                    slot = i % 2
                    cs = i % 4
                    KP = 128 if i < 16 else 32
                    if i < 16:
                        if i >= 4:
                            S.dma('pool', 'ck%d' % cs, out=ckb[cs][:], in_=ck[l, i * 128:(i + 1) * 128, :], writes=[('ckb', cs)])
                            S.dma('pool', 'cv%d' % cs, out=cvb[cs][:], in_=cv[l, i * 128:(i + 1) * 128, :], writes=[('cvb', cs)])
                        pv = psb(slot).rearrange("p (k n) -> p k n", k=8)

                        def tr(e):
                            for h in range(4):
                                ins = e.transpose(out=pv[:, h, :], in_=ckb[cs][:, h * 128:(h + 1) * 128], identity=idb[:])
                            return ins
                        S.op('pe', tr, reads=[('ckb', cs), 'idb'], writes=[('ps', slot)])
                        S.op('act', lambda e: e.activation(out=ckT[slot][:], in_=pv[:, 0:4, :], func=AF.Copy),
                             reads=[('ps', slot)], writes=[('ckT', slot)])
                    sb0 = 2 if slot == 0 else 6
                    SS2 = PS[:, sb0:sb0 + 2, 0:128]
                    if ATSM < 2:
                        continue

                    def qk(e):
                        for h in range(4):
                            for c in range(2):
                                if i < 16:
                                    kT = ckT[slot][c * 64:(c + 1) * 64, h, :]
                                else:
                                    kT = KT[c * 64:(c + 1) * 64, h, 2048:2080]
                                ins = e.matmul(PS[:KP, sb0 + c, h * 32:(h + 1) * 32], lhsT=kT, rhs=QT[c * 64:(c + 1) * 64, h, 2048:2080],
                                               start=True, stop=True)
                        return ins
                    S.op('pe', qk, reads=[('ckT', slot), ('QT', 's')], writes=[('ps', sb0), ('ps', sb0 + 1)])
                    S.op('act', lambda e: e.activation(out=PTs[slot][:KP].rearrange("p (c x) n -> p c (x n)", c=2), in_=SS2[:KP], func=AF.Exp, scale=SCALE),
                         reads=[('ps', sb0), ('ps', sb0 + 1)], writes=[('PTs', slot)])
                    if i < 16:
                        wo_tile(i, (6, 7) if slot == 0 else (2, 3))
                    if ATSM < 3:
                        continue

                    def pvm(e):
                        first = True
                        for c in range(2):
                            for h in range(4):
                                if i < 16:
                                    vv = cvb[cs][:, h * 128:(h + 1) * 128]
                                else:
                                    vv = VB[:32, 16, h * 128:(h + 1) * 128]
                                e.matmul(OS[:, c * 4 + h, :], lhsT=vv, rhs=PTs[slot][:KP, c * 4 + h, :],
                                         start=(i == 0 and first), stop=(i == 16), skip_group_check=True)
                                first = False
                        ins = e.matmul(PS[:, 5, 0:256], lhsT=onesb[:KP, :], rhs=PTs[slot][:KP].rearrange("p g n -> p (g n)"),
                                       start=(i == 0), stop=(i == 16))
                        return ins
                    S.op('pe', pvm, reads=[('PTs', slot), ('cvb', cs), 'onesb'], writes=[('ps', 4), ('ps', 5)])
                if ATSM < 4:
                    S.stopped = True
                S.op('dve', lambda e: e.reciprocal(out=rr[:], in_=LS), reads=[('ps', 5)], writes=['rr'])
                S.op('dve', lambda e: e.tensor_tensor(out=a8[:].rearrange("p c h n -> p (c h) n"), in0=OS, in1=rr[:], op=ALU.mult),
                     reads=[('ps', 4), 'rr'], writes=['a8'])
                S.op('dve', lambda e: e.scalar_tensor_tensor(out=d4[:], in0=a8[:, 1, :, :], scalar=negl, in1=a8[:, 0, :, :],
                                                             op0=ALU.mult, op1=ALU.add), reads=['a8'], writes=['d4'])
                S.op('act', lambda e: e.activation(out=sq4[:], in_=d4[:], func=AF.Square), reads=['d4'], writes=['sq4'])
                S.op('pe', lambda e: e.matmul(PS[:, 0, 0:128], lhsT=onesf[:, 0, :], rhs=sq4[:].rearrange("p h n -> p (h n)"), start=True, stop=True),
                     reads=['sq4', 'onesf0'], writes=[('ps', 0)])
                S.op('dve', lambda e: e.tensor_scalar(out=ms4[:], in0=PS[:, 0, 0:128], scalar1=EPS, scalar2=None, op0=ALU.add),
                     reads=[('ps', 0)], writes=['ms4'])
                S.op('act', lambda e: e.activation(out=ln4[:], in_=ms4[:], func=AF.Ln), reads=['ms4'], writes=['ln4'])
                S.op('act', lambda e: e.activation(out=rs4[:], in_=ln4[:], func=AF.Exp, scale=-0.5), reads=['ln4'], writes=['rs4'])
                S.op('dve', lambda e: e.scalar_tensor_tensor(out=QT[:, :, 2048:2080], in0=d4[:], scalar=gsc,
                                                             in1=rs4[:].rearrange("p (h n) -> p h n", h=4), op0=ALU.mult, op1=ALU.mult),
                     reads=['d4', 'rs4'], writes=[('QT', 's')])
                wo_tile(16, (2, 3))
                S.barrier()

            phase_norm(gffn[l:l + 1, :])
            with ExitStack() as ph:
                DN = [sb(ph, "DN%d" % i, [128, 8, 512], BF16) for i in range(2)]
                sgl = [sb(ph, "sgl%d" % i, [128, 512], F32) for i in range(2)]
                gcount = 0
                dcount = 0
                ev = 0
                for (f0, f1) in FFN_PASSES:
                    nf = f1 - f0
                    slabs = []
                    f = f0
                    while f < f1:
                        w = min(2, f1 - f)
                        slabs.append((f, w))
                        f += w

                    def issue_gu(si):
                        fs, w = slabs[si]
                        slot = (gcount + si) % 2
                        issue_gu_abs(fs, w, slot)
                    if f0 > 0:
                        issue_gu(0)
                    for hh in range(2):
                        S.dma('pool', 'dn%d' % hh, out=DN[hh][:, 0:nf, :], in_=w_dn3[:, f0:f1, hh * 512:(hh + 1) * 512], writes=[('DN', hh)])
                    for si, (fs, w) in enumerate(slabs):
                        slot = (gcount + si) % 2
                        if si + 1 < len(slabs):
                            issue_gu(si + 1)
                        for fo in range(w):
                            fi = fs + fo - f0
                            for (c0, n, tiles) in GROUPS:
                                bg = (ev % 3) * 2
                                ev += 1

                                def gu_mm(e):
                                    for kc in range(8):
                                        e.matmul(PS[:, bg, 0:n], lhsT=GU[slot][:, 0, kc, fo * 128:(fo + 1) * 128], rhs=H[:, kc, c0:c0 + n],
                                                 start=(kc == 0), stop=(kc == 7))
                                    for kc in range(8):
                                        ins = e.matmul(PS[:, bg + 1, 0:n], lhsT=GU[slot][:, 1, kc, fo * 128:(fo + 1) * 128], rhs=H[:, kc, c0:c0 + n],
                                                       start=(kc == 0), stop=(kc == 7))
                                    return ins
                                S.op('pe', gu_mm, reads=[('GUg', slot), ('GUu', slot)], writes=[('ps', bg), ('ps', bg + 1)])
                                ss = ev % 2
                                S.op('act', lambda e: e.activation(out=sgl[ss][:, 0:n], in_=PS[:, bg, 0:n], func=AF.Silu),
                                     reads=[('ps', bg)], writes=[('sgl', ss)])
                                S.op('dve', lambda e: e.tensor_tensor(out=ACTH[:, fi, c0:c0 + n], in0=sgl[ss][:, 0:n], in1=PS[:, bg + 1, 0:n], op=ALU.mult),
                                     reads=[('sgl', ss), ('ps', bg + 1)], writes=[('ACTH', fi)])
                    gcount += len(slabs)
                    lastpass = (f1 == 22)
                    last = lastpass and l == DEPTH - 1
                    mid = lastpass and l < DEPTH - 1
                    if lastpass:
                        GUf = [GU[i][:].rearrange("p a b c -> p (a b c)").bitcast(F32) for i in range(2)]
                        GU1b = GU[1][:].rearrange("p a b c -> p (a b c)")
                        Gfin = GUf[0][:, 0:1024]
                        stf = GUf[0][:, 1024:1024 + 4 * NT].rearrange("p (t k) -> p t k", k=4)
                        yof = [GUf[1][:, i * 1024:(i + 1) * 1024] for i in range(2)]
                        hbf = [GU1b[:, i * 1024:(i + 1) * 1024] for i in range(3)]
                        junkf = GU1b[:, 3072:4096]
                        grow = gfin[0:1, :] if last else gmix[l + 1:l + 2, :]
                    if mid:
                        w_in3n = w_in[l + 1].rearrange("(k p) c -> p k c", p=128)
                        WPn = FA[:, 16640:22784].rearrange("p (k c) -> p k c", k=8)
                        S.dma('pool', 'wp0', out=WPn[:, :, 0:256], in_=w_in3n[:, :, 0:256], writes=['WP0'])
                        S.dma('pool', 'wp1', out=WPn[:, :, 256:768], in_=w_in3n[:, :, 1792:2304], writes=['WP1'])
                    order = [(t, hh) for t in range(NT) for hh in range(2)] if lastpass else [(t, hh) for hh in range(2) for t in range(NT)]
                    for (t, hh) in order:
                        dslot = hh
                        P = TP(t); c = t * 128
                        bank = 4 + ((2 * t + hh) % 4 if lastpass else t % 4)

                        def dn_mm(e):
                            for fi in range(nf):
                                ins = e.matmul(PS[:P, bank, :], lhsT=ACTH[:, fi, c:c + P], rhs=DN[dslot][:, fi, :],
                                               start=(fi == 0), stop=(fi == nf - 1))
                            return ins
                        S.op('pe', dn_mm, reads=[('ACTH', fi) for fi in range(nf)] + [('DN', dslot)], writes=[('ps', bank)])
                        S.op('dve', lambda e: e.tensor_tensor(out=X[:P, t, hh * 512:(hh + 1) * 512], in0=X[:P, t, hh * 512:(hh + 1) * 512],
                                                              in1=PS[:P, bank, :], op=ALU.add),
                             reads=[('ps', bank), ('X', t)], writes=[('X', t)])
                        if lastpass and hh == 1:
                            if t == 0:
                                S.dma('sp', 'Gf', out=Gfin, in_=grow.partition_broadcast(128), reads=[('X', 0)], writes=['Gfin'])

                            def fin1(t):
                                P = TP(t)
                                jout = yof[t % 2][:P] if last else junkf[:P]
                                jkey = ('yo', t % 2) if last else 'junkf'
                                S.op('act', lambda e: e.activation(out=jout, in_=X[:P, t, :], func=AF.Square, accum_out=stf[:P, t, 0:1]),
                                     reads=[('X', t)], writes=[jkey, ('st', t, 0)])

                            def fin2(t):
                                rstd_chain(stf, TP(t), t, None)

                            def fin3(t):
                                P = TP(t)
                                if last:
                                    S.op('dve', lambda e: e.scalar_tensor_tensor(out=yof[t % 2][:P], in0=X[:P, t, :], scalar=stf[:P, t, 3:4],
                                                                                 in1=Gfin[:P], op0=ALU.mult, op1=ALU.mult),
                                         reads=[('X', t), ('st', t, 3), 'Gfin'], writes=[('yo', t % 2)])
                                    S.dma('sp', 'o_y%d' % (t % 2), out=rows_y(t), in_=yof[t % 2][:P], reads=[('yo', t % 2)], is_out=True)
                                else:
                                    S.op('dve', lambda e: e.scalar_tensor_tensor(out=hbf[t % 3][:P], in0=X[:P, t, :], scalar=stf[:P, t, 3:4],
                                                                                 in1=Gfin[:P], op0=ALU.mult, op1=ALU.mult),
                                         reads=[('X', t), ('st', t, 3), 'Gfin'], writes=[('hbf', t % 3)])

                            def fin4(t):
                                if last:
                                    return
                                P = TP(t); c4 = t * 128
                                tbk = t % 4
                                pvf = psb(tbk).rearrange("p (k n) -> p k n", k=8)

                                def tr(e):
                                    for kc in range(8):
                                        ins = e.transpose(out=pvf[:, kc, :P], in_=hbf[t % 3][:P, kc * 128:(kc + 1) * 128], identity=idb[:P, :P])
                                    return ins
                                S.op('pe', tr, reads=[('hbf', t % 3), 'idb'], writes=[('ps', tbk)])
                                S.op('act', lambda e: e.activation(out=H[:, :, c4:c4 + P], in_=pvf[:, :, :P], func=AF.Copy),
                                     reads=[('ps', tbk)], writes=[('H', t)])
                            if t >= 3:
                                fin4(t - 3)
                            if t >= 2:
                                fin3(t - 2)
                            if t >= 1:
                                fin2(t - 1)
                            fin1(t)
                            if t == NT - 1:
                                fin4(t - 2)
                                fin3(t - 1)
                                fin2(t)
                                fin4(t - 1)
                                fin3(t)
                                fin4(t)
                S.barrier()
            phf.close()
            phw.close()

        S.stopped = False

        if True:
            S.finish()
    return nc


def _host_consts():
    half = 32
    inv = (np.float32(10000.0) ** (-np.arange(half, dtype=np.float32) / np.float32(half))).astype(np.float32)
    pos = np.zeros((128, 17), np.float32)
    for t in range(16):
        pos[:, t] = t * 128 + np.arange(128)
    pos[:, 16] = 2048 + np.arange(128)
    ang = (pos[:, :, None] * inv[None, None, :]).astype(np.float32)
    cos_t = np.cos(ang).astype(np.float32)
    sin_t = np.sin(ang).astype(np.float32)
    wins = np.array([[2, 4], [8, 16]])
    invc = np.zeros((128, 2, 16), np.float32)
    invw = np.zeros((128, 2), np.float32)
    for j in range(2):
        for p in range(128):
            w = wins[j][p // 64]
            invw[p, j] = 1.0 / w
            invc[p, j, :] = 1.0 / np.minimum(np.arange(16) + 1, w)
    return cos_t, sin_t, invc, invw


_NC_CACHE = {}


def kernel(x_prompt, x_sample, cache_k, cache_v, state_pool, state_conv,
           norm_mix_g, w_in, pool_w, pool_scale, lambda_qk, diff_norm_g,
           conv_dw, conv_dw_b, conv_ln_g, conv_ln_b, conv_pw, w_out,
           norm_ffn_g, w_gate_up, w_down, final_norm_g):
    f = lambda a: np.ascontiguousarray(np.asarray(a, dtype=np.float32))
    x_prompt, x_sample, cache_k, cache_v = f(x_prompt), f(x_sample), f(cache_k), f(cache_v)
    state_pool, state_conv = f(state_pool), f(state_conv)
    B = 8
    cos_t, sin_t, invc, invw = _host_consts()

    def pc(v):
        return np.asarray(v, np.float32).reshape(2, 128).T
    pvec = np.zeros((2, 128, NPV), np.float32)
    plw = np.zeros((2, 128, 2, 128), np.float32)
    pool_w = f(pool_w); conv_dw = f(conv_dw)
    for l in range(2):
        pvec[l, :, 0:2] = pc(pool_scale[l])
        pvec[l, :, 2:4] = pc(conv_dw_b[l])
        pvec[l, :, 4:6] = pc(conv_ln_g[l])
        pvec[l, :, 6:8] = pc(conv_ln_b[l])
        pvec[l, :, 8] = np.asarray(diff_norm_g[l], np.float32)
        pvec[l, :, 9:11] = invw
        dwl = conv_dw[l].reshape(31, 2, 128)
        pvec[l, :, 11:73] = dwl.transpose(2, 1, 0).reshape(128, 62)
        for j in range(2):
            for hf in range(2):
                plw[l, hf * 64:(hf + 1) * 64, j, hf * 64:(hf + 1) * 64] = pool_w[l, 2 * j + hf]
    common = dict(
        w_in=f(w_in), w_out=f(w_out), w_gu=f(w_gate_up), w_dn=f(w_down), conv_pw=f(conv_pw), plw=plw,
        gmix=f(norm_mix_g), gffn=f(norm_ffn_g), gfin=f(final_norm_g).reshape(1, 1024),
        pvec=pvec, lq=f(lambda_qk).reshape(2, 256),
        ident=np.eye(128, dtype=np.float32), cos_t=cos_t, sin_t=sin_t, invc=invc,
    )
    in_maps = []
    for b in range(B):
        m = dict(common)
        m["xp"] = x_prompt[b]
        m["xs"] = x_sample[b]
        m["ck"] = np.ascontiguousarray(cache_k[:, b].reshape(2, 2048, 512))
        m["cv"] = np.ascontiguousarray(cache_v[:, b].reshape(2, 2048, 512))
        m["stp"] = np.ascontiguousarray(state_pool[:, b].reshape(2, 15, 2, 128).transpose(0, 3, 2, 1))
        m["stc"] = np.ascontiguousarray(state_conv[:, b].reshape(2, 30, 2, 128).transpose(0, 3, 2, 1))
        in_maps.append(m)
    if "nc" not in _NC_CACHE:
        _NC_CACHE["nc"] = build_program()
    nc = _NC_CACHE["nc"]
    res = run_bass_kernel_spmd(nc, in_maps, core_ids=list(range(B)))
    R = res.results

    def st(name):
        return np.stack([np.asarray(R[b][name], np.float32) for b in range(B)], axis=0)
    y_prompt = st("yp")
    y_sample = st("ys")
    nk_p = st("nkp").transpose(1, 0, 2, 3).reshape(2, B, 2048, 4, 2, 64)
    nv_p = st("nvp").transpose(1, 0, 2, 3).reshape(2, B, 2048, 4, 128)
    nk_s = st("nks").transpose(1, 0, 2, 3).reshape(2, B, 32, 4, 2, 64)
    nv_s = st("nvs").transpose(1, 0, 2, 3).reshape(2, B, 32, 4, 128)

    def unT(name, T):
        a = st(name)
        return np.ascontiguousarray(a.transpose(1, 0, 4, 3, 2).reshape(2, B, T, 256))
    np_p = unT("npp", 15); nc_p = unT("ncp", 30); np_s = unT("nps", 15); nc_s = unT("ncs", 30)
    return (y_prompt, y_sample, np.ascontiguousarray(nk_p), np.ascontiguousarray(nv_p), np_p, nc_p,
            np.ascontiguousarray(nk_s), np.ascontiguousarray(nv_s), np_s, nc_s)
```

```python
import math
from contextlib import ExitStack

import numpy as np
import concourse.bass as bass
import concourse.mybir as mybir
from concourse.bass_utils import run_bass_kernel_spmd

F32 = mybir.dt.float32
BF16 = mybir.dt.bfloat16
ALU = mybir.AluOpType
AF = mybir.ActivationFunctionType

EPS = 1e-6
SCALE = 0.125
NT = 17
NTOK = 2080
DEPTH = 2
NPV = 73
FFN_PASSES = [(0, 8), (8, 15), (15, 22)]
GROUPS = [(0, 512, [0, 1, 2, 3]), (512, 512, [4, 5, 6, 7]), (1024, 512, [8, 9, 10, 11]),
          (1536, 512, [12, 13, 14, 15]), (2048, 32, [16])]
NEG = -30000.0


def TP(t):
    return 128 if t < 16 else 32


import os
STOP_AFTER = 0


class _Stop(Exception):
    pass


class Sched:
    def __init__(self, nc, es):
        self.nc = nc
        self.es = es
        self.eng = {'pe': nc.tensor, 'act': nc.scalar, 'dve': nc.vector, 'pool': nc.gpsimd, 'sp': nc.sync}
        self.csem = {e: es.enter_context(nc.semaphore('c_' + e)) for e in ('pe', 'act', 'dve', 'pool')}
        self.ccount = {e: 0 for e in self.csem}
        self.dsem = {}
        self.lastw = {}
        self.readers = {}
        self.waited = {e: {} for e in self.eng}
        self.out_events = []
        self.stopped = False
        self.nbar = 0

    def _wait(self, eng, ev):
        sem, val, src, kind, sid = ev
        w = self.waited[eng]
        if w.get(sid, 0) >= val:
            return
        w[sid] = val
        self.eng[eng].wait_ge(sem, val)

    def _deps(self, eng, reads, writes):
        deps = []
        for k in reads:
            ev = self.lastw.get(k)
            if ev is not None:
                deps.append((ev, True))
        for k in writes:
            ev = self.lastw.get(k)
            if ev is not None:
                deps.append((ev, False))
            for ev in self.readers.get(k, {}).values():
                deps.append((ev, False))
        for ev, raw in deps:
            if ev[3] == 'c' and ev[2] == eng:
                if eng == 'pe' or not raw:
                    continue
            self._wait(eng, ev)

    def _commit(self, ev, reads, writes):
        for k in writes:
            self.lastw[k] = ev
            self.readers[k] = {}
        for k in reads:
            self.readers.setdefault(k, {})[ev[4]] = ev

    def op(self, eng, fn, reads=(), writes=()):
        if self.stopped:
            return
        self._deps(eng, reads, writes)
        ins = fn(self.eng[eng])
        self.ccount[eng] += 1
        ins.then_inc(self.csem[eng], 1)
        ev = (self.csem[eng], self.ccount[eng], eng, 'c', 'c_' + eng)
        self._commit(ev, reads, writes)

    def dma(self, q, key, out, in_, reads=(), writes=(), is_out=False):
        if self.stopped:
            return
        self._deps(q, reads, writes)
        if key not in self.dsem:
            self.dsem[key] = [self.es.enter_context(self.nc.semaphore('d%d' % len(self.dsem))), 0]
        d = self.dsem[key]
        d[1] += 16
        self.eng[q].dma_start(out=out, in_=in_).then_inc(d[0], 16)
        ev = (d[0], d[1], q, 'd', 'd_' + str(key))
        self._commit(ev, reads, writes)
        if is_out:
            self.out_events.append(ev)

    def barrier(self):
        if self.stopped:
            return
        self.nbar += 1
        self._barrier()
        if self.nbar == STOP_AFTER:
            self.stopped = True

    def _barrier(self):
        evs = [(self.csem[e], self.ccount[e], e, 'c', 'c_' + e) for e in self.csem if self.ccount[e] > 0]
        evs += [(d[0], d[1], 'x', 'd', 'd_' + str(k)) for k, d in self.dsem.items() if d[1] > 0]
        for eng in self.eng:
            for ev in evs:
                if ev[3] == 'c' and ev[2] == eng:
                    continue
                self._wait(eng, ev)
        self.lastw = {}
        self.readers = {}

    def finish(self):
        for ev in self.out_events:
            self._wait('sp', ev)


def build_program():
    nc = bass.Bass("TRN2", target_bir_lowering=False)

    def din(name, shape):
        return nc.dram_tensor(name, list(shape), F32, kind="ExternalInput").ap()

    def dout(name, shape):
        return nc.dram_tensor(name, list(shape), F32, kind="ExternalOutput").ap()

    xp = din("xp", [2048, 1024]); xs = din("xs", [32, 1024])
    ck = din("ck", [2, 2048, 512]); cv = din("cv", [2, 2048, 512])
    stp = din("stp", [2, 128, 2, 15]); stc = din("stc", [2, 128, 2, 30])
    w_in = din("w_in", [2, 1024, 2304]); w_out = din("w_out", [2, 1024, 1024])
    w_gu = din("w_gu", [2, 1024, 5632]); w_dn = din("w_dn", [2, 2816, 1024])
    conv_pw = din("conv_pw", [2, 256, 256]); plw = din("plw", [2, 128, 2, 128])
    gmix = din("gmix", [2, 1024]); gffn = din("gffn", [2, 1024]); gfin = din("gfin", [1, 1024])
    pvec = din("pvec", [2, 128, NPV]); lq = din("lq", [2, 256])
    ident = din("ident", [128, 128]); cos_t = din("cos_t", [128, 17, 32]); sin_t = din("sin_t", [128, 17, 32])
    invc = din("invc", [128, 2, 16])

    yp = dout("yp", [2048, 1024]); ys = dout("ys", [32, 1024])
    nkp = dout("nkp", [2, 2048, 512]); nvp = dout("nvp", [2, 2048, 512])
    npp = dout("npp", [2, 128, 2, 15]); ncp = dout("ncp", [2, 128, 2, 30])
    nks = dout("nks", [2, 32, 512]); nvs = dout("nvs", [2, 32, 512])
    nps = dout("nps", [2, 128, 2, 15]); ncs = dout("ncs", [2, 128, 2, 30])

    def rows_y(t):
        return yp[t * 128:(t + 1) * 128, :] if t < 16 else ys[0:32, :]

    def rows_k(l, t):
        return nkp[l, t * 128:(t + 1) * 128, :] if t < 16 else nks[l, 0:32, :]

    def rows_v(l, t):
        return nvp[l, t * 128:(t + 1) * 128, :] if t < 16 else nvs[l, 0:32, :]

    with ExitStack() as es:
        S = Sched(nc, es)

        uid = [0]

        def sb(stack, name, shape, dt):
            uid[0] += 1
            return stack.enter_context(nc.sbuf_tensor("%s_%d" % (name, uid[0]), list(shape), dt))

        PS = es.enter_context(nc.psum_tensor("PS", [128, 8, 512], F32))

        def psb(b):
            return PS[:, b, :].bitcast(BF16)

        X = sb(es, "X", [128, NT, 1024], F32)
        H = sb(es, "H", [128, 8, NTOK], BF16)
        FA = sb(es, "FA", [128, 25344], BF16)
        QT = FA[:, 0:8320].rearrange("p (h n) -> p h n", h=4)
        KT = FA[:, 8320:16640].rearrange("p (h n) -> p h n", h=4)
        VB = FA[:, 16640:25344].rearrange("p (t n) -> p t n", t=NT)
        ACTH = FA[:, 0:16640].rearrange("p (k n) -> p k n", k=8)
        idb = sb(es, "idb", [128, 128], BF16)
        onesb = sb(es, "onesb", [128, 128], BF16)
        onesf = sb(es, "onesf", [128, 3, 128], F32)
        PV = sb(es, "PV", [128, 2, NPV], F32)
        LAM = sb(es, "LAM", [128, 2, 8], F32)
        LQ = sb(es, "LQ", [128, 2, 256], F32)
        ljunk = sb(es, "ljunk", [128, 64], F32)

        S.dma('pool', 'c_id', out=idb[:], in_=ident[:, :], writes=['idb'])
        S.dma('sp', 'c_pv', out=PV[:], in_=pvec.rearrange("l p n -> p l n"), writes=['PV'])
        S.dma('sp', 'c_lq', out=LQ[:, 0, :], in_=lq[0:1, :].partition_broadcast(128), writes=[('LQ', 0)])
        S.dma('sp', 'c_lq1', out=LQ[:, 1, :], in_=lq[1:2, :].partition_broadcast(128), writes=[('LQ', 1)])
        S.dma('sp', 'xs', out=X[:32, 16, :], in_=xs[:, :], writes=[('X', 16)])
        xpv = xp.rearrange("(t p) d -> p t d", p=128)
        for q4 in range(4):
            S.dma('sp', 'x%d' % q4, out=X[:, q4 * 4:(q4 + 1) * 4, :], in_=xpv[:, q4 * 4:(q4 + 1) * 4, :],
                  writes=[('X', t) for t in range(q4 * 4, q4 * 4 + 4)])
        S.op('dve', lambda e: e.memset(onesb[:], 1.0), writes=['onesb'])
        S.op('dve', lambda e: e.memset(onesf[:, 0, :], 1.0 / 128), writes=['onesf0'])
        S.op('dve', lambda e: e.memset(onesf[:, 1, :], 1.0 / 256), writes=['onesf1'])
        S.op('dve', lambda e: e.memset(onesf[:, 2, :], 1.0), writes=['onesf2'])
        for l in range(DEPTH):
            lam_init = 0.8 - 0.6 * math.exp(-0.3 * l)
            S.op('dve', lambda e: e.scalar_tensor_tensor(out=ljunk[:], in0=LQ[:, l, 0:64], scalar=1.0, in1=LQ[:, l, 64:128],
                                                         op0=ALU.mult, op1=ALU.mult, accum_out=LAM[:, l, 0:1]),
                 reads=[('LQ', l)], writes=['ljunk', ('LAM', l, 0)])
            S.op('dve', lambda e: e.scalar_tensor_tensor(out=ljunk[:], in0=LQ[:, l, 128:192], scalar=1.0, in1=LQ[:, l, 192:256],
                                                         op0=ALU.mult, op1=ALU.mult, accum_out=LAM[:, l, 1:2]),
                 reads=[('LQ', l)], writes=['ljunk', ('LAM', l, 1)])
            S.op('act', lambda e: e.activation(out=LAM[:, l, 2:4], in_=LAM[:, l, 0:2], func=AF.Exp),
                 reads=[('LAM', l, 0), ('LAM', l, 1)], writes=[('LAM', l, 2)])
            S.op('dve', lambda e: e.tensor_tensor(out=LAM[:, l, 6:7], in0=LAM[:, l, 3:4], in1=LAM[:, l, 2:3], op=ALU.subtract),
                 reads=[('LAM', l, 2)], writes=[('LAM', l, 6)])
            S.op('dve', lambda e: e.tensor_scalar(out=LAM[:, l, 4:5], in0=LAM[:, l, 6:7], scalar1=-lam_init, scalar2=None, op0=ALU.add),
                 reads=[('LAM', l, 6)], writes=[('LAM', l, 4)])
            S.op('dve', lambda e: e.tensor_scalar(out=LAM[:, l, 5:6], in0=PV[:, l, 8:9], scalar1=(1.0 - lam_init), scalar2=None, op0=ALU.mult),
                 reads=['PV'], writes=[('LAM', l, 5)])
        S._barrier()

        def rstd_chain(stt, P, t, src_key):
            S.op('dve', lambda e: e.tensor_scalar(out=stt[:P, t, 1:2], in0=stt[:P, t, 0:1], scalar1=1.0 / 1024, scalar2=EPS,
                                                  op0=ALU.mult, op1=ALU.add),
                 reads=[('st', t, 0)], writes=[('st', t, 1)])
            S.op('act', lambda e: e.activation(out=stt[:P, t, 2:3], in_=stt[:P, t, 1:2], func=AF.Ln),
                 reads=[('st', t, 1)], writes=[('st', t, 2)])
            S.op('act', lambda e: e.activation(out=stt[:P, t, 3:4], in_=stt[:P, t, 2:3], func=AF.Exp, scale=-0.5),
                 reads=[('st', t, 2)], writes=[('st', t, 3)])

        def phase_norm(g_row, tgroups=None):
            tgroups = tgroups or [list(range(NT))]
            with ExitStack() as ph:
                G = sb(ph, "G", [128, 1024], F32)
                hb = [sb(ph, "hb%d" % i, [128, 1024], BF16) for i in range(3)]
                junk = sb(ph, "junk", [128, 1024], BF16)
                junk2 = sb(ph, "junk2", [128, 1024], BF16)
                stt = sb(ph, "nst", [128, 4, NT], F32)
                S.dma('sp', 'G', out=G[:], in_=g_row.partition_broadcast(128), writes=['G'])
                S.op('dve', lambda e: e.memset(stt[:, 0, :], 1.0), writes=['st0i'])
                for tg in tgroups:
                    t0g, t1g = tg[0], tg[-1] + 1
                    for t in tg:
                        P = TP(t)
                        if t % 2 == 0:
                            S.op('act', lambda e: e.activation(out=junk[:P], in_=X[:P, t, :], func=AF.Square, accum_out=stt[:P, 0, t:t + 1]),
                                 reads=[('X', t), 'st0i'], writes=['junk', ('st0', t)])
                        else:
                            S.op('dve', lambda e: e.scalar_tensor_tensor(out=junk2[:P], in0=X[:P, t, :], scalar=1.0, in1=X[:P, t, :],
                                                                         op0=ALU.mult, op1=ALU.mult, accum_out=stt[:P, 0, t:t + 1]),
                                 reads=[('X', t), 'st0i'], writes=['junk2', ('st0', t)])
                    S.op('dve', lambda e: e.tensor_scalar(out=stt[:, 1, t0g:t1g], in0=stt[:, 0, t0g:t1g], scalar1=1.0 / 1024, scalar2=EPS,
                                                          op0=ALU.mult, op1=ALU.add),
                         reads=[('st0', t) for t in tg], writes=[('st1', t0g)])
                    S.op('act', lambda e: e.activation(out=stt[:, 2, t0g:t1g], in_=stt[:, 1, t0g:t1g], func=AF.Ln),
                         reads=[('st1', t0g)], writes=[('st2', t0g)])
                    S.op('act', lambda e: e.activation(out=stt[:, 3, t0g:t1g], in_=stt[:, 2, t0g:t1g], func=AF.Exp, scale=-0.5),
                         reads=[('st2', t0g)], writes=[('st3', t0g)])
                    for t in tg:
                        P = TP(t); c = t * 128
                        hs = t % 3
                        S.op('dve', lambda e: e.scalar_tensor_tensor(out=hb[hs][:P], in0=X[:P, t, :], scalar=stt[:P, 3, t:t + 1], in1=G[:P],
                                                                     op0=ALU.mult, op1=ALU.mult),
                             reads=[('X', t), ('st3', t0g), 'G'], writes=[('hb', hs)])
                        bank = t % 4
                        pv = psb(bank).rearrange("p (k n) -> p k n", k=8)

                        def tr(e):
                            for kc in range(8):
                                ins = e.transpose(out=pv[:, kc, :P], in_=hb[hs][:P, kc * 128:(kc + 1) * 128], identity=idb[:P, :P])
                            return ins
                        S.op('pe', tr, reads=[('hb', hs), 'idb'], writes=[('ps', bank)])
                        S.op('act', lambda e: e.activation(out=H[:, :, c:c + P], in_=pv[:, :, :P], func=AF.Copy),
                             reads=[('ps', bank)], writes=[('H', t)])
                S.barrier()

        def load_slab(q, key, dst, src3, kcs, c0, w, wkey):
            S.dma(q, key, out=dst, in_=src3[:, kcs[0]:kcs[-1] + 1, c0:c0 + w], writes=[wkey])

        if True:
          for l in range(DEPTH):
            w_in3 = w_in[l].rearrange("(k p) c -> p k c", p=128)
            w_out3 = w_out[l].rearrange("(k p) c -> p k c", p=128)
            w_gu3 = w_gu[l].rearrange("(k p) c -> p k c", p=128)
            w_dn3 = w_dn[l].rearrange("(k p) c -> p k c", p=128)
            pw3 = conv_pw[l].rearrange("(k p) c -> p k c", p=128)

            WP = FA[:, 16640:22784].rearrange("p (k c) -> p k c", k=8)
            WOP = FA[:, 0:4096].rearrange("p (k c) -> p k c", k=4)
            if l == 0:
                S.dma('pool', 'wp0', out=WP[:, :, 0:256], in_=w_in3[:, :, 0:256], reads=[('X', 11)], writes=['WP0'])
                S.dma('pool', 'wp1', out=WP[:, :, 256:768], in_=w_in3[:, :, 1792:2304], writes=['WP1'])
            S.dma('pool', 'wop0', out=WOP[:, 0:2, :], in_=w_out3[:, 0:2, :], writes=['WOP0'])
            S.dma('pool', 'wop1', out=WOP[:, 2:4, :], in_=w_out3[:, 6:8, :], writes=['WOP1'])
            if l == 0:
                phase_norm(gmix[l:l + 1, :], [[0, 1, 2, 3], [4, 5, 6, 7], [8, 9, 10, 11], [12, 13, 14, 15, 16]])

            phq = ExitStack()
            Wq0 = sb(phq, "Wq0", [128, 8, 512], BF16)
            S.dma('pool', 'wq0', out=Wq0[:], in_=w_in3[:, :, 256:768], writes=[('W', 0)])
            with ExitStack() as ph:
                DG = FA[:, 4096:12032].rearrange("p (k c) -> p k c", k=62)
                CU = [FA[:, 12032 + i * 1084:12032 + (i + 1) * 1084].rearrange("p (j c) -> p j c", j=2) for i in range(2)]
                dsb = FA[:, 22784:23808].rearrange("p (j c) -> p j c", j=2)
                MPC2 = [FA[:, 14200:16248].rearrange("p (j c) -> p j c", j=4), sb(ph, "MPC1", [128, 4, 512], BF16)]
                sl2 = [FA[:, 23808:24832].rearrange("p (j c) -> p j c", j=2), sb(ph, "sl1", [128, 2, 512], BF16)]
                PW = sb(ph, "PW", [128, 2, 256], BF16)
                PLW = sb(ph, "PLW", [128, 2, 128], BF16)
                INVC = sb(ph, "INVC", [128, 2, 16], F32)
                PU = [sb(ph, "PU%d" % i, [128, 2, 527], F32) for i in range(2)]
                U32 = sb(ph, "U32", [128, 2, 512], F32)
                WA = sb(ph, "WA", [128, 526], F32); WB = sb(ph, "WB", [128, 524], F32)
                WC = WA; WD = WB
                t16 = sb(ph, "t16", [128, 16], F32)
                sg2 = [sb(ph, "sg%d" % i, [128, 512], F32) for i in range(2)]
                ysb = sb(ph, "ysb", [128, 2, 512], F32)
                ysq = sb(ph, "ysq", [128, 2, 512], F32)
                m2 = sb(ph, "m2", [128, 512], F32); var = sb(ph, "var", [128, 512], F32)

                S.dma('pool', 'pw', out=PW[:], in_=pw3, writes=['PW'])
                S.dma('pool', 'plw', out=PLW[:], in_=plw[l], writes=['PLW'])
                S.dma('sp', 'invc', out=INVC[:], in_=invc[:, :, :], writes=['INVC'])
                S.op('dve', lambda e: e.tensor_tensor(out=DG, in0=idb[:].unsqueeze(1).broadcast_to([128, 62, 128]),
                                                      in1=PV[:, l, 11:73].unsqueeze(2).broadcast_to([128, 62, 128]), op=ALU.mult),
                     reads=['PV', 'idb'], writes=['DG'])
                S.op('dve', lambda e: e.memset(PU[0][:, :, 0:15], 0.0), writes=[('PUc', 0)])
                S.op('dve', lambda e: e.memset(CU[0][:, :, 0:30], 0.0), writes=[('CUc', 0)])

                def stageA(gi):
                    c0, n, tiles = GROUPS[gi]
                    sample = (gi == 4)
                    slot = 0 if sample else gi % 2
                    if sample:
                        S.dma('sp', 'stp', out=PU[0][:, :, 0:15], in_=stp[l], writes=[('PUc', 0)])
                        S.dma('pool', 'stc', out=CU[0][:, :, 0:30], in_=stc[l], writes=[('CUc', 0)])
                    def proj(bank, col):
                        def f(e):
                            for kc in range(8):
                                ins = e.matmul(PS[:, bank, 0:n], lhsT=WP[:, kc, col:col + 128], rhs=H[:, kc, c0:c0 + n],
                                               start=(kc == 0), stop=(kc == 7))
                            return ins
                        S.op('pe', f, reads=['WP0', 'WP1'] + [('H', t) for t in tiles], writes=[('ps', bank)])
                    for j in range(2):
                        proj(j, j * 128)
                        S.op('act', lambda e: e.activation(out=PU[slot][:, j, 15:15 + n], in_=PS[:, j, 0:n], func=AF.Copy),
                             reads=[('ps', j)], writes=[('PUn', slot, j)])
                    if gi == 3 or sample:
                        S.dma('sp', 'o_np%d' % gi, out=(nps[l] if sample else npp[l]), in_=PU[slot][:, :, n:n + 15],
                              reads=[('PUc', slot), ('PUn', slot, 0), ('PUn', slot, 1)], is_out=True)
                    if gi < 3:
                        S.op('dve', lambda e: e.tensor_copy(out=PU[(gi + 1) % 2][:, :, 0:15], in_=PU[slot][:, :, 512:527]),
                             reads=[('PUn', slot, 0), ('PUn', slot, 1)], writes=[('PUc', (gi + 1) % 2)])
                    for j in range(2):
                        ext = PU[slot][:, j, :]
                        rk = [('PUc', slot), ('PUn', slot, j)]
                        S.op('dve', lambda e: e.tensor_tensor(out=WA[:, 0:14 + n], in0=ext[:, 1:15 + n], in1=ext[:, 0:14 + n], op=ALU.add),
                             reads=rk, writes=['WA'])
                        S.op('dve', lambda e: e.tensor_tensor(out=WB[:, 0:12 + n], in0=WA[:, 2:14 + n], in1=WA[:, 0:12 + n], op=ALU.add),
                             reads=['WA'], writes=['WB'])
                        if j == 0:
                            lo, lo_off, hi, hi_off = WA, 14, WB, 12
                            kk = ['WA', 'WB']
                        else:
                            S.op('dve', lambda e: e.tensor_tensor(out=WC[:, 0:8 + n], in0=WB[:, 4:12 + n], in1=WB[:, 0:8 + n], op=ALU.add),
                                 reads=['WB'], writes=['WA'])
                            S.op('dve', lambda e: e.tensor_tensor(out=WD[:, 0:n], in0=WC[:, 8:8 + n], in1=WC[:, 0:n], op=ALU.add),
                                 reads=['WA'], writes=['WB'])
                            lo, lo_off, hi, hi_off = WC, 8, WD, 0
                            kk = ['WA', 'WB']
                        for (src, off, p0) in ((lo, lo_off, 0), (hi, hi_off, 64)):
                            S.op('dve', lambda e: e.scalar_tensor_tensor(out=dsb[p0:p0 + 64, j, 0:n], in0=src[p0:p0 + 64, off:off + n],
                                                                         scalar=PV[p0:p0 + 64, l, 9 + j:10 + j], in1=ext[p0:p0 + 64, 15:15 + n],
                                                                         op0=ALU.mult, op1=ALU.subtract),
                                 reads=kk + rk + ['PV'], writes=[('dsb', j)])
                            if gi == 0:
                                S.op('dve', lambda e: e.tensor_tensor(out=t16[p0:p0 + 64, :], in0=src[p0:p0 + 64, off:off + 16],
                                                                      in1=INVC[p0:p0 + 64, j, :], op=ALU.mult),
                                     reads=kk + ['INVC'], writes=['t16'])
                                S.op('dve', lambda e: e.tensor_tensor(out=dsb[p0:p0 + 64, j, 0:16], in0=t16[p0:p0 + 64, :],
                                                                      in1=ext[p0:p0 + 64, 15:31], op=ALU.subtract),
                                     reads=['t16'] + rk, writes=[('dsb', j)])
                    for j in range(2):
                        proj(2 + 2 * j, 256 + j * 128)
                        proj(3 + 2 * j, 512 + j * 128)
                        S.op('act', lambda e: e.activation(out=sg2[j][:, 0:n], in_=PS[:, 3 + 2 * j, 0:n], func=AF.Sigmoid),
                             reads=[('ps', 3 + 2 * j)], writes=[('sg', j)])
                        S.op('dve', lambda e: e.tensor_tensor(out=CU[slot][:, j, 30:30 + n], in0=PS[:, 2 + 2 * j, 0:n], in1=sg2[j][:, 0:n], op=ALU.mult),
                             reads=[('ps', 2 + 2 * j), ('sg', j)], writes=[('CUn', slot, j)])
                        if gi == 3 or sample:
                            S.op('dve', lambda e: e.tensor_tensor(out=U32[:, j, n - 30:n], in0=PS[:, 2 + 2 * j, n - 30:n], in1=sg2[j][:, n - 30:n], op=ALU.mult),
                                 reads=[('ps', 2 + 2 * j), ('sg', j)], writes=[('U32', j)])
                    if gi == 3 or sample:
                        S.dma('sp', 'o_nc%d' % gi, out=(ncs[l] if sample else ncp[l]), in_=U32[:, :, n - 30:n],
                              reads=[('U32', 0), ('U32', 1)], is_out=True)
                    if gi < 3:
                        S.op('dve', lambda e: e.tensor_copy(out=CU[(gi + 1) % 2][:, :, 0:30], in_=CU[slot][:, :, 512:542]),
                             reads=[('CUn', slot, 0), ('CUn', slot, 1)], writes=[('CUc', (gi + 1) % 2)])

                def stageC(gi):
                    c0, n, tiles = GROUPS[gi]
                    sample = (gi == 4)
                    slot = 0 if sample else gi % 2
                    MPC = MPC2[gi % 2]; sl = sl2[gi % 2]
                    for j in range(2):
                        S.op('pe', lambda e: e.matmul(PS[:, j, 0:n], lhsT=PLW[:, j, :], rhs=dsb[:, j, 0:n], start=True, stop=True),
                             reads=[('dsb', j), 'PLW'], writes=[('ps', j)])
                        S.op('act', lambda e: e.activation(out=MPC[:, j, 0:n], in_=PS[:, j, 0:n], func=AF.Copy, scale=PV[:, l, j:j + 1]),
                             reads=[('ps', j), 'PV'], writes=[('MPC', gi % 2, j)])
                    for j in range(2):
                        def cv_mm(e):
                            for tap in range(31):
                                ins = e.matmul(PS[:, 4 + j, 0:n], lhsT=DG[:, j * 31 + tap, :], rhs=CU[slot][:, j, tap:tap + n],
                                               start=(tap == 0), stop=(tap == 30))
                            return ins
                        S.op('pe', cv_mm, reads=['DG', ('CUc', slot), ('CUn', slot, j)], writes=[('ps', 4 + j)])
                        S.op('act', lambda e: e.activation(out=ysb[:, j, 0:n], in_=PS[:, 4 + j, 0:n], func=AF.Identity, bias=PV[:, l, 2 + j:3 + j]),
                             reads=[('ps', 4 + j), 'PV'], writes=[('ysb', j)])
                        S.op('act', lambda e: e.activation(out=ysq[:, j, 0:n], in_=PS[:, 4 + j, 0:n], func=AF.Square, bias=PV[:, l, 2 + j:3 + j]),
                             reads=[('ps', 4 + j), 'PV'], writes=[('ysq', j)])

                    def st_mm(e):
                        for j in range(2):
                            e.matmul(PS[:, 6, 0:n], lhsT=onesf[:, 1, :], rhs=ysb[:, j, 0:n], start=(j == 0), stop=(j == 1))
                        for j in range(2):
                            ins = e.matmul(PS[:, 7, 0:n], lhsT=onesf[:, 1, :], rhs=ysq[:, j, 0:n], start=(j == 0), stop=(j == 1))
                        return ins
                    S.op('pe', st_mm, reads=[('ysb', 0), ('ysb', 1), ('ysq', 0), ('ysq', 1), 'onesf1'], writes=[('ps', 6), ('ps', 7)])
                    S.op('act', lambda e: e.activation(out=m2[:, 0:n], in_=PS[:, 6, 0:n], func=AF.Square), reads=[('ps', 6)], writes=['m2'])
                    S.op('dve', lambda e: e.scalar_tensor_tensor(out=var[:, 0:n], in0=PS[:, 7, 0:n], scalar=EPS, in1=m2[:, 0:n],
                                                                 op0=ALU.add, op1=ALU.subtract),
                         reads=[('ps', 7), 'm2'], writes=['var'])
                    S.op('act', lambda e: e.activation(out=m2[:, 0:n], in_=var[:, 0:n], func=AF.Ln), reads=['var'], writes=['m2'])
                    S.op('act', lambda e: e.activation(out=var[:, 0:n], in_=m2[:, 0:n], func=AF.Exp, scale=-0.5), reads=['m2'], writes=['var'])
                    for j in range(2):
                        S.op('dve', lambda e: e.tensor_tensor(out=ysq[:, j, 0:n], in0=ysb[:, j, 0:n], in1=PS[:, 6, 0:n], op=ALU.subtract),
                             reads=[('ysb', j), ('ps', 6)], writes=[('ysq', j)])
                        S.op('dve', lambda e: e.tensor_tensor(out=ysb[:, j, 0:n], in0=ysq[:, j, 0:n], in1=var[:, 0:n], op=ALU.mult),
                             reads=[('ysq', j), 'var'], writes=[('ysb', j)])
                        S.op('act', lambda e: e.activation(out=sl[:, j, 0:n], in_=ysb[:, j, 0:n], func=AF.Silu,
                                                           scale=PV[:, l, 4 + j:5 + j], bias=PV[:, l, 6 + j:7 + j]),
                             reads=[('ysb', j), 'PV'], writes=[('sl', gi % 2, j)])

                def stageB(gi):
                    c0, n, tiles = GROUPS[gi]
                    MPC = MPC2[gi % 2]; sl = sl2[gi % 2]
                    for jo in range(2):
                        def pw_mm(e):
                            for j in range(2):
                                ins = e.matmul(PS[:, jo, 0:n], lhsT=PW[:, j, jo * 128:(jo + 1) * 128], rhs=sl[:, j, 0:n],
                                               start=(j == 0), stop=(j == 1))
                            return ins
                        S.op('pe', pw_mm, reads=[('sl', gi % 2, 0), ('sl', gi % 2, 1), 'PW'], writes=[('ps', jo)])
                        S.op('act', lambda e: e.activation(out=MPC[:, 2 + jo, 0:n], in_=PS[:, jo, 0:n], func=AF.Copy),
                             reads=[('ps', jo)], writes=[('MPC', gi % 2, 2 + jo)])
                    for ti, t in enumerate(tiles):
                        P = TP(t)
                        for hh in range(2):
                            bank = (6, 7, 0, 1)[(2 * ti + hh) % 4]

                            def wo_mm(e):
                                for kc in range(4):
                                    ins = e.matmul(PS[:P, bank, :], lhsT=MPC[:, kc, ti * 128:ti * 128 + P],
                                                   rhs=WOP[:, kc, hh * 512:(hh + 1) * 512], start=(kc == 0), stop=(kc == 3))
                                return ins
                            S.op('pe', wo_mm, reads=[('MPC', gi % 2, k) for k in range(4)] + ['WOP0', 'WOP1'], writes=[('ps', bank)])
                            S.op('dve', lambda e: e.tensor_tensor(out=X[:P, t, hh * 512:(hh + 1) * 512], in0=X[:P, t, hh * 512:(hh + 1) * 512],
                                                                  in1=PS[:P, bank, :], op=ALU.add),
                                 reads=[('ps', bank), ('X', t)], writes=[('X', t)])

                for gi in range(len(GROUPS)):
                    stageA(gi)
                    if gi > 0:
                        stageB(gi - 1)
                    stageC(gi)
                stageB(len(GROUPS) - 1)
                S.barrier()

            with ExitStack() as ph:
                W = [Wq0] + [sb(ph, "Wq%d" % i, [128, 8, 512], BF16) for i in (1, 2)]
                COS = sb(ph, "COS", [128, 17, 32], F32); SIN = sb(ph, "SIN", [128, 17, 32], F32)
                zs = [sb(ph, "zs%d" % i, [128, 512], F32) for i in range(3)]
                kr = [sb(ph, "kr%d" % i, [128, 512], F32) for i in range(2)]
                t1 = [sb(ph, "t1%d" % i, [128, 512], F32) for i in range(2)]
                t2 = [sb(ph, "t2%d" % i, [128, 512], F32) for i in range(2)]
                qb = [sb(ph, "qb%d" % i, [128, 512], BF16) for i in range(6)]
                S.dma('sp', 'cos', out=COS[:], in_=cos_t[:, :, :], writes=['COS'])
                S.dma('sp', 'sin', out=SIN[:], in_=sin_t[:, :, :], writes=['SIN'])
                for pi, c0 in ((1, 768), (2, 1280)):
                    S.dma('pool', 'wq%d' % pi, out=W[pi][:], in_=w_in3[:, :, c0:c0 + 512], writes=[('W', pi)])
                pend = []
                cnt = 0
                rc = 0
                NQ0 = 6
                items = [(t, 0) for t in range(NQ0)] + [(t, pi) for t in range(NQ0) for pi in (1, 2)] + \
                        [(t, pi) for t in range(NQ0, NT) for pi in (0, 1, 2)]
                for (t, pi) in items:
                    P = TP(t); c = t * 128
                    nm = ('q', 'k', 'v')[pi]
                    if True:
                        Wc = W[pi]
                        bank = cnt % 4
                        s2 = cnt % 3
                        cnt += 1
                        qs = rc % 6

                        def mm(e):
                            for kc in range(8):
                                ins = e.matmul(PS[:P, bank, :], lhsT=H[:, kc, c:c + P], rhs=Wc[:, kc, :], start=(kc == 0), stop=(kc == 7))
                            return ins
                        S.op('pe', mm, reads=[('H', t), ('W', pi)], writes=[('ps', bank)])
                        while len(pend) > 3:
                            pend.pop(0)()
                        if nm == 'v':
                            S.op('act', lambda e: e.activation(out=zs[s2][:P], in_=PS[:P, bank, :], func=AF.Copy),
                                 reads=[('ps', bank)], writes=[('zs', s2)])
                            S.dma('sp', 'o_v%d' % s2, out=rows_v(l, t), in_=zs[s2][:P], reads=[('zs', s2)], is_out=True)
                            S.op('act', lambda e: e.activation(out=VB[:P, t, :], in_=zs[s2][:P], func=AF.Copy), reads=[('zs', s2)], writes=[('VB', t)])
                            continue
                        r2 = rc % 2
                        rc += 1
                        z4 = PS[:P, bank, :].rearrange("p (g two i) -> p g two i", g=8, two=2)
                        cosb = COS[:P, t:t + 1, :].unsqueeze(1).broadcast_to([P, 8, 2, 32])
                        sinb = SIN[:P, t:t + 1, :].broadcast_to([P, 8, 32])
                        t14 = t1[r2][:P].rearrange("p (g two i) -> p g two i", g=8, two=2)
                        t24 = t2[r2][:P].rearrange("p (g two i) -> p g two i", g=8, two=2)
                        S.op('dve', lambda e: e.tensor_tensor(out=t14, in0=z4, in1=cosb, op=ALU.mult),
                             reads=[('ps', bank), 'COS'], writes=[('t1', r2)])
                        S.op('dve', lambda e: e.tensor_tensor(out=t24[:, :, 0, :], in0=z4[:, :, 1, :], in1=sinb, op=ALU.mult),
                             reads=[('ps', bank), 'SIN'], writes=[('t2a', r2)])
                        S.op('dve', lambda e: e.tensor_tensor(out=t24[:, :, 1, :], in0=z4[:, :, 0, :], in1=sinb, op=ALU.mult),
                             reads=[('ps', bank), 'SIN'], writes=[('t2b', r2)])
                        if nm == 'q':
                            o4 = qb[qs][:P].rearrange("p (g two i) -> p g two i", g=8, two=2)
                        else:
                            o4 = kr[r2][:P].rearrange("p (g two i) -> p g two i", g=8, two=2)
                        okey = ('qb', qs) if nm == 'q' else ('kr', r2)
                        S.op('dve', lambda e: e.tensor_tensor(out=o4[:, :, 0, :], in0=t14[:, :, 0, :], in1=t24[:, :, 0, :], op=ALU.subtract),
                             reads=[('t1', r2), ('t2a', r2)], writes=[(okey, 'a')])
                        S.op('dve', lambda e: e.tensor_tensor(out=o4[:, :, 1, :], in0=t14[:, :, 1, :], in1=t24[:, :, 1, :], op=ALU.add),
                             reads=[('t1', r2), ('t2b', r2)], writes=[(okey, 'b')])
                        if nm == 'k':
                            S.dma('sp', 'o_k%d' % r2, out=rows_k(l, t), in_=kr[r2][:P], reads=[(okey, 'a'), (okey, 'b')], is_out=True)
                            S.op('act', lambda e: e.activation(out=qb[qs][:P], in_=kr[r2][:P], func=AF.Copy),
                                 reads=[(okey, 'a'), (okey, 'b')], writes=[(('qb', qs), 'a'), (('qb', qs), 'b')])

                        def mk(t=t, P=P, c=c, qs=qs, nm=nm, tb=4 + (rc % 4)):
                            def run():
                                pv = psb(tb).rearrange("p (k n) -> p k n", k=8)

                                def tr(e):
                                    for h in range(4):
                                        ins = e.transpose(out=pv[:, h, :P], in_=qb[qs][:P, h * 128:(h + 1) * 128], identity=idb[:P, :P])
                                    return ins
                                S.op('pe', tr, reads=[(('qb', qs), 'a'), (('qb', qs), 'b'), 'idb'], writes=[('ps', tb)])
                                dst = QT if nm == 'q' else KT
                                S.op('act', lambda e: e.activation(out=dst[:, :, c:c + P], in_=pv[:, 0:4, :P], func=AF.Copy),
                                     reads=[('ps', tb)], writes=[(nm + 'T', t)])
                            return run
                        pend.append(mk())
                while pend:
                    pend.pop(0)()
                S.barrier()
            phq.close()

            phw = ExitStack()
            WOA = sb(phw, "WOA", [128, 4, 1024], BF16)
            S.dma('pool', 'woa', out=WOA[:], in_=w_out3[:, 2:6, :], writes=['WOA'])
            ckb = [sb(phw, "ckb%d" % i, [128, 512], BF16) for i in range(4)]
            cvb = [sb(phw, "cvb%d" % i, [128, 512], BF16) for i in range(4)]
            for i in range(4):
                S.dma('pool', 'ck%d' % i, out=ckb[i][:], in_=ck[l, i * 128:(i + 1) * 128, :], writes=[('ckb', i)])
                S.dma('pool', 'cv%d' % i, out=cvb[i][:], in_=cv[l, i * 128:(i + 1) * 128, :], writes=[('cvb', i)])
            with ExitStack() as ph:
                PT = [sb(ph, "PT%d" % i, [128, 2, 512], BF16) for i in range(3)]
                OSB = [sb(ph, "OSB%d" % i, [128, 2, 512], F32) for i in range(2)]
                LNL = [sb(ph, "LNL%d" % i, [128, 2, 512], F32) for i in range(2)]
                dd = sb(ph, "dd", [128, 512], F32); sq = sb(ph, "sq", [128, 512], F32)
                msb = sb(ph, "msb", [128, 512], F32); rs = sb(ph, "rs", [128, 512], F32)
                MB = sb(ph, "MB", [128, 1], F32)
                S.op('dve', lambda e: e.memset(MB[0:64, :], 0.0), writes=['MBa'])
                S.op('dve', lambda e: e.memset(MB[64:128, :], NEG), writes=['MBb'])
                negl = LAM[:, l, 4:5]; gsc = LAM[:, l, 5:6]
                blocks = []
                for n_it, (h, j) in enumerate([(h, j) for h in range(4) for j in range(4)]):
                    for i in range(4 * j + 4):
                        blocks.append((n_it, h, j, i))
                NBK = len(blocks)

                def off_of(j, i):
                    r = i - 4 * j
                    return 128 * r if r > 0 else 0

                def qk(g):
                    n_it, h, j, i = blocks[g]
                    slot = g % 2
                    off = off_of(j, i)
                    q0 = j * 512

                    def f(e):
                        for c in range(2):
                            ins = e.matmul(PS[:, 2 * slot + c, off:512], lhsT=KT[c * 64:(c + 1) * 64, h, i * 128:(i + 1) * 128],
                                           rhs=QT[c * 64:(c + 1) * 64, h, q0 + off:q0 + 512], start=True, stop=True)
                        return ins
                    S.op('pe', f, reads=[('QT', h, j)], writes=[('ps', 2 * slot), ('ps', 2 * slot + 1)])

                def ex(g):
                    n_it, h, j, i = blocks[g]
                    slot = g % 2
                    off = off_of(j, i)
                    ps_ = g % 3
                    diag = (i - 4 * j) >= 0
                    src = PS[:, 2 * slot:2 * slot + 2, :]
                    if diag:
                        S.op('act', lambda e: e.activation(out=PT[ps_][:, :, off:off + 64], in_=src[:, :, off:off + 64], func=AF.Exp,
                                                           scale=SCALE, bias=MB[:, 0:1]),
                             reads=[('ps', 2 * slot), ('ps', 2 * slot + 1), 'MBa', 'MBb'], writes=[('PT', ps_, 'm')])
                        S.op('act', lambda e: e.activation(out=PT[ps_][:, :, off + 64:512], in_=src[:, :, off + 64:512], func=AF.Exp,
                                                           scale=SCALE),
                             reads=[('ps', 2 * slot), ('ps', 2 * slot + 1)], writes=[('PT', ps_)])
                    else:
                        S.op('act', lambda e: e.activation(out=PT[ps_][:, :, :], in_=src, func=AF.Exp, scale=SCALE),
                             reads=[('ps', 2 * slot), ('ps', 2 * slot + 1)], writes=[('PT', ps_), ('PT', ps_, 'm')])

                def pvm(g):
                    n_it, h, j, i = blocks[g]
                    off = off_of(j, i)
                    ps_ = g % 3
                    nb = 4 * j + 4

                    def f(e):
                        for c in range(2):
                            e.matmul(PS[:, 4 + c, off:512], lhsT=VB[:, i, h * 128:(h + 1) * 128], rhs=PT[ps_][:, c, off:512],
                                     start=(i == 0), stop=(i == nb - 1))
                        for c in range(2):
                            ins = e.matmul(PS[:, 6 + c, off:512], lhsT=onesb[:], rhs=PT[ps_][:, c, off:512],
                                           start=(i == 0), stop=(i == nb - 1))
                        return ins
                    S.op('pe', f, reads=[('PT', ps_), ('PT', ps_, 'm'), 'onesb'], writes=[('ps', 4), ('ps', 5), ('ps', 6), ('ps', 7)])

                def mkB(st):
                    def run(sbk):
                        S.op('act', lambda e: e.activation(out=LNL[st][:], in_=LNL[st][:], func=AF.Exp, scale=-1.0),
                             reads=[('LNL', st)], writes=[('LNL', st)])
                        S.op('dve', lambda e: e.tensor_tensor(out=OSB[st][:], in0=OSB[st][:], in1=LNL[st][:], op=ALU.mult),
                             reads=[('OSB', st), ('LNL', st)], writes=[('OSB', st)])
                        S.op('dve', lambda e: e.scalar_tensor_tensor(out=dd[:], in0=OSB[st][:, 1, :], scalar=negl, in1=OSB[st][:, 0, :],
                                                                     op0=ALU.mult, op1=ALU.add),
                             reads=[('OSB', st)], writes=['dd'])
                        S.op('dve', lambda e: e.tensor_tensor(out=sq[:], in0=dd[:], in1=dd[:], op=ALU.mult), reads=['dd'], writes=['sq'])
                    return run

                def mkC1():
                    def run(sbk):
                        S.op('pe', lambda e: e.matmul(PS[:, sbk, :], lhsT=onesf[:, 0, :], rhs=sq[:], start=True, stop=True),
                             reads=['sq', 'onesf0'], writes=[('ps', sbk)])
                        S.op('dve', lambda e: e.tensor_scalar(out=msb[:], in0=PS[:, sbk, :], scalar1=EPS, scalar2=None, op0=ALU.add),
                             reads=[('ps', sbk)], writes=['msb'])
                    return run

                def mkC2(h, j):
                    q0 = j * 512

                    def run():
                        S.op('act', lambda e: e.activation(out=msb[:], in_=msb[:], func=AF.Ln), reads=['msb'], writes=['msb'])
                        S.op('act', lambda e: e.activation(out=rs[:], in_=msb[:], func=AF.Exp, scale=-0.5), reads=['msb'], writes=['rs'])
                        S.op('dve', lambda e: e.scalar_tensor_tensor(out=QT[:, h, q0:q0 + 512], in0=dd[:], scalar=gsc, in1=rs[:],
                                                                     op0=ALU.mult, op1=ALU.mult),
                             reads=['dd', 'rs'], writes=[('QT', h, j)])
                    return run

                pendB = None
                savedC = None
                pendC2 = None
                qk(0)
                qk(1)
                for g in range(NBK):
                    n_it, h, j, i = blocks[g]
                    nb = 4 * j + 4
                    st = n_it % 2
                    ex(g)
                    if pendC2 is not None:
                        pendC2()
                        pendC2 = None
                    pvm(g)
                    if i == nb - 1:
                        S.op('dve', lambda e: e.tensor_copy(out=OSB[st][:], in_=PS[:, 4:6, :]),
                             reads=[('ps', 4), ('ps', 5)], writes=[('OSB', st)])
                        S.op('act', lambda e: e.activation(out=LNL[st][:], in_=PS[:, 6:8, :], func=AF.Ln),
                             reads=[('ps', 6), ('ps', 7)], writes=[('LNL', st)])
                        if savedC is not None:
                            savedC[0](2 * (g % 2))
                            pendC2 = savedC[1]
                        savedC = (mkC1(), mkC2(h, j))
                        pendB = (g + 2, mkB(st))
                    if pendB is not None and pendB[0] <= g:
                        pendB[1](0)
                        pendB = None
                    if g + 2 < NBK:
                        qk(g + 2)
                if pendC2 is not None:
                    pendC2()
                if pendB is not None:
                    pendB[1](0)
                savedC[0](0); savedC[1]()
                S.barrier()

            phf = ExitStack()
            GU = [sb(phf, "GU%d" % i, [128, 2, 8, 256], BF16) for i in range(2)]

            def issue_gu_abs(fs, w, slot):
                S.dma('pool', 'gug%d' % slot, out=GU[slot][:, 0, :, 0:w * 128], in_=w_gu3[:, :, fs * 128:(fs + w) * 128],
                      writes=[('GUg', slot)])
                S.dma('pool', 'guu%d' % slot, out=GU[slot][:, 1, :, 0:w * 128], in_=w_gu3[:, :, 2816 + fs * 128:2816 + (fs + w) * 128],
                      writes=[('GUu', slot)])
            issue_gu_abs(0, 2, 0)
            def wo_tile(t, banks):
                P = TP(t); c = t * 128
                for hh in range(2):
                    bank = banks[hh]

                    def wo_mm(e):
                        for kc in range(4):
                            ins = e.matmul(PS[:P, bank, :], lhsT=QT[:, kc, c:c + P], rhs=WOA[:, kc, hh * 512:(hh + 1) * 512],
                                           start=(kc == 0), stop=(kc == 3))
                        return ins
                    S.op('pe', wo_mm, reads=['WOA', ('QT', 's')] if t == 16 else ['WOA'], writes=[('ps', bank)])
                    S.op('dve', lambda e: e.tensor_tensor(out=X[:P, t, hh * 512:(hh + 1) * 512], in0=X[:P, t, hh * 512:(hh + 1) * 512],
                                                          in1=PS[:P, bank, :], op=ALU.add),
                         reads=[('ps', bank), ('X', t)], writes=[('X', t)])

            with ExitStack() as ph:
                ckT = [sb(ph, "ckT%d" % i, [128, 4, 128], BF16) for i in range(2)]
                PTs = [sb(ph, "PTs%d" % i, [128, 8, 32], BF16) for i in range(2)]
                rr = sb(ph, "rr", [128, 8, 32], F32); a8 = sb(ph, "a8", [128, 2, 4, 32], F32)
                d4 = sb(ph, "d4", [128, 4, 32], F32); sq4 = sb(ph, "sq4", [128, 4, 32], F32)
                ms4 = sb(ph, "ms4", [128, 128], F32); ln4 = sb(ph, "ln4", [128, 128], F32); rs4 = sb(ph, "rs4", [128, 128], F32)
                negl = LAM[:, l, 4:5]; gsc = LAM[:, l, 5:6]
                OS = PS[:, 4, 0:256].rearrange("p (g n) -> p g n", g=8)
                LS = PS[:, 5, 0:256].rearrange("p (g n) -> p g n", g=8)
                ATSM = 9
                for i in range(17 if ATSM > 0 else 0):
                    slot = i % 2
                    cs = i % 4
                    KP = 128 if i < 16 else 32
                    if i < 16:
                        if i >= 4:
                            S.dma('pool', 'ck%d' % cs, out=ckb[cs][:], in_=ck[l, i * 128:(i + 1) * 128, :], writes=[('ckb', cs)])
                            S.dma('pool', 'cv%d' % cs, out=cvb[cs][:], in_=cv[l, i * 128:(i + 1) * 128, :], writes=[('cvb', cs)])
                        pv = psb(slot).rearrange("p (k n) -> p k n", k=8)

                        def tr(e):
                            for h in range(4):
                                ins = e.transpose(out=pv[:, h, :], in_=ckb[cs][:, h * 128:(h + 1) * 128], identity=idb[:])
                            return ins
                        S.op('pe', tr, reads=[('ckb', cs), 'idb'], writes=[('ps', slot)])
                        S.op('act', lambda e: e.activation(out=ckT[slot][:], in_=pv[:, 0:4, :], func=AF.Copy),
                             reads=[('ps', slot)], writes=[('ckT', slot)])
                    sb0 = 2 if slot == 0 else 6
                    SS2 = PS[:, sb0:sb0 + 2, 0:128]
                    if ATSM < 2:
                        continue

                    def qk(e):
                        for h in range(4):
                            for c in range(2):
                                if i < 16:
                                    kT = ckT[slot][c * 64:(c + 1) * 64, h, :]
                                else:
                                    kT = KT[c * 64:(c + 1) * 64, h, 2048:2080]
                                ins = e.matmul(PS[:KP, sb0 + c, h * 32:(h + 1) * 32], lhsT=kT, rhs=QT[c * 64:(c + 1) * 64, h, 2048:2080],
                                               start=True, stop=True)
                        return ins
                    S.op('pe', qk, reads=[('ckT', slot), ('QT', 's')], writes=[('ps', sb0), ('ps', sb0 + 1)])
                    S.op('act', lambda e: e.activation(out=PTs[slot][:KP].rearrange("p (c x) n -> p c (x n)", c=2), in_=SS2[:KP], func=AF.Exp, scale=SCALE),
                         reads=[('ps', sb0), ('ps', sb0 + 1)], writes=[('PTs', slot)])
                    if i < 16:
                        wo_tile(i, (6, 7) if slot == 0 else (2, 3))
                    if ATSM < 3:
                        continue

                    def pvm(e):
                        first = True
                        for c in range(2):
                            for h in range(4):
                                if i < 16:
                                    vv = cvb[cs][:, h * 128:(h + 1) * 128]
                                else:
                                    vv = VB[:32, 16, h * 128:(h + 1) * 128]
                                e.matmul(OS[:, c * 4 + h, :], lhsT=vv, rhs=PTs[slot][:KP, c * 4 + h, :],
                                         start=(i == 0 and first), stop=(i == 16), skip_group_check=True)
                                first = False
                        ins = e.matmul(PS[:, 5, 0:256], lhsT=onesb[:KP, :], rhs=PTs[slot][:KP].rearrange("p g n -> p (g n)"),
                                       start=(i == 0), stop=(i == 16))
                        return ins
                    S.op('pe', pvm, reads=[('PTs', slot), ('cvb', cs), 'onesb'], writes=[('ps', 4), ('ps', 5)])
                if ATSM < 4:
                    S.stopped = True
                S.op('dve', lambda e: e.reciprocal(out=rr[:], in_=LS), reads=[('ps', 5)], writes=['rr'])
                S.op('dve', lambda e: e.tensor_tensor(out=a8[:].rearrange("p c h n -> p (c h) n"), in0=OS, in1=rr[:], op=ALU.mult),
                     reads=[('ps', 4), 'rr'], writes=['a8'])
                S.op('dve', lambda e: e.scalar_tensor_tensor(out=d4[:], in0=a8[:, 1, :, :], scalar=negl, in1=a8[:, 0, :, :],
                                                             op0=ALU.mult, op1=ALU.add), reads=['a8'], writes=['d4'])
                S.op('act', lambda e: e.activation(out=sq4[:], in_=d4[:], func=AF.Square), reads=['d4'], writes=['sq4'])
                S.op('pe', lambda e: e.matmul(PS[:, 0, 0:128], lhsT=onesf[:, 0, :], rhs=sq4[:].rearrange("p h n -> p (h n)"), start=True, stop=True),
                     reads=['sq4', 'onesf0'], writes=[('ps', 0)])
                S.op('dve', lambda e: e.tensor_scalar(out=ms4[:], in0=PS[:, 0, 0:128], scalar1=EPS, scalar2=None, op0=ALU.add),
                     reads=[('ps', 0)], writes=['ms4'])
                S.op('act', lambda e: e.activation(out=ln4[:], in_=ms4[:], func=AF.Ln), reads=['ms4'], writes=['ln4'])
                S.op('act', lambda e: e.activation(out=rs4[:], in_=ln4[:], func=AF.Exp, scale=-0.5), reads=['ln4'], writes=['rs4'])
                S.op('dve', lambda e: e.scalar_tensor_tensor(out=QT[:, :, 2048:2080], in0=d4[:], scalar=gsc,
                                                             in1=rs4[:].rearrange("p (h n) -> p h n", h=4), op0=ALU.mult, op1=ALU.mult),
                     reads=['d4', 'rs4'], writes=[('QT', 's')])
                wo_tile(16, (2, 3))
                S.barrier()

            phase_norm(gffn[l:l + 1, :])
            with ExitStack() as ph:
                DN = [sb(ph, "DN%d" % i, [128, 8, 512], BF16) for i in range(2)]
                sgl = [sb(ph, "sgl%d" % i, [128, 512], F32) for i in range(2)]
                gcount = 0
                dcount = 0
                ev = 0
                for (f0, f1) in FFN_PASSES:
                    nf = f1 - f0
                    slabs = []
                    f = f0
                    while f < f1:
                        w = min(2, f1 - f)
                        slabs.append((f, w))
                        f += w

                    def issue_gu(si):
                        fs, w = slabs[si]
                        slot = (gcount + si) % 2
                        issue_gu_abs(fs, w, slot)
                    if f0 > 0:
                        issue_gu(0)
                    for hh in range(2):
                        S.dma('pool', 'dn%d' % hh, out=DN[hh][:, 0:nf, :], in_=w_dn3[:, f0:f1, hh * 512:(hh + 1) * 512], writes=[('DN', hh)])
                    for si, (fs, w) in enumerate(slabs):
                        slot = (gcount + si) % 2
                        if si + 1 < len(slabs):
                            issue_gu(si + 1)
                        for fo in range(w):
                            fi = fs + fo - f0
                            for (c0, n, tiles) in GROUPS:
                                bg = (ev % 3) * 2
                                ev += 1

                                def gu_mm(e):
                                    for kc in range(8):
                                        e.matmul(PS[:, bg, 0:n], lhsT=GU[slot][:, 0, kc, fo * 128:(fo + 1) * 128], rhs=H[:, kc, c0:c0 + n],
                                                 start=(kc == 0), stop=(kc == 7))
                                    for kc in range(8):
                                        ins = e.matmul(PS[:, bg + 1, 0:n], lhsT=GU[slot][:, 1, kc, fo * 128:(fo + 1) * 128], rhs=H[:, kc, c0:c0 + n],
                                                       start=(kc == 0), stop=(kc == 7))
                                    return ins
                                S.op('pe', gu_mm, reads=[('GUg', slot), ('GUu', slot)], writes=[('ps', bg), ('ps', bg + 1)])
                                ss = ev % 2
                                S.op('act', lambda e: e.activation(out=sgl[ss][:, 0:n], in_=PS[:, bg, 0:n], func=AF.Silu),
                                     reads=[('ps', bg)], writes=[('sgl', ss)])
                                S.op('dve', lambda e: e.tensor_tensor(out=ACTH[:, fi, c0:c0 + n], in0=sgl[ss][:, 0:n], in1=PS[:, bg + 1, 0:n], op=ALU.mult),
                                     reads=[('sgl', ss), ('ps', bg + 1)], writes=[('ACTH', fi)])
                    gcount += len(slabs)
                    lastpass = (f1 == 22)
                    last = lastpass and l == DEPTH - 1
                    mid = lastpass and l < DEPTH - 1
                    if lastpass:
                        GUf = [GU[i][:].rearrange("p a b c -> p (a b c)").bitcast(F32) for i in range(2)]
                        GU1b = GU[1][:].rearrange("p a b c -> p (a b c)")
                        Gfin = GUf[0][:, 0:1024]
                        stf = GUf[0][:, 1024:1024 + 4 * NT].rearrange("p (t k) -> p t k", k=4)
                        yof = [GUf[1][:, i * 1024:(i + 1) * 1024] for i in range(2)]
                        hbf = [GU1b[:, i * 1024:(i + 1) * 1024] for i in range(3)]
                        junkf = GU1b[:, 3072:4096]
                        grow = gfin[0:1, :] if last else gmix[l + 1:l + 2, :]
                    if mid:
                        w_in3n = w_in[l + 1].rearrange("(k p) c -> p k c", p=128)
                        WPn = FA[:, 16640:22784].rearrange("p (k c) -> p k c", k=8)
                        S.dma('pool', 'wp0', out=WPn[:, :, 0:256], in_=w_in3n[:, :, 0:256], writes=['WP0'])
                        S.dma('pool', 'wp1', out=WPn[:, :, 256:768], in_=w_in3n[:, :, 1792:2304], writes=['WP1'])
                    order = [(t, hh) for t in range(NT) for hh in range(2)] if lastpass else [(t, hh) for hh in range(2) for t in range(NT)]
                    for (t, hh) in order:
                        dslot = hh
                        P = TP(t); c = t * 128
                        bank = 4 + ((2 * t + hh) % 4 if lastpass else t % 4)

                        def dn_mm(e):
                            for fi in range(nf):
                                ins = e.matmul(PS[:P, bank, :], lhsT=ACTH[:, fi, c:c + P], rhs=DN[dslot][:, fi, :],
                                               start=(fi == 0), stop=(fi == nf - 1))
                            return ins
                        S.op('pe', dn_mm, reads=[('ACTH', fi) for fi in range(nf)] + [('DN', dslot)], writes=[('ps', bank)])
                        S.op('dve', lambda e: e.tensor_tensor(out=X[:P, t, hh * 512:(hh + 1) * 512], in0=X[:P, t, hh * 512:(hh + 1) * 512],
                                                              in1=PS[:P, bank, :], op=ALU.add),
                             reads=[('ps', bank), ('X', t)], writes=[('X', t)])
                        if lastpass and hh == 1:
                            if t == 0:
                                S.dma('sp', 'Gf', out=Gfin, in_=grow.partition_broadcast(128), reads=[('X', 0)], writes=['Gfin'])

                            def fin1(t):
                                P = TP(t)
                                jout = yof[t % 2][:P] if last else junkf[:P]
                                jkey = ('yo', t % 2) if last else 'junkf'
                                S.op('act', lambda e: e.activation(out=jout, in_=X[:P, t, :], func=AF.Square, accum_out=stf[:P, t, 0:1]),
                                     reads=[('X', t)], writes=[jkey, ('st', t, 0)])

                            def fin2(t):
                                rstd_chain(stf, TP(t), t, None)

                            def fin3(t):
                                P = TP(t)
                                if last:
                                    S.op('dve', lambda e: e.scalar_tensor_tensor(out=yof[t % 2][:P], in0=X[:P, t, :], scalar=stf[:P, t, 3:4],
                                                                                 in1=Gfin[:P], op0=ALU.mult, op1=ALU.mult),
                                         reads=[('X', t), ('st', t, 3), 'Gfin'], writes=[('yo', t % 2)])
                                    S.dma('sp', 'o_y%d' % (t % 2), out=rows_y(t), in_=yof[t % 2][:P], reads=[('yo', t % 2)], is_out=True)
                                else:
                                    S.op('dve', lambda e: e.scalar_tensor_tensor(out=hbf[t % 3][:P], in0=X[:P, t, :], scalar=stf[:P, t, 3:4],
                                                                                 in1=Gfin[:P], op0=ALU.mult, op1=ALU.mult),
                                         reads=[('X', t), ('st', t, 3), 'Gfin'], writes=[('hbf', t % 3)])

                            def fin4(t):
                                if last:
                                    return
                                P = TP(t); c4 = t * 128
                                tbk = t % 4
                                pvf = psb(tbk).rearrange("p (k n) -> p k n", k=8)

                                def tr(e):
                                    for kc in range(8):
                                        ins = e.transpose(out=pvf[:, kc, :P], in_=hbf[t % 3][:P, kc * 128:(kc + 1) * 128], identity=idb[:P, :P])
                                    return ins
                                S.op('pe', tr, reads=[('hbf', t % 3), 'idb'], writes=[('ps', tbk)])
                                S.op('act', lambda e: e.activation(out=H[:, :, c4:c4 + P], in_=pvf[:, :, :P], func=AF.Copy),
                                     reads=[('ps', tbk)], writes=[('H', t)])
                            if t >= 3:
                                fin4(t - 3)
                            if t >= 2:
                                fin3(t - 2)
                            if t >= 1:
                                fin2(t - 1)
                            fin1(t)
                            if t == NT - 1:
                                fin4(t - 2)
                                fin3(t - 1)
                                fin2(t)
                                fin4(t - 1)
                                fin3(t)
                                fin4(t)
                S.barrier()
            phf.close()
            phw.close()

        S.stopped = False

        if True:
            S.finish()
    return nc


def _host_consts():
    half = 32
    inv = (np.float32(10000.0) ** (-np.arange(half, dtype=np.float32) / np.float32(half))).astype(np.float32)
    pos = np.zeros((128, 17), np.float32)
    for t in range(16):
        pos[:, t] = t * 128 + np.arange(128)
    pos[:, 16] = 2048 + np.arange(128)
    ang = (pos[:, :, None] * inv[None, None, :]).astype(np.float32)
    cos_t = np.cos(ang).astype(np.float32)
    sin_t = np.sin(ang).astype(np.float32)
    wins = np.array([[2, 4], [8, 16]])
    invc = np.zeros((128, 2, 16), np.float32)
    invw = np.zeros((128, 2), np.float32)
    for j in range(2):
        for p in range(128):
            w = wins[j][p // 64]
            invw[p, j] = 1.0 / w
            invc[p, j, :] = 1.0 / np.minimum(np.arange(16) + 1, w)
    return cos_t, sin_t, invc, invw


_NC_CACHE = {}


def kernel(x_prompt, x_sample, cache_k, cache_v, state_pool, state_conv,
           norm_mix_g, w_in, pool_w, pool_scale, lambda_qk, diff_norm_g,
           conv_dw, conv_dw_b, conv_ln_g, conv_ln_b, conv_pw, w_out,
           norm_ffn_g, w_gate_up, w_down, final_norm_g):
    f = lambda a: np.ascontiguousarray(np.asarray(a, dtype=np.float32))
    x_prompt, x_sample, cache_k, cache_v = f(x_prompt), f(x_sample), f(cache_k), f(cache_v)
    state_pool, state_conv = f(state_pool), f(state_conv)
    B = 8
    cos_t, sin_t, invc, invw = _host_consts()

    def pc(v):
        return np.asarray(v, np.float32).reshape(2, 128).T
    pvec = np.zeros((2, 128, NPV), np.float32)
    plw = np.zeros((2, 128, 2, 128), np.float32)
    pool_w = f(pool_w); conv_dw = f(conv_dw)
    for l in range(2):
        pvec[l, :, 0:2] = pc(pool_scale[l])
        pvec[l, :, 2:4] = pc(conv_dw_b[l])
        pvec[l, :, 4:6] = pc(conv_ln_g[l])
        pvec[l, :, 6:8] = pc(conv_ln_b[l])
        pvec[l, :, 8] = np.asarray(diff_norm_g[l], np.float32)
        pvec[l, :, 9:11] = invw
        dwl = conv_dw[l].reshape(31, 2, 128)
        pvec[l, :, 11:73] = dwl.transpose(2, 1, 0).reshape(128, 62)
        for j in range(2):
            for hf in range(2):
                plw[l, hf * 64:(hf + 1) * 64, j, hf * 64:(hf + 1) * 64] = pool_w[l, 2 * j + hf]
    common = dict(
        w_in=f(w_in), w_out=f(w_out), w_gu=f(w_gate_up), w_dn=f(w_down), conv_pw=f(conv_pw), plw=plw,
        gmix=f(norm_mix_g), gffn=f(norm_ffn_g), gfin=f(final_norm_g).reshape(1, 1024),
        pvec=pvec, lq=f(lambda_qk).reshape(2, 256),
        ident=np.eye(128, dtype=np.float32), cos_t=cos_t, sin_t=sin_t, invc=invc,
    )
    in_maps = []
    for b in range(B):
        m = dict(common)
        m["xp"] = x_prompt[b]
        m["xs"] = x_sample[b]
        m["ck"] = np.ascontiguousarray(cache_k[:, b].reshape(2, 2048, 512))
        m["cv"] = np.ascontiguousarray(cache_v[:, b].reshape(2, 2048, 512))
        m["stp"] = np.ascontiguousarray(state_pool[:, b].reshape(2, 15, 2, 128).transpose(0, 3, 2, 1))
        m["stc"] = np.ascontiguousarray(state_conv[:, b].reshape(2, 30, 2, 128).transpose(0, 3, 2, 1))
        in_maps.append(m)
    if "nc" not in _NC_CACHE:
        _NC_CACHE["nc"] = build_program()
    nc = _NC_CACHE["nc"]
    res = run_bass_kernel_spmd(nc, in_maps, core_ids=list(range(B)))
    R = res.results

    def st(name):
        return np.stack([np.asarray(R[b][name], np.float32) for b in range(B)], axis=0)
    y_prompt = st("yp")
    y_sample = st("ys")
    nk_p = st("nkp").transpose(1, 0, 2, 3).reshape(2, B, 2048, 4, 2, 64)
    nv_p = st("nvp").transpose(1, 0, 2, 3).reshape(2, B, 2048, 4, 128)
    nk_s = st("nks").transpose(1, 0, 2, 3).reshape(2, B, 32, 4, 2, 64)
    nv_s = st("nvs").transpose(1, 0, 2, 3).reshape(2, B, 32, 4, 128)

    def unT(name, T):
        a = st(name)
        return np.ascontiguousarray(a.transpose(1, 0, 4, 3, 2).reshape(2, B, T, 256))
    np_p = unT("npp", 15); nc_p = unT("ncp", 30); np_s = unT("nps", 15); nc_s = unT("ncs", 30)
    return (y_prompt, y_sample, np.ascontiguousarray(nk_p), np.ascontiguousarray(nv_p), np_p, nc_p,
            np.ascontiguousarray(nk_s), np.ascontiguousarray(nv_s), np_s, nc_s)
```

```python
import math
from contextlib import ExitStack

import numpy as np
import concourse.bass as bass
import concourse.mybir as mybir
from concourse.bass_utils import run_bass_kernel_spmd

F32 = mybir.dt.float32
BF16 = mybir.dt.bfloat16
ALU = mybir.AluOpType
AF = mybir.ActivationFunctionType

EPS = 1e-6
SCALE = 0.125
NT = 17
NTOK = 2080
DEPTH = 2
NPV = 73
FFN_PASSES = [(0, 8), (8, 15), (15, 22)]
GROUPS = [(0, 512, [0, 1, 2, 3]), (512, 512, [4, 5, 6, 7]), (1024, 512, [8, 9, 10, 11]),
          (1536, 512, [12, 13, 14, 15]), (2048, 32, [16])]
NEG = -30000.0


def TP(t):
    return 128 if t < 16 else 32


import os
STOP_AFTER = 0


class _Stop(Exception):
    pass


class Sched:
    def __init__(self, nc, es):
        self.nc = nc
        self.es = es
        self.eng = {'pe': nc.tensor, 'act': nc.scalar, 'dve': nc.vector, 'pool': nc.gpsimd, 'sp': nc.sync}
        self.csem = {e: es.enter_context(nc.semaphore('c_' + e)) for e in ('pe', 'act', 'dve', 'pool')}
        self.ccount = {e: 0 for e in self.csem}
        self.dsem = {}
        self.lastw = {}
        self.readers = {}
        self.waited = {e: {} for e in self.eng}
        self.out_events = []
        self.stopped = False
        self.nbar = 0

    def _wait(self, eng, ev):
        sem, val, src, kind, sid = ev
        w = self.waited[eng]
        if w.get(sid, 0) >= val:
            return
        w[sid] = val
        self.eng[eng].wait_ge(sem, val)

    def _deps(self, eng, reads, writes):
        deps = []
        for k in reads:
            ev = self.lastw.get(k)
            if ev is not None:
                deps.append((ev, True))
        for k in writes:
            ev = self.lastw.get(k)
            if ev is not None:
                deps.append((ev, False))
            for ev in self.readers.get(k, {}).values():
                deps.append((ev, False))
        for ev, raw in deps:
            if ev[3] == 'c' and ev[2] == eng:
                if eng == 'pe' or not raw:
                    continue
            self._wait(eng, ev)

    def _commit(self, ev, reads, writes):
        for k in writes:
            self.lastw[k] = ev
            self.readers[k] = {}
        for k in reads:
            self.readers.setdefault(k, {})[ev[4]] = ev

    def op(self, eng, fn, reads=(), writes=()):
        if self.stopped:
            return
        self._deps(eng, reads, writes)
        ins = fn(self.eng[eng])
        self.ccount[eng] += 1
        ins.then_inc(self.csem[eng], 1)
        ev = (self.csem[eng], self.ccount[eng], eng, 'c', 'c_' + eng)
        self._commit(ev, reads, writes)

    def dma(self, q, key, out, in_, reads=(), writes=(), is_out=False):
        if self.stopped:
            return
        self._deps(q, reads, writes)
        if key not in self.dsem:
            self.dsem[key] = [self.es.enter_context(self.nc.semaphore('d%d' % len(self.dsem))), 0]
        d = self.dsem[key]
        d[1] += 16
        self.eng[q].dma_start(out=out, in_=in_).then_inc(d[0], 16)
        ev = (d[0], d[1], q, 'd', 'd_' + str(key))
        self._commit(ev, reads, writes)
        if is_out:
            self.out_events.append(ev)

    def barrier(self):
        if self.stopped:
            return
        self.nbar += 1
        self._barrier()
        if self.nbar == STOP_AFTER:
            self.stopped = True

    def _barrier(self):
        evs = [(self.csem[e], self.ccount[e], e, 'c', 'c_' + e) for e in self.csem if self.ccount[e] > 0]
        evs += [(d[0], d[1], 'x', 'd', 'd_' + str(k)) for k, d in self.dsem.items() if d[1] > 0]
        for eng in self.eng:
            for ev in evs:
                if ev[3] == 'c' and ev[2] == eng:
                    continue
                self._wait(eng, ev)
        self.lastw = {}
        self.readers = {}

    def finish(self):
        for ev in self.out_events:
            self._wait('sp', ev)


def build_program():
    nc = bass.Bass("TRN2", target_bir_lowering=False)

    def din(name, shape):
        return nc.dram_tensor(name, list(shape), F32, kind="ExternalInput").ap()

    def dout(name, shape):
        return nc.dram_tensor(name, list(shape), F32, kind="ExternalOutput").ap()

    xp = din("xp", [2048, 1024]); xs = din("xs", [32, 1024])
    ck = din("ck", [2, 2048, 512]); cv = din("cv", [2, 2048, 512])
    stp = din("stp", [2, 128, 2, 15]); stc = din("stc", [2, 128, 2, 30])
    w_in = din("w_in", [2, 1024, 2304]); w_out = din("w_out", [2, 1024, 1024])
    w_gu = din("w_gu", [2, 1024, 5632]); w_dn = din("w_dn", [2, 2816, 1024])
    conv_pw = din("conv_pw", [2, 256, 256]); plw = din("plw", [2, 128, 2, 128])
    gmix = din("gmix", [2, 1024]); gffn = din("gffn", [2, 1024]); gfin = din("gfin", [1, 1024])
    pvec = din("pvec", [2, 128, NPV]); lq = din("lq", [2, 256])
    ident = din("ident", [128, 128]); cos_t = din("cos_t", [128, 17, 32]); sin_t = din("sin_t", [128, 17, 32])
    invc = din("invc", [128, 2, 16])

    yp = dout("yp", [2048, 1024]); ys = dout("ys", [32, 1024])
    nkp = dout("nkp", [2, 2048, 512]); nvp = dout("nvp", [2, 2048, 512])
    npp = dout("npp", [2, 128, 2, 15]); ncp = dout("ncp", [2, 128, 2, 30])
    nks = dout("nks", [2, 32, 512]); nvs = dout("nvs", [2, 32, 512])
    nps = dout("nps", [2, 128, 2, 15]); ncs = dout("ncs", [2, 128, 2, 30])

    def rows_y(t):
        return yp[t * 128:(t + 1) * 128, :] if t < 16 else ys[0:32, :]

    def rows_k(l, t):
        return nkp[l, t * 128:(t + 1) * 128, :] if t < 16 else nks[l, 0:32, :]

    def rows_v(l, t):
        return nvp[l, t * 128:(t + 1) * 128, :] if t < 16 else nvs[l, 0:32, :]

    with ExitStack() as es:
        S = Sched(nc, es)

        uid = [0]

        def sb(stack, name, shape, dt):
            uid[0] += 1
            return stack.enter_context(nc.sbuf_tensor("%s_%d" % (name, uid[0]), list(shape), dt))

        PS = es.enter_context(nc.psum_tensor("PS", [128, 8, 512], F32))

        def psb(b):
            return PS[:, b, :].bitcast(BF16)

        X = sb(es, "X", [128, NT, 1024], F32)
        H = sb(es, "H", [128, 8, NTOK], BF16)
        FA = sb(es, "FA", [128, 25344], BF16)
        QT = FA[:, 0:8320].rearrange("p (h n) -> p h n", h=4)
        KT = FA[:, 8320:16640].rearrange("p (h n) -> p h n", h=4)
        VB = FA[:, 16640:25344].rearrange("p (t n) -> p t n", t=NT)
        ACTH = FA[:, 0:16640].rearrange("p (k n) -> p k n", k=8)
        idb = sb(es, "idb", [128, 128], BF16)
        onesb = sb(es, "onesb", [128, 128], BF16)
        onesf = sb(es, "onesf", [128, 3, 128], F32)
        PV = sb(es, "PV", [128, 2, NPV], F32)
        LAM = sb(es, "LAM", [128, 2, 8], F32)
        LQ = sb(es, "LQ", [128, 2, 256], F32)
        ljunk = sb(es, "ljunk", [128, 64], F32)

        S.dma('pool', 'c_id', out=idb[:], in_=ident[:, :], writes=['idb'])
        S.dma('sp', 'c_pv', out=PV[:], in_=pvec.rearrange("l p n -> p l n"), writes=['PV'])
        S.dma('sp', 'c_lq', out=LQ[:, 0, :], in_=lq[0:1, :].partition_broadcast(128), writes=[('LQ', 0)])
        S.dma('sp', 'c_lq1', out=LQ[:, 1, :], in_=lq[1:2, :].partition_broadcast(128), writes=[('LQ', 1)])
        S.dma('sp', 'xs', out=X[:32, 16, :], in_=xs[:, :], writes=[('X', 16)])
        xpv = xp.rearrange("(t p) d -> p t d", p=128)
        for q4 in range(4):
            S.dma('sp', 'x%d' % q4, out=X[:, q4 * 4:(q4 + 1) * 4, :], in_=xpv[:, q4 * 4:(q4 + 1) * 4, :],
                  writes=[('X', t) for t in range(q4 * 4, q4 * 4 + 4)])
        S.op('dve', lambda e: e.memset(onesb[:], 1.0), writes=['onesb'])
        S.op('dve', lambda e: e.memset(onesf[:, 0, :], 1.0 / 128), writes=['onesf0'])
        S.op('dve', lambda e: e.memset(onesf[:, 1, :], 1.0 / 256), writes=['onesf1'])
        S.op('dve', lambda e: e.memset(onesf[:, 2, :], 1.0), writes=['onesf2'])
        for l in range(DEPTH):
            lam_init = 0.8 - 0.6 * math.exp(-0.3 * l)
            S.op('dve', lambda e: e.scalar_tensor_tensor(out=ljunk[:], in0=LQ[:, l, 0:64], scalar=1.0, in1=LQ[:, l, 64:128],
                                                         op0=ALU.mult, op1=ALU.mult, accum_out=LAM[:, l, 0:1]),
                 reads=[('LQ', l)], writes=['ljunk', ('LAM', l, 0)])
            S.op('dve', lambda e: e.scalar_tensor_tensor(out=ljunk[:], in0=LQ[:, l, 128:192], scalar=1.0, in1=LQ[:, l, 192:256],
                                                         op0=ALU.mult, op1=ALU.mult, accum_out=LAM[:, l, 1:2]),
                 reads=[('LQ', l)], writes=['ljunk', ('LAM', l, 1)])
            S.op('act', lambda e: e.activation(out=LAM[:, l, 2:4], in_=LAM[:, l, 0:2], func=AF.Exp),
                 reads=[('LAM', l, 0), ('LAM', l, 1)], writes=[('LAM', l, 2)])
            S.op('dve', lambda e: e.tensor_tensor(out=LAM[:, l, 6:7], in0=LAM[:, l, 3:4], in1=LAM[:, l, 2:3], op=ALU.subtract),
                 reads=[('LAM', l, 2)], writes=[('LAM', l, 6)])
            S.op('dve', lambda e: e.tensor_scalar(out=LAM[:, l, 4:5], in0=LAM[:, l, 6:7], scalar1=-lam_init, scalar2=None, op0=ALU.add),
                 reads=[('LAM', l, 6)], writes=[('LAM', l, 4)])
            S.op('dve', lambda e: e.tensor_scalar(out=LAM[:, l, 5:6], in0=PV[:, l, 8:9], scalar1=(1.0 - lam_init), scalar2=None, op0=ALU.mult),
                 reads=['PV'], writes=[('LAM', l, 5)])
        S._barrier()

        def rstd_chain(stt, P, t, src_key):
            S.op('dve', lambda e: e.tensor_scalar(out=stt[:P, t, 1:2], in0=stt[:P, t, 0:1], scalar1=1.0 / 1024, scalar2=EPS,
                                                  op0=ALU.mult, op1=ALU.add),
                 reads=[('st', t, 0)], writes=[('st', t, 1)])
            S.op('act', lambda e: e.activation(out=stt[:P, t, 2:3], in_=stt[:P, t, 1:2], func=AF.Ln),
                 reads=[('st', t, 1)], writes=[('st', t, 2)])
            S.op('act', lambda e: e.activation(out=stt[:P, t, 3:4], in_=stt[:P, t, 2:3], func=AF.Exp, scale=-0.5),
                 reads=[('st', t, 2)], writes=[('st', t, 3)])

        def phase_norm(g_row, tgroups=None, stt_ext=None):
            tgroups = tgroups or [list(range(NT))]
            with ExitStack() as ph:
                G = sb(ph, "G", [128, 1024], F32)
                hb = [sb(ph, "hb%d" % i, [128, 1024], BF16) for i in range(3)]
                junk = sb(ph, "junk", [128, 1024], BF16)
                junk2 = sb(ph, "junk2", [128, 1024], BF16)
                stt = sb(ph, "nst", [128, 4, NT], F32) if stt_ext is None else stt_ext
                S.dma('sp', 'G', out=G[:], in_=g_row.partition_broadcast(128), writes=['G'])
                if stt_ext is None:
                    S.op('dve', lambda e: e.memset(stt[:, 0, :], 1.0), writes=['st0i'])
                for tg in tgroups:
                    t0g, t1g = tg[0], tg[-1] + 1
                    for t in (tg if stt_ext is None else []):
                        P = TP(t)
                        if t % 2 == 0:
                            S.op('act', lambda e: e.activation(out=junk[:P], in_=X[:P, t, :], func=AF.Square, accum_out=stt[:P, 0, t:t + 1]),
                                 reads=[('X', t), 'st0i'], writes=['junk', ('st0', t)])
                        else:
                            S.op('dve', lambda e: e.scalar_tensor_tensor(out=junk2[:P], in0=X[:P, t, :], scalar=1.0, in1=X[:P, t, :],
                                                                         op0=ALU.mult, op1=ALU.mult, accum_out=stt[:P, 0, t:t + 1]),
                                 reads=[('X', t), 'st0i'], writes=['junk2', ('st0', t)])
                    S.op('dve', lambda e: e.tensor_scalar(out=stt[:, 1, t0g:t1g], in0=stt[:, 0, t0g:t1g], scalar1=1.0 / 1024, scalar2=EPS,
                                                          op0=ALU.mult, op1=ALU.add),
                         reads=[('st0', t) for t in tg], writes=[('st1', t0g)])
                    S.op('act', lambda e: e.activation(out=stt[:, 2, t0g:t1g], in_=stt[:, 1, t0g:t1g], func=AF.Ln),
                         reads=[('st1', t0g)], writes=[('st2', t0g)])
                    S.op('act', lambda e: e.activation(out=stt[:, 3, t0g:t1g], in_=stt[:, 2, t0g:t1g], func=AF.Exp, scale=-0.5),
                         reads=[('st2', t0g)], writes=[('st3', t0g)])
                    for t in tg:
                        P = TP(t); c = t * 128
                        hs = t % 3
                        S.op('dve', lambda e: e.scalar_tensor_tensor(out=hb[hs][:P], in0=X[:P, t, :], scalar=stt[:P, 3, t:t + 1], in1=G[:P],
                                                                     op0=ALU.mult, op1=ALU.mult),
                             reads=[('X', t), ('st3', t0g), 'G'], writes=[('hb', hs)])
                        bank = t % 4
                        pv = psb(bank).rearrange("p (k n) -> p k n", k=8)

                        def tr(e):
                            for kc in range(8):
                                ins = e.transpose(out=pv[:, kc, :P], in_=hb[hs][:P, kc * 128:(kc + 1) * 128], identity=idb[:P, :P])
                            return ins
                        S.op('pe', tr, reads=[('hb', hs), 'idb'], writes=[('ps', bank)])
                        S.op('act', lambda e: e.activation(out=H[:, :, c:c + P], in_=pv[:, :, :P], func=AF.Copy),
                             reads=[('ps', bank)], writes=[('H', t)])
                S.barrier()

        def load_slab(q, key, dst, src3, kcs, c0, w, wkey):
            S.dma(q, key, out=dst, in_=src3[:, kcs[0]:kcs[-1] + 1, c0:c0 + w], writes=[wkey])

        if True:
          for l in range(DEPTH):
            w_in3 = w_in[l].rearrange("(k p) c -> p k c", p=128)
            w_out3 = w_out[l].rearrange("(k p) c -> p k c", p=128)
            w_gu3 = w_gu[l].rearrange("(k p) c -> p k c", p=128)
            w_dn3 = w_dn[l].rearrange("(k p) c -> p k c", p=128)
            pw3 = conv_pw[l].rearrange("(k p) c -> p k c", p=128)

            WP = FA[:, 16640:22784].rearrange("p (k c) -> p k c", k=8)
            WOP = FA[:, 0:4096].rearrange("p (k c) -> p k c", k=4)
            if l == 0:
                S.dma('pool', 'wp0', out=WP[:, :, 0:256], in_=w_in3[:, :, 0:256], reads=[('X', 11)], writes=['WP0'])
                S.dma('pool', 'wp1', out=WP[:, :, 256:768], in_=w_in3[:, :, 1792:2304], writes=['WP1'])
            S.dma('pool', 'wop0', out=WOP[:, 0:2, :], in_=w_out3[:, 0:2, :], writes=['WOP0'])
            S.dma('pool', 'wop1', out=WOP[:, 2:4, :], in_=w_out3[:, 6:8, :], writes=['WOP1'])
            if l == 0:
                phase_norm(gmix[l:l + 1, :], [[0, 1, 2, 3], [4, 5, 6, 7], [8, 9, 10, 11], [12, 13, 14, 15, 16]])

            phq = ExitStack()
            Wq0 = sb(phq, "Wq0", [128, 8, 512], BF16)
            S.dma('pool', 'wq0', out=Wq0[:], in_=w_in3[:, :, 256:768], writes=[('W', 0)])
            with ExitStack() as ph:
                DG = FA[:, 4096:12032].rearrange("p (k c) -> p k c", k=62)
                CU = [FA[:, 12032 + i * 1084:12032 + (i + 1) * 1084].rearrange("p (j c) -> p j c", j=2) for i in range(2)]
                dsb = FA[:, 22784:23808].rearrange("p (j c) -> p j c", j=2)
                MPC2 = [FA[:, 14200:16248].rearrange("p (j c) -> p j c", j=4), sb(ph, "MPC1", [128, 4, 512], BF16)]
                sl2 = [FA[:, 23808:24832].rearrange("p (j c) -> p j c", j=2), sb(ph, "sl1", [128, 2, 512], BF16)]
                PW = sb(ph, "PW", [128, 2, 256], BF16)
                PLW = sb(ph, "PLW", [128, 2, 128], BF16)
                INVC = sb(ph, "INVC", [128, 2, 16], F32)
                PU = [sb(ph, "PU%d" % i, [128, 2, 527], F32) for i in range(2)]
                U32 = sb(ph, "U32", [128, 2, 512], F32)
                WA = sb(ph, "WA", [128, 526], F32); WB = sb(ph, "WB", [128, 524], F32)
                WC = WA; WD = WB
                t16 = sb(ph, "t16", [128, 16], F32)
                sg2 = [sb(ph, "sg%d" % i, [128, 512], F32) for i in range(2)]
                ysb = sb(ph, "ysb", [128, 2, 512], F32)
                ysq = sb(ph, "ysq", [128, 2, 512], F32)
                m2 = sb(ph, "m2", [128, 512], F32); var = sb(ph, "var", [128, 512], F32)

                S.dma('pool', 'pw', out=PW[:], in_=pw3, writes=['PW'])
                S.dma('pool', 'plw', out=PLW[:], in_=plw[l], writes=['PLW'])
                S.dma('sp', 'invc', out=INVC[:], in_=invc[:, :, :], writes=['INVC'])
                S.op('dve', lambda e: e.tensor_tensor(out=DG, in0=idb[:].unsqueeze(1).broadcast_to([128, 62, 128]),
                                                      in1=PV[:, l, 11:73].unsqueeze(2).broadcast_to([128, 62, 128]), op=ALU.mult),
                     reads=['PV', 'idb'], writes=['DG'])
                S.op('dve', lambda e: e.memset(PU[0][:, :, 0:15], 0.0), writes=[('PUc', 0)])
                S.op('dve', lambda e: e.memset(CU[0][:, :, 0:30], 0.0), writes=[('CUc', 0)])

                def stageA(gi):
                    c0, n, tiles = GROUPS[gi]
                    sample = (gi == 4)
                    slot = 0 if sample else gi % 2
                    if sample:
                        S.dma('sp', 'stp', out=PU[0][:, :, 0:15], in_=stp[l], writes=[('PUc', 0)])
                        S.dma('pool', 'stc', out=CU[0][:, :, 0:30], in_=stc[l], writes=[('CUc', 0)])
                    def proj(bank, col):
                        def f(e):
                            for kc in range(8):
                                ins = e.matmul(PS[:, bank, 0:n], lhsT=WP[:, kc, col:col + 128], rhs=H[:, kc, c0:c0 + n],
                                               start=(kc == 0), stop=(kc == 7))
                            return ins
                        S.op('pe', f, reads=['WP0', 'WP1'] + [('H', t) for t in tiles], writes=[('ps', bank)])
                    for j in range(2):
                        proj(j, j * 128)
                        S.op('act', lambda e: e.activation(out=PU[slot][:, j, 15:15 + n], in_=PS[:, j, 0:n], func=AF.Copy),
                             reads=[('ps', j)], writes=[('PUn', slot, j)])
                    if gi == 3 or sample:
                        S.dma('sp', 'o_np%d' % gi, out=(nps[l] if sample else npp[l]), in_=PU[slot][:, :, n:n + 15],
                              reads=[('PUc', slot), ('PUn', slot, 0), ('PUn', slot, 1)], is_out=True)
                    if gi < 3:
                        S.op('dve', lambda e: e.tensor_copy(out=PU[(gi + 1) % 2][:, :, 0:15], in_=PU[slot][:, :, 512:527]),
                             reads=[('PUn', slot, 0), ('PUn', slot, 1)], writes=[('PUc', (gi + 1) % 2)])
                    for j in range(2):
                        ext = PU[slot][:, j, :]
                        rk = [('PUc', slot), ('PUn', slot, j)]
                        S.op('dve', lambda e: e.tensor_tensor(out=WA[:, 0:14 + n], in0=ext[:, 1:15 + n], in1=ext[:, 0:14 + n], op=ALU.add),
                             reads=rk, writes=['WA'])
                        S.op('dve', lambda e: e.tensor_tensor(out=WB[:, 0:12 + n], in0=WA[:, 2:14 + n], in1=WA[:, 0:12 + n], op=ALU.add),
                             reads=['WA'], writes=['WB'])
                        if j == 0:
                            lo, lo_off, hi, hi_off = WA, 14, WB, 12
                            kk = ['WA', 'WB']
                        else:
                            S.op('dve', lambda e: e.tensor_tensor(out=WC[:, 0:8 + n], in0=WB[:, 4:12 + n], in1=WB[:, 0:8 + n], op=ALU.add),
                                 reads=['WB'], writes=['WA'])
                            S.op('dve', lambda e: e.tensor_tensor(out=WD[:, 0:n], in0=WC[:, 8:8 + n], in1=WC[:, 0:n], op=ALU.add),
                                 reads=['WA'], writes=['WB'])
                            lo, lo_off, hi, hi_off = WC, 8, WD, 0
                            kk = ['WA', 'WB']
                        for (src, off, p0) in ((lo, lo_off, 0), (hi, hi_off, 64)):
                            S.op('dve', lambda e: e.scalar_tensor_tensor(out=dsb[p0:p0 + 64, j, 0:n], in0=src[p0:p0 + 64, off:off + n],
                                                                         scalar=PV[p0:p0 + 64, l, 9 + j:10 + j], in1=ext[p0:p0 + 64, 15:15 + n],
                                                                         op0=ALU.mult, op1=ALU.subtract),
                                 reads=kk + rk + ['PV'], writes=[('dsb', j)])
                            if gi == 0:
                                S.op('dve', lambda e: e.tensor_tensor(out=t16[p0:p0 + 64, :], in0=src[p0:p0 + 64, off:off + 16],
                                                                      in1=INVC[p0:p0 + 64, j, :], op=ALU.mult),
                                     reads=kk + ['INVC'], writes=['t16'])
                                S.op('dve', lambda e: e.tensor_tensor(out=dsb[p0:p0 + 64, j, 0:16], in0=t16[p0:p0 + 64, :],
                                                                      in1=ext[p0:p0 + 64, 15:31], op=ALU.subtract),
                                     reads=['t16'] + rk, writes=[('dsb', j)])
                    for j in range(2):
                        proj(2 + 2 * j, 256 + j * 128)
                        proj(3 + 2 * j, 512 + j * 128)
                        S.op('act', lambda e: e.activation(out=sg2[j][:, 0:n], in_=PS[:, 3 + 2 * j, 0:n], func=AF.Sigmoid),
                             reads=[('ps', 3 + 2 * j)], writes=[('sg', j)])
                        S.op('dve', lambda e: e.tensor_tensor(out=CU[slot][:, j, 30:30 + n], in0=PS[:, 2 + 2 * j, 0:n], in1=sg2[j][:, 0:n], op=ALU.mult),
                             reads=[('ps', 2 + 2 * j), ('sg', j)], writes=[('CUn', slot, j)])
                        if gi == 3 or sample:
                            S.op('dve', lambda e: e.tensor_tensor(out=U32[:, j, n - 30:n], in0=PS[:, 2 + 2 * j, n - 30:n], in1=sg2[j][:, n - 30:n], op=ALU.mult),
                                 reads=[('ps', 2 + 2 * j), ('sg', j)], writes=[('U32', j)])
                    if gi == 3 or sample:
                        S.dma('sp', 'o_nc%d' % gi, out=(ncs[l] if sample else ncp[l]), in_=U32[:, :, n - 30:n],
                              reads=[('U32', 0), ('U32', 1)], is_out=True)
                    if gi < 3:
                        S.op('dve', lambda e: e.tensor_copy(out=CU[(gi + 1) % 2][:, :, 0:30], in_=CU[slot][:, :, 512:542]),
                             reads=[('CUn', slot, 0), ('CUn', slot, 1)], writes=[('CUc', (gi + 1) % 2)])

                def stageC(gi):
                    c0, n, tiles = GROUPS[gi]
                    sample = (gi == 4)
                    slot = 0 if sample else gi % 2
                    MPC = MPC2[gi % 2]; sl = sl2[gi % 2]
                    for j in range(2):
                        S.op('pe', lambda e: e.matmul(PS[:, j, 0:n], lhsT=PLW[:, j, :], rhs=dsb[:, j, 0:n], start=True, stop=True),
                             reads=[('dsb', j), 'PLW'], writes=[('ps', j)])
                        S.op('act', lambda e: e.activation(out=MPC[:, j, 0:n], in_=PS[:, j, 0:n], func=AF.Copy, scale=PV[:, l, j:j + 1]),
                             reads=[('ps', j), 'PV'], writes=[('MPC', gi % 2, j)])
                    for j in range(2):
                        def cv_mm(e):
                            for tap in range(31):
                                ins = e.matmul(PS[:, 4 + j, 0:n], lhsT=DG[:, j * 31 + tap, :], rhs=CU[slot][:, j, tap:tap + n],
                                               start=(tap == 0), stop=(tap == 30))
                            return ins
                        S.op('pe', cv_mm, reads=['DG', ('CUc', slot), ('CUn', slot, j)], writes=[('ps', 4 + j)])
                        S.op('act', lambda e: e.activation(out=ysb[:, j, 0:n], in_=PS[:, 4 + j, 0:n], func=AF.Identity, bias=PV[:, l, 2 + j:3 + j]),
                             reads=[('ps', 4 + j), 'PV'], writes=[('ysb', j)])
                        S.op('act', lambda e: e.activation(out=ysq[:, j, 0:n], in_=PS[:, 4 + j, 0:n], func=AF.Square, bias=PV[:, l, 2 + j:3 + j]),
                             reads=[('ps', 4 + j), 'PV'], writes=[('ysq', j)])

                    def st_mm(e):
                        for j in range(2):
                            e.matmul(PS[:, 6, 0:n], lhsT=onesf[:, 1, :], rhs=ysb[:, j, 0:n], start=(j == 0), stop=(j == 1))
                        for j in range(2):
                            ins = e.matmul(PS[:, 7, 0:n], lhsT=onesf[:, 1, :], rhs=ysq[:, j, 0:n], start=(j == 0), stop=(j == 1))
                        return ins
                    S.op('pe', st_mm, reads=[('ysb', 0), ('ysb', 1), ('ysq', 0), ('ysq', 1), 'onesf1'], writes=[('ps', 6), ('ps', 7)])
                    S.op('act', lambda e: e.activation(out=m2[:, 0:n], in_=PS[:, 6, 0:n], func=AF.Square), reads=[('ps', 6)], writes=['m2'])
                    S.op('dve', lambda e: e.scalar_tensor_tensor(out=var[:, 0:n], in0=PS[:, 7, 0:n], scalar=EPS, in1=m2[:, 0:n],
                                                                 op0=ALU.add, op1=ALU.subtract),
                         reads=[('ps', 7), 'm2'], writes=['var'])
                    S.op('act', lambda e: e.activation(out=m2[:, 0:n], in_=var[:, 0:n], func=AF.Ln), reads=['var'], writes=['m2'])
                    S.op('act', lambda e: e.activation(out=var[:, 0:n], in_=m2[:, 0:n], func=AF.Exp, scale=-0.5), reads=['m2'], writes=['var'])
                    for j in range(2):
                        S.op('dve', lambda e: e.tensor_tensor(out=ysq[:, j, 0:n], in0=ysb[:, j, 0:n], in1=PS[:, 6, 0:n], op=ALU.subtract),
                             reads=[('ysb', j), ('ps', 6)], writes=[('ysq', j)])
                        S.op('dve', lambda e: e.tensor_tensor(out=ysb[:, j, 0:n], in0=ysq[:, j, 0:n], in1=var[:, 0:n], op=ALU.mult),
                             reads=[('ysq', j), 'var'], writes=[('ysb', j)])
                        S.op('act', lambda e: e.activation(out=sl[:, j, 0:n], in_=ysb[:, j, 0:n], func=AF.Silu,
                                                           scale=PV[:, l, 4 + j:5 + j], bias=PV[:, l, 6 + j:7 + j]),
                             reads=[('ysb', j), 'PV'], writes=[('sl', gi % 2, j)])

                def stageB(gi):
                    c0, n, tiles = GROUPS[gi]
                    MPC = MPC2[gi % 2]; sl = sl2[gi % 2]
                    for jo in range(2):
                        def pw_mm(e):
                            for j in range(2):
                                ins = e.matmul(PS[:, jo, 0:n], lhsT=PW[:, j, jo * 128:(jo + 1) * 128], rhs=sl[:, j, 0:n],
                                               start=(j == 0), stop=(j == 1))
                            return ins
                        S.op('pe', pw_mm, reads=[('sl', gi % 2, 0), ('sl', gi % 2, 1), 'PW'], writes=[('ps', jo)])
                        S.op('act', lambda e: e.activation(out=MPC[:, 2 + jo, 0:n], in_=PS[:, jo, 0:n], func=AF.Copy),
                             reads=[('ps', jo)], writes=[('MPC', gi % 2, 2 + jo)])
                    for ti, t in enumerate(tiles):
                        P = TP(t)
                        for hh in range(2):
                            bank = (6, 7, 0, 1)[(2 * ti + hh) % 4]

                            def wo_mm(e):
                                for kc in range(4):
                                    ins = e.matmul(PS[:P, bank, :], lhsT=MPC[:, kc, ti * 128:ti * 128 + P],
                                                   rhs=WOP[:, kc, hh * 512:(hh + 1) * 512], start=(kc == 0), stop=(kc == 3))
                                return ins
                            S.op('pe', wo_mm, reads=[('MPC', gi % 2, k) for k in range(4)] + ['WOP0', 'WOP1'], writes=[('ps', bank)])
                            S.op('dve', lambda e: e.tensor_tensor(out=X[:P, t, hh * 512:(hh + 1) * 512], in0=X[:P, t, hh * 512:(hh + 1) * 512],
                                                                  in1=PS[:P, bank, :], op=ALU.add),
                                 reads=[('ps', bank), ('X', t)], writes=[('X', t)])

                for gi in range(len(GROUPS)):
                    stageA(gi)
                    if gi > 0:
                        stageB(gi - 1)
                    stageC(gi)
                stageB(len(GROUPS) - 1)
                S.barrier()

            with ExitStack() as ph:
                W = [Wq0] + [sb(ph, "Wq%d" % i, [128, 8, 512], BF16) for i in (1, 2)]
                COS = sb(ph, "COS", [128, 17, 32], F32); SIN = sb(ph, "SIN", [128, 17, 32], F32)
                zs = [sb(ph, "zs%d" % i, [128, 512], F32) for i in range(3)]
                kr = [sb(ph, "kr%d" % i, [128, 512], F32) for i in range(2)]
                t1 = [sb(ph, "t1%d" % i, [128, 512], F32) for i in range(2)]
                t2 = [sb(ph, "t2%d" % i, [128, 512], F32) for i in range(2)]
                qb = [sb(ph, "qb%d" % i, [128, 512], BF16) for i in range(6)]
                S.dma('sp', 'cos', out=COS[:], in_=cos_t[:, :, :], writes=['COS'])
                S.dma('sp', 'sin', out=SIN[:], in_=sin_t[:, :, :], writes=['SIN'])
                for pi, c0 in ((1, 768), (2, 1280)):
                    S.dma('pool', 'wq%d' % pi, out=W[pi][:], in_=w_in3[:, :, c0:c0 + 512], writes=[('W', pi)])
                pend = []
                cnt = 0
                rc = 0
                NQ0 = 6
                items = [(t, 0) for t in range(NQ0)] + [(t, pi) for t in range(NQ0) for pi in (1, 2)] + \
                        [(t, pi) for t in range(NQ0, NT) for pi in (0, 1, 2)]
                for (t, pi) in items:
                    P = TP(t); c = t * 128
                    nm = ('q', 'k', 'v')[pi]
                    if True:
                        Wc = W[pi]
                        bank = cnt % 4
                        s2 = cnt % 3
                        cnt += 1
                        qs = rc % 6

                        def mm(e):
                            for kc in range(8):
                                ins = e.matmul(PS[:P, bank, :], lhsT=H[:, kc, c:c + P], rhs=Wc[:, kc, :], start=(kc == 0), stop=(kc == 7))
                            return ins
                        S.op('pe', mm, reads=[('H', t), ('W', pi)], writes=[('ps', bank)])
                        while len(pend) > 3:
                            pend.pop(0)()
                        if nm == 'v':
                            S.op('act', lambda e: e.activation(out=zs[s2][:P], in_=PS[:P, bank, :], func=AF.Copy),
                                 reads=[('ps', bank)], writes=[('zs', s2)])
                            S.dma('sp', 'o_v%d' % s2, out=rows_v(l, t), in_=zs[s2][:P], reads=[('zs', s2)], is_out=True)
                            S.op('act', lambda e: e.activation(out=VB[:P, t, :], in_=zs[s2][:P], func=AF.Copy), reads=[('zs', s2)], writes=[('VB', t)])
                            continue
                        r2 = rc % 2
                        rc += 1
                        z4 = PS[:P, bank, :].rearrange("p (g two i) -> p g two i", g=8, two=2)
                        cosb = COS[:P, t:t + 1, :].unsqueeze(1).broadcast_to([P, 8, 2, 32])
                        sinb = SIN[:P, t:t + 1, :].broadcast_to([P, 8, 32])
                        t14 = t1[r2][:P].rearrange("p (g two i) -> p g two i", g=8, two=2)
                        t24 = t2[r2][:P].rearrange("p (g two i) -> p g two i", g=8, two=2)
                        S.op('dve', lambda e: e.tensor_tensor(out=t14, in0=z4, in1=cosb, op=ALU.mult),
                             reads=[('ps', bank), 'COS'], writes=[('t1', r2)])
                        S.op('dve', lambda e: e.tensor_tensor(out=t24[:, :, 0, :], in0=z4[:, :, 1, :], in1=sinb, op=ALU.mult),
                             reads=[('ps', bank), 'SIN'], writes=[('t2a', r2)])
                        S.op('dve', lambda e: e.tensor_tensor(out=t24[:, :, 1, :], in0=z4[:, :, 0, :], in1=sinb, op=ALU.mult),
                             reads=[('ps', bank), 'SIN'], writes=[('t2b', r2)])
                        if nm == 'q':
                            o4 = qb[qs][:P].rearrange("p (g two i) -> p g two i", g=8, two=2)
                        else:
                            o4 = kr[r2][:P].rearrange("p (g two i) -> p g two i", g=8, two=2)
                        okey = ('qb', qs) if nm == 'q' else ('kr', r2)
                        S.op('dve', lambda e: e.tensor_tensor(out=o4[:, :, 0, :], in0=t14[:, :, 0, :], in1=t24[:, :, 0, :], op=ALU.subtract),
                             reads=[('t1', r2), ('t2a', r2)], writes=[(okey, 'a')])
                        S.op('dve', lambda e: e.tensor_tensor(out=o4[:, :, 1, :], in0=t14[:, :, 1, :], in1=t24[:, :, 1, :], op=ALU.add),
                             reads=[('t1', r2), ('t2b', r2)], writes=[(okey, 'b')])
                        if nm == 'k':
                            S.dma('sp', 'o_k%d' % r2, out=rows_k(l, t), in_=kr[r2][:P], reads=[(okey, 'a'), (okey, 'b')], is_out=True)
                            S.op('act', lambda e: e.activation(out=qb[qs][:P], in_=kr[r2][:P], func=AF.Copy),
                                 reads=[(okey, 'a'), (okey, 'b')], writes=[(('qb', qs), 'a'), (('qb', qs), 'b')])

                        def mk(t=t, P=P, c=c, qs=qs, nm=nm, tb=4 + (rc % 4)):
                            def run():
                                pv = psb(tb).rearrange("p (k n) -> p k n", k=8)

                                def tr(e):
                                    for h in range(4):
                                        ins = e.transpose(out=pv[:, h, :P], in_=qb[qs][:P, h * 128:(h + 1) * 128], identity=idb[:P, :P])
                                    return ins
                                S.op('pe', tr, reads=[(('qb', qs), 'a'), (('qb', qs), 'b'), 'idb'], writes=[('ps', tb)])
                                dst = QT if nm == 'q' else KT
                                S.op('act', lambda e: e.activation(out=dst[:, :, c:c + P], in_=pv[:, 0:4, :P], func=AF.Copy),
                                     reads=[('ps', tb)], writes=[(nm + 'T', t)])
                            return run
                        pend.append(mk())
                while pend:
                    pend.pop(0)()
                S.barrier()
            phq.close()

            phw = ExitStack()
            WOA = sb(phw, "WOA", [128, 4, 1024], BF16)
            S.dma('pool', 'woa', out=WOA[:], in_=w_out3[:, 2:6, :], writes=['WOA'])
            ckb = [sb(phw, "ckb%d" % i, [128, 512], BF16) for i in range(4)]
            cvb = [sb(phw, "cvb%d" % i, [128, 512], BF16) for i in range(4)]
            for i in range(4):
                S.dma('pool', 'ck%d' % i, out=ckb[i][:], in_=ck[l, i * 128:(i + 1) * 128, :], writes=[('ckb', i)])
                S.dma('pool', 'cv%d' % i, out=cvb[i][:], in_=cv[l, i * 128:(i + 1) * 128, :], writes=[('cvb', i)])
            with ExitStack() as ph:
                PT = [sb(ph, "PT%d" % i, [128, 2, 512], BF16) for i in range(3)]
                OSB = [sb(ph, "OSB%d" % i, [128, 2, 512], F32) for i in range(2)]
                LNL = [sb(ph, "LNL%d" % i, [128, 2, 512], F32) for i in range(2)]
                dd = sb(ph, "dd", [128, 512], F32); sq = sb(ph, "sq", [128, 512], F32)
                msb = sb(ph, "msb", [128, 512], F32); rs = sb(ph, "rs", [128, 512], F32)
                MB = sb(ph, "MB", [128, 1], F32)
                S.op('dve', lambda e: e.memset(MB[0:64, :], 0.0), writes=['MBa'])
                S.op('dve', lambda e: e.memset(MB[64:128, :], NEG), writes=['MBb'])
                negl = LAM[:, l, 4:5]; gsc = LAM[:, l, 5:6]
                blocks = []
                for n_it, (h, j) in enumerate([(h, j) for h in range(4) for j in range(4)]):
                    for i in range(4 * j + 4):
                        blocks.append((n_it, h, j, i))
                NBK = len(blocks)

                def off_of(j, i):
                    r = i - 4 * j
                    return 128 * r if r > 0 else 0

                def qk(g):
                    n_it, h, j, i = blocks[g]
                    slot = g % 2
                    off = off_of(j, i)
                    q0 = j * 512

                    def f(e):
                        for c in range(2):
                            ins = e.matmul(PS[:, 2 * slot + c, off:512], lhsT=KT[c * 64:(c + 1) * 64, h, i * 128:(i + 1) * 128],
                                           rhs=QT[c * 64:(c + 1) * 64, h, q0 + off:q0 + 512], start=True, stop=True)
                        return ins
                    S.op('pe', f, reads=[('QT', h, j)], writes=[('ps', 2 * slot), ('ps', 2 * slot + 1)])

                def ex(g):
                    n_it, h, j, i = blocks[g]
                    slot = g % 2
                    off = off_of(j, i)
                    ps_ = g % 3
                    diag = (i - 4 * j) >= 0
                    src = PS[:, 2 * slot:2 * slot + 2, :]
                    if diag:
                        S.op('act', lambda e: e.activation(out=PT[ps_][:, :, off:off + 64], in_=src[:, :, off:off + 64], func=AF.Exp,
                                                           scale=SCALE, bias=MB[:, 0:1]),
                             reads=[('ps', 2 * slot), ('ps', 2 * slot + 1), 'MBa', 'MBb'], writes=[('PT', ps_, 'm')])
                        S.op('act', lambda e: e.activation(out=PT[ps_][:, :, off + 64:512], in_=src[:, :, off + 64:512], func=AF.Exp,
                                                           scale=SCALE),
                             reads=[('ps', 2 * slot), ('ps', 2 * slot + 1)], writes=[('PT', ps_)])
                    else:
                        S.op('act', lambda e: e.activation(out=PT[ps_][:, :, :], in_=src, func=AF.Exp, scale=SCALE),
                             reads=[('ps', 2 * slot), ('ps', 2 * slot + 1)], writes=[('PT', ps_), ('PT', ps_, 'm')])

                def pvm(g):
                    n_it, h, j, i = blocks[g]
                    off = off_of(j, i)
                    ps_ = g % 3
                    nb = 4 * j + 4

                    def f(e):
                        for c in range(2):
                            e.matmul(PS[:, 4 + c, off:512], lhsT=VB[:, i, h * 128:(h + 1) * 128], rhs=PT[ps_][:, c, off:512],
                                     start=(i == 0), stop=(i == nb - 1))
                        for c in range(2):
                            ins = e.matmul(PS[:, 6 + c, off:512], lhsT=onesb[:], rhs=PT[ps_][:, c, off:512],
                                           start=(i == 0), stop=(i == nb - 1))
                        return ins
                    S.op('pe', f, reads=[('PT', ps_), ('PT', ps_, 'm'), 'onesb'], writes=[('ps', 4), ('ps', 5), ('ps', 6), ('ps', 7)])

                def mkB(st):
                    def run(sbk):
                        S.op('act', lambda e: e.activation(out=LNL[st][:], in_=LNL[st][:], func=AF.Exp, scale=-1.0),
                             reads=[('LNL', st)], writes=[('LNL', st)])
                        S.op('dve', lambda e: e.tensor_tensor(out=OSB[st][:], in0=OSB[st][:], in1=LNL[st][:], op=ALU.mult),
                             reads=[('OSB', st), ('LNL', st)], writes=[('OSB', st)])
                        S.op('dve', lambda e: e.scalar_tensor_tensor(out=dd[:], in0=OSB[st][:, 1, :], scalar=negl, in1=OSB[st][:, 0, :],
                                                                     op0=ALU.mult, op1=ALU.add),
                             reads=[('OSB', st)], writes=['dd'])
                        S.op('dve', lambda e: e.tensor_tensor(out=sq[:], in0=dd[:], in1=dd[:], op=ALU.mult), reads=['dd'], writes=['sq'])
                    return run

                def mkC1():
                    def run(sbk):
                        S.op('pe', lambda e: e.matmul(PS[:, sbk, :], lhsT=onesf[:, 0, :], rhs=sq[:], start=True, stop=True),
                             reads=['sq', 'onesf0'], writes=[('ps', sbk)])
                        S.op('dve', lambda e: e.tensor_scalar(out=msb[:], in0=PS[:, sbk, :], scalar1=EPS, scalar2=None, op0=ALU.add),
                             reads=[('ps', sbk)], writes=['msb'])
                    return run

                def mkC2(h, j):
                    q0 = j * 512

                    def run():
                        S.op('act', lambda e: e.activation(out=msb[:], in_=msb[:], func=AF.Ln), reads=['msb'], writes=['msb'])
                        S.op('act', lambda e: e.activation(out=rs[:], in_=msb[:], func=AF.Exp, scale=-0.5), reads=['msb'], writes=['rs'])
                        S.op('dve', lambda e: e.scalar_tensor_tensor(out=QT[:, h, q0:q0 + 512], in0=dd[:], scalar=gsc, in1=rs[:],
                                                                     op0=ALU.mult, op1=ALU.mult),
                             reads=['dd', 'rs'], writes=[('QT', h, j)])
                    return run

                pendB = None
                savedC = None
                pendC2 = None
                qk(0)
                qk(1)
                for g in range(NBK):
                    n_it, h, j, i = blocks[g]
                    nb = 4 * j + 4
                    st = n_it % 2
                    ex(g)
                    if pendC2 is not None:
                        pendC2()
                        pendC2 = None
                    pvm(g)
                    if i == nb - 1:
                        S.op('dve', lambda e: e.tensor_copy(out=OSB[st][:], in_=PS[:, 4:6, :]),
                             reads=[('ps', 4), ('ps', 5)], writes=[('OSB', st)])
                        S.op('act', lambda e: e.activation(out=LNL[st][:], in_=PS[:, 6:8, :], func=AF.Ln),
                             reads=[('ps', 6), ('ps', 7)], writes=[('LNL', st)])
                        if savedC is not None:
                            savedC[0](2 * (g % 2))
                            pendC2 = savedC[1]
                        savedC = (mkC1(), mkC2(h, j))
                        pendB = (g + 2, mkB(st))
                    if pendB is not None and pendB[0] <= g:
                        pendB[1](0)
                        pendB = None
                    if g + 2 < NBK:
                        qk(g + 2)
                if pendC2 is not None:
                    pendC2()
                if pendB is not None:
                    pendB[1](0)
                savedC[0](0); savedC[1]()
                S.barrier()

            phf = ExitStack()
            GU = [sb(phf, "GU%d" % i, [128, 2, 8, 256], BF16) for i in range(2)]
            sttF = sb(phf, "sttF", [128, 4, NT], F32)
            junkF = GU[1][:].rearrange("p a b c -> p (a b c)")[:, 0:1024]
            S.op('dve', lambda e: e.memset(sttF[:, 0, :], 1.0), writes=['stF0i'])

            def issue_gu_abs(fs, w, slot):
                S.dma('pool', 'gug%d' % slot, out=GU[slot][:, 0, :, 0:w * 128], in_=w_gu3[:, :, fs * 128:(fs + w) * 128],
                      writes=[('GUg', slot)])
                S.dma('pool', 'guu%d' % slot, out=GU[slot][:, 1, :, 0:w * 128], in_=w_gu3[:, :, 2816 + fs * 128:2816 + (fs + w) * 128],
                      writes=[('GUu', slot)])
            issue_gu_abs(0, 2, 0)
            def wo_tile(t, banks):
                P = TP(t); c = t * 128
                for hh in range(2):
                    bank = banks[hh]

                    def wo_mm(e):
                        for kc in range(4):
                            ins = e.matmul(PS[:P, bank, :], lhsT=QT[:, kc, c:c + P], rhs=WOA[:, kc, hh * 512:(hh + 1) * 512],
                                           start=(kc == 0), stop=(kc == 3))
                        return ins
                    S.op('pe', wo_mm, reads=['WOA', ('QT', 's')] if t == 16 else ['WOA'], writes=[('ps', bank)])
                    S.op('dve', lambda e: e.tensor_tensor(out=X[:P, t, hh * 512:(hh + 1) * 512], in0=X[:P, t, hh * 512:(hh + 1) * 512],
                                                          in1=PS[:P, bank, :], op=ALU.add),
                         reads=[('ps', bank), ('X', t)], writes=[('X', t)])
                if t == 16:
                    S.op('act', lambda e: e.activation(out=junkF[:P], in_=X[:P, t, :], func=AF.Square, accum_out=sttF[:P, 0, t:t + 1]),
                         reads=[('X', t), 'stF0i'], writes=['junkF', ('stF0', t)])
                else:
                    S.op('dve', lambda e: e.scalar_tensor_tensor(out=junkF[:P], in0=X[:P, t, :], scalar=1.0, in1=X[:P, t, :],
                                                                 op0=ALU.mult, op1=ALU.mult, accum_out=sttF[:P, 0, t:t + 1]),
                         reads=[('X', t), 'stF0i'], writes=['junkF', ('stF0', t)])

            with ExitStack() as ph:
                ckT = [sb(ph, "ckT%d" % i, [128, 4, 128], BF16) for i in range(2)]
                PTs = [sb(ph, "PTs%d" % i, [128, 8, 32], BF16) for i in range(2)]
                rr = sb(ph, "rr", [128, 8, 32], F32); a8 = sb(ph, "a8", [128, 2, 4, 32], F32)
                d4 = sb(ph, "d4", [128, 4, 32], F32); sq4 = sb(ph, "sq4", [128, 4, 32], F32)
                ms4 = sb(ph, "ms4", [128, 128], F32); ln4 = sb(ph, "ln4", [128, 128], F32); rs4 = sb(ph, "rs4", [128, 128], F32)
                negl = LAM[:, l, 4:5]; gsc = LAM[:, l, 5:6]
                OS = PS[:, 4, 0:256].rearrange("p (g n) -> p g n", g=8)
                LS = PS[:, 5, 0:256].rearrange("p (g n) -> p g n", g=8)
                ATSM = 9
                for i in range(17 if ATSM > 0 else 0):
                    slot = i % 2
                    cs = i % 4
                    KP = 128 if i < 16 else 32
                    if i < 16:
                        if i >= 4:
                            S.dma('pool', 'ck%d' % cs, out=ckb[cs][:], in_=ck[l, i * 128:(i + 1) * 128, :], writes=[('ckb', cs)])
                            S.dma('pool', 'cv%d' % cs, out=cvb[cs][:], in_=cv[l, i * 128:(i + 1) * 128, :], writes=[('cvb', cs)])
                        pv = psb(slot).rearrange("p (k n) -> p k n", k=8)

                        def tr(e):
                            for h in range(4):
                                ins = e.transpose(out=pv[:, h, :], in_=ckb[cs][:, h * 128:(h + 1) * 128], identity=idb[:])
                            return ins
                        S.op('pe', tr, reads=[('ckb', cs), 'idb'], writes=[('ps', slot)])
                        S.op('act', lambda e: e.activation(out=ckT[slot][:], in_=pv[:, 0:4, :], func=AF.Copy),
                             reads=[('ps', slot)], writes=[('ckT', slot)])
                    sb0 = 2 if slot == 0 else 6
                    SS2 = PS[:, sb0:sb0 + 2, 0:128]
                    if ATSM < 2:
                        continue

                    def qk(e):
                        for h in range(4):
                            for c in range(2):
                                if i < 16:
                                    kT = ckT[slot][c * 64:(c + 1) * 64, h, :]
                                else:
                                    kT = KT[c * 64:(c + 1) * 64, h, 2048:2080]
                                ins = e.matmul(PS[:KP, sb0 + c, h * 32:(h + 1) * 32], lhsT=kT, rhs=QT[c * 64:(c + 1) * 64, h, 2048:2080],
                                               start=True, stop=True)
                        return ins
                    S.op('pe', qk, reads=[('ckT', slot), ('QT', 's')], writes=[('ps', sb0), ('ps', sb0 + 1)])
                    S.op('act', lambda e: e.activation(out=PTs[slot][:KP].rearrange("p (c x) n -> p c (x n)", c=2), in_=SS2[:KP], func=AF.Exp, scale=SCALE),
                         reads=[('ps', sb0), ('ps', sb0 + 1)], writes=[('PTs', slot)])
                    if i < 16:
                        wo_tile(i, (6, 7) if slot == 0 else (2, 3))
                    if ATSM < 3:
                        continue

                    def pvm(e):
                        first = True
                        for c in range(2):
                            for h in range(4):
                                if i < 16:
                                    vv = cvb[cs][:, h * 128:(h + 1) * 128]
                                else:
                                    vv = VB[:32, 16, h * 128:(h + 1) * 128]
                                e.matmul(OS[:, c * 4 + h, :], lhsT=vv, rhs=PTs[slot][:KP, c * 4 + h, :],
                                         start=(i == 0 and first), stop=(i == 16), skip_group_check=True)
                                first = False
                        ins = e.matmul(PS[:, 5, 0:256], lhsT=onesb[:KP, :], rhs=PTs[slot][:KP].rearrange("p g n -> p (g n)"),
                                       start=(i == 0), stop=(i == 16))
                        return ins
                    S.op('pe', pvm, reads=[('PTs', slot), ('cvb', cs), 'onesb'], writes=[('ps', 4), ('ps', 5)])
                if ATSM < 4:
                    S.stopped = True
                S.op('dve', lambda e: e.reciprocal(out=rr[:], in_=LS), reads=[('ps', 5)], writes=['rr'])
                S.op('dve', lambda e: e.tensor_tensor(out=a8[:].rearrange("p c h n -> p (c h) n"), in0=OS, in1=rr[:], op=ALU.mult),
                     reads=[('ps', 4), 'rr'], writes=['a8'])
                S.op('dve', lambda e: e.scalar_tensor_tensor(out=d4[:], in0=a8[:, 1, :, :], scalar=negl, in1=a8[:, 0, :, :],
                                                             op0=ALU.mult, op1=ALU.add), reads=['a8'], writes=['d4'])
                S.op('act', lambda e: e.activation(out=sq4[:], in_=d4[:], func=AF.Square), reads=['d4'], writes=['sq4'])
                S.op('pe', lambda e: e.matmul(PS[:, 0, 0:128], lhsT=onesf[:, 0, :], rhs=sq4[:].rearrange("p h n -> p (h n)"), start=True, stop=True),
                     reads=['sq4', 'onesf0'], writes=[('ps', 0)])
                S.op('dve', lambda e: e.tensor_scalar(out=ms4[:], in0=PS[:, 0, 0:128], scalar1=EPS, scalar2=None, op0=ALU.add),
                     reads=[('ps', 0)], writes=['ms4'])
                S.op('act', lambda e: e.activation(out=ln4[:], in_=ms4[:], func=AF.Ln), reads=['ms4'], writes=['ln4'])
                S.op('act', lambda e: e.activation(out=rs4[:], in_=ln4[:], func=AF.Exp, scale=-0.5), reads=['ln4'], writes=['rs4'])
                S.op('dve', lambda e: e.scalar_tensor_tensor(out=QT[:, :, 2048:2080], in0=d4[:], scalar=gsc,
                                                             in1=rs4[:].rearrange("p (h n) -> p h n", h=4), op0=ALU.mult, op1=ALU.mult),
                     reads=['d4', 'rs4'], writes=[('QT', 's')])
                wo_tile(16, (2, 3))
                S.barrier()

            phase_norm(gffn[l:l + 1, :], stt_ext=sttF)
            with ExitStack() as ph:
                DN = [sb(ph, "DN%d" % i, [128, 8, 512], BF16) for i in range(2)]
                sgl = [sb(ph, "sgl%d" % i, [128, 512], F32) for i in range(2)]
                gcount = 0
                dcount = 0
                ev = 0
                for (f0, f1) in FFN_PASSES:
                    nf = f1 - f0
                    slabs = []
                    f = f0
                    while f < f1:
                        w = min(2, f1 - f)
                        slabs.append((f, w))
                        f += w

                    def issue_gu(si):
                        fs, w = slabs[si]
                        slot = (gcount + si) % 2
                        issue_gu_abs(fs, w, slot)
                    if f0 > 0:
                        issue_gu(0)
                    for hh in range(2):
                        S.dma('pool', 'dn%d' % hh, out=DN[hh][:, 0:nf, :], in_=w_dn3[:, f0:f1, hh * 512:(hh + 1) * 512], writes=[('DN', hh)])
                    for si, (fs, w) in enumerate(slabs):
                        slot = (gcount + si) % 2
                        if si + 1 < len(slabs):
                            issue_gu(si + 1)
                        for fo in range(w):
                            fi = fs + fo - f0
                            for (c0, n, tiles) in GROUPS:
                                bg = (ev % 3) * 2
                                ev += 1

                                def gu_mm(e):
                                    for kc in range(8):
                                        e.matmul(PS[:, bg, 0:n], lhsT=GU[slot][:, 0, kc, fo * 128:(fo + 1) * 128], rhs=H[:, kc, c0:c0 + n],
                                                 start=(kc == 0), stop=(kc == 7))
                                    for kc in range(8):
                                        ins = e.matmul(PS[:, bg + 1, 0:n], lhsT=GU[slot][:, 1, kc, fo * 128:(fo + 1) * 128], rhs=H[:, kc, c0:c0 + n],
                                                       start=(kc == 0), stop=(kc == 7))
                                    return ins
                                S.op('pe', gu_mm, reads=[('GUg', slot), ('GUu', slot)], writes=[('ps', bg), ('ps', bg + 1)])
                                ss = ev % 2
                                S.op('act', lambda e: e.activation(out=sgl[ss][:, 0:n], in_=PS[:, bg, 0:n], func=AF.Silu),
                                     reads=[('ps', bg)], writes=[('sgl', ss)])
                                S.op('dve', lambda e: e.tensor_tensor(out=ACTH[:, fi, c0:c0 + n], in0=sgl[ss][:, 0:n], in1=PS[:, bg + 1, 0:n], op=ALU.mult),
                                     reads=[('sgl', ss), ('ps', bg + 1)], writes=[('ACTH', fi)])
                    gcount += len(slabs)
                    lastpass = (f1 == 22)
                    last = lastpass and l == DEPTH - 1
                    mid = lastpass and l < DEPTH - 1
                    if lastpass:
                        GUf = [GU[i][:].rearrange("p a b c -> p (a b c)").bitcast(F32) for i in range(2)]
                        GU1b = GU[1][:].rearrange("p a b c -> p (a b c)")
                        Gfin = GUf[0][:, 0:1024]
                        stf = GUf[0][:, 1024:1024 + 4 * NT].rearrange("p (t k) -> p t k", k=4)
                        yof = [GUf[1][:, i * 1024:(i + 1) * 1024] for i in range(2)]
                        hbf = [GU1b[:, i * 1024:(i + 1) * 1024] for i in range(3)]
                        junkf = GU1b[:, 3072:4096]
                        grow = gfin[0:1, :] if last else gmix[l + 1:l + 2, :]
                    if mid:
                        w_in3n = w_in[l + 1].rearrange("(k p) c -> p k c", p=128)
                        WPn = FA[:, 16640:22784].rearrange("p (k c) -> p k c", k=8)
                        S.dma('pool', 'wp0', out=WPn[:, :, 0:256], in_=w_in3n[:, :, 0:256], writes=['WP0'])
                        S.dma('pool', 'wp1', out=WPn[:, :, 256:768], in_=w_in3n[:, :, 1792:2304], writes=['WP1'])
                    order = [(t, hh) for t in range(NT) for hh in range(2)] if lastpass else [(t, hh) for hh in range(2) for t in range(NT)]
                    for (t, hh) in order:
                        dslot = hh
                        P = TP(t); c = t * 128
                        bank = 4 + ((2 * t + hh) % 4 if lastpass else t % 4)

                        def dn_mm(e):
                            for fi in range(nf):
                                ins = e.matmul(PS[:P, bank, :], lhsT=ACTH[:, fi, c:c + P], rhs=DN[dslot][:, fi, :],
                                               start=(fi == 0), stop=(fi == nf - 1))
                            return ins
                        S.op('pe', dn_mm, reads=[('ACTH', fi) for fi in range(nf)] + [('DN', dslot)], writes=[('ps', bank)])
                        S.op('dve', lambda e: e.tensor_tensor(out=X[:P, t, hh * 512:(hh + 1) * 512], in0=X[:P, t, hh * 512:(hh + 1) * 512],
                                                              in1=PS[:P, bank, :], op=ALU.add),
                             reads=[('ps', bank), ('X', t)], writes=[('X', t)])
                        if lastpass and hh == 1:
                            if t == 0:
                                S.dma('sp', 'Gf', out=Gfin, in_=grow.partition_broadcast(128), reads=[('X', 0)], writes=['Gfin'])

                            def fin1(t):
                                P = TP(t)
                                jout = yof[t % 2][:P] if last else junkf[:P]
                                jkey = ('yo', t % 2) if last else 'junkf'
                                S.op('act', lambda e: e.activation(out=jout, in_=X[:P, t, :], func=AF.Square, accum_out=stf[:P, t, 0:1]),
                                     reads=[('X', t)], writes=[jkey, ('st', t, 0)])

                            def fin2(t):
                                rstd_chain(stf, TP(t), t, None)

                            def fin3(t):
                                P = TP(t)
                                if last:
                                    S.op('dve', lambda e: e.scalar_tensor_tensor(out=yof[t % 2][:P], in0=X[:P, t, :], scalar=stf[:P, t, 3:4],
                                                                                 in1=Gfin[:P], op0=ALU.mult, op1=ALU.mult),
                                         reads=[('X', t), ('st', t, 3), 'Gfin'], writes=[('yo', t % 2)])
                                    S.dma('sp', 'o_y%d' % (t % 2), out=rows_y(t), in_=yof[t % 2][:P], reads=[('yo', t % 2)], is_out=True)
                                else:
                                    S.op('dve', lambda e: e.scalar_tensor_tensor(out=hbf[t % 3][:P], in0=X[:P, t, :], scalar=stf[:P, t, 3:4],
                                                                                 in1=Gfin[:P], op0=ALU.mult, op1=ALU.mult),
                                         reads=[('X', t), ('st', t, 3), 'Gfin'], writes=[('hbf', t % 3)])

                            def fin4(t):
                                if last:
                                    return
                                P = TP(t); c4 = t * 128
                                tbk = t % 4
                                pvf = psb(tbk).rearrange("p (k n) -> p k n", k=8)

                                def tr(e):
                                    for kc in range(8):
                                        ins = e.transpose(out=pvf[:, kc, :P], in_=hbf[t % 3][:P, kc * 128:(kc + 1) * 128], identity=idb[:P, :P])
                                    return ins
                                S.op('pe', tr, reads=[('hbf', t % 3), 'idb'], writes=[('ps', tbk)])
                                S.op('act', lambda e: e.activation(out=H[:, :, c4:c4 + P], in_=pvf[:, :, :P], func=AF.Copy),
                                     reads=[('ps', tbk)], writes=[('H', t)])
                            if t >= 3:
                                fin4(t - 3)
                            if t >= 2:
                                fin3(t - 2)
                            if t >= 1:
                                fin2(t - 1)
                            fin1(t)
                            if t == NT - 1:
                                fin4(t - 2)
                                fin3(t - 1)
                                fin2(t)
                                fin4(t - 1)
                                fin3(t)
                                fin4(t)
                S.barrier()
            phf.close()
            phw.close()

        S.stopped = False

        if True:
            S.finish()
    return nc


def _host_consts():
    half = 32
    inv = (np.float32(10000.0) ** (-np.arange(half, dtype=np.float32) / np.float32(half))).astype(np.float32)
    pos = np.zeros((128, 17), np.float32)
    for t in range(16):
        pos[:, t] = t * 128 + np.arange(128)
    pos[:, 16] = 2048 + np.arange(128)
    ang = (pos[:, :, None] * inv[None, None, :]).astype(np.float32)
    cos_t = np.cos(ang).astype(np.float32)
    sin_t = np.sin(ang).astype(np.float32)
    wins = np.array([[2, 4], [8, 16]])
    invc = np.zeros((128, 2, 16), np.float32)
    invw = np.zeros((128, 2), np.float32)
    for j in range(2):
        for p in range(128):
            w = wins[j][p // 64]
            invw[p, j] = 1.0 / w
            invc[p, j, :] = 1.0 / np.minimum(np.arange(16) + 1, w)
    return cos_t, sin_t, invc, invw


_NC_CACHE = {}


def kernel(x_prompt, x_sample, cache_k, cache_v, state_pool, state_conv,
           norm_mix_g, w_in, pool_w, pool_scale, lambda_qk, diff_norm_g,
           conv_dw, conv_dw_b, conv_ln_g, conv_ln_b, conv_pw, w_out,
           norm_ffn_g, w_gate_up, w_down, final_norm_g):
    f = lambda a: np.ascontiguousarray(np.asarray(a, dtype=np.float32))
    x_prompt, x_sample, cache_k, cache_v = f(x_prompt), f(x_sample), f(cache_k), f(cache_v)
    state_pool, state_conv = f(state_pool), f(state_conv)
    B = 8
    cos_t, sin_t, invc, invw = _host_consts()

    def pc(v):
        return np.asarray(v, np.float32).reshape(2, 128).T
    pvec = np.zeros((2, 128, NPV), np.float32)
    plw = np.zeros((2, 128, 2, 128), np.float32)
    pool_w = f(pool_w); conv_dw = f(conv_dw)
    for l in range(2):
        pvec[l, :, 0:2] = pc(pool_scale[l])
        pvec[l, :, 2:4] = pc(conv_dw_b[l])
        pvec[l, :, 4:6] = pc(conv_ln_g[l])
        pvec[l, :, 6:8] = pc(conv_ln_b[l])
        pvec[l, :, 8] = np.asarray(diff_norm_g[l], np.float32)
        pvec[l, :, 9:11] = invw
        dwl = conv_dw[l].reshape(31, 2, 128)
        pvec[l, :, 11:73] = dwl.transpose(2, 1, 0).reshape(128, 62)
        for j in range(2):
            for hf in range(2):
                plw[l, hf * 64:(hf + 1) * 64, j, hf * 64:(hf + 1) * 64] = pool_w[l, 2 * j + hf]
    common = dict(
        w_in=f(w_in), w_out=f(w_out), w_gu=f(w_gate_up), w_dn=f(w_down), conv_pw=f(conv_pw), plw=plw,
        gmix=f(norm_mix_g), gffn=f(norm_ffn_g), gfin=f(final_norm_g).reshape(1, 1024),
        pvec=pvec, lq=f(lambda_qk).reshape(2, 256),
        ident=np.eye(128, dtype=np.float32), cos_t=cos_t, sin_t=sin_t, invc=invc,
    )
    in_maps = []
    for b in range(B):
        m = dict(common)
        m["xp"] = x_prompt[b]
        m["xs"] = x_sample[b]
        m["ck"] = np.ascontiguousarray(cache_k[:, b].reshape(2, 2048, 512))
        m["cv"] = np.ascontiguousarray(cache_v[:, b].reshape(2, 2048, 512))
        m["stp"] = np.ascontiguousarray(state_pool[:, b].reshape(2, 15, 2, 128).transpose(0, 3, 2, 1))
        m["stc"] = np.ascontiguousarray(state_conv[:, b].reshape(2, 30, 2, 128).transpose(0, 3, 2, 1))
        in_maps.append(m)
    if "nc" not in _NC_CACHE:
        _NC_CACHE["nc"] = build_program()
    nc = _NC_CACHE["nc"]
    res = run_bass_kernel_spmd(nc, in_maps, core_ids=list(range(B)))
    R = res.results

    def st(name):
        return np.stack([np.asarray(R[b][name], np.float32) for b in range(B)], axis=0)
    y_prompt = st("yp")
    y_sample = st("ys")
    nk_p = st("nkp").transpose(1, 0, 2, 3).reshape(2, B, 2048, 4, 2, 64)
    nv_p = st("nvp").transpose(1, 0, 2, 3).reshape(2, B, 2048, 4, 128)
    nk_s = st("nks").transpose(1, 0, 2, 3).reshape(2, B, 32, 4, 2, 64)
    nv_s = st("nvs").transpose(1, 0, 2, 3).reshape(2, B, 32, 4, 128)

    def unT(name, T):
        a = st(name)
        return np.ascontiguousarray(a.transpose(1, 0, 4, 3, 2).reshape(2, B, T, 256))
    np_p = unT("npp", 15); nc_p = unT("ncp", 30); np_s = unT("nps", 15); nc_s = unT("ncs", 30)
    return (y_prompt, y_sample, np.ascontiguousarray(nk_p), np.ascontiguousarray(nv_p), np_p, nc_p,
            np.ascontiguousarray(nk_s), np.ascontiguousarray(nv_s), np_s, nc_s)
```

```python
import math
from contextlib import ExitStack

import numpy as np
import concourse.bass as bass
import concourse.mybir as mybir
from concourse.bass_utils import run_bass_kernel_spmd

F32 = mybir.dt.float32
BF16 = mybir.dt.bfloat16
ALU = mybir.AluOpType
AF = mybir.ActivationFunctionType

EPS = 1e-6
SCALE = 0.125
NT = 17
NTOK = 2080
DEPTH = 2
NPV = 73
FFN_PASSES = [(0, 8), (8, 15), (15, 22)]
GROUPS = [(0, 512, [0, 1, 2, 3]), (512, 512, [4, 5, 6, 7]), (1024, 512, [8, 9, 10, 11]),
          (1536, 512, [12, 13, 14, 15]), (2048, 32, [16])]
NEG = -30000.0


def TP(t):
    return 128 if t < 16 else 32


import os
STOP_AFTER = 0


class _Stop(Exception):
    pass


class Sched:
    def __init__(self, nc, es):
        self.nc = nc
        self.es = es
        self.eng = {'pe': nc.tensor, 'act': nc.scalar, 'dve': nc.vector, 'pool': nc.gpsimd, 'sp': nc.sync}
        self.csem = {e: es.enter_context(nc.semaphore('c_' + e)) for e in ('pe', 'act', 'dve', 'pool')}
        self.ccount = {e: 0 for e in self.csem}
        self.dsem = {}
        self.lastw = {}
        self.readers = {}
        self.waited = {e: {} for e in self.eng}
        self.out_events = []
        self.stopped = False
        self.nbar = 0

    def _wait(self, eng, ev):
        sem, val, src, kind, sid = ev
        w = self.waited[eng]
        if w.get(sid, 0) >= val:
            return
        w[sid] = val
        self.eng[eng].wait_ge(sem, val)

    def _deps(self, eng, reads, writes):
        deps = []
        for k in reads:
            ev = self.lastw.get(k)
            if ev is not None:
                deps.append((ev, True))
        for k in writes:
            ev = self.lastw.get(k)
            if ev is not None:
                deps.append((ev, False))
            for ev in self.readers.get(k, {}).values():
                deps.append((ev, False))
        for ev, raw in deps:
            if ev[3] == 'c' and ev[2] == eng:
                if eng == 'pe' or not raw:
                    continue
            self._wait(eng, ev)

    def _commit(self, ev, reads, writes):
        for k in writes:
            self.lastw[k] = ev
            self.readers[k] = {}
        for k in reads:
            self.readers.setdefault(k, {})[ev[4]] = ev

    def op(self, eng, fn, reads=(), writes=()):
        if self.stopped:
            return
        self._deps(eng, reads, writes)
        ins = fn(self.eng[eng])
        self.ccount[eng] += 1
        ins.then_inc(self.csem[eng], 1)
        ev = (self.csem[eng], self.ccount[eng], eng, 'c', 'c_' + eng)
        self._commit(ev, reads, writes)

    def dma(self, q, key, out, in_, reads=(), writes=(), is_out=False):
        if self.stopped:
            return
        self._deps(q, reads, writes)
        if key not in self.dsem:
            self.dsem[key] = [self.es.enter_context(self.nc.semaphore('d%d' % len(self.dsem))), 0]
        d = self.dsem[key]
        d[1] += 16
        self.eng[q].dma_start(out=out, in_=in_).then_inc(d[0], 16)
        ev = (d[0], d[1], q, 'd', 'd_' + str(key))
        self._commit(ev, reads, writes)
        if is_out:
            self.out_events.append(ev)

    def barrier(self):
        if self.stopped:
            return
        self.nbar += 1
        self._barrier()
        if self.nbar == STOP_AFTER:
            self.stopped = True

    def _barrier(self):
        evs = [(self.csem[e], self.ccount[e], e, 'c', 'c_' + e) for e in self.csem if self.ccount[e] > 0]
        evs += [(d[0], d[1], 'x', 'd', 'd_' + str(k)) for k, d in self.dsem.items() if d[1] > 0]
        for eng in self.eng:
            for ev in evs:
                if ev[3] == 'c' and ev[2] == eng:
                    continue
                self._wait(eng, ev)
        self.lastw = {}
        self.readers = {}

    def finish(self):
        for ev in self.out_events:
            self._wait('sp', ev)


def build_program():
    nc = bass.Bass("TRN2", target_bir_lowering=False)

    def din(name, shape):
        return nc.dram_tensor(name, list(shape), F32, kind="ExternalInput").ap()

    def dout(name, shape):
        return nc.dram_tensor(name, list(shape), F32, kind="ExternalOutput").ap()

    xp = din("xp", [2048, 1024]); xs = din("xs", [32, 1024])
    ck = din("ck", [2, 2048, 512]); cv = din("cv", [2, 2048, 512])
    stp = din("stp", [2, 128, 2, 15]); stc = din("stc", [2, 128, 2, 30])
    w_in = din("w_in", [2, 1024, 2304]); w_out = din("w_out", [2, 1024, 1024])
    w_gu = din("w_gu", [2, 1024, 5632]); w_dn = din("w_dn", [2, 2816, 1024])
    conv_pw = din("conv_pw", [2, 256, 256]); plw = din("plw", [2, 128, 2, 128])
    gmix = din("gmix", [2, 1024]); gffn = din("gffn", [2, 1024]); gfin = din("gfin", [1, 1024])
    pvec = din("pvec", [2, 128, NPV]); lq = din("lq", [2, 256])
    ident = din("ident", [128, 128]); cos_t = din("cos_t", [128, 17, 32]); sin_t = din("sin_t", [128, 17, 32])
    invc = din("invc", [128, 2, 16])

    yp = dout("yp", [2048, 1024]); ys = dout("ys", [32, 1024])
    nkp = dout("nkp", [2, 2048, 512]); nvp = dout("nvp", [2, 2048, 512])
    npp = dout("npp", [2, 128, 2, 15]); ncp = dout("ncp", [2, 128, 2, 30])
    nks = dout("nks", [2, 32, 512]); nvs = dout("nvs", [2, 32, 512])
    nps = dout("nps", [2, 128, 2, 15]); ncs = dout("ncs", [2, 128, 2, 30])

    def rows_y(t):
        return yp[t * 128:(t + 1) * 128, :] if t < 16 else ys[0:32, :]

    def rows_k(l, t):
        return nkp[l, t * 128:(t + 1) * 128, :] if t < 16 else nks[l, 0:32, :]

    def rows_v(l, t):
        return nvp[l, t * 128:(t + 1) * 128, :] if t < 16 else nvs[l, 0:32, :]

    with ExitStack() as es:
        S = Sched(nc, es)

        uid = [0]

        def sb(stack, name, shape, dt):
            uid[0] += 1
            return stack.enter_context(nc.sbuf_tensor("%s_%d" % (name, uid[0]), list(shape), dt))

        PS = es.enter_context(nc.psum_tensor("PS", [128, 8, 512], F32))

        def psb(b):
            return PS[:, b, :].bitcast(BF16)

        X = sb(es, "X", [128, NT, 1024], F32)
        H = sb(es, "H", [128, 8, NTOK], BF16)
        FA = sb(es, "FA", [128, 25344], BF16)
        QT = FA[:, 0:8320].rearrange("p (h n) -> p h n", h=4)
        KT = FA[:, 8320:16640].rearrange("p (h n) -> p h n", h=4)
        VB = FA[:, 16640:25344].rearrange("p (t n) -> p t n", t=NT)
        ACTH = FA[:, 0:16640].rearrange("p (k n) -> p k n", k=8)
        idb = sb(es, "idb", [128, 128], BF16)
        onesb = sb(es, "onesb", [128, 128], BF16)
        onesf = sb(es, "onesf", [128, 3, 128], F32)
        PV = sb(es, "PV", [128, 2, NPV], F32)
        LAM = sb(es, "LAM", [128, 2, 8], F32)
        LQ = sb(es, "LQ", [128, 2, 256], F32)
        ljunk = sb(es, "ljunk", [128, 64], F32)

        S.dma('pool', 'c_id', out=idb[:], in_=ident[:, :], writes=['idb'])
        S.dma('sp', 'c_pv', out=PV[:], in_=pvec.rearrange("l p n -> p l n"), writes=['PV'])
        S.dma('sp', 'c_lq', out=LQ[:, 0, :], in_=lq[0:1, :].partition_broadcast(128), writes=[('LQ', 0)])
        S.dma('sp', 'c_lq1', out=LQ[:, 1, :], in_=lq[1:2, :].partition_broadcast(128), writes=[('LQ', 1)])
        S.dma('sp', 'xs', out=X[:32, 16, :], in_=xs[:, :], writes=[('X', 16)])
        xpv = xp.rearrange("(t p) d -> p t d", p=128)
        for q4 in range(4):
            S.dma('sp', 'x%d' % q4, out=X[:, q4 * 4:(q4 + 1) * 4, :], in_=xpv[:, q4 * 4:(q4 + 1) * 4, :],
                  writes=[('X', t) for t in range(q4 * 4, q4 * 4 + 4)])
        S.op('dve', lambda e: e.memset(onesb[:], 1.0), writes=['onesb'])
        S.op('dve', lambda e: e.memset(onesf[:, 0, :], 1.0 / 128), writes=['onesf0'])
        S.op('dve', lambda e: e.memset(onesf[:, 1, :], 1.0 / 256), writes=['onesf1'])
        S.op('dve', lambda e: e.memset(onesf[:, 2, :], 1.0), writes=['onesf2'])
        for l in range(DEPTH):
            lam_init = 0.8 - 0.6 * math.exp(-0.3 * l)
            S.op('dve', lambda e: e.scalar_tensor_tensor(out=ljunk[:], in0=LQ[:, l, 0:64], scalar=1.0, in1=LQ[:, l, 64:128],
                                                         op0=ALU.mult, op1=ALU.mult, accum_out=LAM[:, l, 0:1]),
                 reads=[('LQ', l)], writes=['ljunk', ('LAM', l, 0)])
            S.op('dve', lambda e: e.scalar_tensor_tensor(out=ljunk[:], in0=LQ[:, l, 128:192], scalar=1.0, in1=LQ[:, l, 192:256],
                                                         op0=ALU.mult, op1=ALU.mult, accum_out=LAM[:, l, 1:2]),
                 reads=[('LQ', l)], writes=['ljunk', ('LAM', l, 1)])
            S.op('act', lambda e: e.activation(out=LAM[:, l, 2:4], in_=LAM[:, l, 0:2], func=AF.Exp),
                 reads=[('LAM', l, 0), ('LAM', l, 1)], writes=[('LAM', l, 2)])
            S.op('dve', lambda e: e.tensor_tensor(out=LAM[:, l, 6:7], in0=LAM[:, l, 3:4], in1=LAM[:, l, 2:3], op=ALU.subtract),
                 reads=[('LAM', l, 2)], writes=[('LAM', l, 6)])
            S.op('dve', lambda e: e.tensor_scalar(out=LAM[:, l, 4:5], in0=LAM[:, l, 6:7], scalar1=-lam_init, scalar2=None, op0=ALU.add),
                 reads=[('LAM', l, 6)], writes=[('LAM', l, 4)])
            S.op('dve', lambda e: e.tensor_scalar(out=LAM[:, l, 5:6], in0=PV[:, l, 8:9], scalar1=(1.0 - lam_init), scalar2=None, op0=ALU.mult),
                 reads=['PV'], writes=[('LAM', l, 5)])
        S._barrier()

        def rstd_chain(stt, P, t, src_key):
            S.op('dve', lambda e: e.tensor_scalar(out=stt[:P, t, 1:2], in0=stt[:P, t, 0:1], scalar1=1.0 / 1024, scalar2=EPS,
                                                  op0=ALU.mult, op1=ALU.add),
                 reads=[('st', t, 0)], writes=[('st', t, 1)])
            S.op('act', lambda e: e.activation(out=stt[:P, t, 2:3], in_=stt[:P, t, 1:2], func=AF.Ln),
                 reads=[('st', t, 1)], writes=[('st', t, 2)])
            S.op('act', lambda e: e.activation(out=stt[:P, t, 3:4], in_=stt[:P, t, 2:3], func=AF.Exp, scale=-0.5),
                 reads=[('st', t, 2)], writes=[('st', t, 3)])

        def phase_norm(g_row, tgroups=None, stt_ext=None):
            tgroups = tgroups or [list(range(NT))]
            with ExitStack() as ph:
                G = sb(ph, "G", [128, 1024], F32)
                hb = [sb(ph, "hb%d" % i, [128, 1024], BF16) for i in range(3)]
                junk = sb(ph, "junk", [128, 1024], BF16)
                junk2 = sb(ph, "junk2", [128, 1024], BF16)
                stt = sb(ph, "nst", [128, 4, NT], F32) if stt_ext is None else stt_ext
                S.dma('sp', 'G', out=G[:], in_=g_row.partition_broadcast(128), writes=['G'])
                if stt_ext is None:
                    S.op('dve', lambda e: e.memset(stt[:, 0, :], 1.0), writes=['st0i'])
                for tg in tgroups:
                    t0g, t1g = tg[0], tg[-1] + 1
                    for t in (tg if stt_ext is None else []):
                        P = TP(t)
                        if t % 2 == 0:
                            S.op('act', lambda e: e.activation(out=junk[:P], in_=X[:P, t, :], func=AF.Square, accum_out=stt[:P, 0, t:t + 1]),
                                 reads=[('X', t), 'st0i'], writes=['junk', ('st0', t)])
                        else:
                            S.op('dve', lambda e: e.scalar_tensor_tensor(out=junk2[:P], in0=X[:P, t, :], scalar=1.0, in1=X[:P, t, :],
                                                                         op0=ALU.mult, op1=ALU.mult, accum_out=stt[:P, 0, t:t + 1]),
                                 reads=[('X', t), 'st0i'], writes=['junk2', ('st0', t)])
                    S.op('dve', lambda e: e.tensor_scalar(out=stt[:, 1, t0g:t1g], in0=stt[:, 0, t0g:t1g], scalar1=1.0 / 1024, scalar2=EPS,
                                                          op0=ALU.mult, op1=ALU.add),
                         reads=[('st0', t) for t in tg], writes=[('st1', t0g)])
                    S.op('act', lambda e: e.activation(out=stt[:, 2, t0g:t1g], in_=stt[:, 1, t0g:t1g], func=AF.Ln),
                         reads=[('st1', t0g)], writes=[('st2', t0g)])
                    S.op('act', lambda e: e.activation(out=stt[:, 3, t0g:t1g], in_=stt[:, 2, t0g:t1g], func=AF.Exp, scale=-0.5),
                         reads=[('st2', t0g)], writes=[('st3', t0g)])
                    for t in tg:
                        P = TP(t); c = t * 128
                        hs = t % 3
                        S.op('dve', lambda e: e.scalar_tensor_tensor(out=hb[hs][:P], in0=X[:P, t, :], scalar=stt[:P, 3, t:t + 1], in1=G[:P],
                                                                     op0=ALU.mult, op1=ALU.mult),
                             reads=[('X', t), ('st3', t0g), 'G'], writes=[('hb', hs)])
                        bank = t % 4
                        pv = psb(bank).rearrange("p (k n) -> p k n", k=8)

                        def tr(e):
                            for kc in range(8):
                                ins = e.transpose(out=pv[:, kc, :P], in_=hb[hs][:P, kc * 128:(kc + 1) * 128], identity=idb[:P, :P])
                            return ins
                        S.op('pe', tr, reads=[('hb', hs), 'idb'], writes=[('ps', bank)])
                        S.op('act', lambda e: e.activation(out=H[:, :, c:c + P], in_=pv[:, :, :P], func=AF.Copy),
                             reads=[('ps', bank)], writes=[('H', t)])
                S.barrier()

        def load_slab(q, key, dst, src3, kcs, c0, w, wkey):
            S.dma(q, key, out=dst, in_=src3[:, kcs[0]:kcs[-1] + 1, c0:c0 + w], writes=[wkey])

        if True:
          for l in range(DEPTH):
            w_in3 = w_in[l].rearrange("(k p) c -> p k c", p=128)
            w_out3 = w_out[l].rearrange("(k p) c -> p k c", p=128)
            w_gu3 = w_gu[l].rearrange("(k p) c -> p k c", p=128)
            w_dn3 = w_dn[l].rearrange("(k p) c -> p k c", p=128)
            pw3 = conv_pw[l].rearrange("(k p) c -> p k c", p=128)

            WP = FA[:, 16640:22784].rearrange("p (k c) -> p k c", k=8)
            WOP = FA[:, 0:4096].rearrange("p (k c) -> p k c", k=4)
            if l == 0:
                S.dma('pool', 'wp0', out=WP[:, :, 0:256], in_=w_in3[:, :, 0:256], reads=[('X', 11)], writes=['WP0'])
                S.dma('pool', 'wp1', out=WP[:, :, 256:768], in_=w_in3[:, :, 1792:2304], writes=['WP1'])
            S.dma('pool', 'wop0', out=WOP[:, 0:2, :], in_=w_out3[:, 0:2, :], writes=['WOP0'])
            S.dma('pool', 'wop1', out=WOP[:, 2:4, :], in_=w_out3[:, 6:8, :], writes=['WOP1'])
            if l == 0:
                phase_norm(gmix[l:l + 1, :], [[0, 1, 2, 3], [4, 5, 6, 7], [8, 9, 10, 11], [12, 13, 14, 15, 16]])

            phq = ExitStack()
            Wq0 = sb(phq, "Wq0", [128, 8, 512], BF16)
            S.dma('pool', 'wq0', out=Wq0[:], in_=w_in3[:, :, 256:768], writes=[('W', 0)])
            with ExitStack() as ph:
                DG = FA[:, 4096:12032].rearrange("p (k c) -> p k c", k=62)
                CU = [FA[:, 12032 + i * 1084:12032 + (i + 1) * 1084].rearrange("p (j c) -> p j c", j=2) for i in range(2)]
                dsb = FA[:, 22784:23808].rearrange("p (j c) -> p j c", j=2)
                MPC2 = [FA[:, 14200:16248].rearrange("p (j c) -> p j c", j=4), sb(ph, "MPC1", [128, 4, 512], BF16)]
                sl2 = [FA[:, 23808:24832].rearrange("p (j c) -> p j c", j=2), sb(ph, "sl1", [128, 2, 512], BF16)]
                PW = sb(ph, "PW", [128, 2, 256], BF16)
                PLW = sb(ph, "PLW", [128, 2, 128], BF16)
                INVC = sb(ph, "INVC", [128, 2, 16], F32)
                PU = [sb(ph, "PU%d" % i, [128, 2, 527], F32) for i in range(2)]
                U32 = sb(ph, "U32", [128, 2, 512], F32)
                WA = sb(ph, "WA", [128, 526], F32); WB = sb(ph, "WB", [128, 524], F32)
                WC = WA; WD = WB
                t16 = sb(ph, "t16", [128, 16], F32)
                sg2 = [sb(ph, "sg%d" % i, [128, 512], F32) for i in range(2)]
                ysb = sb(ph, "ysb", [128, 2, 512], F32)
                ysq = sb(ph, "ysq", [128, 2, 512], F32)
                m2 = sb(ph, "m2", [128, 512], F32); var = sb(ph, "var", [128, 512], F32)

                S.dma('pool', 'pw', out=PW[:], in_=pw3, writes=['PW'])
                S.dma('pool', 'plw', out=PLW[:], in_=plw[l], writes=['PLW'])
                S.dma('sp', 'invc', out=INVC[:], in_=invc[:, :, :], writes=['INVC'])
                S.op('dve', lambda e: e.tensor_tensor(out=DG, in0=idb[:].unsqueeze(1).broadcast_to([128, 62, 128]),
                                                      in1=PV[:, l, 11:73].unsqueeze(2).broadcast_to([128, 62, 128]), op=ALU.mult),
                     reads=['PV', 'idb'], writes=['DG'])
                S.op('dve', lambda e: e.memset(PU[0][:, :, 0:15], 0.0), writes=[('PUc', 0)])
                S.op('dve', lambda e: e.memset(CU[0][:, :, 0:30], 0.0), writes=[('CUc', 0)])

                def stageA(gi):
                    c0, n, tiles = GROUPS[gi]
                    sample = (gi == 4)
                    slot = 0 if sample else gi % 2
                    if sample:
                        S.dma('sp', 'stp', out=PU[0][:, :, 0:15], in_=stp[l], writes=[('PUc', 0)])
                        S.dma('pool', 'stc', out=CU[0][:, :, 0:30], in_=stc[l], writes=[('CUc', 0)])
                    def proj(bank, col):
                        def f(e):
                            for kc in range(8):
                                ins = e.matmul(PS[:, bank, 0:n], lhsT=WP[:, kc, col:col + 128], rhs=H[:, kc, c0:c0 + n],
                                               start=(kc == 0), stop=(kc == 7))
                            return ins
                        S.op('pe', f, reads=['WP0', 'WP1'] + [('H', t) for t in tiles], writes=[('ps', bank)])
                    for j in range(2):
                        proj(j, j * 128)
                        S.op('act', lambda e: e.activation(out=PU[slot][:, j, 15:15 + n], in_=PS[:, j, 0:n], func=AF.Copy),
                             reads=[('ps', j)], writes=[('PUn', slot, j)])
                    if gi == 3 or sample:
                        S.dma('sp', 'o_np%d' % gi, out=(nps[l] if sample else npp[l]), in_=PU[slot][:, :, n:n + 15],
                              reads=[('PUc', slot), ('PUn', slot, 0), ('PUn', slot, 1)], is_out=True)
                    if gi < 3:
                        S.op('dve', lambda e: e.tensor_copy(out=PU[(gi + 1) % 2][:, :, 0:15], in_=PU[slot][:, :, 512:527]),
                             reads=[('PUn', slot, 0), ('PUn', slot, 1)], writes=[('PUc', (gi + 1) % 2)])
                    for j in range(2):
                        ext = PU[slot][:, j, :]
                        rk = [('PUc', slot), ('PUn', slot, j)]
                        S.op('dve', lambda e: e.tensor_tensor(out=WA[:, 0:14 + n], in0=ext[:, 1:15 + n], in1=ext[:, 0:14 + n], op=ALU.add),
                             reads=rk, writes=['WA'])
                        S.op('dve', lambda e: e.tensor_tensor(out=WB[:, 0:12 + n], in0=WA[:, 2:14 + n], in1=WA[:, 0:12 + n], op=ALU.add),
                             reads=['WA'], writes=['WB'])
                        if j == 0:
                            lo, lo_off, hi, hi_off = WA, 14, WB, 12
                            kk = ['WA', 'WB']
                        else:
                            S.op('dve', lambda e: e.tensor_tensor(out=WC[:, 0:8 + n], in0=WB[:, 4:12 + n], in1=WB[:, 0:8 + n], op=ALU.add),
                                 reads=['WB'], writes=['WA'])
                            S.op('dve', lambda e: e.tensor_tensor(out=WD[:, 0:n], in0=WC[:, 8:8 + n], in1=WC[:, 0:n], op=ALU.add),
                                 reads=['WA'], writes=['WB'])
                            lo, lo_off, hi, hi_off = WC, 8, WD, 0
                            kk = ['WA', 'WB']
                        for (src, off, p0) in ((lo, lo_off, 0), (hi, hi_off, 64)):
                            S.op('dve', lambda e: e.scalar_tensor_tensor(out=dsb[p0:p0 + 64, j, 0:n], in0=src[p0:p0 + 64, off:off + n],
                                                                         scalar=PV[p0:p0 + 64, l, 9 + j:10 + j], in1=ext[p0:p0 + 64, 15:15 + n],
                                                                         op0=ALU.mult, op1=ALU.subtract),
                                 reads=kk + rk + ['PV'], writes=[('dsb', j)])
                            if gi == 0:
                                S.op('dve', lambda e: e.tensor_tensor(out=t16[p0:p0 + 64, :], in0=src[p0:p0 + 64, off:off + 16],
                                                                      in1=INVC[p0:p0 + 64, j, :], op=ALU.mult),
                                     reads=kk + ['INVC'], writes=['t16'])
                                S.op('dve', lambda e: e.tensor_tensor(out=dsb[p0:p0 + 64, j, 0:16], in0=t16[p0:p0 + 64, :],
                                                                      in1=ext[p0:p0 + 64, 15:31], op=ALU.subtract),
                                     reads=['t16'] + rk, writes=[('dsb', j)])
                    for j in range(2):
                        proj(2 + 2 * j, 256 + j * 128)
                        proj(3 + 2 * j, 512 + j * 128)
                        S.op('act', lambda e: e.activation(out=sg2[j][:, 0:n], in_=PS[:, 3 + 2 * j, 0:n], func=AF.Sigmoid),
                             reads=[('ps', 3 + 2 * j)], writes=[('sg', j)])
                        S.op('dve', lambda e: e.tensor_tensor(out=CU[slot][:, j, 30:30 + n], in0=PS[:, 2 + 2 * j, 0:n], in1=sg2[j][:, 0:n], op=ALU.mult),
                             reads=[('ps', 2 + 2 * j), ('sg', j)], writes=[('CUn', slot, j)])
                        if gi == 3 or sample:
                            S.op('dve', lambda e: e.tensor_tensor(out=U32[:, j, n - 30:n], in0=PS[:, 2 + 2 * j, n - 30:n], in1=sg2[j][:, n - 30:n], op=ALU.mult),
                                 reads=[('ps', 2 + 2 * j), ('sg', j)], writes=[('U32', j)])
                    if gi == 3 or sample:
                        S.dma('sp', 'o_nc%d' % gi, out=(ncs[l] if sample else ncp[l]), in_=U32[:, :, n - 30:n],
                              reads=[('U32', 0), ('U32', 1)], is_out=True)
                    if gi < 3:
                        S.op('dve', lambda e: e.tensor_copy(out=CU[(gi + 1) % 2][:, :, 0:30], in_=CU[slot][:, :, 512:542]),
                             reads=[('CUn', slot, 0), ('CUn', slot, 1)], writes=[('CUc', (gi + 1) % 2)])

                def stageC(gi):
                    c0, n, tiles = GROUPS[gi]
                    sample = (gi == 4)
                    slot = 0 if sample else gi % 2
                    MPC = MPC2[gi % 2]; sl = sl2[gi % 2]
                    for j in range(2):
                        S.op('pe', lambda e: e.matmul(PS[:, j, 0:n], lhsT=PLW[:, j, :], rhs=dsb[:, j, 0:n], start=True, stop=True),
                             reads=[('dsb', j), 'PLW'], writes=[('ps', j)])
                        S.op('act', lambda e: e.activation(out=MPC[:, j, 0:n], in_=PS[:, j, 0:n], func=AF.Copy, scale=PV[:, l, j:j + 1]),
                             reads=[('ps', j), 'PV'], writes=[('MPC', gi % 2, j)])
                    for j in range(2):
                        def cv_mm(e):
                            for tap in range(31):
                                ins = e.matmul(PS[:, 4 + j, 0:n], lhsT=DG[:, j * 31 + tap, :], rhs=CU[slot][:, j, tap:tap + n],
                                               start=(tap == 0), stop=(tap == 30))
                            return ins
                        S.op('pe', cv_mm, reads=['DG', ('CUc', slot), ('CUn', slot, j)], writes=[('ps', 4 + j)])
                        S.op('act', lambda e: e.activation(out=ysb[:, j, 0:n], in_=PS[:, 4 + j, 0:n], func=AF.Identity, bias=PV[:, l, 2 + j:3 + j]),
                             reads=[('ps', 4 + j), 'PV'], writes=[('ysb', j)])
                        S.op('act', lambda e: e.activation(out=ysq[:, j, 0:n], in_=PS[:, 4 + j, 0:n], func=AF.Square, bias=PV[:, l, 2 + j:3 + j]),
                             reads=[('ps', 4 + j), 'PV'], writes=[('ysq', j)])

                    def st_mm(e):
                        for j in range(2):
                            e.matmul(PS[:, 6, 0:n], lhsT=onesf[:, 1, :], rhs=ysb[:, j, 0:n], start=(j == 0), stop=(j == 1))
                        for j in range(2):
                            ins = e.matmul(PS[:, 7, 0:n], lhsT=onesf[:, 1, :], rhs=ysq[:, j, 0:n], start=(j == 0), stop=(j == 1))
                        return ins
                    S.op('pe', st_mm, reads=[('ysb', 0), ('ysb', 1), ('ysq', 0), ('ysq', 1), 'onesf1'], writes=[('ps', 6), ('ps', 7)])
                    S.op('act', lambda e: e.activation(out=m2[:, 0:n], in_=PS[:, 6, 0:n], func=AF.Square), reads=[('ps', 6)], writes=['m2'])
                    S.op('dve', lambda e: e.scalar_tensor_tensor(out=var[:, 0:n], in0=PS[:, 7, 0:n], scalar=EPS, in1=m2[:, 0:n],
                                                                 op0=ALU.add, op1=ALU.subtract),
                         reads=[('ps', 7), 'm2'], writes=['var'])
                    S.op('act', lambda e: e.activation(out=m2[:, 0:n], in_=var[:, 0:n], func=AF.Ln), reads=['var'], writes=['m2'])
                    S.op('act', lambda e: e.activation(out=var[:, 0:n], in_=m2[:, 0:n], func=AF.Exp, scale=-0.5), reads=['m2'], writes=['var'])
                    for j in range(2):
                        S.op('dve', lambda e: e.tensor_tensor(out=ysq[:, j, 0:n], in0=ysb[:, j, 0:n], in1=PS[:, 6, 0:n], op=ALU.subtract),
                             reads=[('ysb', j), ('ps', 6)], writes=[('ysq', j)])
                        S.op('dve', lambda e: e.tensor_tensor(out=ysb[:, j, 0:n], in0=ysq[:, j, 0:n], in1=var[:, 0:n], op=ALU.mult),
                             reads=[('ysq', j), 'var'], writes=[('ysb', j)])
                        S.op('act', lambda e: e.activation(out=sl[:, j, 0:n], in_=ysb[:, j, 0:n], func=AF.Silu,
                                                           scale=PV[:, l, 4 + j:5 + j], bias=PV[:, l, 6 + j:7 + j]),
                             reads=[('ysb', j), 'PV'], writes=[('sl', gi % 2, j)])

                def stageB(gi):
                    c0, n, tiles = GROUPS[gi]
                    MPC = MPC2[gi % 2]; sl = sl2[gi % 2]
                    for jo in range(2):
                        def pw_mm(e):
                            for j in range(2):
                                ins = e.matmul(PS[:, jo, 0:n], lhsT=PW[:, j, jo * 128:(jo + 1) * 128], rhs=sl[:, j, 0:n],
                                               start=(j == 0), stop=(j == 1))
                            return ins
                        S.op('pe', pw_mm, reads=[('sl', gi % 2, 0), ('sl', gi % 2, 1), 'PW'], writes=[('ps', jo)])
                        S.op('act', lambda e: e.activation(out=MPC[:, 2 + jo, 0:n], in_=PS[:, jo, 0:n], func=AF.Copy),
                             reads=[('ps', jo)], writes=[('MPC', gi % 2, 2 + jo)])
                    for ti, t in enumerate(tiles):
                        P = TP(t)
                        for hh in range(2):
                            bank = (6, 7, 0, 1)[(2 * ti + hh) % 4]

                            def wo_mm(e):
                                for kc in range(4):
                                    ins = e.matmul(PS[:P, bank, :], lhsT=MPC[:, kc, ti * 128:ti * 128 + P],
                                                   rhs=WOP[:, kc, hh * 512:(hh + 1) * 512], start=(kc == 0), stop=(kc == 3))
                                return ins
                            S.op('pe', wo_mm, reads=[('MPC', gi % 2, k) for k in range(4)] + ['WOP0', 'WOP1'], writes=[('ps', bank)])
                            S.op('dve', lambda e: e.tensor_tensor(out=X[:P, t, hh * 512:(hh + 1) * 512], in0=X[:P, t, hh * 512:(hh + 1) * 512],
                                                                  in1=PS[:P, bank, :], op=ALU.add),
                                 reads=[('ps', bank), ('X', t)], writes=[('X', t)])

                for gi in range(len(GROUPS)):
                    stageA(gi)
                    if gi > 0:
                        stageB(gi - 1)
                    stageC(gi)
                stageB(len(GROUPS) - 1)
                S.barrier()

            with ExitStack() as ph:
                W = [Wq0] + [sb(ph, "Wq%d" % i, [128, 8, 512], BF16) for i in (1, 2)]
                COS = sb(ph, "COS", [128, 17, 32], F32); SIN = sb(ph, "SIN", [128, 17, 32], F32)
                zs = [sb(ph, "zs%d" % i, [128, 512], F32) for i in range(3)]
                kr = [sb(ph, "kr%d" % i, [128, 512], F32) for i in range(2)]
                t1 = [sb(ph, "t1%d" % i, [128, 512], F32) for i in range(2)]
                t2 = [sb(ph, "t2%d" % i, [128, 512], F32) for i in range(2)]
                qb = [sb(ph, "qb%d" % i, [128, 512], BF16) for i in range(6)]
                S.dma('sp', 'cos', out=COS[:], in_=cos_t[:, :, :], writes=['COS'])
                S.dma('sp', 'sin', out=SIN[:], in_=sin_t[:, :, :], writes=['SIN'])
                for pi, c0 in ((1, 768), (2, 1280)):
                    S.dma('pool', 'wq%d' % pi, out=W[pi][:], in_=w_in3[:, :, c0:c0 + 512], writes=[('W', pi)])
                pend = []
                cnt = 0
                rc = 0
                NQ0 = 10
                items = [(t, 0) for t in range(NQ0)] + [(t, pi) for t in range(NQ0) for pi in (1, 2)] + \
                        [(t, pi) for t in range(NQ0, NT) for pi in (0, 1, 2)]
                for (t, pi) in items:
                    P = TP(t); c = t * 128
                    nm = ('q', 'k', 'v')[pi]
                    if True:
                        Wc = W[pi]
                        bank = cnt % 4
                        s2 = cnt % 3
                        cnt += 1
                        qs = rc % 6

                        def mm(e):
                            for kc in range(8):
                                ins = e.matmul(PS[:P, bank, :], lhsT=H[:, kc, c:c + P], rhs=Wc[:, kc, :], start=(kc == 0), stop=(kc == 7))
                            return ins
                        S.op('pe', mm, reads=[('H', t), ('W', pi)], writes=[('ps', bank)])
                        while len(pend) > 3:
                            pend.pop(0)()
                        if nm == 'v':
                            S.op('act', lambda e: e.activation(out=zs[s2][:P], in_=PS[:P, bank, :], func=AF.Copy),
                                 reads=[('ps', bank)], writes=[('zs', s2)])
                            S.dma('sp', 'o_v%d' % s2, out=rows_v(l, t), in_=zs[s2][:P], reads=[('zs', s2)], is_out=True)
                            S.op('act', lambda e: e.activation(out=VB[:P, t, :], in_=zs[s2][:P], func=AF.Copy), reads=[('zs', s2)], writes=[('VB', t)])
                            continue
                        r2 = rc % 2
                        rc += 1
                        z4 = PS[:P, bank, :].rearrange("p (g two i) -> p g two i", g=8, two=2)
                        cosb = COS[:P, t:t + 1, :].unsqueeze(1).broadcast_to([P, 8, 2, 32])
                        sinb = SIN[:P, t:t + 1, :].broadcast_to([P, 8, 32])
                        t14 = t1[r2][:P].rearrange("p (g two i) -> p g two i", g=8, two=2)
                        t24 = t2[r2][:P].rearrange("p (g two i) -> p g two i", g=8, two=2)
                        S.op('dve', lambda e: e.tensor_tensor(out=t14, in0=z4, in1=cosb, op=ALU.mult),
                             reads=[('ps', bank), 'COS'], writes=[('t1', r2)])
                        S.op('dve', lambda e: e.tensor_tensor(out=t24[:, :, 0, :], in0=z4[:, :, 1, :], in1=sinb, op=ALU.mult),
                             reads=[('ps', bank), 'SIN'], writes=[('t2a', r2)])
                        S.op('dve', lambda e: e.tensor_tensor(out=t24[:, :, 1, :], in0=z4[:, :, 0, :], in1=sinb, op=ALU.mult),
                             reads=[('ps', bank), 'SIN'], writes=[('t2b', r2)])
                        if nm == 'q':
                            o4 = qb[qs][:P].rearrange("p (g two i) -> p g two i", g=8, two=2)
                        else:
                            o4 = kr[r2][:P].rearrange("p (g two i) -> p g two i", g=8, two=2)
                        okey = ('qb', qs) if nm == 'q' else ('kr', r2)
                        S.op('dve', lambda e: e.tensor_tensor(out=o4[:, :, 0, :], in0=t14[:, :, 0, :], in1=t24[:, :, 0, :], op=ALU.subtract),
                             reads=[('t1', r2), ('t2a', r2)], writes=[(okey, 'a')])
                        S.op('dve', lambda e: e.tensor_tensor(out=o4[:, :, 1, :], in0=t14[:, :, 1, :], in1=t24[:, :, 1, :], op=ALU.add),
                             reads=[('t1', r2), ('t2b', r2)], writes=[(okey, 'b')])
                        if nm == 'k':
                            S.dma('sp', 'o_k%d' % r2, out=rows_k(l, t), in_=kr[r2][:P], reads=[(okey, 'a'), (okey, 'b')], is_out=True)
                            S.op('act', lambda e: e.activation(out=qb[qs][:P], in_=kr[r2][:P], func=AF.Copy),
                                 reads=[(okey, 'a'), (okey, 'b')], writes=[(('qb', qs), 'a'), (('qb', qs), 'b')])

                        def mk(t=t, P=P, c=c, qs=qs, nm=nm, tb=4 + (rc % 4)):
                            def run():
                                pv = psb(tb).rearrange("p (k n) -> p k n", k=8)

                                def tr(e):
                                    for h in range(4):
                                        ins = e.transpose(out=pv[:, h, :P], in_=qb[qs][:P, h * 128:(h + 1) * 128], identity=idb[:P, :P])
                                    return ins
                                S.op('pe', tr, reads=[(('qb', qs), 'a'), (('qb', qs), 'b'), 'idb'], writes=[('ps', tb)])
                                dst = QT if nm == 'q' else KT
                                S.op('act', lambda e: e.activation(out=dst[:, :, c:c + P], in_=pv[:, 0:4, :P], func=AF.Copy),
                                     reads=[('ps', tb)], writes=[(nm + 'T', t)])
                            return run
                        pend.append(mk())
                while pend:
                    pend.pop(0)()
                S.barrier()
            phq.close()

            phw = ExitStack()
            WOA = sb(phw, "WOA", [128, 4, 1024], BF16)
            S.dma('pool', 'woa', out=WOA[:], in_=w_out3[:, 2:6, :], writes=['WOA'])
            ckb = [sb(phw, "ckb%d" % i, [128, 512], BF16) for i in range(4)]
            cvb = [sb(phw, "cvb%d" % i, [128, 512], BF16) for i in range(4)]
            for i in range(4):
                S.dma('pool', 'ck%d' % i, out=ckb[i][:], in_=ck[l, i * 128:(i + 1) * 128, :], writes=[('ckb', i)])
                S.dma('pool', 'cv%d' % i, out=cvb[i][:], in_=cv[l, i * 128:(i + 1) * 128, :], writes=[('cvb', i)])
            with ExitStack() as ph:
                PT = [sb(ph, "PT%d" % i, [128, 2, 512], BF16) for i in range(3)]
                OSB = [sb(ph, "OSB%d" % i, [128, 2, 512], F32) for i in range(2)]
                LNL = [sb(ph, "LNL%d" % i, [128, 2, 512], F32) for i in range(2)]
                dd = sb(ph, "dd", [128, 512], F32); sq = sb(ph, "sq", [128, 512], F32)
                msb = sb(ph, "msb", [128, 512], F32); rs = sb(ph, "rs", [128, 512], F32)
                MB = sb(ph, "MB", [128, 1], F32)
                S.op('dve', lambda e: e.memset(MB[0:64, :], 0.0), writes=['MBa'])
                S.op('dve', lambda e: e.memset(MB[64:128, :], NEG), writes=['MBb'])
                negl = LAM[:, l, 4:5]; gsc = LAM[:, l, 5:6]
                blocks = []
                for n_it, (h, j) in enumerate([(h, j) for h in range(4) for j in range(4)]):
                    for i in range(4 * j + 4):
                        blocks.append((n_it, h, j, i))
                NBK = len(blocks)

                def off_of(j, i):
                    r = i - 4 * j
                    return 128 * r if r > 0 else 0

                def qk(g):
                    n_it, h, j, i = blocks[g]
                    slot = g % 2
                    off = off_of(j, i)
                    q0 = j * 512

                    def f(e):
                        for c in range(2):
                            ins = e.matmul(PS[:, 2 * slot + c, off:512], lhsT=KT[c * 64:(c + 1) * 64, h, i * 128:(i + 1) * 128],
                                           rhs=QT[c * 64:(c + 1) * 64, h, q0 + off:q0 + 512], start=True, stop=True)
                        return ins
                    S.op('pe', f, reads=[('QT', h, j)], writes=[('ps', 2 * slot), ('ps', 2 * slot + 1)])

                def ex(g):
                    n_it, h, j, i = blocks[g]
                    slot = g % 2
                    off = off_of(j, i)
                    ps_ = g % 3
                    diag = (i - 4 * j) >= 0
                    src = PS[:, 2 * slot:2 * slot + 2, :]
                    if diag:
                        S.op('act', lambda e: e.activation(out=PT[ps_][:, :, off:off + 64], in_=src[:, :, off:off + 64], func=AF.Exp,
                                                           scale=SCALE, bias=MB[:, 0:1]),
                             reads=[('ps', 2 * slot), ('ps', 2 * slot + 1), 'MBa', 'MBb'], writes=[('PT', ps_, 'm')])
                        S.op('act', lambda e: e.activation(out=PT[ps_][:, :, off + 64:512], in_=src[:, :, off + 64:512], func=AF.Exp,
                                                           scale=SCALE),
                             reads=[('ps', 2 * slot), ('ps', 2 * slot + 1)], writes=[('PT', ps_)])
                    else:
                        S.op('act', lambda e: e.activation(out=PT[ps_][:, :, :], in_=src, func=AF.Exp, scale=SCALE),
                             reads=[('ps', 2 * slot), ('ps', 2 * slot + 1)], writes=[('PT', ps_), ('PT', ps_, 'm')])

                def pvm(g):
                    n_it, h, j, i = blocks[g]
                    off = off_of(j, i)
                    ps_ = g % 3
                    nb = 4 * j + 4

                    def f(e):
                        for c in range(2):
                            e.matmul(PS[:, 4 + c, off:512], lhsT=VB[:, i, h * 128:(h + 1) * 128], rhs=PT[ps_][:, c, off:512],
                                     start=(i == 0), stop=(i == nb - 1))
                        for c in range(2):
                            ins = e.matmul(PS[:, 6 + c, off:512], lhsT=onesb[:], rhs=PT[ps_][:, c, off:512],
                                           start=(i == 0), stop=(i == nb - 1))
                        return ins
                    S.op('pe', f, reads=[('PT', ps_), ('PT', ps_, 'm'), 'onesb'], writes=[('ps', 4), ('ps', 5), ('ps', 6), ('ps', 7)])

                def mkB(st):
                    def run(sbk):
                        S.op('act', lambda e: e.activation(out=LNL[st][:], in_=LNL[st][:], func=AF.Exp, scale=-1.0),
                             reads=[('LNL', st)], writes=[('LNL', st)])
                        S.op('dve', lambda e: e.tensor_tensor(out=OSB[st][:], in0=OSB[st][:], in1=LNL[st][:], op=ALU.mult),
                             reads=[('OSB', st), ('LNL', st)], writes=[('OSB', st)])
                        S.op('dve', lambda e: e.scalar_tensor_tensor(out=dd[:], in0=OSB[st][:, 1, :], scalar=negl, in1=OSB[st][:, 0, :],
                                                                     op0=ALU.mult, op1=ALU.add),
                             reads=[('OSB', st)], writes=['dd'])
                        S.op('dve', lambda e: e.tensor_tensor(out=sq[:], in0=dd[:], in1=dd[:], op=ALU.mult), reads=['dd'], writes=['sq'])
                    return run

                def mkC1():
                    def run(sbk):
                        S.op('pe', lambda e: e.matmul(PS[:, sbk, :], lhsT=onesf[:, 0, :], rhs=sq[:], start=True, stop=True),
                             reads=['sq', 'onesf0'], writes=[('ps', sbk)])
                        S.op('dve', lambda e: e.tensor_scalar(out=msb[:], in0=PS[:, sbk, :], scalar1=EPS, scalar2=None, op0=ALU.add),
                             reads=[('ps', sbk)], writes=['msb'])
                    return run

                def mkC2(h, j):
                    q0 = j * 512

                    def run():
                        S.op('act', lambda e: e.activation(out=msb[:], in_=msb[:], func=AF.Ln), reads=['msb'], writes=['msb'])
                        S.op('act', lambda e: e.activation(out=rs[:], in_=msb[:], func=AF.Exp, scale=-0.5), reads=['msb'], writes=['rs'])
                        S.op('dve', lambda e: e.scalar_tensor_tensor(out=QT[:, h, q0:q0 + 512], in0=dd[:], scalar=gsc, in1=rs[:],
                                                                     op0=ALU.mult, op1=ALU.mult),
                             reads=['dd', 'rs'], writes=[('QT', h, j)])
                    return run

                pendB = None
                savedC = None
                pendC2 = None
                qk(0)
                qk(1)
                for g in range(NBK):
                    n_it, h, j, i = blocks[g]
                    nb = 4 * j + 4
                    st = n_it % 2
                    ex(g)
                    if pendC2 is not None:
                        pendC2()
                        pendC2 = None
                    pvm(g)
                    if i == nb - 1:
                        S.op('dve', lambda e: e.tensor_copy(out=OSB[st][:], in_=PS[:, 4:6, :]),
                             reads=[('ps', 4), ('ps', 5)], writes=[('OSB', st)])
                        S.op('act', lambda e: e.activation(out=LNL[st][:], in_=PS[:, 6:8, :], func=AF.Ln),
                             reads=[('ps', 6), ('ps', 7)], writes=[('LNL', st)])
                        if savedC is not None:
                            savedC[0](2 * (g % 2))
                            pendC2 = savedC[1]
                        savedC = (mkC1(), mkC2(h, j))
                        pendB = (g + 2, mkB(st))
                    if pendB is not None and pendB[0] <= g:
                        pendB[1](0)
                        pendB = None
                    if g + 2 < NBK:
                        qk(g + 2)
                if pendC2 is not None:
                    pendC2()
                if pendB is not None:
                    pendB[1](0)
                savedC[0](0); savedC[1]()
                S.barrier()

            phf = ExitStack()
            GU = [sb(phf, "GU%d" % i, [128, 2, 8, 256], BF16) for i in range(2)]
            sttF = sb(phf, "sttF", [128, 4, NT], F32)
            junkF = GU[1][:].rearrange("p a b c -> p (a b c)")[:, 0:1024]
            S.op('dve', lambda e: e.memset(sttF[:, 0, :], 1.0), writes=['stF0i'])

            def issue_gu_abs(fs, w, slot):
                S.dma('pool', 'gug%d' % slot, out=GU[slot][:, 0, :, 0:w * 128], in_=w_gu3[:, :, fs * 128:(fs + w) * 128],
                      writes=[('GUg', slot)])
                S.dma('pool', 'guu%d' % slot, out=GU[slot][:, 1, :, 0:w * 128], in_=w_gu3[:, :, 2816 + fs * 128:2816 + (fs + w) * 128],
                      writes=[('GUu', slot)])
            issue_gu_abs(0, 2, 0)
            def wo_tile(t, banks):
                P = TP(t); c = t * 128
                for hh in range(2):
                    bank = banks[hh]

                    def wo_mm(e):
                        for kc in range(4):
                            ins = e.matmul(PS[:P, bank, :], lhsT=QT[:, kc, c:c + P], rhs=WOA[:, kc, hh * 512:(hh + 1) * 512],
                                           start=(kc == 0), stop=(kc == 3))
                        return ins
                    S.op('pe', wo_mm, reads=['WOA', ('QT', 's')] if t == 16 else ['WOA'], writes=[('ps', bank)])
                    S.op('dve', lambda e: e.tensor_tensor(out=X[:P, t, hh * 512:(hh + 1) * 512], in0=X[:P, t, hh * 512:(hh + 1) * 512],
                                                          in1=PS[:P, bank, :], op=ALU.add),
                         reads=[('ps', bank), ('X', t)], writes=[('X', t)])
                if t == 16:
                    S.op('act', lambda e: e.activation(out=junkF[:P], in_=X[:P, t, :], func=AF.Square, accum_out=sttF[:P, 0, t:t + 1]),
                         reads=[('X', t), 'stF0i'], writes=['junkF', ('stF0', t)])
                else:
                    S.op('dve', lambda e: e.scalar_tensor_tensor(out=junkF[:P], in0=X[:P, t, :], scalar=1.0, in1=X[:P, t, :],
                                                                 op0=ALU.mult, op1=ALU.mult, accum_out=sttF[:P, 0, t:t + 1]),
                         reads=[('X', t), 'stF0i'], writes=['junkF', ('stF0', t)])

            with ExitStack() as ph:
                ckT = [sb(ph, "ckT%d" % i, [128, 4, 128], BF16) for i in range(2)]
                PTs = [sb(ph, "PTs%d" % i, [128, 8, 32], BF16) for i in range(2)]
                rr = sb(ph, "rr", [128, 8, 32], F32); a8 = sb(ph, "a8", [128, 2, 4, 32], F32)
                d4 = sb(ph, "d4", [128, 4, 32], F32); sq4 = sb(ph, "sq4", [128, 4, 32], F32)
                ms4 = sb(ph, "ms4", [128, 128], F32); ln4 = sb(ph, "ln4", [128, 128], F32); rs4 = sb(ph, "rs4", [128, 128], F32)
                negl = LAM[:, l, 4:5]; gsc = LAM[:, l, 5:6]
                OS = PS[:, 4, 0:256].rearrange("p (g n) -> p g n", g=8)
                LS = PS[:, 5, 0:256].rearrange("p (g n) -> p g n", g=8)
                ATSM = 9
                for i in range(17 if ATSM > 0 else 0):
                    slot = i % 2
                    cs = i % 4
                    KP = 128 if i < 16 else 32
                    if i < 16:
                        if i >= 4:
                            S.dma('pool', 'ck%d' % cs, out=ckb[cs][:], in_=ck[l, i * 128:(i + 1) * 128, :], writes=[('ckb', cs)])
                            S.dma('pool', 'cv%d' % cs, out=cvb[cs][:], in_=cv[l, i * 128:(i + 1) * 128, :], writes=[('cvb', cs)])
                        pv = psb(slot).rearrange("p (k n) -> p k n", k=8)

                        def tr(e):
                            for h in range(4):
                                ins = e.transpose(out=pv[:, h, :], in_=ckb[cs][:, h * 128:(h + 1) * 128], identity=idb[:])
                            return ins
                        S.op('pe', tr, reads=[('ckb', cs), 'idb'], writes=[('ps', slot)])
                        S.op('act', lambda e: e.activation(out=ckT[slot][:], in_=pv[:, 0:4, :], func=AF.Copy),
                             reads=[('ps', slot)], writes=[('ckT', slot)])
                    sb0 = 2 if slot == 0 else 6
                    SS2 = PS[:, sb0:sb0 + 2, 0:128]
                    if ATSM < 2:
                        continue

                    def qk(e):
                        for h in range(4):
                            for c in range(2):
                                if i < 16:
                                    kT = ckT[slot][c * 64:(c + 1) * 64, h, :]
                                else:
                                    kT = KT[c * 64:(c + 1) * 64, h, 2048:2080]
                                ins = e.matmul(PS[:KP, sb0 + c, h * 32:(h + 1) * 32], lhsT=kT, rhs=QT[c * 64:(c + 1) * 64, h, 2048:2080],
                                               start=True, stop=True)
                        return ins
                    S.op('pe', qk, reads=[('ckT', slot), ('QT', 's')], writes=[('ps', sb0), ('ps', sb0 + 1)])
                    S.op('act', lambda e: e.activation(out=PTs[slot][:KP].rearrange("p (c x) n -> p c (x n)", c=2), in_=SS2[:KP], func=AF.Exp, scale=SCALE),
                         reads=[('ps', sb0), ('ps', sb0 + 1)], writes=[('PTs', slot)])
                    if i < 16:
                        wo_tile(i, (6, 7) if slot == 0 else (2, 3))
                    if ATSM < 3:
                        continue

                    def pvm(e):
                        first = True
                        for c in range(2):
                            for h in range(4):
                                if i < 16:
                                    vv = cvb[cs][:, h * 128:(h + 1) * 128]
                                else:
                                    vv = VB[:32, 16, h * 128:(h + 1) * 128]
                                e.matmul(OS[:, c * 4 + h, :], lhsT=vv, rhs=PTs[slot][:KP, c * 4 + h, :],
                                         start=(i == 0 and first), stop=(i == 16), skip_group_check=True)
                                first = False
                        ins = e.matmul(PS[:, 5, 0:256], lhsT=onesb[:KP, :], rhs=PTs[slot][:KP].rearrange("p g n -> p (g n)"),
                                       start=(i == 0), stop=(i == 16))
                        return ins
                    S.op('pe', pvm, reads=[('PTs', slot), ('cvb', cs), 'onesb'], writes=[('ps', 4), ('ps', 5)])
                if ATSM < 4:
                    S.stopped = True
                S.op('dve', lambda e: e.reciprocal(out=rr[:], in_=LS), reads=[('ps', 5)], writes=['rr'])
                S.op('dve', lambda e: e.tensor_tensor(out=a8[:].rearrange("p c h n -> p (c h) n"), in0=OS, in1=rr[:], op=ALU.mult),
                     reads=[('ps', 4), 'rr'], writes=['a8'])
                S.op('dve', lambda e: e.scalar_tensor_tensor(out=d4[:], in0=a8[:, 1, :, :], scalar=negl, in1=a8[:, 0, :, :],
                                                             op0=ALU.mult, op1=ALU.add), reads=['a8'], writes=['d4'])
                S.op('act', lambda e: e.activation(out=sq4[:], in_=d4[:], func=AF.Square), reads=['d4'], writes=['sq4'])
                S.op('pe', lambda e: e.matmul(PS[:, 0, 0:128], lhsT=onesf[:, 0, :], rhs=sq4[:].rearrange("p h n -> p (h n)"), start=True, stop=True),
                     reads=['sq4', 'onesf0'], writes=[('ps', 0)])
                S.op('dve', lambda e: e.tensor_scalar(out=ms4[:], in0=PS[:, 0, 0:128], scalar1=EPS, scalar2=None, op0=ALU.add),
                     reads=[('ps', 0)], writes=['ms4'])
                S.op('act', lambda e: e.activation(out=ln4[:], in_=ms4[:], func=AF.Ln), reads=['ms4'], writes=['ln4'])
                S.op('act', lambda e: e.activation(out=rs4[:], in_=ln4[:], func=AF.Exp, scale=-0.5), reads=['ln4'], writes=['rs4'])
                S.op('dve', lambda e: e.scalar_tensor_tensor(out=QT[:, :, 2048:2080], in0=d4[:], scalar=gsc,
                                                             in1=rs4[:].rearrange("p (h n) -> p h n", h=4), op0=ALU.mult, op1=ALU.mult),
                     reads=['d4', 'rs4'], writes=[('QT', 's')])
                wo_tile(16, (2, 3))
                S.barrier()

            phase_norm(gffn[l:l + 1, :], stt_ext=sttF)
            with ExitStack() as ph:
                DN = [sb(ph, "DN%d" % i, [128, 8, 512], BF16) for i in range(2)]
                sgl = [sb(ph, "sgl%d" % i, [128, 512], F32) for i in range(2)]
                gcount = 0
                dcount = 0
                ev = 0
                for (f0, f1) in FFN_PASSES:
                    nf = f1 - f0
                    slabs = []
                    f = f0
                    while f < f1:
                        w = min(2, f1 - f)
                        slabs.append((f, w))
                        f += w

                    def issue_gu(si):
                        fs, w = slabs[si]
                        slot = (gcount + si) % 2
                        issue_gu_abs(fs, w, slot)
                    if f0 > 0:
                        issue_gu(0)
                    for hh in range(2):
                        S.dma('pool', 'dn%d' % hh, out=DN[hh][:, 0:nf, :], in_=w_dn3[:, f0:f1, hh * 512:(hh + 1) * 512], writes=[('DN', hh)])
                    for si, (fs, w) in enumerate(slabs):
                        slot = (gcount + si) % 2
                        if si + 1 < len(slabs):
                            issue_gu(si + 1)
                        for fo in range(w):
                            fi = fs + fo - f0
                            for (c0, n, tiles) in GROUPS:
                                bg = (ev % 3) * 2
                                ev += 1

                                def gu_mm(e):
                                    for kc in range(8):
                                        e.matmul(PS[:, bg, 0:n], lhsT=GU[slot][:, 0, kc, fo * 128:(fo + 1) * 128], rhs=H[:, kc, c0:c0 + n],
                                                 start=(kc == 0), stop=(kc == 7))
                                    for kc in range(8):
                                        ins = e.matmul(PS[:, bg + 1, 0:n], lhsT=GU[slot][:, 1, kc, fo * 128:(fo + 1) * 128], rhs=H[:, kc, c0:c0 + n],
                                                       start=(kc == 0), stop=(kc == 7))
                                    return ins
                                S.op('pe', gu_mm, reads=[('GUg', slot), ('GUu', slot)], writes=[('ps', bg), ('ps', bg + 1)])
                                ss = ev % 2
                                S.op('act', lambda e: e.activation(out=sgl[ss][:, 0:n], in_=PS[:, bg, 0:n], func=AF.Silu),
                                     reads=[('ps', bg)], writes=[('sgl', ss)])
                                S.op('dve', lambda e: e.tensor_tensor(out=ACTH[:, fi, c0:c0 + n], in0=sgl[ss][:, 0:n], in1=PS[:, bg + 1, 0:n], op=ALU.mult),
                                     reads=[('sgl', ss), ('ps', bg + 1)], writes=[('ACTH', fi)])
                    gcount += len(slabs)
                    lastpass = (f1 == 22)
                    last = lastpass and l == DEPTH - 1
                    mid = lastpass and l < DEPTH - 1
                    if lastpass:
                        GUf = [GU[i][:].rearrange("p a b c -> p (a b c)").bitcast(F32) for i in range(2)]
                        GU1b = GU[1][:].rearrange("p a b c -> p (a b c)")
                        Gfin = GUf[0][:, 0:1024]
                        stf = GUf[0][:, 1024:1024 + 4 * NT].rearrange("p (t k) -> p t k", k=4)
                        yof = [GUf[1][:, i * 1024:(i + 1) * 1024] for i in range(2)]
                        hbf = [GU1b[:, i * 1024:(i + 1) * 1024] for i in range(3)]
                        junkf = GU1b[:, 3072:4096]
                        grow = gfin[0:1, :] if last else gmix[l + 1:l + 2, :]
                    if mid:
                        w_in3n = w_in[l + 1].rearrange("(k p) c -> p k c", p=128)
                        WPn = FA[:, 16640:22784].rearrange("p (k c) -> p k c", k=8)
                        S.dma('pool', 'wp0', out=WPn[:, :, 0:256], in_=w_in3n[:, :, 0:256], writes=['WP0'])
                        S.dma('pool', 'wp1', out=WPn[:, :, 256:768], in_=w_in3n[:, :, 1792:2304], writes=['WP1'])
                    order = [(t, hh) for t in range(NT) for hh in range(2)] if lastpass else [(t, hh) for hh in range(2) for t in range(NT)]
                    for (t, hh) in order:
                        dslot = hh
                        P = TP(t); c = t * 128
                        bank = 4 + ((2 * t + hh) % 4 if lastpass else t % 4)

                        def dn_mm(e):
                            for fi in range(nf):
                                ins = e.matmul(PS[:P, bank, :], lhsT=ACTH[:, fi, c:c + P], rhs=DN[dslot][:, fi, :],
                                               start=(fi == 0), stop=(fi == nf - 1))
                            return ins
                        S.op('pe', dn_mm, reads=[('ACTH', fi) for fi in range(nf)] + [('DN', dslot)], writes=[('ps', bank)])
                        S.op('dve', lambda e: e.tensor_tensor(out=X[:P, t, hh * 512:(hh + 1) * 512], in0=X[:P, t, hh * 512:(hh + 1) * 512],
                                                              in1=PS[:P, bank, :], op=ALU.add),
                             reads=[('ps', bank), ('X', t)], writes=[('X', t)])
                        if lastpass and hh == 1:
                            if t == 0:
                                S.dma('sp', 'Gf', out=Gfin, in_=grow.partition_broadcast(128), reads=[('X', 0)], writes=['Gfin'])

                            def fin1(t):
                                P = TP(t)
                                jout = yof[t % 2][:P] if last else junkf[:P]
                                jkey = ('yo', t % 2) if last else 'junkf'
                                S.op('act', lambda e: e.activation(out=jout, in_=X[:P, t, :], func=AF.Square, accum_out=stf[:P, t, 0:1]),
                                     reads=[('X', t)], writes=[jkey, ('st', t, 0)])

                            def fin2(t):
                                rstd_chain(stf, TP(t), t, None)

                            def fin3(t):
                                P = TP(t)
                                if last:
                                    S.op('dve', lambda e: e.scalar_tensor_tensor(out=yof[t % 2][:P], in0=X[:P, t, :], scalar=stf[:P, t, 3:4],
                                                                                 in1=Gfin[:P], op0=ALU.mult, op1=ALU.mult),
                                         reads=[('X', t), ('st', t, 3), 'Gfin'], writes=[('yo', t % 2)])
                                    S.dma('sp', 'o_y%d' % (t % 2), out=rows_y(t), in_=yof[t % 2][:P], reads=[('yo', t % 2)], is_out=True)
                                else:
                                    S.op('dve', lambda e: e.scalar_tensor_tensor(out=hbf[t % 3][:P], in0=X[:P, t, :], scalar=stf[:P, t, 3:4],
                                                                                 in1=Gfin[:P], op0=ALU.mult, op1=ALU.mult),
                                         reads=[('X', t), ('st', t, 3), 'Gfin'], writes=[('hbf', t % 3)])

                            def fin4(t):
                                if last:
                                    return
                                P = TP(t); c4 = t * 128
                                tbk = t % 4
                                pvf = psb(tbk).rearrange("p (k n) -> p k n", k=8)

                                def tr(e):
                                    for kc in range(8):
                                        ins = e.transpose(out=pvf[:, kc, :P], in_=hbf[t % 3][:P, kc * 128:(kc + 1) * 128], identity=idb[:P, :P])
                                    return ins
                                S.op('pe', tr, reads=[('hbf', t % 3), 'idb'], writes=[('ps', tbk)])
                                S.op('act', lambda e: e.activation(out=H[:, :, c4:c4 + P], in_=pvf[:, :, :P], func=AF.Copy),
                                     reads=[('ps', tbk)], writes=[('H', t)])
                            if t >= 3:
                                fin4(t - 3)
                            if t >= 2:
                                fin3(t - 2)
                            if t >= 1:
                                fin2(t - 1)
                            fin1(t)
                            if t == NT - 1:
                                fin4(t - 2)
                                fin3(t - 1)
                                fin2(t)
                                fin4(t - 1)
                                fin3(t)
                                fin4(t)
                S.barrier()
            phf.close()
            phw.close()

        S.stopped = False

        if True:
            S.finish()
    return nc


def _host_consts():
    half = 32
    inv = (np.float32(10000.0) ** (-np.arange(half, dtype=np.float32) / np.float32(half))).astype(np.float32)
    pos = np.zeros((128, 17), np.float32)
    for t in range(16):
        pos[:, t] = t * 128 + np.arange(128)
    pos[:, 16] = 2048 + np.arange(128)
    ang = (pos[:, :, None] * inv[None, None, :]).astype(np.float32)
    cos_t = np.cos(ang).astype(np.float32)
    sin_t = np.sin(ang).astype(np.float32)
    wins = np.array([[2, 4], [8, 16]])
    invc = np.zeros((128, 2, 16), np.float32)
    invw = np.zeros((128, 2), np.float32)
    for j in range(2):
        for p in range(128):
            w = wins[j][p // 64]
            invw[p, j] = 1.0 / w
            invc[p, j, :] = 1.0 / np.minimum(np.arange(16) + 1, w)
    return cos_t, sin_t, invc, invw


_NC_CACHE = {}


def kernel(x_prompt, x_sample, cache_k, cache_v, state_pool, state_conv,
           norm_mix_g, w_in, pool_w, pool_scale, lambda_qk, diff_norm_g,
           conv_dw, conv_dw_b, conv_ln_g, conv_ln_b, conv_pw, w_out,
           norm_ffn_g, w_gate_up, w_down, final_norm_g):
    f = lambda a: np.ascontiguousarray(np.asarray(a, dtype=np.float32))
    x_prompt, x_sample, cache_k, cache_v = f(x_prompt), f(x_sample), f(cache_k), f(cache_v)
    state_pool, state_conv = f(state_pool), f(state_conv)
    B = 8
    cos_t, sin_t, invc, invw = _host_consts()

    def pc(v):
        return np.asarray(v, np.float32).reshape(2, 128).T
    pvec = np.zeros((2, 128, NPV), np.float32)
    plw = np.zeros((2, 128, 2, 128), np.float32)
    pool_w = f(pool_w); conv_dw = f(conv_dw)
    for l in range(2):
        pvec[l, :, 0:2] = pc(pool_scale[l])
        pvec[l, :, 2:4] = pc(conv_dw_b[l])
        pvec[l, :, 4:6] = pc(conv_ln_g[l])
        pvec[l, :, 6:8] = pc(conv_ln_b[l])
        pvec[l, :, 8] = np.asarray(diff_norm_g[l], np.float32)
        pvec[l, :, 9:11] = invw
        dwl = conv_dw[l].reshape(31, 2, 128)
        pvec[l, :, 11:73] = dwl.transpose(2, 1, 0).reshape(128, 62)
        for j in range(2):
            for hf in range(2):
                plw[l, hf * 64:(hf + 1) * 64, j, hf * 64:(hf + 1) * 64] = pool_w[l, 2 * j + hf]
    common = dict(
        w_in=f(w_in), w_out=f(w_out), w_gu=f(w_gate_up), w_dn=f(w_down), conv_pw=f(conv_pw), plw=plw,
        gmix=f(norm_mix_g), gffn=f(norm_ffn_g), gfin=f(final_norm_g).reshape(1, 1024),
        pvec=pvec, lq=f(lambda_qk).reshape(2, 256),
        ident=np.eye(128, dtype=np.float32), cos_t=cos_t, sin_t=sin_t, invc=invc,
    )
    in_maps = []
    for b in range(B):
        m = dict(common)
        m["xp"] = x_prompt[b]
        m["xs"] = x_sample[b]
        m["ck"] = np.ascontiguousarray(cache_k[:, b].reshape(2, 2048, 512))
        m["cv"] = np.ascontiguousarray(cache_v[:, b].reshape(2, 2048, 512))
        m["stp"] = np.ascontiguousarray(state_pool[:, b].reshape(2, 15, 2, 128).transpose(0, 3, 2, 1))
        m["stc"] = np.ascontiguousarray(state_conv[:, b].reshape(2, 30, 2, 128).transpose(0, 3, 2, 1))
        in_maps.append(m)
    if "nc" not in _NC_CACHE:
        _NC_CACHE["nc"] = build_program()
    nc = _NC_CACHE["nc"]
    res = run_bass_kernel_spmd(nc, in_maps, core_ids=list(range(B)))
    R = res.results

    def st(name):
        return np.stack([np.asarray(R[b][name], np.float32) for b in range(B)], axis=0)
    y_prompt = st("yp")
    y_sample = st("ys")
    nk_p = st("nkp").transpose(1, 0, 2, 3).reshape(2, B, 2048, 4, 2, 64)
    nv_p = st("nvp").transpose(1, 0, 2, 3).reshape(2, B, 2048, 4, 128)
    nk_s = st("nks").transpose(1, 0, 2, 3).reshape(2, B, 32, 4, 2, 64)
    nv_s = st("nvs").transpose(1, 0, 2, 3).reshape(2, B, 32, 4, 128)

    def unT(name, T):
        a = st(name)
        return np.ascontiguousarray(a.transpose(1, 0, 4, 3, 2).reshape(2, B, T, 256))
    np_p = unT("npp", 15); nc_p = unT("ncp", 30); np_s = unT("nps", 15); nc_s = unT("ncs", 30)
    return (y_prompt, y_sample, np.ascontiguousarray(nk_p), np.ascontiguousarray(nv_p), np_p, nc_p,
            np.ascontiguousarray(nk_s), np.ascontiguousarray(nv_s), np_s, nc_s)
```
